# Optimizing a Trainium2 kernel written in Bass

```python
import math
import jax
import jax.numpy as jnp
from jax import lax
import numpy as np

D_MODEL = 1024
BATCH = 16
SEQ = 2048
DEPTH = 4

N_EVEN = (DEPTH + 1) // 2
N_ODD = DEPTH // 2
GRID_W = 64
CONV_DIM = 512
CONV_WIDTH = 31
SSM_DIM = 512
SSM_GROUP = 16
SSM_GROUPS = SSM_DIM // SSM_GROUP
SSM_STATE = 64
MIX_DIM = CONV_DIM + SSM_DIM
AB_IN_DIM = 2 * CONV_DIM + SSM_DIM
NA_HEADS = 16
NA_HEAD_DIM = D_MODEL // NA_HEADS
NA_WIN_ROWS = 8
NA_WIN_COLS = 16
FFN_DIM = -(-(8 * D_MODEL) // (3 * 256)) * 256
NORM_EPS = 1e-6
LN_EPS = 1e-5
MASK_VALUE = -1e30

kernel_name = 'hybrid_conv_s5_natten_encoder'


def rms_norm(x, gain):
    xf = x.astype(jnp.float32)
    y = xf * lax.rsqrt(jnp.mean(xf * xf, axis=-1, keepdims=True) + NORM_EPS)
    return (y * gain.astype(jnp.float32)).astype(x.dtype)


def layer_norm(x, gain, bias):
    xf = x.astype(jnp.float32)
    xc = xf - jnp.mean(xf, axis=-1, keepdims=True)
    var = jnp.mean(xc * xc, axis=-1, keepdims=True)
    y = xc * lax.rsqrt(var + LN_EPS) * gain.astype(jnp.float32) + bias.astype(jnp.float32)
    return y.astype(x.dtype)


def swiglu_ffn(h, w_gate, w_up, w_down):
    return (jax.nn.silu(h @ w_gate) * (h @ w_up)) @ w_down


def conformer_conv(za, zg, conv_w, conv_b, ln_g, ln_b):
    a = za * jax.nn.sigmoid(zg)
    a = lax.conv_general_dilated(
        a, conv_w[:, None, :].astype(a.dtype), window_strides=(1,),
        padding=[(CONV_WIDTH // 2, CONV_WIDTH // 2)],
        dimension_numbers=('NWC', 'WIO', 'NWC'),
        feature_group_count=CONV_DIM) + conv_b
    return jax.nn.silu(layer_norm(a, ln_g, ln_b))


def _complex_affine_combine(e1, e2):
    a1r, a1i, b1r, b1i = e1
    a2r, a2i, b2r, b2i = e2
    return (a2r * a1r - a2i * a1i,
            a2r * a1i + a2i * a1r,
            a2r * b1r - a2i * b1i + b2r,
            a2r * b1i + a2i * b1r + b2i)


def s5_direction(u, lam_re, lam_im, log_step, b_re, b_im, c_re, c_im):
    dt = jnp.exp(log_step.astype(jnp.float32))[:, None]
    lr = lam_re.astype(jnp.float32)
    li = lam_im.astype(jnp.float32)
    mag = jnp.exp(lr * dt)
    ar = mag * jnp.cos(li * dt)
    ai = mag * jnp.sin(li * dt)
    den = lr * lr + li * li
    pr = ar - 1.0
    coef_re = (pr * lr + ai * li) / den
    coef_im = (ai * lr - pr * li) / den
    bu_re = jnp.einsum('bsgc,gnc->bsgn', u, b_re.astype(jnp.float32))
    bu_im = jnp.einsum('bsgc,gnc->bsgn', u, b_im.astype(jnp.float32))
    x_re = coef_re * bu_re - coef_im * bu_im
    x_im = coef_re * bu_im + coef_im * bu_re
    seq = u.shape[1]
    a_re = jnp.broadcast_to(ar, (1, seq) + ar.shape)
    a_im = jnp.broadcast_to(ai, (1, seq) + ai.shape)
    _, _, s_re, s_im = lax.associative_scan(
        _complex_affine_combine, (a_re, a_im, x_re, x_im), axis=1)
    return (jnp.einsum('bsgn,gcn->bsgc', s_re, c_re.astype(jnp.float32))
            - jnp.einsum('bsgn,gcn->bsgc', s_im, c_im.astype(jnp.float32)))


def s5_mixer(u, lam_re, lam_im, log_step, b_re, b_im, c_re, c_im, d_skip, glu_w, glu_b):
    bsz, seq, _ = u.shape
    uf = u.astype(jnp.float32)
    ug = uf.reshape(bsz, seq, SSM_GROUPS, SSM_GROUP)
    y_fwd = s5_direction(ug, lam_re[0], lam_im[0], log_step[0], b_re[0], b_im[0], c_re[0], c_im[0])
    y_bwd = jnp.flip(s5_direction(jnp.flip(ug, axis=1), lam_re[1], lam_im[1], log_step[1],
                                  b_re[1], b_im[1], c_re[1], c_im[1]), axis=1)
    y = (y_fwd + y_bwd).reshape(bsz, seq, SSM_DIM) + d_skip.astype(jnp.float32) * uf
    y = jax.nn.gelu(y.astype(u.dtype))
    return y * jax.nn.sigmoid(y @ glu_w + glu_b)


def conv_ssm_layer(h, w_in, conv_w, conv_b, ln_g, ln_b, lam_re, lam_im, log_step,
                   b_re, b_im, c_re, c_im, d_skip, glu_w, glu_b, w_out):
    z = h @ w_in
    za = z[..., :CONV_DIM]
    zg = z[..., CONV_DIM:2 * CONV_DIM]
    u = z[..., 2 * CONV_DIM:]
    ya = conformer_conv(za, zg, conv_w, conv_b, ln_g, ln_b)
    yb = s5_mixer(u, lam_re, lam_im, log_step, b_re, b_im, c_re, c_im, d_skip, glu_w, glu_b)
    return jnp.concatenate([ya, yb], axis=-1) @ w_out


def neighbourhood_attention(h, w_qkv, q_gain, k_gain, rpb, w_out):
    bsz, seq, _ = h.shape
    rows = seq // GRID_W
    kr = min(NA_WIN_ROWS, rows)
    qkv = (h @ w_qkv).reshape(bsz, seq, 3, NA_HEADS, NA_HEAD_DIM)
    q = rms_norm(qkv[:, :, 0], q_gain)
    k = rms_norm(qkv[:, :, 1], k_gain)
    v = qkv[:, :, 2]

    def to_grid(t):
        return t.reshape(bsz, rows, GRID_W, NA_HEADS, NA_HEAD_DIM).transpose(0, 3, 1, 2, 4)

    q, k, v = to_grid(q), to_grid(k), to_grid(v)
    row_ids = jnp.arange(rows)
    row_start = jnp.clip(row_ids - kr // 2, 0, rows - kr)
    cols = jnp.arange(GRID_W)
    col_start = jnp.clip(cols - NA_WIN_COLS // 2, 0, GRID_W - NA_WIN_COLS)
    col_mask = ((cols[None, :] >= col_start[:, None])
                & (cols[None, :] < col_start[:, None] + NA_WIN_COLS))
    col_idx = jnp.clip(cols[None, :] - cols[:, None], 1 - NA_WIN_COLS, NA_WIN_COLS - 1) + (NA_WIN_COLS - 1)
    scale = NA_HEAD_DIM ** -0.5

    def attend_row(args):
        r, rs = args
        q_r = lax.dynamic_index_in_dim(q, r, axis=2, keepdims=False)
        k_blk = lax.dynamic_slice_in_dim(k, rs, kr, axis=2)
        v_blk = lax.dynamic_slice_in_dim(v, rs, kr, axis=2)
        s = jnp.einsum('bhqd,bhjkd->bhqjk', q_r, k_blk).astype(jnp.float32) * scale
        row_idx = rs + jnp.arange(kr) - r + (NA_WIN_ROWS - 1)
        bias = rpb[:, row_idx][:, :, col_idx]
        s = s + jnp.transpose(bias, (0, 2, 1, 3)).astype(jnp.float32)[None]
        s = jnp.where(col_mask[None, None, :, None, :], s, MASK_VALUE)
        p = jax.nn.softmax(s.reshape(bsz, NA_HEADS, GRID_W, kr * GRID_W), axis=-1).reshape(s.shape)
        return jnp.einsum('bhqjk,bhjkd->bhqd', p.astype(v_blk.dtype), v_blk)

    out = lax.map(attend_row, (row_ids, row_start))
    out = out.transpose(1, 0, 3, 2, 4).reshape(bsz, seq, D_MODEL)
    return out @ w_out


def setup_inputs(seed: int = 0) -> dict:
    key = jax.random.key(seed)
    ks = jax.random.split(key, 27)

    def nrm(k, shape, scale):
        return scale * jax.random.normal(k, shape, jnp.float32)

    lam_im_base = jnp.pi * jnp.arange(SSM_STATE, dtype=jnp.float32)
    return {
        'x': nrm(ks[0], (BATCH, SEQ, D_MODEL), 1.0),
        'mix_norm': 1.0 + nrm(ks[1], (DEPTH, D_MODEL), 0.02),
        'ffn_norm': 1.0 + nrm(ks[2], (DEPTH, D_MODEL), 0.02),
        'ffn_w_gate': nrm(ks[3], (DEPTH, D_MODEL, FFN_DIM), D_MODEL ** -0.5),
        'ffn_w_up': nrm(ks[4], (DEPTH, D_MODEL, FFN_DIM), D_MODEL ** -0.5),
        'ffn_w_down': nrm(ks[5], (DEPTH, FFN_DIM, D_MODEL), FFN_DIM ** -0.5),
        'ab_w_in': nrm(ks[6], (N_EVEN, D_MODEL, AB_IN_DIM), D_MODEL ** -0.5),
        'conv_w': nrm(ks[7], (N_EVEN, CONV_WIDTH, CONV_DIM), CONV_WIDTH ** -0.5),
        'conv_b': nrm(ks[8], (N_EVEN, CONV_DIM), 0.01),
        'conv_ln_g': 1.0 + nrm(ks[9], (N_EVEN, CONV_DIM), 0.02),
        'conv_ln_b': nrm(ks[10], (N_EVEN, CONV_DIM), 0.01),
        'ssm_lambda_re': -0.5 + nrm(ks[11], (N_EVEN, 2, SSM_GROUPS, SSM_STATE), 0.01),
        'ssm_lambda_im': lam_im_base + nrm(ks[12], (N_EVEN, 2, SSM_GROUPS, SSM_STATE), 0.01),
        'ssm_log_step': jax.random.uniform(ks[13], (N_EVEN, 2, SSM_GROUPS), jnp.float32,
                                           minval=math.log(1e-3), maxval=math.log(1e-1)),
        'ssm_b_re': nrm(ks[14], (N_EVEN, 2, SSM_GROUPS, SSM_STATE, SSM_GROUP), (2 * SSM_GROUP) ** -0.5),
        'ssm_b_im': nrm(ks[15], (N_EVEN, 2, SSM_GROUPS, SSM_STATE, SSM_GROUP), (2 * SSM_GROUP) ** -0.5),
        'ssm_c_re': nrm(ks[16], (N_EVEN, 2, SSM_GROUPS, SSM_GROUP, SSM_STATE), (2 * SSM_STATE) ** -0.5),
        'ssm_c_im': nrm(ks[17], (N_EVEN, 2, SSM_GROUPS, SSM_GROUP, SSM_STATE), (2 * SSM_STATE) ** -0.5),
        'ssm_d': nrm(ks[18], (N_EVEN, SSM_DIM), 0.5),
        'ssm_glu_w': nrm(ks[19], (N_EVEN, SSM_DIM, SSM_DIM), SSM_DIM ** -0.5),
        'ssm_glu_b': nrm(ks[20], (N_EVEN, SSM_DIM), 0.01),
        'ab_w_out': nrm(ks[21], (N_EVEN, MIX_DIM, D_MODEL), MIX_DIM ** -0.5),
        'na_w_qkv': nrm(ks[22], (N_ODD, D_MODEL, 3 * D_MODEL), D_MODEL ** -0.5),
        'na_q_norm': 1.0 + nrm(ks[23], (N_ODD, NA_HEAD_DIM), 0.02),
        'na_k_norm': 1.0 + nrm(ks[24], (N_ODD, NA_HEAD_DIM), 0.02),
        'na_rpb': nrm(ks[25], (N_ODD, NA_HEADS, 2 * NA_WIN_ROWS - 1, 2 * NA_WIN_COLS - 1), 0.1),
        'na_w_out': nrm(ks[26], (N_ODD, D_MODEL, D_MODEL), D_MODEL ** -0.5),
    }


def reference(x, mix_norm, ffn_norm, ffn_w_gate, ffn_w_up, ffn_w_down, ab_w_in, conv_w, conv_b,
              conv_ln_g, conv_ln_b, ssm_lambda_re, ssm_lambda_im, ssm_log_step, ssm_b_re, ssm_b_im,
              ssm_c_re, ssm_c_im, ssm_d, ssm_glu_w, ssm_glu_b, ab_w_out, na_w_qkv, na_q_norm,
              na_k_norm, na_rpb, na_w_out):
    for layer in range(DEPTH):
        i = layer // 2
        h = rms_norm(x, mix_norm[layer])
        if layer % 2 == 0:
            h = conv_ssm_layer(h, ab_w_in[i], conv_w[i], conv_b[i], conv_ln_g[i], conv_ln_b[i],
                               ssm_lambda_re[i], ssm_lambda_im[i], ssm_log_step[i],
                               ssm_b_re[i], ssm_b_im[i], ssm_c_re[i], ssm_c_im[i],
                               ssm_d[i], ssm_glu_w[i], ssm_glu_b[i], ab_w_out[i])
        else:
            h = neighbourhood_attention(h, na_w_qkv[i], na_q_norm[i], na_k_norm[i],
                                        na_rpb[i], na_w_out[i])
        x = x + h
        x = x + swiglu_ffn(rms_norm(x, ffn_norm[layer]), ffn_w_gate[layer],
                           ffn_w_up[layer], ffn_w_down[layer])
    return x
```

```python
import math
import numpy as np
import concourse.bass as bass
import concourse.mybir as mybir
from concourse.bass_utils import run_bass_kernel_spmd

F32 = mybir.dt.float32
BF16 = mybir.dt.bfloat16
I32 = mybir.dt.int32
AF = mybir.ActivationFunctionType
ALU = mybir.AluOpType
AX = mybir.AxisListType

DBG = {}
NCORES = 8
D = 1024
SEQ = 2048
TOK = 2 * SEQ
FF = 2816
NM = FF // 128
DEPTH = 4
EPS = 1e-6


class Slot:
    __slots__ = ("name", "lw", "rs")

    def __init__(self, name):
        self.name = name
        self.lw = None
        self.rs = []


class Ins:
    __slots__ = ("eng", "fn", "deps", "is_dma", "sem", "semval", "inc", "idx", "force")

    def __init__(self, eng, fn, is_dma, sem, semval):
        self.eng = eng
        self.fn = fn
        self.deps = []
        self.is_dma = is_dma
        self.sem = sem
        self.semval = semval
        self.inc = False
        self.idx = -1
        self.force = False


ENGS = ("pe", "act", "dve", "pool", "sp")


class Prog:
    def __init__(self, nc):
        self.nc = nc
        self.streams = {e: [] for e in ENGS}
        self.dma_sems = {}
        self.dma_cnt = {}
        self.slots = []

    def slot(self, name):
        s = Slot(name)
        self.slots.append(s)
        return s

    def slots_n(self, name, n):
        return [self.slot(f"{name}{i}") for i in range(n)]

    def _add(self, ins, reads, writes):
        e = ins.eng
        best = {}
        dl = []

        def need(p):
            if p is None or p is ins:
                return
            if p.is_dma:
                if p not in dl:
                    dl.append(p)
            elif ins.is_dma or p.eng != e or ins.force:
                q = best.get(p.eng)
                if q is None or q.idx < p.idx:
                    best[p.eng] = p

        for s in reads:
            need(s.lw)
        for s in writes:
            need(s.lw)
            for r in s.rs:
                need(r)
        ins.deps = dl + list(best.values())
        for s in reads:
            rs = s.rs
            if rs and (not ins.is_dma) and (not rs[-1].is_dma) and rs[-1].eng == e:
                rs[-1] = ins
            else:
                rs.append(ins)
        for s in writes:
            s.lw = ins
            s.rs = []
        ins.idx = len(self.streams[e])
        self.streams[e].append(ins)
        return ins

    def op(self, eng, fn, reads=(), writes=(), force=False):
        ins = Ins(eng, fn, False, None, 0)
        ins.force = force
        return self._add(ins, reads, writes)

    def dma(self, eng, semname, fn, reads=(), writes=()):
        if semname not in self.dma_sems:
            self.dma_sems[semname] = self.nc.alloc_semaphore(name="d_" + semname)
            self.dma_cnt[semname] = 0
        self.dma_cnt[semname] += 16
        return self._add(Ins(eng, fn, True, self.dma_sems[semname], self.dma_cnt[semname]), reads, writes)

    def barrier(self):
        lasts = []
        for e in ENGS:
            st = self.streams[e]
            if st:
                lasts.append(st[-1])
        dmas = [i for e in ENGS for i in self.streams[e] if i.is_dma and not getattr(i, "_barr", False)]
        for e in ENGS:
            ins = Ins(e, None, False, None, 0)
            for p in lasts:
                if p.eng != e and not p.is_dma and p.fn is not None:
                    ins.deps.append(p)
            for p in dmas:
                ins.deps.append(p)
            ins.idx = len(self.streams[e])
            self.streams[e].append(ins)
        for s in self.slots:
            s.lw = None
            s.rs = []

    def emit(self):
        nc = self.nc
        for e in ENGS:
            for ins in self.streams[e]:
                for p in ins.deps:
                    if not p.is_dma:
                        p.inc = True
        rank = {}
        sems = {}
        for e in ENGS:
            sems[e] = nc.alloc_semaphore(name="s_" + e)
            c = 0
            last_real = None
            for ins in self.streams[e]:
                if ins.fn is not None and not ins.is_dma:
                    last_real = ins
                if ins.inc:
                    assert ins.fn is not None and not ins.is_dma
                    c += 1
                    rank[ins] = c
        handles = {"pe": nc.tensor, "act": nc.scalar, "dve": nc.vector, "pool": nc.gpsimd, "sp": nc.sync}
        streams = self.streams

        def replay(e, h):
            known = {}
            for ins in streams[e]:
                for p in ins.deps:
                    if p.is_dma:
                        key, val, sem = ("d", id(p.sem)), p.semval, p.sem
                    else:
                        key, val, sem = ("c", p.eng), rank[p], sems[p.eng]
                    if known.get(key, 0) >= val:
                        continue
                    known[key] = val
                    h.wait_ge(sem, val)
                if ins.fn is None:
                    continue
                r = ins.fn(h)
                if ins.is_dma:
                    r.then_inc(ins.sem, 16)
                elif ins.inc:
                    r.then_inc(sems[e], 1)

        with nc.Block() as block:
            @block.tensor
            def _(h):
                replay("pe", h)

            @block.scalar
            def _(h):
                replay("act", h)

            @block.vector
            def _(h):
                replay("dve", h)

            @block.gpsimd
            def _(h):
                replay("pool", h)

            @block.sync
            def _(h):
                replay("sp", h)


class Arena:
    def __init__(self, nc, name, nelem, dtype):
        self.t = nc.alloc_sbuf_tensor(name, [128, nelem], dtype)
        self.n = nelem
        self.off = 0

    def reset(self):
        self.off = 0

    def take(self, *shape):
        n = int(np.prod(shape))
        assert self.off + n <= self.n, (self.off, n, self.n)
        ap = self.t[:, self.off:self.off + n]
        self.off += n
        if len(shape) == 2:
            ap = ap.rearrange("p (a b) -> p a b", b=shape[1])
        elif len(shape) == 3:
            ap = ap.rearrange("p (a b c) -> p a b c", b=shape[1], c=shape[2])
        elif len(shape) == 4:
            ap = ap.rearrange("p (a b c d) -> p a b c d", b=shape[1], c=shape[2], d=shape[3])
        return ap


class Ctx:
    pass


def setup_ctx(nc):
    c = Ctx()
    c.nc = nc
    c.P = Prog(nc)
    c.fa = Arena(nc, "fa", 12800, F32)
    c.ba = Arena(nc, "ba", 78848, BF16)
    c.ps = [nc.alloc_psum_tensor(f"ps{i}", [128, 512], F32) for i in range(7)]
    c.psb = nc.alloc_psum_tensor("psb", [128, 1024], BF16)
    c.ps_slots = [c.P.slot(f"ps{i}") for i in range(7)]
    c.psb_slot = c.P.slot("psb")
    c.ident = nc.alloc_sbuf_tensor("ident_sb", [128, 128], BF16)
    c.ident_slot = c.P.slot("ident")
    c.bd = nc.alloc_sbuf_tensor("bd_sb", [128, 128], BF16)
    c.bd_slot = c.P.slot("bd")
    c.cst = nc.alloc_sbuf_tensor("cst_sb", [128, 8], F32)
    c.cst_slot = c.P.slot("cst")
    return c


def load_ident(c, ident_dram):
    P = c.P
    P.dma("pool", "ident", lambda h: h.dma_start(out=c.ident[:], in_=ident_dram[0]), writes=[c.ident_slot])
    P.dma("pool", "ident", lambda h: h.dma_start(out=c.bd[:], in_=ident_dram[1]), writes=[c.bd_slot])
    for i, v in enumerate((64.0 * EPS, EPS, 1e-5, 0.0, 1.0, math.pi / 2)):
        P.op("dve", lambda h, i=i, v=v: h.memset(c.cst[:, i:i + 1], v), writes=[c.cst_slot])


def norm_transpose(c, xt, xt_slot, gain_bc, gain_slot, xn, xn_slot, ss, ss_slot, hT, hT_slot, col0):
    P = c.P
    P.op("act", lambda h: h.activation(out=xn, in_=xt, func=AF.Square, accum_out=ss[:, 0:1]),
         reads=[xt_slot], writes=[xn_slot, ss_slot])
    P.op("dve", lambda h: h.tensor_scalar(out=ss[:, 1:2], in0=ss[:, 0:1], scalar1=1.0 / D, scalar2=EPS,
                                           op0=ALU.mult, op1=ALU.add), reads=[ss_slot], writes=[ss_slot])
    P.op("act", lambda h: h.activation(out=ss[:, 2:3], in_=ss[:, 1:2], func=AF.Sqrt), reads=[ss_slot], writes=[ss_slot])
    P.op("dve", lambda h: h.reciprocal(out=ss[:, 3:4], in_=ss[:, 2:3]), reads=[ss_slot], writes=[ss_slot])
    P.op("dve", lambda h: h.scalar_tensor_tensor(out=xn, in0=xt, scalar=ss[:, 3:4], in1=gain_bc,
                                                  op0=ALU.mult, op1=ALU.mult),
         reads=[xt_slot, ss_slot, gain_slot], writes=[xn_slot], force=True)
    for k in range(8):
        P.op("pe", lambda h, k=k: h.transpose(out=c.psb[:, k * 128:(k + 1) * 128], in_=xn[:, k * 128:(k + 1) * 128],
                                              identity=c.ident[:]),
             reads=[xn_slot, c.ident_slot], writes=[c.psb_slot])
    P.op("act", lambda h: h.copy(out=hT[:, :, col0:col0 + 128],
                                 in_=c.psb[:, :].rearrange("p (k t) -> p k t", t=128)),
         reads=[c.psb_slot], writes=[hT_slot])


def ffn_stage(c, xin, xin_slot, xout, xout_slot, wg, wu, wd, gain_row, ntok=TOK, TB=1024):
    P = c.P
    fa, ba = c.fa, c.ba
    fa.reset()
    ba.reset()
    NT = TB // 128
    Wd = ba.take(NM, 1024)
    hT = ba.take(8, TB)
    actT = ba.take(NM, TB)
    wgb = [ba.take(8, 128) for _ in range(2)]
    wub = [ba.take(8, 128) for _ in range(2)]
    xn = [ba.take(1, 1024)[:, 0, :] for _ in range(2)]
    xt = [fa.take(1, 1024)[:, 0, :] for _ in range(NT)]
    ost = [fa.take(1, 1024)[:, 0, :] for _ in range(2)]
    gbc = fa.take(1, 1024)[:, 0, :]
    sg = [fa.take(1, 512)[:, 0, :] for _ in range(2)]
    ss = [fa.take(1, 4)[:, 0, :] for _ in range(2)]

    s_Wd = P.slot("Wd")
    s_hT = [P.slot(f"hT{t}") for t in range(NT)]
    s_act = [[P.slot(f"act{m}_{h}") for h in range(TB // 512)] for m in range(NM)]
    s_wg = P.slots_n("wg", 2)
    s_wu = P.slots_n("wu", 2)
    s_xn = P.slots_n("xn", 2)
    s_xt = P.slots_n("xt", NT)
    s_ost = P.slots_n("ost", 2)
    s_gbc = P.slot("gbc")
    s_sg = P.slots_n("sg", 2)
    s_ss = P.slots_n("ss", 2)

    P.dma("sp", "gbc", lambda h: h.dma_start(out=gbc, in_=gain_row.partition_broadcast(128)), writes=[s_gbc])
    wdv = wd.rearrange("(m p) o -> p m o", p=128)
    for q in range(2):
        P.dma("pool", "Wd", lambda h, q=q: h.dma_start(out=Wd[:, q * 11:(q + 1) * 11, :], in_=wdv[:, q * 11:(q + 1) * 11, :]),
              writes=[s_Wd])

    nblk = ntok // TB
    widx = 0
    oidx = 0
    nidx = 0
    uidx = 0
    for b in range(nblk):
        for t in range(NT):
            r0 = b * TB + t * 128
            P.dma("sp", f"xt{t}", lambda h, t=t, r0=r0: h.dma_start(out=xt[t], in_=xin[r0:r0 + 128, :]),
                  reads=[xin_slot], writes=[s_xt[t]])
        for t in range(NT):
            j = nidx % 2
            nidx += 1
            norm_transpose(c, xt[t], s_xt[t], gbc, s_gbc, xn[j], s_xn[j], ss[j], s_ss[j], hT, s_hT[t], t * 128)
        for m in range(NM):
            j = widx % 2
            widx += 1
            P.dma("pool", f"wg{j}", lambda h, j=j, m=m: h.dma_start(out=wgb[j], in_=wg[m]), writes=[s_wg[j]])
            P.dma("pool", f"wu{j}", lambda h, j=j, m=m: h.dma_start(out=wub[j], in_=wu[m]), writes=[s_wu[j]])
            for hh in range(TB // 512):
                u = uidx % 2
                uidx += 1
                pg, pu = c.ps[2 * u], c.ps[2 * u + 1]
                sgs, sus = c.ps_slots[2 * u], c.ps_slots[2 * u + 1]
                hs = s_hT[hh * 4:(hh + 1) * 4]
                for k in range(8):
                    P.op("pe", lambda h, j=j, k=k, hh=hh, pg=pg: h.matmul(
                        pg[:, :], lhsT=wgb[j][:, k, :], rhs=hT[:, k, hh * 512:(hh + 1) * 512],
                        start=(k == 0), stop=(k == 7)), reads=[s_wg[j]] + hs, writes=[sgs])
                for k in range(8):
                    P.op("pe", lambda h, j=j, k=k, hh=hh, pu=pu: h.matmul(
                        pu[:, :], lhsT=wub[j][:, k, :], rhs=hT[:, k, hh * 512:(hh + 1) * 512],
                        start=(k == 0), stop=(k == 7)), reads=[s_wu[j]] + hs, writes=[sus])
                P.op("act", lambda h, u=u, pg=pg: h.activation(out=sg[u], in_=pg[:, :], func=AF.Silu),
                     reads=[sgs], writes=[s_sg[u]])
                P.op("dve", lambda h, u=u, pu=pu, m=m, hh=hh: h.tensor_tensor(
                    out=actT[:, m, hh * 512:(hh + 1) * 512], in0=sg[u], in1=pu[:, :], op=ALU.mult),
                    reads=[s_sg[u], sus], writes=[s_act[m][hh]])
        for t in range(NT):
            o = oidx % 2
            oidx += 1
            r0 = b * TB + t * 128
            for half in range(2):
                pd = c.ps[4 + half]
                sd = c.ps_slots[4 + half]
                for m in range(NM):
                    P.op("pe", lambda h, m=m, t=t, half=half, pd=pd: h.matmul(
                        pd[:, :], lhsT=actT[:, m, t * 128:(t + 1) * 128], rhs=Wd[:, m, half * 512:(half + 1) * 512],
                        start=(m == 0), stop=(m == NM - 1)),
                        reads=[s_act[m][t // 4], s_Wd], writes=[sd])
                P.op("dve", lambda h, o=o, t=t, half=half, pd=pd: h.tensor_tensor(
                    out=ost[o][:, half * 512:(half + 1) * 512], in0=pd[:, :], in1=xt[t][:, half * 512:(half + 1) * 512],
                    op=ALU.add), reads=[sd, s_xt[t]], writes=[s_ost[o]])
            P.dma("sp", f"ost{o}", lambda h, o=o, r0=r0: h.dma_start(out=xout[r0:r0 + 128, :], in_=ost[o]),
                  reads=[s_ost[o]], writes=[xout_slot])
    P.barrier()


def na_stage(c, xin, xin_slot, xout, xout_slot, wqkv, wout, biasT, qkg, gain_row, tok0):
    P = c.P
    fa, ba = c.fa, c.ba
    fa.reset()
    ba.reset()
    hT = ba.take(8, SEQ)
    attnT = ba.take(8, SEQ)
    qT = ba.take(2, SEQ)
    kT = ba.take(2, SEQ)
    Vaug = ba.take(2, 16, 4, 65)
    atok = ba.take(32, 256)
    bT = ba.take(4, 14, 64)
    wq = ba.take(8, 768)
    PT = [ba.take(1, 256)[:, 0, :] for _ in range(2)]
    xn = [ba.take(1, 1024)[:, 0, :] for _ in range(2)]
    sq = [ba.take(1, 512)[:, 0, :] for _ in range(2)]
    xt = [fa.take(1, 1024)[:, 0, :] for _ in range(2)]
    ost = [fa.take(1, 1024)[:, 0, :] for _ in range(2)]
    gbc = fa.take(1, 1024)[:, 0, :]
    raw = [fa.take(1, 512)[:, 0, :] for _ in range(2)]
    rinv = [fa.take(1, 512)[:, 0, :] for _ in range(2)]
    ss = [fa.take(1, 4)[:, 0, :] for _ in range(2)]
    rec = [fa.take(1, 2)[:, 0, :] for _ in range(2)]
    gq = fa.take(1, 2)[:, 0, :]

    s_hT = P.slots_n("n_hT", 16)
    s_attnT = P.slots_n("n_attnT", 8)
    s_qT = P.slot("n_qT")
    s_kT = P.slot("n_kT")
    s_V = P.slot("n_V")
    s_atok = P.slot("n_atok")
    s_bT = P.slot("n_bT")
    s_wq = P.slot("n_wq")
    s_PT = P.slots_n("n_PT", 2)
    s_xn = P.slots_n("n_xn", 2)
    s_sq = P.slots_n("n_sq", 2)
    s_xt = P.slots_n("n_xt", 2)
    s_ost = P.slots_n("n_ost", 2)
    s_gbc = P.slot("n_gbc")
    s_raw = P.slots_n("n_raw", 2)
    s_rinv = P.slots_n("n_rinv", 2)
    s_ss = P.slots_n("n_ss", 2)
    s_rec = P.slots_n("n_rec", 2)
    s_gq = P.slot("n_gq")
    ps, pss = c.ps, c.ps_slots

    P.dma("sp", "gbc", lambda h: h.dma_start(out=gbc, in_=gain_row.partition_broadcast(128)), writes=[s_gbc])
    P.dma("sp", "gq", lambda h: h.dma_start(out=gq, in_=qkg), writes=[s_gq])
    P.op("dve", lambda h: h.memset(Vaug[:, :, :, :, 64:65], 1.0), writes=[s_V])

    for t in range(16):
        j = t % 2
        r0 = tok0 + t * 128
        P.dma("sp", f"nxt{j}", lambda h, j=j, r0=r0: h.dma_start(out=xt[j], in_=xin[r0:r0 + 128, :]),
              reads=[xin_slot], writes=[s_xt[j]])
        norm_transpose(c, xt[j], s_xt[j], gbc, s_gbc, xn[j], s_xn[j], ss[j], s_ss[j], hT, s_hT[t], t * 128)

    ui = 0
    for hg in DBG.get('hgs', range(4)):
        for part in range(3):
            P.dma("pool", "nwq", lambda h, hg=hg, part=part: h.dma_start(
                out=wq[:, :, part * 256:(part + 1) * 256], in_=wqkv[hg][:, :, part * 256:(part + 1) * 256]),
                writes=[s_wq])
        P.dma("pool", "nbT", lambda h, hg=hg: h.dma_start(out=bT, in_=biasT[hg]), writes=[s_bT])
        for part in DBG.get('qk', range(2)):
            dst, s_dst = (qT, s_qT) if part == 0 else (kT, s_kT)
            for cc in range(2):
                for tb in range(4):
                    u = ui % 2
                    ui += 1
                    pq, spq = ps[u], pss[u]
                    pn, spn = ps[2 + u], pss[2 + u]
                    for k in range(8):
                        P.op("pe", lambda h, k=k, part=part, cc=cc, tb=tb, pq=pq: h.matmul(
                            pq[:, :], lhsT=wq[:, k, part * 256 + cc * 128: part * 256 + cc * 128 + 128],
                            rhs=hT[:, k, tb * 512:(tb + 1) * 512], start=(k == 0), stop=(k == 7)),
                            reads=[s_wq] + s_hT[tb * 4:(tb + 1) * 4], writes=[spq])
                    P.op("act", lambda h, u=u, pq=pq: h.activation(out=sq[u], in_=pq[:, :], func=AF.Square),
                         reads=[spq], writes=[s_sq[u]])
                    P.op("dve", lambda h, u=u, pq=pq: h.tensor_copy(out=raw[u], in_=pq[:, :]),
                         reads=[spq, s_sq[u]], writes=[s_raw[u]])
                    P.op("pe", lambda h, u=u, pn=pn: h.matmul(pn[:, :], lhsT=c.bd[:], rhs=sq[u], start=True, stop=True),
                         reads=[s_sq[u], c.bd_slot], writes=[spn])
                    if part == 0:
                        P.op("act", lambda h, u=u, pn=pn: h.activation(out=rinv[u], in_=pn[:, :], func=AF.Ln,
                                                                       bias=c.cst[:, 0:1], scale=1.0),
                             reads=[spn, c.cst_slot], writes=[s_rinv[u]])
                    else:
                        P.op("act", lambda h, u=u, pn=pn: h.activation(out=rinv[u], in_=pn[:, :], func=AF.Ln,
                                                                       bias=c.cst[:, 1:2], scale=1.0 / 64),
                             reads=[spn, c.cst_slot], writes=[s_rinv[u]])
                    P.op("act", lambda h, u=u: h.activation(out=rinv[u], in_=rinv[u], func=AF.Exp, scale=-0.5),
                         reads=[s_rinv[u]], writes=[s_rinv[u]])
                    P.op("dve", lambda h, u=u, part=part, cc=cc, tb=tb, dst=dst: h.scalar_tensor_tensor(
                        out=dst[:, cc, tb * 512:(tb + 1) * 512], in0=raw[u], scalar=gq[:, part:part + 1], in1=rinv[u],
                        op0=ALU.mult, op1=ALU.mult), reads=[s_raw[u], s_rinv[u], s_gq], writes=[s_dst])
        for par in DBG.get('vpar', range(2)):
            for i in range(16 - par):
                u = ui % 2
                ui += 1
                pv, spv = ps[u], pss[u]
                t0 = par * 64 + i * 128
                for k in range(8):
                    P.op("pe", lambda h, k=k, t0=t0, pv=pv: h.matmul(
                        pv[:, 0:256], lhsT=hT[:, k, t0:t0 + 128], rhs=wq[:, k, 512:768], start=(k == 0), stop=(k == 7)),
                        reads=[s_wq] + s_hT[t0 // 128:(t0 + 127) // 128 + 1], writes=[spv])
                P.op("act", lambda h, par=par, i=i, pv=pv: h.copy(
                    out=Vaug[:, par, i, :, 0:64], in_=pv[:, 0:256].rearrange("p (a b) -> p a b", b=64)),
                    reads=[spv], writes=[s_V])
        for r in DBG.get('rows', range(32)):
            rs = min(max(r - 4, 0), 24)
            par = rs % 2
            tile0 = rs // 2
            for hh in range(4):
                u = ui % 2
                ui += 1
                cc = hh // 2
                pb = (hh % 2) * 64
                sc, ssc = ps[4 + u], pss[4 + u]
                po, spo = ps[2 + u], pss[2 + u]
                for j in range(4):
                    rho = rs + 2 * j - r + 7
                    P.op("pe", lambda h, j=j, pb=pb, cc=cc, rs=rs, r=r, sc=sc: h.matmul(
                        sc[:, j * 64:(j + 1) * 64], lhsT=kT[pb:pb + 64, cc, rs * 64 + j * 128: rs * 64 + j * 128 + 128],
                        rhs=qT[pb:pb + 64, cc, r * 64:(r + 1) * 64], start=True, stop=False),
                        reads=[s_kT, s_qT], writes=[ssc])
                    P.op("pe", lambda h, j=j, hh=hh, rho=rho, sc=sc: h.matmul(
                        sc[:, j * 64:(j + 1) * 64], lhsT=c.ident[:], rhs=bT[:, hh, rho, :], start=False, stop=True),
                        reads=[s_bT, c.ident_slot], writes=[ssc])
                P.op("act", lambda h, u=u, sc=sc: h.activation(out=PT[u], in_=sc[:, 0:256], func=AF.Exp),
                     reads=[ssc], writes=[s_PT[u]])
                for j in range(4):
                    P.op("pe", lambda h, j=j, u=u, par=par, tile0=tile0, hh=hh, po=po: h.matmul(
                        po[0:64, 0:65], lhsT=PT[u][:, j * 64:(j + 1) * 64], rhs=Vaug[:, par, tile0 + j, hh, :],
                        start=(j == 0), stop=(j == 3)), reads=[s_PT[u], s_V], writes=[spo])
                P.op("dve", lambda h, u=u, po=po: h.reciprocal(out=rec[u][0:64, 0:1], in_=po[0:64, 64:65]),
                     reads=[spo], writes=[s_rec[u]])
                P.op("dve", lambda h, u=u, po=po, r=r, hh=hh: h.tensor_scalar(
                    out=atok[0:64, r, hh * 64:(hh + 1) * 64], in0=po[0:64, 0:64], scalar1=rec[u][0:64, 0:1], scalar2=None,
                    op0=ALU.mult), reads=[spo, s_rec[u]], writes=[s_atok], force=True)
        for c2 in DBG.get('c2s', range(2)):
            for rb in range(2):
                for rr in range(16):
                    r = rb * 16 + rr
                    P.op("pe", lambda h, rr=rr, r=r, c2=c2: h.transpose(
                        out=c.psb[:, rr * 64:(rr + 1) * 64], in_=atok[0:64, r, c2 * 128:(c2 + 1) * 128],
                        identity=c.ident[0:64, 0:64]), reads=[s_atok, c.ident_slot], writes=[c.psb_slot])
                P.op("act", lambda h, hg=hg, c2=c2, rb=rb: h.copy(
                    out=attnT[:, hg * 2 + c2, rb * 1024:(rb + 1) * 1024], in_=c.psb[:, :]),
                    reads=[c.psb_slot], writes=[s_attnT[hg * 2 + c2]])
    if DBG.get('dump') is not None:
        dbg = DBG['dump']
        s_dbg = P.slot("dbg")
        P.dma("pool", "dbg", lambda h: h.dma_start(out=dbg[:, 0:4096], in_=qT.rearrange("p a b -> p (a b)")), reads=[s_qT], writes=[s_dbg])
        P.dma("pool", "dbg", lambda h: h.dma_start(out=dbg[:, 4096:8192], in_=kT.rearrange("p a b -> p (a b)")), reads=[s_kT], writes=[s_dbg])
        P.dma("pool", "dbg", lambda h: h.dma_start(out=dbg[:, 8192:16384], in_=atok.rearrange("p a b -> p (a b)")), reads=[s_atok], writes=[s_dbg])
    P.barrier()
    Wo = hT[:, :, 0:1024]
    s_Wo = P.slot("n_Wo")
    for q in range(2):
        P.dma("pool", "nWo", lambda h, q=q: h.dma_start(out=Wo[:, q * 4:(q + 1) * 4, :], in_=wout[:, q * 4:(q + 1) * 4, :]),
              writes=[s_Wo])
    for t in range(16):
        j = t % 2
        r0 = tok0 + t * 128
        P.dma("sp", f"nxt{j}", lambda h, j=j, r0=r0: h.dma_start(out=xt[j], in_=xin[r0:r0 + 128, :]),
              reads=[xin_slot], writes=[s_xt[j]])
        for half in range(2):
            pd, sd = ps[half], pss[half]
            for k in range(8):
                P.op("pe", lambda h, k=k, t=t, half=half, pd=pd: h.matmul(
                    pd[:, :], lhsT=attnT[:, k, t * 128:(t + 1) * 128], rhs=Wo[:, k, half * 512:(half + 1) * 512],
                    start=(k == 0), stop=(k == 7)), reads=[s_attnT[k], s_Wo], writes=[sd])
            P.op("dve", lambda h, j=j, half=half, pd=pd: h.tensor_tensor(
                out=ost[j][:, half * 512:(half + 1) * 512], in0=pd[:, :], in1=xt[j][:, half * 512:(half + 1) * 512],
                op=ALU.add), reads=[sd, s_xt[j]], writes=[s_ost[j]])
        P.dma("sp", f"nost{j}", lambda h, j=j, r0=r0: h.dma_start(out=xout[r0:r0 + 128, :], in_=ost[j]),
              reads=[s_ost[j]], writes=[xout_slot])
    P.barrier()


def cv(ap, p0, npart, off, dims):
    n = ap.ap[0][0]
    return bass.AP(tensor=ap.tensor, offset=ap.offset + p0 * n + off, ap=[[n, npart]] + [list(d) for d in dims])


GELU_FUNC = [None]


def ab_stage(c, xin, xin_slot, xout, xout_slot, W, gain_row, ntok):
    P = c.P
    fa, ba = c.fa, c.ba
    fa.reset()
    ba.reset()
    NK = 257
    R1 = ba.take(1, 16448)[:, 0, :]
    uT = ba.take(4, SEQ)
    aT = ba.take(4, SEQ + 30)
    yaT = ba.take(4, SEQ)
    Ht = ba.take(32, 2, 128)
    Gt = ba.take(32, 2, 128)
    Mt = ba.take(32, 128)
    selin = ba.take(2, 8, 128)
    selout = ba.take(8, 8, 128)
    wsl = [ba.take(8, 128) for _ in range(2)]
    gluw = ba.take(4, 512)
    xn = [ba.take(1, 1024)[:, 0, :] for _ in range(2)]
    hT = R1[:, 0:16384].rearrange("p (a b) -> p a b", b=SEQ)
    cdiag = R1[:, 0:15872].rearrange("p (q k m) -> p q k m", k=31, m=128)
    XS = R1
    Wo = R1[:, 0:8192].rearrange("p (a b) -> p a b", b=1024)
    UY = aT[:, :, :].rearrange("p a b -> p (a b)")[:, 0:8192].rearrange("p (g k) -> p g k", k=256)
    Qt = R1[:, 0:8192].rearrange("p (r e) -> p r e", r=2)
    Pt = R1[:, 8192:16384].rearrange("p (r e) -> p r e", r=2)
    Hp = uT[:, :, :].rearrange("p a b -> p (a b)").rearrange("p (r e) -> p r e", r=2)
    gbc = fa.take(1, 1024)[:, 0, :]
    lam = fa.take(3, 32)
    Apw = fa.take(2, 32, 32)
    msk = fa.take(2, 128)
    dcol = fa.take(1, 32)[:, 0, :]
    cw = fa.take(4, 31)
    vec = fa.take(1, 20)[:, 0, :]
    ones32 = fa.take(1, 128)[:, 0, :]
    etab = fa.take(1, 32)[:, 0, :]
    TA = fa.take(1, 64)[:, 0, :]
    TB = fa.take(1, 64)[:, 0, :]
    Sf = fa.take(2, 64)
    st1 = fa.take(1, 64)[:, 0, :]
    st2 = fa.take(1, 64)[:, 0, :]
    sm = fa.take(12, 32)
    ss = [fa.take(1, 4)[:, 0, :] for _ in range(2)]
    tmpA = fa.take(1, 128)[:, 0, :]
    tmpB = fa.take(1, 128)[:, 0, :]
    Dreg = fa.take(1, 7168)[:, 0, :]
    Bp = Dreg[:, 0:1024]
    Cp = Dreg[:, 1024:2048]
    Bb = Dreg[:, 2048:3072]
    t1 = Dreg[:, 3072:4096]
    t2 = Dreg[:, 4096:5120]
    w5 = [Dreg[:, 5120:6144], Dreg[:, 6144:7168], Dreg[:, 2048:3072]]
    xt = [Dreg[:, 0:1024], Dreg[:, 1024:2048]]
    ost = [Dreg[:, 2048:3072], Dreg[:, 3072:4096]]
    sgm = [Dreg[:, 0:512], Dreg[:, 512:1024]]
    acv = Dreg[:, 0:2048].rearrange("p (q t) -> p q t", t=512)
    sqv = [Dreg[:, 2048:2560], Dreg[:, 2560:3072]]
    meanv = Dreg[:, 3072:3584]
    rstdv = Dreg[:, 3584:4096]
    tmpv = Dreg[:, 4096:4608]
    ynv = [Dreg[:, 4608:5120], Dreg[:, 5120:5632]]
    ps, pss = c.ps, c.ps_slots
    ident = c.ident

    S = lambda n: P.slot("ab_" + n)
    s_tab = S("tab")
    s_D = S("Dreg")
    s_R1 = S("R1")
    s_uT = S("uT")
    s_aT = S("aT")
    s_ya = S("yaT")
    s_H, s_G, s_M = S("H"), S("G"), S("M")
    s_sel = S("sel")
    s_wsl = [S("wsl0"), S("wsl1")]
    s_gluw = S("gluw")
    s_xn = [S("xn0"), S("xn1")]
    s_gbc = S("gbc")
    s_ss = [S("ss0"), S("ss1")]
    s_tmp = S("tmpAB")

    def dv(fn, r, w, force=True):
        P.op("dve", fn, reads=r, writes=w, force=force)

    def ac(fn, r, w, force=True):
        P.op("act", fn, reads=r, writes=w, force=force)

    P.dma("sp", "gbc", lambda h: h.dma_start(out=gbc, in_=gain_row.partition_broadcast(128)), writes=[s_gbc])
    for dst, src in ((lam, W["lam"]), (msk, W["mask"]), (dcol, W["dcol"]), (cw, W["cw"]), (vec, W["vec"]),
                     (ones32, W["ones"]), (etab, W["etab"]), (Bp, W["B"]), (Cp, W["C"])):
        P.dma("sp", "abtab", lambda h, dst=dst, src=src: h.dma_start(out=dst, in_=src), writes=[s_tab])
    P.dma("pool", "absel", lambda h: h.dma_start(out=selin, in_=W["selin"]), writes=[s_sel])
    for q in range(2):
        P.dma("pool", "absel", lambda h, q=q: h.dma_start(out=selout[:, q * 4:(q + 1) * 4], in_=W["selout"][:, q * 4:(q + 1) * 4]),
              writes=[s_sel])
    P.dma("pool", "abglu", lambda h: h.dma_start(out=gluw, in_=W["gluw"]), writes=[s_gluw])

    T = [s_tab]
    sm_ = lambda i: sm[:, i, :]
    m_dt, m_lrd, m_lid, m_pr, m_den, m_cre, m_cim, m_x, m_y = (sm_(i) for i in range(9))
    lr, li, ls = lam[:, 0, :], lam[:, 1, :], lam[:, 2, :]
    ac(lambda h: h.activation(out=m_dt, in_=ls, func=AF.Exp), T, T)
    dv(lambda h: h.tensor_tensor(out=m_lrd, in0=lr, in1=m_dt, op=ALU.mult), T, T)
    dv(lambda h: h.tensor_tensor(out=m_lid, in0=li, in1=m_dt, op=ALU.mult), T, T)
    bc_g = lambda a: cv(a, 0, 128, 0, [[0, 32], [1, 32]])
    bc_e = cv(etab, 0, 128, 0, [[1, 32], [0, 32]])
    v3 = lambda a: cv(a, 0, 128, 0, [[32, 32], [1, 32]])
    argm, ang, kf = w5
    ki = t1[:, 0:1024].bitcast(I32)
    dv(lambda h: h.tensor_tensor(out=v3(argm), in0=bc_g(m_lrd), in1=bc_e, op=ALU.mult), T, T)
    ac(lambda h: h.activation(out=argm, in_=argm, func=AF.Exp), T, T)
    dv(lambda h: h.tensor_tensor(out=v3(ang), in0=bc_g(m_lid), in1=bc_e, op=ALU.mult), T, T)
    trig = t2[:, 0:1024]
    for ri, shift in ((1, 0.0), (0, math.pi / 2)):
        src = ang
        if shift != 0.0:
            dv(lambda h: h.tensor_scalar(out=ang, in0=ang, scalar1=shift, scalar2=None, op0=ALU.add), T, T)
        dv(lambda h: h.tensor_scalar(out=kf, in0=ang, scalar1=1.0 / (2 * math.pi), scalar2=None, op0=ALU.mult), T, T)
        dv(lambda h: h.tensor_copy(out=ki, in_=kf), T, T)
        dv(lambda h: h.tensor_copy(out=kf, in_=ki), T, T)
        dv(lambda h: h.scalar_tensor_tensor(out=kf, in0=kf, scalar=-2 * math.pi, in1=ang, op0=ALU.mult, op1=ALU.add), T, T)
        ac(lambda h: h.activation(out=trig, in_=kf, func=AF.Sin), T, T)
        dv(lambda h, ri=ri: h.tensor_tensor(out=Apw[:, ri, :, :].rearrange("p a b -> p (a b)"), in0=argm, in1=trig, op=ALU.mult), T, T)
    for (p0_, j1) in ((0, 16), (64, 23)):
        hp = lambda a, p0_=p0_: cv(a, p0_, 64, 0, [[1, 32]])
        A1r = cv(Apw, p0_, 64, j1 * 32, [[1, 32]])
        A1i = cv(Apw, p0_, 64, 1024 + j1 * 32, [[1, 32]])
        lr_, li_ = hp(lr), hp(li)
        dv(lambda h, hp=hp, A1r=A1r: h.tensor_scalar(out=hp(m_pr), in0=A1r, scalar1=-1.0, scalar2=None, op0=ALU.add), T, T)
        dv(lambda h, hp=hp, lr_=lr_: h.tensor_tensor(out=hp(m_x), in0=lr_, in1=lr_, op=ALU.mult), T, T)
        dv(lambda h, hp=hp, li_=li_: h.tensor_tensor(out=hp(m_y), in0=li_, in1=li_, op=ALU.mult), T, T)
        dv(lambda h, hp=hp: h.tensor_tensor(out=hp(m_den), in0=hp(m_x), in1=hp(m_y), op=ALU.add), T, T)
        dv(lambda h, hp=hp: h.reciprocal(out=hp(m_den), in_=hp(m_den)), T, T)
        dv(lambda h, hp=hp, lr_=lr_: h.tensor_tensor(out=hp(m_x), in0=hp(m_pr), in1=lr_, op=ALU.mult), T, T)
        dv(lambda h, hp=hp, li_=li_, A1i=A1i: h.tensor_tensor(out=hp(m_y), in0=A1i, in1=li_, op=ALU.mult), T, T)
        dv(lambda h, hp=hp: h.tensor_tensor(out=hp(m_x), in0=hp(m_x), in1=hp(m_y), op=ALU.add), T, T)
        dv(lambda h, hp=hp: h.tensor_tensor(out=hp(m_cre), in0=hp(m_x), in1=hp(m_den), op=ALU.mult), T, T)
        dv(lambda h, hp=hp, lr_=lr_, A1i=A1i: h.tensor_tensor(out=hp(m_x), in0=A1i, in1=lr_, op=ALU.mult), T, T)
        dv(lambda h, hp=hp, li_=li_: h.tensor_tensor(out=hp(m_y), in0=hp(m_pr), in1=li_, op=ALU.mult), T, T)
        dv(lambda h, hp=hp: h.tensor_tensor(out=hp(m_x), in0=hp(m_x), in1=hp(m_y), op=ALU.subtract), T, T)
        dv(lambda h, hp=hp: h.tensor_tensor(out=hp(m_cim), in0=hp(m_x), in1=hp(m_den), op=ALU.mult), T, T)
    bcc = lambda a: cv(a, 0, 128, 0, [[1, 32], [0, 16]])
    g16 = lambda a, off: cv(a, 0, 128, off, [[16, 32], [1, 16]])
    for (o_off, a_, x_off, b_, y_off, op) in ((0, m_cre, 0, m_cim, 512, ALU.subtract), (512, m_cre, 512, m_cim, 0, ALU.add)):
        dv(lambda h, a_=a_, x_off=x_off: h.tensor_tensor(out=g16(t1, 0), in0=bcc(a_), in1=g16(Bp, x_off), op=ALU.mult), T, T)
        dv(lambda h, b_=b_, y_off=y_off: h.tensor_tensor(out=g16(t2, 0), in0=bcc(b_), in1=g16(Bp, y_off), op=ALU.mult), T, T)
        dv(lambda h, o_off=o_off, op=op: h.tensor_tensor(out=Bb[:, o_off:o_off + 512], in0=t1[:, 0:512], in1=t2[:, 0:512], op=op), T, T)

    def fam(dst, d_goff, d_rioff, V, negim, j0):
        np_, p0, js = 128, 0, 1
        for qq in range(4):
            Ar = cv(Apw, p0, np_, j0 * 32 + 8 * qq, [[1, 8], [js * 32, 8], [0, 16]])
            Ai = cv(Apw, p0, np_, 1024 + j0 * 32 + 8 * qq, [[1, 8], [js * 32, 8], [0, 16]])
            Vr = cv(V, p0, np_, 8 * qq * 16, [[16, 8], [0, 8], [1, 16]])
            Vi = cv(V, p0, np_, 512 + 8 * qq * 16, [[16, 8], [0, 8], [1, 16]])
            T1 = cv(t1, p0, np_, 0, [[128, 8], [16, 8], [1, 16]])
            T2 = cv(t2, p0, np_, 0, [[128, 8], [16, 8], [1, 16]])
            dre = cv(dst, p0, np_, 8 * qq * d_goff, [[d_goff, 8], [16, 8], [1, 16]])
            dim_ = cv(dst, p0, np_, 8 * qq * d_goff + d_rioff, [[d_goff, 8], [16, 8], [1, 16]])
            dv(lambda h, Ar=Ar, Vr=Vr, T1=T1: h.tensor_tensor(out=T1, in0=Ar, in1=Vr, op=ALU.mult), T, T)
            dv(lambda h, Ai=Ai, Vi=Vi, T2=T2: h.tensor_tensor(out=T2, in0=Ai, in1=Vi, op=ALU.mult), T, T)
            dv(lambda h, dre=dre, T1=T1, T2=T2: h.tensor_tensor(out=dre, in0=T1, in1=T2, op=ALU.subtract), T, T)
            dv(lambda h, Ar=Ar, Vi=Vi, T1=T1: h.tensor_tensor(out=T1, in0=Ar, in1=Vi, op=ALU.mult), T, T)
            dv(lambda h, Ai=Ai, Vr=Vr, T2=T2: h.tensor_tensor(out=T2, in0=Ai, in1=Vr, op=ALU.mult), T, T)
            if negim:
                dv(lambda h, T1=T1: h.tensor_scalar(out=T1, in0=T1, scalar1=-1.0, scalar2=None, op0=ALU.mult), T, T)
                dv(lambda h, dim_=dim_, T1=T1, T2=T2: h.tensor_tensor(out=dim_, in0=T1, in1=T2, op=ALU.subtract), T, T)
            else:
                dv(lambda h, dim_=dim_, T1=T1, T2=T2: h.tensor_tensor(out=dim_, in0=T1, in1=T2, op=ALU.add), T, T)

    fam(Qt, 128, 4096, Bb, False, 0)
    fam(Pt, 128, 4096, Cp, True, 8)
    fam(Gt, 256, 128, Cp, True, 16)
    fam(Hp, 128, 4096, Bb, False, 24)
    for (p0_, j8) in ((0, 23), (64, 16)):
        Dr = cv(Apw, p0_, 64, j8 * 32, [[1, 32]])
        Di = cv(Apw, p0_, 64, 1024 + j8 * 32, [[1, 32]])
        hq = lambda a, off, p0_=p0_: cv(a, p0_, 64, off, [[1, 32]])
        dv(lambda h, hq=hq, Dr=Dr: h.tensor_copy(out=hq(TA, 0), in_=Dr), T, T)
        dv(lambda h, hq=hq, Dr=Dr: h.tensor_copy(out=hq(TA, 32), in_=Dr), T, T)
        dv(lambda h, hq=hq, Di=Di: h.tensor_scalar(out=hq(TB, 0), in0=Di, scalar1=-1.0, scalar2=None, op0=ALU.mult), T, T)
        dv(lambda h, hq=hq, Di=Di: h.tensor_copy(out=hq(TB, 32), in_=Di), T, T)
    for g in range(32):
        u = g % 2
        pf, pb = ps[2 * u], ps[2 * u + 1]
        sf_, sb_ = pss[2 * u], pss[2 * u + 1]
        for (pp, sp_, p0) in ((pf, sf_, 0), (pb, sb_, 64)):
            for ri in range(2):
                P.op("pe", lambda h, pp=pp, p0=p0, ri=ri, g=g: h.matmul(
                    pp[:, 0:128], lhsT=Qt[p0:p0 + 64, ri, g * 128:(g + 1) * 128], rhs=Pt[p0:p0 + 64, ri, g * 128:(g + 1) * 128],
                    start=(ri == 0), stop=(ri == 1)), reads=T, writes=[sp_])
        dv(lambda h, pf=pf: h.tensor_tensor(out=tmpA, in0=pf[:, 0:128], in1=msk[:, 0, :], op=ALU.mult), [sf_] + T, [s_tmp])
        dv(lambda h, pb=pb: h.tensor_tensor(out=tmpB, in0=pb[:, 0:128], in1=msk[:, 1, :], op=ALU.mult), [sb_] + T, [s_tmp])
        dv(lambda h: h.tensor_tensor(out=tmpA, in0=tmpA, in1=tmpB, op=ALU.add), [s_tmp], [s_tmp])
        dv(lambda h, g=g: h.scalar_tensor_tensor(out=Mt[:, g, :], in0=ident[:], scalar=dcol[:, g:g + 1], in1=tmpA,
                                                  op0=ALU.mult, op1=ALU.add), [s_tmp, c.ident_slot] + T, [s_M])
    for b in range(8):
        for gg in range(4):
            for ri in range(2):
                g = b * 4 + gg
                sl = gg * 2 + ri
                P.op("pe", lambda h, g=g, ri=ri, sl=sl: h.transpose(
                    out=c.psb[:, sl * 128:(sl + 1) * 128], in_=Hp[:, ri, g * 128:(g + 1) * 128], identity=ident[:]),
                    reads=T + [c.ident_slot], writes=[c.psb_slot])
        ac(lambda h, b=b: h.copy(out=Ht[:, b * 4:(b + 1) * 4, :, :].rearrange("p a b c -> p (a b c)"), in_=c.psb[:, :]),
           [c.psb_slot], [s_H])
    P.barrier()

    for sq_ in range(ntok // SEQ):
        tok0 = sq_ * SEQ
        s_hT = [S(f"hT{t}") for t in range(16)]
        s_xt1 = [S("xt1_0"), S("xt1_1")]
        for t in range(16):
            j = t % 2
            r0 = tok0 + t * 128
            P.dma("sp", f"abxt{j}", lambda h, j=j, r0=r0: h.dma_start(out=xt[j], in_=xin[r0:r0 + 128, :]),
                  reads=[xin_slot], writes=[s_xt1[j]])
            norm_transpose(c, xt[j], s_xt1[j], gbc, s_gbc, xn[j], s_xn[j], ss[j], s_ss[j], hT, s_hT[t], t * 128)
        P.barrier()
        s_sg = [S("sg0"), S("sg1")]
        s_aTq = [S(f"aT{q}") for q in range(4)]
        s_uTq = [S(f"uT{q}") for q in range(4)]
        for q in range(4):
            dv(lambda h, q=q: h.memset(aT[:, q, 0:15], 0.0), [], [s_aTq[q]], force=False)
            dv(lambda h, q=q: h.memset(aT[:, q, SEQ + 15:SEQ + 30], 0.0), [], [s_aTq[q]], force=False)
        wi = 0
        ui = 0

        def load_slab(oc):
            nonlocal wi
            j = wi % 2
            wi += 1
            P.dma("pool", f"abw{j}", lambda h, j=j, oc=oc: h.dma_start(out=wsl[j], in_=W["win"][oc]), writes=[s_wsl[j]])
            return j

        for q in range(4):
            ja = load_slab(q)
            jg = load_slab(q + 4)
            for tb in range(4):
                u = ui % 2
                ui += 1
                pa, pg = ps[2 * u], ps[2 * u + 1]
                sa, sg_ = pss[2 * u], pss[2 * u + 1]
                for (pp, sp_, jj) in ((pa, sa, ja), (pg, sg_, jg)):
                    for k in range(8):
                        P.op("pe", lambda h, pp=pp, jj=jj, k=k, tb=tb: h.matmul(
                            pp[:, :], lhsT=wsl[jj][:, k, :], rhs=hT[:, k, tb * 512:(tb + 1) * 512],
                            start=(k == 0), stop=(k == 7)), reads=[s_wsl[jj]] + s_hT[tb * 4:(tb + 1) * 4], writes=[sp_])
                ac(lambda h, u=u, pg=pg: h.activation(out=sgm[u], in_=pg[:, :], func=AF.Sigmoid), [sg_], [s_sg[u]], force=False)
                dv(lambda h, u=u, pa=pa, q=q, tb=tb: h.tensor_tensor(
                    out=aT[:, q, 15 + tb * 512:15 + (tb + 1) * 512], in0=sgm[u], in1=pa[:, :], op=ALU.mult),
                    [s_sg[u], sa], [s_aTq[q]], force=False)
        for q in range(4):
            ju = load_slab(q + 8)
            for tb in range(4):
                u = ui % 2
                ui += 1
                pu_, su_ = ps[4 + u], pss[4 + u]
                for k in range(8):
                    P.op("pe", lambda h, pu_=pu_, ju=ju, k=k, tb=tb: h.matmul(
                        pu_[:, :], lhsT=wsl[ju][:, k, :], rhs=hT[:, k, tb * 512:(tb + 1) * 512],
                        start=(k == 0), stop=(k == 7)), reads=[s_wsl[ju]] + s_hT[tb * 4:(tb + 1) * 4], writes=[su_])
                ac(lambda h, pu_=pu_, q=q, tb=tb: h.copy(
                    out=cv(uT, 0, 128, q * SEQ + tb * 64, [[256, 8], [1, 64]]),
                    in_=pu_[:, :].rearrange("p (k t) -> p t k", t=8)), [su_], [s_uTq[q]], force=False)
        P.barrier()
        s_cd = [S(f"cd{q}") for q in range(4)]
        for q in range(4):
            for k in range(31):
                dv(lambda h, q=q, k=k: h.tensor_scalar(out=cdiag[:, q, k, :], in0=ident[:], scalar1=cw[:, q, k:k + 1],
                                                        scalar2=None, op0=ALU.mult), [c.ident_slot], [s_cd[q]], force=False)
        s_ac = [S(f"ac{q}") for q in range(4)]
        s_sq = [S("sq0"), S("sq1")]
        s_mean, s_rstd, s_tv = S("mean"), S("rstd"), S("tmpv")
        s_yn = [S("yn0"), S("yn1")]
        for tb in range(4):
            for q in range(4):
                pc, spc = ps[q % 2], pss[q % 2]
                for k in range(31):
                    P.op("pe", lambda h, pc=pc, q=q, k=k, tb=tb: h.matmul(
                        pc[:, :], lhsT=cdiag[:, q, k, :], rhs=aT[:, q, tb * 512 + k: tb * 512 + k + 512],
                        start=(k == 0), stop=(k == 30)), reads=[s_cd[q]], writes=[spc])
                ac(lambda h, pc=pc, q=q: h.activation(out=acv[:, q, :], in_=pc[:, :], func=AF.Identity,
                                                      bias=vec[:, q:q + 1], scale=1.0), [spc], [s_ac[q]], force=False)
            pm, spm = ps[2], pss[2]
            pe2, spe2 = ps[3], pss[3]
            for q in range(4):
                P.op("pe", lambda h, q=q: h.matmul(pm[:, :], lhsT=cv(ones32, 0, 128, 0, [[1, 128]]), rhs=acv[:, q, :],
                                                   start=(q == 0), stop=(q == 3)), reads=[s_ac[q]], writes=[spm])
            for q in range(4):
                u = q % 2
                ac(lambda h, q=q, u=u: h.activation(out=sqv[u], in_=acv[:, q, :], func=AF.Square), [s_ac[q]], [s_sq[u]], force=False)
                P.op("pe", lambda h, q=q, u=u: h.matmul(pe2[:, :], lhsT=cv(ones32, 0, 128, 0, [[1, 128]]), rhs=sqv[u],
                                                        start=(q == 0), stop=(q == 3)), reads=[s_sq[u]], writes=[spe2])
            dv(lambda h: h.tensor_copy(out=meanv, in_=pm[:, :]), [spm], [s_mean], force=False)
            dv(lambda h: h.tensor_tensor(out=tmpv, in0=meanv, in1=meanv, op=ALU.mult), [s_mean], [s_tv])
            dv(lambda h: h.tensor_tensor(out=tmpv, in0=pe2[:, :], in1=tmpv, op=ALU.subtract), [spe2, s_tv], [s_tv])
            ac(lambda h: h.activation(out=rstdv, in_=tmpv, func=AF.Ln, bias=c.cst[:, 2:3], scale=1.0), [s_tv, c.cst_slot], [s_rstd])
            ac(lambda h: h.activation(out=rstdv, in_=rstdv, func=AF.Exp, scale=-0.5), [s_rstd], [s_rstd])
            for q in range(4):
                u = q % 2
                dv(lambda h, q=q, u=u: h.tensor_tensor(out=ynv[u], in0=acv[:, q, :], in1=meanv, op=ALU.subtract),
                   [s_ac[q], s_mean], [s_yn[u]], force=False)
                dv(lambda h, u=u: h.tensor_tensor(out=ynv[u], in0=ynv[u], in1=rstdv, op=ALU.mult), [s_yn[u], s_rstd], [s_yn[u]])
                ac(lambda h, q=q, u=u, tb=tb: h.activation(out=yaT[:, q, tb * 512:(tb + 1) * 512], in_=ynv[u], func=AF.Silu,
                                                           scale=vec[:, 4 + q:5 + q], bias=vec[:, 8 + q:9 + q]),
                   [s_yn[u]], [s_ya], force=False)
        P.barrier()
        s_UY = [S(f"UY{g}") for g in range(32)]
        s_X = [S("Xf"), S("Xb")]
        s_hist = [S("histf"), S("histb")]
        XSv = lambda p0, ri, g, c0, n: cv(XS, p0, 64, (ri * 32 + g) * NK + c0, [[1, n]])
        dv(lambda h: h.memset(cv(XS, 0, 64, 0, [[NK, 64]]), 0.0), [], [s_hist[0]], force=False)
        dv(lambda h: h.memset(cv(XS, 64, 64, 255, [[NK, 64]]), 0.0), [], [s_hist[1]], force=False)
        for g in range(32):
            q, j, par = g // 8, (g % 8) // 2, g % 2
            u = g % 2
            pu_, su_ = ps[u], pss[u]
            for tp in range(8):
                rhs = cv(uT, 32 * j, 32, q * SEQ + tp * 256, [[1, 256]])
                P.op("pe", lambda h, pu_=pu_, j=j, par=par, tp=tp, rhs=rhs: h.matmul(
                    pu_[:, 0:256], lhsT=selin[32 * j:32 * j + 32, par, tp, :], rhs=rhs, start=(tp == 0), stop=(tp == 7),
                    tile_position=(32 * j, 0)), reads=[s_uTq[q], s_sel], writes=[su_])
            ac(lambda h, pu_=pu_, g=g: h.copy(out=UY[:, g, :], in_=pu_[:, 0:256]), [su_], [s_UY[g]], force=False)
            for ri in range(2):
                px, spx = ps[2 + ri], pss[2 + ri]
                P.op("pe", lambda h, px=px, g=g, ri=ri: h.matmul(px[:, 0:256], lhsT=Ht[:, g, ri, :], rhs=UY[:, g, :],
                                                               start=True, stop=True), reads=[s_H, s_UY[g]], writes=[spx])
                dv(lambda h, px=px, g=g, ri=ri: h.tensor_copy(out=XSv(0, ri, g, 1, 256), in_=px[0:64, 0:256]),
                   [spx], [s_X[0]], force=False)
                dv(lambda h, px=px, g=g, ri=ri: h.tensor_copy(out=XSv(64, ri, g, 0, 255), in_=px[64:128, 1:256]),
                   [spx], [s_X[1]], force=False)
        dirs = []
        for (eng, p0, d, cols) in (("dve", 0, 0, list(range(1, 256))), ("pool", 64, 1, list(range(254, -1, -1)))):
            s_S = [S(f"S{d}_0"), S(f"S{d}_1")]
            s_t1_, s_t2_ = S(f"t1_{d}"), S(f"t2_{d}")
            v2 = lambda a, off=0, p0=p0: cv(a, p0, 64, off, [[32, 2], [1, 32]])
            vsw = lambda a, off=0, p0=p0: cv(a, p0, 64, off + 32, [[-32, 2], [1, 32]])
            P.op(eng, lambda h, p0=p0: h.memset(cv(Sf, p0, 64, 0, [[1, 64]]), 0.0), writes=[s_S[0]])
            dirs.append((eng, p0, d, cols, s_S, s_t1_, s_t2_, v2, vsw))
        for i in range(255):
            for (eng, p0, d, cols, s_S, s_t1_, s_t2_, v2, vsw) in dirs:
                col = cols[i]
                pv_, cu = i % 2, (i + 1) % 2
                xcol = cv(XS, p0, 64, col, [[32 * NK, 2], [NK, 32]])
                P.op(eng, lambda h, pv_=pv_, v2=v2: h.tensor_tensor(out=v2(st1), in0=v2(TA), in1=v2(Sf, pv_ * 64), op=ALU.mult),
                     reads=[s_S[pv_]], writes=[s_t1_], force=True)
                P.op(eng, lambda h, pv_=pv_, v2=v2, vsw=vsw: h.tensor_tensor(out=v2(st2), in0=v2(TB), in1=vsw(Sf, pv_ * 64), op=ALU.mult),
                     reads=[s_S[pv_]], writes=[s_t2_], force=True)
                P.op(eng, lambda h, v2=v2, xcol=xcol: h.tensor_tensor(out=v2(st2), in0=v2(st2), in1=xcol, op=ALU.add),
                     reads=[s_t2_, s_X[d]], writes=[s_t2_], force=True)
                P.op(eng, lambda h, cu=cu, v2=v2: h.tensor_tensor(out=v2(Sf, cu * 64), in0=v2(st1), in1=v2(st2), op=ALU.add),
                     reads=[s_t1_, s_t2_], writes=[s_S[cu]], force=True)
                P.op("act", lambda h, cu=cu, v2=v2, xcol=xcol: h.copy(out=xcol, in_=v2(Sf, cu * 64)),
                     reads=[s_S[cu]], writes=[s_hist[d]])
        for g in range(32):
            u = g % 2
            py, spy = ps[4 + u], pss[4 + u]
            P.op("pe", lambda h, py=py, g=g: h.matmul(py[:, 0:256], lhsT=Mt[:, g, :], rhs=UY[:, g, :], start=True, stop=False),
                 reads=[s_M, s_UY[g]], writes=[spy])
            for ri in range(2):
                P.op("pe", lambda h, py=py, g=g, ri=ri: h.matmul(
                    py[:, 0:256], lhsT=Gt[:, g, ri, :], rhs=cv(XS, 0, 128, (ri * 32 + g) * NK, [[1, 256]]),
                    start=False, stop=(ri == 1)), reads=[s_G, s_hist[0], s_hist[1], s_X[0], s_X[1]], writes=[spy])
            ac(lambda h, py=py, g=g: h.copy(out=UY[:, g, :], in_=py[:, 0:256]), [spy], [s_UY[g]], force=False)
        s_yg = [S(f"yg{q}") for q in range(4)]
        ci = 0
        for q in range(4):
            for tau in range(8):
                u = ci % 2
                ci += 1
                pz, spz = ps[u], pss[u]
                for g8 in range(8):
                    P.op("pe", lambda h, pz=pz, tau=tau, g8=g8, q=q: h.matmul(
                        pz[:, 0:256], lhsT=selout[:, tau, g8, :], rhs=UY[:, 8 * q + g8, :], start=(g8 == 0), stop=(g8 == 7)),
                        reads=[s_sel, s_UY[8 * q + g8]], writes=[spz])
                gelu_evac(c, pz, spz, cv(uT, 0, 128, q * SEQ + tau, [[8, 256]]), s_yg[q], s_uTq[q])
        s_sg2 = [S("sg2_0"), S("sg2_1")]
        for tb in range(4):
            for o in range(4):
                pz, spz = ps[2 + o], pss[2 + o]
                for kq in range(4):
                    P.op("pe", lambda h, pz=pz, kq=kq, o=o, tb=tb: h.matmul(
                        pz[:, :], lhsT=gluw[:, kq, o * 128:(o + 1) * 128], rhs=uT[:, kq, tb * 512:(tb + 1) * 512],
                        start=(kq == 0), stop=(kq == 3)), reads=[s_gluw] + s_yg, writes=[spz])
            for o in range(4):
                u = o % 2
                pz, spz = ps[2 + o], pss[2 + o]
                ac(lambda h, pz=pz, o=o, u=u: h.activation(out=sgm[u], in_=pz[:, :], func=AF.Sigmoid,
                                                           bias=vec[:, 12 + o:13 + o], scale=1.0), [spz], [s_sg2[u]], force=False)
                dv(lambda h, o=o, u=u, tb=tb: h.tensor_tensor(out=uT[:, o, tb * 512:(tb + 1) * 512],
                                                              in0=uT[:, o, tb * 512:(tb + 1) * 512], in1=sgm[u], op=ALU.mult),
                   [s_sg2[u], s_yg[o]], [s_yg[o]], force=False)
        P.barrier()
        s_Wo = S("Wo")
        for q in range(2):
            P.dma("pool", "abWo", lambda h, q=q: h.dma_start(out=Wo[:, q * 4:(q + 1) * 4, :], in_=W["wout"][:, q * 4:(q + 1) * 4, :]),
                  writes=[s_Wo])
        s_xt2 = [S("xt2_0"), S("xt2_1")]
        s_ost = [S("ost0"), S("ost1")]
        for t in range(16):
            j = t % 2
            r0 = tok0 + t * 128
            P.dma("sp", f"abxt{j}", lambda h, j=j, r0=r0: h.dma_start(out=xt[j], in_=xin[r0:r0 + 128, :]),
                  reads=[xin_slot], writes=[s_xt2[j]])
            for half in range(2):
                pd, sd = ps[half], pss[half]
                for k in range(8):
                    src = yaT if k < 4 else uT
                    P.op("pe", lambda h, k=k, t=t, half=half, pd=pd, src=src: h.matmul(
                        pd[:, :], lhsT=src[:, k % 4, t * 128:(t + 1) * 128], rhs=Wo[:, k, half * 512:(half + 1) * 512],
                        start=(k == 0), stop=(k == 7)), reads=[s_Wo], writes=[sd])
                dv(lambda h, j=j, half=half, pd=pd: h.tensor_tensor(
                    out=ost[j][:, half * 512:(half + 1) * 512], in0=pd[:, :], in1=xt[j][:, half * 512:(half + 1) * 512],
                    op=ALU.add), [sd, s_xt2[j]], [s_ost[j]], force=False)
            P.dma("sp", f"abost{j}", lambda h, j=j, r0=r0: h.dma_start(out=xout[r0:r0 + 128, :], in_=ost[j]),
                  reads=[s_ost[j]], writes=[xout_slot])
        P.barrier()


def gelu_evac(c, pz, spz, dst, s_dst, s_dst2):
    P = c.P
    P.op("act", lambda h: h.activation(out=dst, in_=pz[:, 0:256], func=AF.Gelu_apprx_tanh), reads=[spz], writes=[s_dst, s_dst2])

def build_program(plan=None, ntok=TOK):
    if plan is None:
        plan = []
        for l in range(DEPTH):
            plan.append(("ab" if l % 2 == 0 else "na", l))
            plan.append(("ffn", l))
    nc = bass.Bass("TRN2", target_bir_lowering=False)
    c = setup_ctx(nc)
    P = c.P
    x = nc.dram_tensor("x", [ntok, D], F32, kind="ExternalInput").ap()
    out = nc.dram_tensor("out", [ntok, D], F32, kind="ExternalOutput").ap()
    ident = nc.dram_tensor("ident", [2, 128, 128], F32, kind="ExternalInput").ap()
    wg = nc.dram_tensor("ffn_wg", [DEPTH, NM, 128, 8, 128], F32, kind="ExternalInput").ap()
    wu = nc.dram_tensor("ffn_wu", [DEPTH, NM, 128, 8, 128], F32, kind="ExternalInput").ap()
    wd = nc.dram_tensor("ffn_wd", [DEPTH, FF, D], F32, kind="ExternalInput").ap()
    fnorm = nc.dram_tensor("ffn_norm", [DEPTH, D], F32, kind="ExternalInput").ap()
    mnorm = nc.dram_tensor("mix_norm", [DEPTH, D], F32, kind="ExternalInput").ap()
    na_wqkv = nc.dram_tensor("na_wqkv", [2, 4, 128, 8, 768], F32, kind="ExternalInput").ap()
    na_wout = nc.dram_tensor("na_wout", [2, 128, 8, 1024], F32, kind="ExternalInput").ap()
    na_bias = nc.dram_tensor("na_bias", [2, 4, 128, 4, 14, 64], F32, kind="ExternalInput").ap()
    na_qkg = nc.dram_tensor("na_qkg", [2, 128, 2], F32, kind="ExternalInput").ap()
    abd = {}
    for nm, shp in AB_SHAPES.items():
        abd[nm] = nc.dram_tensor("ab_" + nm, list(shp), F32, kind="ExternalInput").ap()
    scr = [nc.dram_tensor(f"scr{i}", [ntok, D], F32, kind="Internal").ap() for i in range(2)]
    if DBG.get('dump_on'):
        DBG['dump'] = nc.dram_tensor("dbg", [128, 16384], F32, kind="ExternalOutput").ap()
    s_scr = [P.slot("scr0"), P.slot("scr1")]
    s_x = P.slot("x_dram")
    s_out = P.slot("out_dram")
    load_ident(c, ident)
    cur, s_cur = x, s_x
    for si, st in enumerate(plan):
        if si == len(plan) - 1:
            dst, s_dst = out, s_out
        else:
            dst, s_dst = scr[si % 2], s_scr[si % 2]
        kind, l = st
        if kind == "ffn":
            ffn_stage(c, cur, s_cur, dst, s_dst, wg[l], wu[l], wd[l], fnorm[l:l + 1, :], ntok=ntok)
        elif kind == "na":
            i = l // 2
            for sq_ in range(ntok // SEQ):
                na_stage(c, cur, s_cur, dst, s_dst, na_wqkv[i], na_wout[i], na_bias[i], na_qkg[i],
                         mnorm[l:l + 1, :], sq_ * SEQ)
        elif kind == "ab":
            i = l // 2
            Wd_ = {k: (v[i] if k in AB_PER_LAYER else v) for k, v in abd.items()}
            ab_stage(c, cur, s_cur, dst, s_dst, Wd_, mnorm[l:l + 1, :], ntok)
        cur, s_cur = dst, s_dst
    fin = Ins("sp", None, False, None, 0)
    for e in ENGS:
        for i in P.streams[e]:
            if i.is_dma:
                fin.deps.append(i)
    fin.idx = len(P.streams["sp"])
    P.streams["sp"].append(fin)
    P.emit()
    return nc


AB_PER_LAYER = ("win", "wout", "gluw", "vec", "cw", "lam", "B", "C", "dcol")
AB_SHAPES = {
    "win": (2, 12, 128, 8, 128), "wout": (2, 128, 8, 1024), "gluw": (2, 128, 4, 512), "vec": (2, 128, 20),
    "cw": (2, 128, 4, 31), "lam": (2, 128, 3, 32), "B": (2, 128, 2, 32, 16), "C": (2, 128, 2, 32, 16),
    "dcol": (2, 128, 32), "selin": (128, 2, 8, 128), "selout": (128, 8, 8, 128), "mask": (128, 2, 128),
    "etab": (128, 32), "ones": (128, 128),
}


def ab_host_layout(inputs):
    g = np.ascontiguousarray
    f = lambda k: np.asarray(inputs[k], dtype=np.float32)
    d = {}
    d["win"] = g(f("ab_w_in").reshape(2, 8, 128, 12, 128).transpose(0, 3, 2, 1, 4))
    d["wout"] = g(f("ab_w_out").reshape(2, 8, 128, 1024).transpose(0, 2, 1, 3))
    d["gluw"] = g(f("ssm_glu_w").reshape(2, 4, 128, 512).transpose(0, 2, 1, 3))
    vec = np.zeros((2, 128, 20), np.float32)
    for j, k in enumerate(("conv_b", "conv_ln_g", "conv_ln_b", "ssm_glu_b")):
        vec[:, :, 4 * j:4 * j + 4] = f(k).reshape(2, 4, 128).transpose(0, 2, 1)
    d["vec"] = vec
    d["cw"] = g(f("conv_w").reshape(2, 31, 4, 128).transpose(0, 3, 2, 1))
    lam = np.empty((2, 2, 64, 3, 32), np.float32)
    lam[:, :, :, 0] = f("ssm_lambda_re").transpose(0, 1, 3, 2)
    lam[:, :, :, 1] = f("ssm_lambda_im").transpose(0, 1, 3, 2)
    lam[:, :, :, 2] = f("ssm_log_step")[:, :, None, :]
    d["lam"] = g(lam.reshape(2, 128, 3, 32))
    B = np.stack([f("ssm_b_re"), f("ssm_b_im")], axis=1)
    d["B"] = g(B.transpose(0, 2, 4, 1, 3, 5).reshape(2, 128, 2, 32, 16))
    C = np.stack([f("ssm_c_re"), f("ssm_c_im")], axis=1)
    d["C"] = g(C.transpose(0, 2, 5, 1, 3, 4).reshape(2, 128, 2, 32, 16))
    dsk = f("ssm_d").reshape(2, 32, 16)
    d["dcol"] = g(np.broadcast_to(dsk.transpose(0, 2, 1)[:, None], (2, 8, 16, 32)).reshape(2, 128, 32))
    selin = np.zeros((4, 2, 16, 2, 8, 8, 16), np.float32)
    selout = np.zeros((8, 16, 8, 8, 8, 16), np.float32)
    for cc in range(16):
        for t in range(8):
            selin[:, 0, cc, 0, t, t, cc] = 1.0
            selin[:, 1, cc, 1, t, t, cc] = 1.0
            for g8 in range(8):
                selout[t, cc, t, g8, g8, cc] = 1.0
    d["selin"] = selin.reshape(128, 2, 8, 128)
    d["selout"] = selout.reshape(128, 8, 8, 128)
    tp = np.repeat(np.arange(8), 16)
    d["mask"] = g(np.stack([(tp[:, None] <= tp[None, :]), (tp[:, None] >= tp[None, :])], axis=1).astype(np.float32))
    tt = np.arange(8, dtype=np.float32)
    ef = np.concatenate([-tt, tt, tt + 1, 7 - tt])
    eb = np.concatenate([tt, -tt, 8 - tt, tt])
    d["etab"] = g(np.concatenate([np.broadcast_to(ef, (64, 32)), np.broadcast_to(eb, (64, 32))], axis=0))
    d["ones"] = np.full((128, 128), 1.0 / 512, np.float32)
    return {"ab_" + k: v for k, v in d.items()}


def host_layout(inputs):
    g = np.ascontiguousarray
    d = {}
    bd = np.zeros((128, 128), np.float32)
    bd[:64, :64] = 1.0
    bd[64:, 64:] = 1.0
    d["ident"] = np.stack([np.eye(128, dtype=np.float32), bd])
    for nm, key in (("ffn_wg", "ffn_w_gate"), ("ffn_wu", "ffn_w_up")):
        w = np.asarray(inputs[key], dtype=np.float32).reshape(DEPTH, 8, 128, NM, 128)
        d[nm] = g(w.transpose(0, 3, 2, 1, 4))
    d["ffn_wd"] = g(np.asarray(inputs["ffn_w_down"], dtype=np.float32))
    d["ffn_norm"] = g(np.asarray(inputs["ffn_norm"], dtype=np.float32))
    d["mix_norm"] = g(np.asarray(inputs["mix_norm"], dtype=np.float32))
    wqkv = np.asarray(inputs["na_w_qkv"], dtype=np.float32).reshape(2, 8, 128, 3, 4, 256)
    d["na_wqkv"] = g(wqkv.transpose(0, 4, 2, 1, 3, 5).reshape(2, 4, 128, 8, 768))
    d["na_wout"] = g(np.asarray(inputs["na_w_out"], dtype=np.float32).reshape(2, 8, 128, 1024).transpose(0, 2, 1, 3))
    rpb = np.asarray(inputs["na_rpb"], dtype=np.float32)
    qc = np.arange(64)[None, :]
    kc = np.arange(64)[:, None]
    cidx = np.clip(kc - qc, -15, 15) + 15
    cstart = np.clip(qc - 8, 0, 48)
    cmask = (kc >= cstart) & (kc < cstart + 16)
    tab = np.empty((2, 16, 14, 2, 64, 64), np.float32)
    for rho in range(14):
        for jj in range(2):
            tab[:, :, rho, jj] = np.where(cmask[None, None], rpb[:, :, rho + jj][:, :, cidx], np.float32(-30000.0))
    tab = tab.reshape(2, 4, 4, 14, 2, 64, 64).transpose(0, 1, 4, 5, 2, 3, 6).reshape(2, 4, 128, 4, 14, 64)
    d["na_bias"] = g(tab)
    qg = np.asarray(inputs["na_q_norm"], dtype=np.float32)
    kg = np.asarray(inputs["na_k_norm"], dtype=np.float32)
    d["na_qkg"] = g(np.stack([np.tile(qg, (1, 2)), np.tile(kg, (1, 2))], axis=-1))
    d.update(ab_host_layout(inputs))
    return d


def kernel(**inputs):
    x = np.asarray(inputs["x"], dtype=np.float32)
    shared = host_layout(inputs)
    nc = build_program()
    in_maps = []
    for i in range(NCORES):
        m = dict(shared)
        m["x"] = np.ascontiguousarray(x[2 * i:2 * i + 2].reshape(TOK, D))
        in_maps.append(m)
    res = run_bass_kernel_spmd(nc, in_maps, core_ids=list(range(NCORES)))
    outs = [res.results[i]["out"].reshape(2, SEQ, D) for i in range(NCORES)]
    return np.concatenate(outs, axis=0).astype(np.float32)
```

```python
import math
import numpy as np
import concourse.bass as bass
import concourse.mybir as mybir
from concourse.bass_utils import run_bass_kernel_spmd

F32 = mybir.dt.float32
BF16 = mybir.dt.bfloat16
I32 = mybir.dt.int32
AF = mybir.ActivationFunctionType
ALU = mybir.AluOpType
AX = mybir.AxisListType

DBG = {}
NCORES = 8
D = 1024
SEQ = 2048
TOK = 2 * SEQ
FF = 2816
NM = FF // 128
DEPTH = 4
EPS = 1e-6


class Slot:
    __slots__ = ("name", "lw", "rs")

    def __init__(self, name):
        self.name = name
        self.lw = None
        self.rs = []


class Ins:
    __slots__ = ("eng", "fn", "deps", "is_dma", "sem", "semval", "inc", "idx", "force")

    def __init__(self, eng, fn, is_dma, sem, semval):
        self.eng = eng
        self.fn = fn
        self.deps = []
        self.is_dma = is_dma
        self.sem = sem
        self.semval = semval
        self.inc = False
        self.idx = -1
        self.force = False


ENGS = ("pe", "act", "dve", "pool", "sp")


class Prog:
    def __init__(self, nc):
        self.nc = nc
        self.streams = {e: [] for e in ENGS}
        self.dma_sems = {}
        self.dma_cnt = {}
        self.slots = []

    def slot(self, name):
        s = Slot(name)
        self.slots.append(s)
        return s

    def slots_n(self, name, n):
        return [self.slot(f"{name}{i}") for i in range(n)]

    def _add(self, ins, reads, writes):
        e = ins.eng
        best = {}
        dl = []

        def need(p):
            if p is None or p is ins:
                return
            if p.is_dma:
                if p not in dl:
                    dl.append(p)
            elif ins.is_dma or p.eng != e or ins.force:
                q = best.get(p.eng)
                if q is None or q.idx < p.idx:
                    best[p.eng] = p

        for s in reads:
            need(s.lw)
        for s in writes:
            need(s.lw)
            for r in s.rs:
                need(r)
        ins.deps = dl + list(best.values())
        for s in reads:
            rs = s.rs
            if rs and (not ins.is_dma) and (not rs[-1].is_dma) and rs[-1].eng == e:
                rs[-1] = ins
            else:
                rs.append(ins)
        for s in writes:
            s.lw = ins
            s.rs = []
        ins.idx = len(self.streams[e])
        self.streams[e].append(ins)
        return ins

    def op(self, eng, fn, reads=(), writes=(), force=False):
        ins = Ins(eng, fn, False, None, 0)
        ins.force = force
        return self._add(ins, reads, writes)

    def dma(self, eng, semname, fn, reads=(), writes=()):
        if semname not in self.dma_sems:
            self.dma_sems[semname] = self.nc.alloc_semaphore(name="d_" + semname)
            self.dma_cnt[semname] = 0
        self.dma_cnt[semname] += 16
        return self._add(Ins(eng, fn, True, self.dma_sems[semname], self.dma_cnt[semname]), reads, writes)

    def barrier(self):
        lasts = []
        for e in ENGS:
            st = self.streams[e]
            if st:
                lasts.append(st[-1])
        dmas = [i for e in ENGS for i in self.streams[e] if i.is_dma and not getattr(i, "_barr", False)]
        for e in ENGS:
            ins = Ins(e, None, False, None, 0)
            for p in lasts:
                if p.eng != e and not p.is_dma and p.fn is not None:
                    ins.deps.append(p)
            for p in dmas:
                ins.deps.append(p)
            ins.idx = len(self.streams[e])
            self.streams[e].append(ins)
        for s in self.slots:
            s.lw = None
            s.rs = []

    def emit(self):
        nc = self.nc
        for e in ENGS:
            for ins in self.streams[e]:
                for p in ins.deps:
                    if not p.is_dma:
                        p.inc = True
        rank = {}
        sems = {}
        for e in ENGS:
            sems[e] = nc.alloc_semaphore(name="s_" + e)
            c = 0
            last_real = None
            for ins in self.streams[e]:
                if ins.fn is not None and not ins.is_dma:
                    last_real = ins
                if ins.inc:
                    assert ins.fn is not None and not ins.is_dma
                    c += 1
                    rank[ins] = c
        handles = {"pe": nc.tensor, "act": nc.scalar, "dve": nc.vector, "pool": nc.gpsimd, "sp": nc.sync}
        streams = self.streams

        def replay(e, h):
            known = {}
            for ins in streams[e]:
                for p in ins.deps:
                    if p.is_dma:
                        key, val, sem = ("d", id(p.sem)), p.semval, p.sem
                    else:
                        key, val, sem = ("c", p.eng), rank[p], sems[p.eng]
                    if known.get(key, 0) >= val:
                        continue
                    known[key] = val
                    h.wait_ge(sem, val)
                if ins.fn is None:
                    continue
                r = ins.fn(h)
                if ins.is_dma:
                    r.then_inc(ins.sem, 16)
                elif ins.inc:
                    r.then_inc(sems[e], 1)

        with nc.Block() as block:
            @block.tensor
            def _(h):
                replay("pe", h)

            @block.scalar
            def _(h):
                replay("act", h)

            @block.vector
            def _(h):
                replay("dve", h)

            @block.gpsimd
            def _(h):
                replay("pool", h)

            @block.sync
            def _(h):
                replay("sp", h)


class Arena:
    def __init__(self, nc, name, nelem, dtype):
        self.t = nc.alloc_sbuf_tensor(name, [128, nelem], dtype)
        self.n = nelem
        self.off = 0

    def reset(self):
        self.off = 0

    def take(self, *shape):
        n = int(np.prod(shape))
        assert self.off + n <= self.n, (self.off, n, self.n)
        ap = self.t[:, self.off:self.off + n]
        self.off += n
        if len(shape) == 2:
            ap = ap.rearrange("p (a b) -> p a b", b=shape[1])
        elif len(shape) == 3:
            ap = ap.rearrange("p (a b c) -> p a b c", b=shape[1], c=shape[2])
        elif len(shape) == 4:
            ap = ap.rearrange("p (a b c d) -> p a b c d", b=shape[1], c=shape[2], d=shape[3])
        return ap


class Ctx:
    pass


def setup_ctx(nc):
    c = Ctx()
    c.nc = nc
    c.P = Prog(nc)
    c.fa = Arena(nc, "fa", 12800, F32)
    c.ba = Arena(nc, "ba", 78848, BF16)
    c.ps = [nc.alloc_psum_tensor(f"ps{i}", [128, 512], F32) for i in range(7)]
    c.psb = nc.alloc_psum_tensor("psb", [128, 1024], BF16)
    c.ps_slots = [c.P.slot(f"ps{i}") for i in range(7)]
    c.psb_slot = c.P.slot("psb")
    c.ident = nc.alloc_sbuf_tensor("ident_sb", [128, 128], BF16)
    c.ident_slot = c.P.slot("ident")
    c.bd = nc.alloc_sbuf_tensor("bd_sb", [128, 128], BF16)
    c.bd_slot = c.P.slot("bd")
    c.cst = nc.alloc_sbuf_tensor("cst_sb", [128, 8], F32)
    c.cst_slot = c.P.slot("cst")
    return c


def load_ident(c, ident_dram):
    P = c.P
    P.dma("pool", "ident", lambda h: h.dma_start(out=c.ident[:], in_=ident_dram[0]), writes=[c.ident_slot])
    P.dma("pool", "ident", lambda h: h.dma_start(out=c.bd[:], in_=ident_dram[1]), writes=[c.bd_slot])
    for i, v in enumerate((64.0 * EPS, EPS, 1e-5, 0.0, 1.0, math.pi / 2)):
        P.op("dve", lambda h, i=i, v=v: h.memset(c.cst[:, i:i + 1], v), writes=[c.cst_slot])


def norm_transpose(c, xt, xt_slot, gain_bc, gain_slot, xn, xn_slot, ss, ss_slot, hT, hT_slot, col0):
    P = c.P
    P.op("act", lambda h: h.activation(out=xn, in_=xt, func=AF.Square, accum_out=ss[:, 0:1]),
         reads=[xt_slot], writes=[xn_slot, ss_slot])
    P.op("dve", lambda h: h.tensor_scalar(out=ss[:, 1:2], in0=ss[:, 0:1], scalar1=1.0 / D, scalar2=EPS,
                                           op0=ALU.mult, op1=ALU.add), reads=[ss_slot], writes=[ss_slot])
    P.op("act", lambda h: h.activation(out=ss[:, 2:3], in_=ss[:, 1:2], func=AF.Sqrt), reads=[ss_slot], writes=[ss_slot])
    P.op("dve", lambda h: h.reciprocal(out=ss[:, 3:4], in_=ss[:, 2:3]), reads=[ss_slot], writes=[ss_slot])
    P.op("dve", lambda h: h.scalar_tensor_tensor(out=xn, in0=xt, scalar=ss[:, 3:4], in1=gain_bc,
                                                  op0=ALU.mult, op1=ALU.mult),
         reads=[xt_slot, ss_slot, gain_slot], writes=[xn_slot], force=True)
    for k in range(8):
        P.op("pe", lambda h, k=k: h.transpose(out=c.psb[:, k * 128:(k + 1) * 128], in_=xn[:, k * 128:(k + 1) * 128],
                                              identity=c.ident[:]),
             reads=[xn_slot, c.ident_slot], writes=[c.psb_slot])
    P.op("act", lambda h: h.copy(out=hT[:, :, col0:col0 + 128],
                                 in_=c.psb[:, :].rearrange("p (k t) -> p k t", t=128)),
         reads=[c.psb_slot], writes=[hT_slot])


def ffn_stage(c, xin, xin_slot, xout, xout_slot, wg, wu, wd, gain_row, ntok=TOK, TB=1024):
    P = c.P
    fa, ba = c.fa, c.ba
    fa.reset()
    ba.reset()
    NT = TB // 128
    Wd = ba.take(NM, 1024)
    hT = ba.take(8, TB)
    actT = ba.take(NM, TB)
    wgb = [ba.take(8, 128) for _ in range(2)]
    wub = [ba.take(8, 128) for _ in range(2)]
    xn = [ba.take(1, 1024)[:, 0, :] for _ in range(2)]
    xt = [fa.take(1, 1024)[:, 0, :] for _ in range(NT)]
    ost = [fa.take(1, 1024)[:, 0, :] for _ in range(2)]
    gbc = fa.take(1, 1024)[:, 0, :]
    sg = [fa.take(1, 512)[:, 0, :] for _ in range(2)]
    ss = [fa.take(1, 4)[:, 0, :] for _ in range(2)]

    s_Wd = P.slot("Wd")
    s_hT = [P.slot(f"hT{t}") for t in range(NT)]
    s_act = [[P.slot(f"act{m}_{h}") for h in range(TB // 512)] for m in range(NM)]
    s_wg = P.slots_n("wg", 2)
    s_wu = P.slots_n("wu", 2)
    s_xn = P.slots_n("xn", 2)
    s_xt = P.slots_n("xt", NT)
    s_ost = P.slots_n("ost", 2)
    s_gbc = P.slot("gbc")
    s_sg = P.slots_n("sg", 2)
    s_ss = P.slots_n("ss", 2)

    P.dma("sp", "gbc", lambda h: h.dma_start(out=gbc, in_=gain_row.partition_broadcast(128)), writes=[s_gbc])
    wdv = wd.rearrange("(m p) o -> p m o", p=128)
    for q in range(2):
        P.dma("pool", "Wd", lambda h, q=q: h.dma_start(out=Wd[:, q * 11:(q + 1) * 11, :], in_=wdv[:, q * 11:(q + 1) * 11, :]),
              writes=[s_Wd])

    nblk = ntok // TB
    widx = 0
    oidx = 0
    nidx = 0
    uidx = 0
    for b in range(nblk):
        for t in range(NT):
            r0 = b * TB + t * 128
            P.dma("sp", f"xt{t}", lambda h, t=t, r0=r0: h.dma_start(out=xt[t], in_=xin[r0:r0 + 128, :]),
                  reads=[xin_slot], writes=[s_xt[t]])
        for t in range(NT):
            j = nidx % 2
            nidx += 1
            norm_transpose(c, xt[t], s_xt[t], gbc, s_gbc, xn[j], s_xn[j], ss[j], s_ss[j], hT, s_hT[t], t * 128)
        for m in range(NM):
            j = widx % 2
            widx += 1
            P.dma("pool", f"wg{j}", lambda h, j=j, m=m: h.dma_start(out=wgb[j], in_=wg[m]), writes=[s_wg[j]])
            P.dma("pool", f"wu{j}", lambda h, j=j, m=m: h.dma_start(out=wub[j], in_=wu[m]), writes=[s_wu[j]])
            for hh in range(TB // 512):
                u = uidx % 2
                uidx += 1
                pg, pu = c.ps[2 * u], c.ps[2 * u + 1]
                sgs, sus = c.ps_slots[2 * u], c.ps_slots[2 * u + 1]
                hs = s_hT[hh * 4:(hh + 1) * 4]
                for k in range(8):
                    P.op("pe", lambda h, j=j, k=k, hh=hh, pg=pg: h.matmul(
                        pg[:, :], lhsT=wgb[j][:, k, :], rhs=hT[:, k, hh * 512:(hh + 1) * 512],
                        start=(k == 0), stop=(k == 7)), reads=[s_wg[j]] + hs, writes=[sgs])
                for k in range(8):
                    P.op("pe", lambda h, j=j, k=k, hh=hh, pu=pu: h.matmul(
                        pu[:, :], lhsT=wub[j][:, k, :], rhs=hT[:, k, hh * 512:(hh + 1) * 512],
                        start=(k == 0), stop=(k == 7)), reads=[s_wu[j]] + hs, writes=[sus])
                P.op("act", lambda h, u=u, pg=pg: h.activation(out=sg[u], in_=pg[:, :], func=AF.Silu),
                     reads=[sgs], writes=[s_sg[u]])
                P.op("dve", lambda h, u=u, pu=pu, m=m, hh=hh: h.tensor_tensor(
                    out=actT[:, m, hh * 512:(hh + 1) * 512], in0=sg[u], in1=pu[:, :], op=ALU.mult),
                    reads=[s_sg[u], sus], writes=[s_act[m][hh]])
        for t in range(NT):
            o = oidx % 2
            oidx += 1
            r0 = b * TB + t * 128
            for half in range(2):
                pd = c.ps[4 + half]
                sd = c.ps_slots[4 + half]
                for m in range(NM):
                    P.op("pe", lambda h, m=m, t=t, half=half, pd=pd: h.matmul(
                        pd[:, :], lhsT=actT[:, m, t * 128:(t + 1) * 128], rhs=Wd[:, m, half * 512:(half + 1) * 512],
                        start=(m == 0), stop=(m == NM - 1)),
                        reads=[s_act[m][t // 4], s_Wd], writes=[sd])
                P.op("dve", lambda h, o=o, t=t, half=half, pd=pd: h.tensor_tensor(
                    out=ost[o][:, half * 512:(half + 1) * 512], in0=pd[:, :], in1=xt[t][:, half * 512:(half + 1) * 512],
                    op=ALU.add), reads=[sd, s_xt[t]], writes=[s_ost[o]])
            P.dma("sp", f"ost{o}", lambda h, o=o, r0=r0: h.dma_start(out=xout[r0:r0 + 128, :], in_=ost[o]),
                  reads=[s_ost[o]], writes=[xout_slot])
    P.barrier()


def cv_ps(pt, off, dims, p0=0, npart=128):
    a = pt[:, :]
    n = a.ap[0][0]
    return bass.AP(tensor=a.tensor, offset=a.offset + p0 * n + off, ap=[[n, npart]] + [list(d) for d in dims])


def _na_runs():
    out = []
    rs = [min(max(r - 4, 0), 24) for r in range(32)]
    for hf in range(2):
        runs = []
        for par in range(2):
            for ti in range(16):
                row0 = 2 * ti + par
                if row0 + 1 > 31:
                    continue
                for grp in range(2):
                    rows = [r for r in range(16 * hf + 8 * grp, 16 * hf + 8 * grp + 8)
                            if rs[r] % 2 == par and rs[r] <= row0 <= rs[r] + 6]
                    while rows:
                        if len(rows) == 1:
                            run, rows = rows, []
                            st = 1
                        else:
                            st = rows[1] - rows[0]
                            k = 2
                            while k < len(rows) and rows[k] - rows[k - 1] == st:
                                k += 1
                            run, rows = rows[:k], rows[k:]
                        runs.append((par, ti, grp, run[0], st, len(run)))
        out.append(runs)
    return out


NA_RUNS = _na_runs()

def na_stage(c, xin, xin_slot, xout, xout_slot, wqkv, wout, biasT, qkg, gain_row, tok0):
    P = c.P
    fa, ba = c.fa, c.ba
    fa.reset()
    ba.reset()
    hT = ba.take(8, SEQ)
    attnT = ba.take(8, SEQ)
    qT = ba.take(2, SEQ)
    kT = ba.take(2, SEQ)
    Vpad = ba.take(2, 16, 4, 128)
    EB = ba.take(4, 14, 64)
    cpad = ba.take(3, 128)
    zrhs = ba.take(1, 512)[:, 0, :]
    bT = ba.take(4, 14, 64)
    wq = ba.take(8, 768)
    PT = [ba.take(1, 512)[:, 0, :] for _ in range(3)]
    xn = [ba.take(1, 1024)[:, 0, :] for _ in range(2)]
    sq = [ba.take(1, 512)[:, 0, :] for _ in range(2)]
    xt = [fa.take(1, 1024)[:, 0, :] for _ in range(2)]
    ost = [fa.take(1, 1024)[:, 0, :] for _ in range(2)]
    gbc = fa.take(1, 1024)[:, 0, :]
    raw = [fa.take(1, 512)[:, 0, :] for _ in range(2)]
    rinv = [fa.take(1, 512)[:, 0, :] for _ in range(2)]
    ss = [fa.take(1, 4)[:, 0, :] for _ in range(2)]
    rec = [fa.take(1, 2)[:, 0, :] for _ in range(2)]
    gq = fa.take(1, 2)[:, 0, :]

    s_hT = P.slots_n("n_hT", 16)
    s_attnT = P.slots_n("n_attnT", 8)
    s_qT = P.slot("n_qT")
    s_kT = P.slot("n_kT")
    s_V = P.slot("n_V")
    s_bT = P.slot("n_bT")
    s_wq = P.slot("n_wq")
    s_PT = P.slots_n("n_PT", 3)
    s_EB = P.slot("n_EB")
    s_cpad = P.slot("n_cpad")
    s_xn = P.slots_n("n_xn", 2)
    s_sq = P.slots_n("n_sq", 2)
    s_xt = P.slots_n("n_xt", 2)
    s_ost = P.slots_n("n_ost", 2)
    s_gbc = P.slot("n_gbc")
    s_raw = P.slots_n("n_raw", 2)
    s_rinv = P.slots_n("n_rinv", 2)
    s_ss = P.slots_n("n_ss", 2)
    s_rec = P.slots_n("n_rec", 2)
    s_gq = P.slot("n_gq")
    ps, pss = c.ps, c.ps_slots

    P.dma("sp", "gbc", lambda h: h.dma_start(out=gbc, in_=gain_row.partition_broadcast(128)), writes=[s_gbc])
    P.dma("sp", "gq", lambda h: h.dma_start(out=gq, in_=qkg), writes=[s_gq])
    P.op("dve", lambda h: h.memset(Vpad[:, :, :, :, :].rearrange("p a b c d -> p (a b c d)"), 0.0), writes=[s_V])
    P.op("dve", lambda h: h.memset(cpad[:, :, :].rearrange("p a b -> p (a b)"), 0.0), writes=[s_cpad])
    P.op("dve", lambda h: h.memset(cpad[:, 0, 0:64], 1.0), writes=[s_cpad])
    P.op("dve", lambda h: h.memset(cpad[:, 1, 64:128], 1.0), writes=[s_cpad])
    P.op("dve", lambda h: h.memset(zrhs, 0.0), writes=[s_cpad])

    for t in range(16):
        j = t % 2
        r0 = tok0 + t * 128
        P.dma("sp", f"nxt{j}", lambda h, j=j, r0=r0: h.dma_start(out=xt[j], in_=xin[r0:r0 + 128, :]),
              reads=[xin_slot], writes=[s_xt[j]])
        norm_transpose(c, xt[j], s_xt[j], gbc, s_gbc, xn[j], s_xn[j], ss[j], s_ss[j], hT, s_hT[t], t * 128)

    ui = 0
    for hg in DBG.get('hgs', range(4)):
        for part in range(3):
            P.dma("pool", "nwq", lambda h, hg=hg, part=part: h.dma_start(
                out=wq[:, :, part * 256:(part + 1) * 256], in_=wqkv[hg][:, :, part * 256:(part + 1) * 256]),
                writes=[s_wq])
        P.dma("pool", "nbT", lambda h, hg=hg: h.dma_start(out=bT, in_=biasT[hg]), writes=[s_bT])
        for part in DBG.get('qk', range(2)):
            dst, s_dst = (qT, s_qT) if part == 0 else (kT, s_kT)
            for cc in range(2):
                for tb in range(4):
                    u = ui % 2
                    ui += 1
                    pq, spq = ps[u], pss[u]
                    pn, spn = ps[2 + u], pss[2 + u]
                    for k in range(8):
                        P.op("pe", lambda h, k=k, part=part, cc=cc, tb=tb, pq=pq: h.matmul(
                            pq[:, :], lhsT=wq[:, k, part * 256 + cc * 128: part * 256 + cc * 128 + 128],
                            rhs=hT[:, k, tb * 512:(tb + 1) * 512], start=(k == 0), stop=(k == 7)),
                            reads=[s_wq] + s_hT[tb * 4:(tb + 1) * 4], writes=[spq])
                    P.op("act", lambda h, u=u, pq=pq: h.activation(out=sq[u], in_=pq[:, :], func=AF.Square),
                         reads=[spq], writes=[s_sq[u]])
                    P.op("dve", lambda h, u=u, pq=pq: h.tensor_copy(out=raw[u], in_=pq[:, :]),
                         reads=[spq, s_sq[u]], writes=[s_raw[u]])
                    P.op("pe", lambda h, u=u, pn=pn: h.matmul(pn[:, :], lhsT=c.bd[:], rhs=sq[u], start=True, stop=True),
                         reads=[s_sq[u], c.bd_slot], writes=[spn])
                    if part == 0:
                        P.op("act", lambda h, u=u, pn=pn: h.activation(out=rinv[u], in_=pn[:, :], func=AF.Ln,
                                                                       bias=c.cst[:, 0:1], scale=1.0),
                             reads=[spn, c.cst_slot], writes=[s_rinv[u]])
                    else:
                        P.op("act", lambda h, u=u, pn=pn: h.activation(out=rinv[u], in_=pn[:, :], func=AF.Ln,
                                                                       bias=c.cst[:, 1:2], scale=1.0 / 64),
                             reads=[spn, c.cst_slot], writes=[s_rinv[u]])
                    P.op("act", lambda h, u=u: h.activation(out=rinv[u], in_=rinv[u], func=AF.Exp, scale=-0.5),
                         reads=[s_rinv[u]], writes=[s_rinv[u]])
                    P.op("dve", lambda h, u=u, part=part, cc=cc, tb=tb, dst=dst: h.scalar_tensor_tensor(
                        out=dst[:, cc, tb * 512:(tb + 1) * 512], in0=raw[u], scalar=gq[:, part:part + 1], in1=rinv[u],
                        op0=ALU.mult, op1=ALU.mult), reads=[s_raw[u], s_rinv[u], s_gq], writes=[s_dst])
        for par in DBG.get('vpar', range(2)):
            for i in range(16 - par):
                u = ui % 2
                ui += 1
                pv, spv = ps[u], pss[u]
                t0 = par * 64 + i * 128
                for k in range(8):
                    P.op("pe", lambda h, k=k, t0=t0, pv=pv: h.matmul(
                        pv[:, 0:256], lhsT=hT[:, k, t0:t0 + 128], rhs=wq[:, k, 512:768], start=(k == 0), stop=(k == 7)),
                        reads=[s_wq] + s_hT[t0 // 128:(t0 + 127) // 128 + 1], writes=[spv])
                for hp in range(2):
                    P.op("act", lambda h, par=par, i=i, pv=pv, hp=hp: h.copy(
                        out=cv(Vpad, 0, 128, ((par * 16 + i) * 4 + hp) * 128 + hp * 64, [[256, 2], [1, 64]]),
                        in_=cv_ps(pv, hp * 64, [[128, 2], [1, 64]])), reads=[spv], writes=[s_V])
        P.op("act", lambda h: h.activation(out=EB[:, :, :, :].rearrange("p a b c -> p (a b c)"),
                                           in_=bT[:, :, :, :].rearrange("p a b c -> p (a b c)"), func=AF.Exp),
             reads=[s_bT], writes=[s_EB])
        for cc in range(2):
            for hf in range(2):
                for b in range(4):
                    P.op("pe", lambda h, b=b: h.matmul(ps[b][:, :], lhsT=cpad[:, 2, :], rhs=zrhs, start=True, stop=True,
                                                       skip_group_check=True), reads=[s_cpad], writes=[pss[b]])
                units = [(par, ti, grp, r0_, st_, n_, hp) for (par, ti, grp, r0_, st_, n_) in NA_RUNS[hf] for hp in range(2)]
                LA = 2

                def emit_front(idx, cc=cc):
                    (par, ti, grp, r0_, st_, n_, hp) = units[idx]
                    row0 = 2 * ti + par
                    hh = cc * 2 + hp
                    pb = hp * 64
                    w3 = idx % 3
                    sc, ssc = ps[4 + w3], pss[4 + w3]
                    nq = 64 * n_
                    rho0 = row0 - r0_ + 7
                    P.op("pe", lambda h: h.matmul(
                        sc[:, 0:nq], lhsT=kT[pb:pb + 64, cc, row0 * 64: row0 * 64 + 128],
                        rhs=cv(qT, pb, 64, cc * SEQ + r0_ * 64, [[st_ * 64, n_], [1, 64]]), start=True, stop=True),
                        reads=[s_kT, s_qT], writes=[ssc])
                    P.op("act", lambda h: h.activation(out=PT[w3][:, 0:nq], in_=sc[:, 0:nq], func=AF.Exp),
                         reads=[ssc], writes=[s_PT[w3]])
                    P.op("dve", lambda h: h.tensor_tensor(
                        out=cv(PT[w3], 0, 128, 0, [[64, n_], [1, 64]]), in0=cv(PT[w3], 0, 128, 0, [[64, n_], [1, 64]]),
                        in1=cv(EB, 0, 128, hh * 896 + rho0 * 64, [[-st_ * 64, n_], [1, 64]]), op=ALU.mult),
                        reads=[s_PT[w3], s_EB], writes=[s_PT[w3]])

                def emit_back(idx, cc=cc, hf=hf):
                    (par, ti, grp, r0_, st_, n_, hp) = units[idx]
                    hh = cc * 2 + hp
                    w3 = idx % 3
                    nq = 64 * n_
                    c0 = (r0_ - 16 * hf - 8 * grp) * 64
                    for (bank, lw_) in ((grp, Vpad[:, par, ti, hh, :]), (2 + grp, cpad[:, hp, :])):
                        P.op("pe", lambda h, bank=bank, lw_=lw_: h.matmul(
                            cv_ps(ps[bank], c0, [[st_ * 64, n_], [1, 64]]), lhsT=lw_, rhs=PT[w3][:, 0:nq], start=False, stop=True,
                            skip_group_check=True), reads=[s_PT[w3], s_V, s_cpad], writes=[pss[bank]])

                for idx in range(len(units) + LA):
                    if idx < len(units):
                        emit_front(idx)
                    if idx - LA >= 0:
                        emit_back(idx - LA)
                for grp in range(2):
                    u = grp
                    P.op("act", lambda h, u=u, grp=grp: h.activation(out=rinv[u], in_=ps[2 + grp][:, :], func=AF.Ln),
                         reads=[pss[2 + grp]], writes=[s_rinv[u]])
                    P.op("act", lambda h, u=u: h.activation(out=rinv[u], in_=rinv[u], func=AF.Exp, scale=-1.0),
                         reads=[s_rinv[u]], writes=[s_rinv[u]])
                    P.op("dve", lambda h, u=u, grp=grp, hg=hg, cc=cc, hf=hf: h.tensor_tensor(
                        out=attnT[:, hg * 2 + cc, hf * 1024 + grp * 512: hf * 1024 + (grp + 1) * 512], in0=ps[grp][:, :], in1=rinv[u],
                        op=ALU.mult), reads=[pss[grp], s_rinv[u]], writes=[s_attnT[hg * 2 + cc]])
    if DBG.get('dump') is not None:
        dbg = DBG['dump']
        s_dbg = P.slot("dbg")
        P.dma("pool", "dbg", lambda h: h.dma_start(out=dbg[:, 0:4096], in_=qT.rearrange("p a b -> p (a b)")), reads=[s_qT], writes=[s_dbg])
        P.dma("pool", "dbg", lambda h: h.dma_start(out=dbg[:, 4096:8192], in_=kT.rearrange("p a b -> p (a b)")), reads=[s_kT], writes=[s_dbg])
    P.barrier()
    Wo = hT[:, :, 0:1024]
    s_Wo = P.slot("n_Wo")
    for q in range(2):
        P.dma("pool", "nWo", lambda h, q=q: h.dma_start(out=Wo[:, q * 4:(q + 1) * 4, :], in_=wout[:, q * 4:(q + 1) * 4, :]),
              writes=[s_Wo])
    for t in range(16):
        j = t % 2
        r0 = tok0 + t * 128
        P.dma("sp", f"nxt{j}", lambda h, j=j, r0=r0: h.dma_start(out=xt[j], in_=xin[r0:r0 + 128, :]),
              reads=[xin_slot], writes=[s_xt[j]])
        for half in range(2):
            pd, sd = ps[half], pss[half]
            for k in range(8):
                P.op("pe", lambda h, k=k, t=t, half=half, pd=pd: h.matmul(
                    pd[:, :], lhsT=attnT[:, k, t * 128:(t + 1) * 128], rhs=Wo[:, k, half * 512:(half + 1) * 512],
                    start=(k == 0), stop=(k == 7)), reads=[s_attnT[k], s_Wo], writes=[sd])
            P.op("dve", lambda h, j=j, half=half, pd=pd: h.tensor_tensor(
                out=ost[j][:, half * 512:(half + 1) * 512], in0=pd[:, :], in1=xt[j][:, half * 512:(half + 1) * 512],
                op=ALU.add), reads=[sd, s_xt[j]], writes=[s_ost[j]])
        P.dma("sp", f"nost{j}", lambda h, j=j, r0=r0: h.dma_start(out=xout[r0:r0 + 128, :], in_=ost[j]),
              reads=[s_ost[j]], writes=[xout_slot])
    P.barrier()


def cv(ap, p0, npart, off, dims):
    n = ap.ap[0][0]
    return bass.AP(tensor=ap.tensor, offset=ap.offset + p0 * n + off, ap=[[n, npart]] + [list(d) for d in dims])


GELU_FUNC = [None]


def ab_stage(c, xin, xin_slot, xout, xout_slot, W, gain_row, ntok):
    P = c.P
    fa, ba = c.fa, c.ba
    fa.reset()
    ba.reset()
    NK = 257
    R1 = ba.take(1, 16448)[:, 0, :]
    uT = ba.take(4, SEQ)
    aT = ba.take(4, SEQ + 30)
    yaT = ba.take(4, SEQ)
    Ht = ba.take(32, 2, 128)
    Gt = ba.take(32, 2, 128)
    Mt = ba.take(32, 128)
    selin = ba.take(2, 8, 128)
    selout = ba.take(8, 8, 128)
    wsl = [ba.take(8, 128) for _ in range(2)]
    gluw = ba.take(4, 512)
    xn = [ba.take(1, 1024)[:, 0, :] for _ in range(2)]
    hT = R1[:, 0:16384].rearrange("p (a b) -> p a b", b=SEQ)
    cdiag = R1[:, 0:15872].rearrange("p (q k m) -> p q k m", k=31, m=128)
    XS = R1
    Wo = R1[:, 0:8192].rearrange("p (a b) -> p a b", b=1024)
    UY = aT[:, :, :].rearrange("p a b -> p (a b)")[:, 0:8192].rearrange("p (g k) -> p g k", k=256)
    Qt = R1[:, 0:8192].rearrange("p (r e) -> p r e", r=2)
    Pt = R1[:, 8192:16384].rearrange("p (r e) -> p r e", r=2)
    Hp = uT[:, :, :].rearrange("p a b -> p (a b)").rearrange("p (r e) -> p r e", r=2)
    gbc = fa.take(1, 1024)[:, 0, :]
    lam = fa.take(3, 32)
    Apw = fa.take(2, 32, 32)
    msk = fa.take(2, 128)
    dcol = fa.take(1, 32)[:, 0, :]
    cw = fa.take(4, 31)
    vec = fa.take(1, 20)[:, 0, :]
    ones32 = fa.take(1, 128)[:, 0, :]
    etab = fa.take(1, 32)[:, 0, :]
    TA = fa.take(1, 64)[:, 0, :]
    TB = fa.take(1, 64)[:, 0, :]
    Sf = fa.take(2, 64)
    st1 = fa.take(1, 64)[:, 0, :]
    st2 = fa.take(1, 64)[:, 0, :]
    sm = fa.take(12, 32)
    ss = [fa.take(1, 4)[:, 0, :] for _ in range(2)]
    tmpA = fa.take(1, 128)[:, 0, :]
    tmpB = fa.take(1, 128)[:, 0, :]
    Dreg = fa.take(1, 7168)[:, 0, :]
    Bp = Dreg[:, 0:1024]
    Cp = Dreg[:, 1024:2048]
    Bb = Dreg[:, 2048:3072]
    t1 = Dreg[:, 3072:4096]
    t2 = Dreg[:, 4096:5120]
    w5 = [Dreg[:, 5120:6144], Dreg[:, 6144:7168], Dreg[:, 2048:3072]]
    xt = [Dreg[:, 0:1024], Dreg[:, 1024:2048]]
    ost = [Dreg[:, 2048:3072], Dreg[:, 3072:4096]]
    sgm = [Dreg[:, 0:512], Dreg[:, 512:1024]]
    acv = Dreg[:, 0:2048].rearrange("p (q t) -> p q t", t=512)
    sqv = [Dreg[:, 2048:2560], Dreg[:, 2560:3072]]
    meanv = Dreg[:, 3072:3584]
    rstdv = Dreg[:, 3584:4096]
    tmpv = Dreg[:, 4096:4608]
    ynv = [Dreg[:, 4608:5120], Dreg[:, 5120:5632]]
    ps, pss = c.ps, c.ps_slots
    ident = c.ident

    S = lambda n: P.slot("ab_" + n)
    s_tab = S("tab")
    s_D = S("Dreg")
    s_R1 = S("R1")
    s_uT = S("uT")
    s_aT = S("aT")
    s_ya = S("yaT")
    s_H, s_G, s_M = S("H"), S("G"), S("M")
    s_sel = S("sel")
    s_wsl = [S("wsl0"), S("wsl1")]
    s_gluw = S("gluw")
    s_xn = [S("xn0"), S("xn1")]
    s_gbc = S("gbc")
    s_ss = [S("ss0"), S("ss1")]
    s_tmp = S("tmpAB")

    def dv(fn, r, w, force=True):
        P.op("dve", fn, reads=r, writes=w, force=force)

    def ac(fn, r, w, force=True):
        P.op("act", fn, reads=r, writes=w, force=force)

    P.dma("sp", "gbc", lambda h: h.dma_start(out=gbc, in_=gain_row.partition_broadcast(128)), writes=[s_gbc])
    for dst, src in ((lam, W["lam"]), (msk, W["mask"]), (dcol, W["dcol"]), (cw, W["cw"]), (vec, W["vec"]),
                     (ones32, W["ones"]), (etab, W["etab"]), (Bp, W["B"]), (Cp, W["C"])):
        P.dma("sp", "abtab", lambda h, dst=dst, src=src: h.dma_start(out=dst, in_=src), writes=[s_tab])
    P.dma("pool", "absel", lambda h: h.dma_start(out=selin, in_=W["selin"]), writes=[s_sel])
    for q in range(2):
        P.dma("pool", "absel", lambda h, q=q: h.dma_start(out=selout[:, q * 4:(q + 1) * 4], in_=W["selout"][:, q * 4:(q + 1) * 4]),
              writes=[s_sel])
    P.dma("pool", "abglu", lambda h: h.dma_start(out=gluw, in_=W["gluw"]), writes=[s_gluw])

    T = [s_tab]
    sm_ = lambda i: sm[:, i, :]
    m_dt, m_lrd, m_lid, m_pr, m_den, m_cre, m_cim, m_x, m_y = (sm_(i) for i in range(9))
    lr, li, ls = lam[:, 0, :], lam[:, 1, :], lam[:, 2, :]
    ac(lambda h: h.activation(out=m_dt, in_=ls, func=AF.Exp), T, T)
    dv(lambda h: h.tensor_tensor(out=m_lrd, in0=lr, in1=m_dt, op=ALU.mult), T, T)
    dv(lambda h: h.tensor_tensor(out=m_lid, in0=li, in1=m_dt, op=ALU.mult), T, T)
    bc_g = lambda a: cv(a, 0, 128, 0, [[0, 32], [1, 32]])
    bc_e = cv(etab, 0, 128, 0, [[1, 32], [0, 32]])
    v3 = lambda a: cv(a, 0, 128, 0, [[32, 32], [1, 32]])
    argm, ang, kf = w5
    ki = t1[:, 0:1024].bitcast(I32)
    dv(lambda h: h.tensor_tensor(out=v3(argm), in0=bc_g(m_lrd), in1=bc_e, op=ALU.mult), T, T)
    ac(lambda h: h.activation(out=argm, in_=argm, func=AF.Exp), T, T)
    dv(lambda h: h.tensor_tensor(out=v3(ang), in0=bc_g(m_lid), in1=bc_e, op=ALU.mult), T, T)
    trig = t2[:, 0:1024]
    for ri, shift in ((1, 0.0), (0, math.pi / 2)):
        src = ang
        if shift != 0.0:
            dv(lambda h: h.tensor_scalar(out=ang, in0=ang, scalar1=shift, scalar2=None, op0=ALU.add), T, T)
        dv(lambda h: h.tensor_scalar(out=kf, in0=ang, scalar1=1.0 / (2 * math.pi), scalar2=None, op0=ALU.mult), T, T)
        dv(lambda h: h.tensor_copy(out=ki, in_=kf), T, T)
        dv(lambda h: h.tensor_copy(out=kf, in_=ki), T, T)
        dv(lambda h: h.scalar_tensor_tensor(out=kf, in0=kf, scalar=-2 * math.pi, in1=ang, op0=ALU.mult, op1=ALU.add), T, T)
        ac(lambda h: h.activation(out=trig, in_=kf, func=AF.Sin), T, T)
        dv(lambda h, ri=ri: h.tensor_tensor(out=Apw[:, ri, :, :].rearrange("p a b -> p (a b)"), in0=argm, in1=trig, op=ALU.mult), T, T)
    for (p0_, j1) in ((0, 16), (64, 23)):
        hp = lambda a, p0_=p0_: cv(a, p0_, 64, 0, [[1, 32]])
        A1r = cv(Apw, p0_, 64, j1 * 32, [[1, 32]])
        A1i = cv(Apw, p0_, 64, 1024 + j1 * 32, [[1, 32]])
        lr_, li_ = hp(lr), hp(li)
        dv(lambda h, hp=hp, A1r=A1r: h.tensor_scalar(out=hp(m_pr), in0=A1r, scalar1=-1.0, scalar2=None, op0=ALU.add), T, T)
        dv(lambda h, hp=hp, lr_=lr_: h.tensor_tensor(out=hp(m_x), in0=lr_, in1=lr_, op=ALU.mult), T, T)
        dv(lambda h, hp=hp, li_=li_: h.tensor_tensor(out=hp(m_y), in0=li_, in1=li_, op=ALU.mult), T, T)
        dv(lambda h, hp=hp: h.tensor_tensor(out=hp(m_den), in0=hp(m_x), in1=hp(m_y), op=ALU.add), T, T)
        dv(lambda h, hp=hp: h.reciprocal(out=hp(m_den), in_=hp(m_den)), T, T)
        dv(lambda h, hp=hp, lr_=lr_: h.tensor_tensor(out=hp(m_x), in0=hp(m_pr), in1=lr_, op=ALU.mult), T, T)
        dv(lambda h, hp=hp, li_=li_, A1i=A1i: h.tensor_tensor(out=hp(m_y), in0=A1i, in1=li_, op=ALU.mult), T, T)
        dv(lambda h, hp=hp: h.tensor_tensor(out=hp(m_x), in0=hp(m_x), in1=hp(m_y), op=ALU.add), T, T)
        dv(lambda h, hp=hp: h.tensor_tensor(out=hp(m_cre), in0=hp(m_x), in1=hp(m_den), op=ALU.mult), T, T)
        dv(lambda h, hp=hp, lr_=lr_, A1i=A1i: h.tensor_tensor(out=hp(m_x), in0=A1i, in1=lr_, op=ALU.mult), T, T)
        dv(lambda h, hp=hp, li_=li_: h.tensor_tensor(out=hp(m_y), in0=hp(m_pr), in1=li_, op=ALU.mult), T, T)
        dv(lambda h, hp=hp: h.tensor_tensor(out=hp(m_x), in0=hp(m_x), in1=hp(m_y), op=ALU.subtract), T, T)
        dv(lambda h, hp=hp: h.tensor_tensor(out=hp(m_cim), in0=hp(m_x), in1=hp(m_den), op=ALU.mult), T, T)
    bcc = lambda a: cv(a, 0, 128, 0, [[1, 32], [0, 16]])
    g16 = lambda a, off: cv(a, 0, 128, off, [[16, 32], [1, 16]])
    for (o_off, a_, x_off, b_, y_off, op) in ((0, m_cre, 0, m_cim, 512, ALU.subtract), (512, m_cre, 512, m_cim, 0, ALU.add)):
        dv(lambda h, a_=a_, x_off=x_off: h.tensor_tensor(out=g16(t1, 0), in0=bcc(a_), in1=g16(Bp, x_off), op=ALU.mult), T, T)
        dv(lambda h, b_=b_, y_off=y_off: h.tensor_tensor(out=g16(t2, 0), in0=bcc(b_), in1=g16(Bp, y_off), op=ALU.mult), T, T)
        dv(lambda h, o_off=o_off, op=op: h.tensor_tensor(out=Bb[:, o_off:o_off + 512], in0=t1[:, 0:512], in1=t2[:, 0:512], op=op), T, T)

    def fam(dst, d_goff, d_rioff, V, negim, j0):
        np_, p0, js = 128, 0, 1
        for qq in range(4):
            Ar = cv(Apw, p0, np_, j0 * 32 + 8 * qq, [[1, 8], [js * 32, 8], [0, 16]])
            Ai = cv(Apw, p0, np_, 1024 + j0 * 32 + 8 * qq, [[1, 8], [js * 32, 8], [0, 16]])
            Vr = cv(V, p0, np_, 8 * qq * 16, [[16, 8], [0, 8], [1, 16]])
            Vi = cv(V, p0, np_, 512 + 8 * qq * 16, [[16, 8], [0, 8], [1, 16]])
            T1 = cv(t1, p0, np_, 0, [[128, 8], [16, 8], [1, 16]])
            T2 = cv(t2, p0, np_, 0, [[128, 8], [16, 8], [1, 16]])
            dre = cv(dst, p0, np_, 8 * qq * d_goff, [[d_goff, 8], [16, 8], [1, 16]])
            dim_ = cv(dst, p0, np_, 8 * qq * d_goff + d_rioff, [[d_goff, 8], [16, 8], [1, 16]])
            dv(lambda h, Ar=Ar, Vr=Vr, T1=T1: h.tensor_tensor(out=T1, in0=Ar, in1=Vr, op=ALU.mult), T, T)
            dv(lambda h, Ai=Ai, Vi=Vi, T2=T2: h.tensor_tensor(out=T2, in0=Ai, in1=Vi, op=ALU.mult), T, T)
            dv(lambda h, dre=dre, T1=T1, T2=T2: h.tensor_tensor(out=dre, in0=T1, in1=T2, op=ALU.subtract), T, T)
            dv(lambda h, Ar=Ar, Vi=Vi, T1=T1: h.tensor_tensor(out=T1, in0=Ar, in1=Vi, op=ALU.mult), T, T)
            dv(lambda h, Ai=Ai, Vr=Vr, T2=T2: h.tensor_tensor(out=T2, in0=Ai, in1=Vr, op=ALU.mult), T, T)
            if negim:
                dv(lambda h, T1=T1: h.tensor_scalar(out=T1, in0=T1, scalar1=-1.0, scalar2=None, op0=ALU.mult), T, T)
                dv(lambda h, dim_=dim_, T1=T1, T2=T2: h.tensor_tensor(out=dim_, in0=T1, in1=T2, op=ALU.subtract), T, T)
            else:
                dv(lambda h, dim_=dim_, T1=T1, T2=T2: h.tensor_tensor(out=dim_, in0=T1, in1=T2, op=ALU.add), T, T)

    fam(Qt, 128, 4096, Bb, False, 0)
    fam(Pt, 128, 4096, Cp, True, 8)
    fam(Gt, 256, 128, Cp, True, 16)
    fam(Hp, 128, 4096, Bb, False, 24)
    for (p0_, j8) in ((0, 23), (64, 16)):
        Dr = cv(Apw, p0_, 64, j8 * 32, [[1, 32]])
        Di = cv(Apw, p0_, 64, 1024 + j8 * 32, [[1, 32]])
        hq = lambda a, off, p0_=p0_: cv(a, p0_, 64, off, [[1, 32]])
        dv(lambda h, hq=hq, Dr=Dr: h.tensor_copy(out=hq(TA, 0), in_=Dr), T, T)
        dv(lambda h, hq=hq, Dr=Dr: h.tensor_copy(out=hq(TA, 32), in_=Dr), T, T)
        dv(lambda h, hq=hq, Di=Di: h.tensor_scalar(out=hq(TB, 0), in0=Di, scalar1=-1.0, scalar2=None, op0=ALU.mult), T, T)
        dv(lambda h, hq=hq, Di=Di: h.tensor_copy(out=hq(TB, 32), in_=Di), T, T)
    for g in range(32):
        u = g % 2
        pf, pb = ps[2 * u], ps[2 * u + 1]
        sf_, sb_ = pss[2 * u], pss[2 * u + 1]
        for (pp, sp_, p0) in ((pf, sf_, 0), (pb, sb_, 64)):
            for ri in range(2):
                P.op("pe", lambda h, pp=pp, p0=p0, ri=ri, g=g: h.matmul(
                    pp[:, 0:128], lhsT=Qt[p0:p0 + 64, ri, g * 128:(g + 1) * 128], rhs=Pt[p0:p0 + 64, ri, g * 128:(g + 1) * 128],
                    start=(ri == 0), stop=(ri == 1)), reads=T, writes=[sp_])
        dv(lambda h, pf=pf: h.tensor_tensor(out=tmpA, in0=pf[:, 0:128], in1=msk[:, 0, :], op=ALU.mult), [sf_] + T, [s_tmp])
        dv(lambda h, pb=pb: h.tensor_tensor(out=tmpB, in0=pb[:, 0:128], in1=msk[:, 1, :], op=ALU.mult), [sb_] + T, [s_tmp])
        dv(lambda h: h.tensor_tensor(out=tmpA, in0=tmpA, in1=tmpB, op=ALU.add), [s_tmp], [s_tmp])
        dv(lambda h, g=g: h.scalar_tensor_tensor(out=Mt[:, g, :], in0=ident[:], scalar=dcol[:, g:g + 1], in1=tmpA,
                                                  op0=ALU.mult, op1=ALU.add), [s_tmp, c.ident_slot] + T, [s_M])
    for b in range(8):
        for gg in range(4):
            for ri in range(2):
                g = b * 4 + gg
                sl = gg * 2 + ri
                P.op("pe", lambda h, g=g, ri=ri, sl=sl: h.transpose(
                    out=c.psb[:, sl * 128:(sl + 1) * 128], in_=Hp[:, ri, g * 128:(g + 1) * 128], identity=ident[:]),
                    reads=T + [c.ident_slot], writes=[c.psb_slot])
        ac(lambda h, b=b: h.copy(out=Ht[:, b * 4:(b + 1) * 4, :, :].rearrange("p a b c -> p (a b c)"), in_=c.psb[:, :]),
           [c.psb_slot], [s_H])
    P.barrier()

    for sq_ in range(ntok // SEQ):
        tok0 = sq_ * SEQ
        s_hT = [S(f"hT{t}") for t in range(16)]
        s_xt1 = [S("xt1_0"), S("xt1_1")]
        for t in range(16):
            j = t % 2
            r0 = tok0 + t * 128
            P.dma("sp", f"abxt{j}", lambda h, j=j, r0=r0: h.dma_start(out=xt[j], in_=xin[r0:r0 + 128, :]),
                  reads=[xin_slot], writes=[s_xt1[j]])
            norm_transpose(c, xt[j], s_xt1[j], gbc, s_gbc, xn[j], s_xn[j], ss[j], s_ss[j], hT, s_hT[t], t * 128)
        P.barrier()
        s_sg = [S("sg0"), S("sg1")]
        s_aTq = [S(f"aT{q}") for q in range(4)]
        s_uTq = [S(f"uT{q}") for q in range(4)]
        for q in range(4):
            dv(lambda h, q=q: h.memset(aT[:, q, 0:15], 0.0), [], [s_aTq[q]], force=False)
            dv(lambda h, q=q: h.memset(aT[:, q, SEQ + 15:SEQ + 30], 0.0), [], [s_aTq[q]], force=False)
        wi = 0
        ui = 0

        def load_slab(oc):
            nonlocal wi
            j = wi % 2
            wi += 1
            P.dma("pool", f"abw{j}", lambda h, j=j, oc=oc: h.dma_start(out=wsl[j], in_=W["win"][oc]), writes=[s_wsl[j]])
            return j

        for q in range(4):
            ja = load_slab(q)
            jg = load_slab(q + 4)
            for tb in range(4):
                u = ui % 2
                ui += 1
                pa, pg = ps[2 * u], ps[2 * u + 1]
                sa, sg_ = pss[2 * u], pss[2 * u + 1]
                for (pp, sp_, jj) in ((pa, sa, ja), (pg, sg_, jg)):
                    for k in range(8):
                        P.op("pe", lambda h, pp=pp, jj=jj, k=k, tb=tb: h.matmul(
                            pp[:, :], lhsT=wsl[jj][:, k, :], rhs=hT[:, k, tb * 512:(tb + 1) * 512],
                            start=(k == 0), stop=(k == 7)), reads=[s_wsl[jj]] + s_hT[tb * 4:(tb + 1) * 4], writes=[sp_])
                ac(lambda h, u=u, pg=pg: h.activation(out=sgm[u], in_=pg[:, :], func=AF.Sigmoid), [sg_], [s_sg[u]], force=False)
                dv(lambda h, u=u, pa=pa, q=q, tb=tb: h.tensor_tensor(
                    out=aT[:, q, 15 + tb * 512:15 + (tb + 1) * 512], in0=sgm[u], in1=pa[:, :], op=ALU.mult),
                    [s_sg[u], sa], [s_aTq[q]], force=False)
        for q in range(4):
            ju = load_slab(q + 8)
            for tb in range(4):
                u = ui % 2
                ui += 1
                pu_, su_ = ps[4 + u], pss[4 + u]
                for k in range(8):
                    P.op("pe", lambda h, pu_=pu_, ju=ju, k=k, tb=tb: h.matmul(
                        pu_[:, :], lhsT=wsl[ju][:, k, :], rhs=hT[:, k, tb * 512:(tb + 1) * 512],
                        start=(k == 0), stop=(k == 7)), reads=[s_wsl[ju]] + s_hT[tb * 4:(tb + 1) * 4], writes=[su_])
                ac(lambda h, pu_=pu_, q=q, tb=tb: h.copy(
                    out=cv(uT, 0, 128, q * SEQ + tb * 64, [[256, 8], [1, 64]]),
                    in_=pu_[:, :].rearrange("p (k t) -> p t k", t=8)), [su_], [s_uTq[q]], force=False)
        P.barrier()
        s_cd = [S(f"cd{q}") for q in range(4)]
        for q in range(4):
            for k in range(31):
                dv(lambda h, q=q, k=k: h.tensor_scalar(out=cdiag[:, q, k, :], in0=ident[:], scalar1=cw[:, q, k:k + 1],
                                                        scalar2=None, op0=ALU.mult), [c.ident_slot], [s_cd[q]], force=False)
        s_ac = [S(f"ac{q}") for q in range(4)]
        s_sq = [S("sq0"), S("sq1")]
        s_mean, s_rstd, s_tv = S("mean"), S("rstd"), S("tmpv")
        s_yn = [S("yn0"), S("yn1")]
        for tb in range(4):
            for q in range(4):
                pc, spc = ps[q % 2], pss[q % 2]
                for k in range(31):
                    P.op("pe", lambda h, pc=pc, q=q, k=k, tb=tb: h.matmul(
                        pc[:, :], lhsT=cdiag[:, q, k, :], rhs=aT[:, q, tb * 512 + k: tb * 512 + k + 512],
                        start=(k == 0), stop=(k == 30)), reads=[s_cd[q]], writes=[spc])
                ac(lambda h, pc=pc, q=q: h.activation(out=acv[:, q, :], in_=pc[:, :], func=AF.Identity,
                                                      bias=vec[:, q:q + 1], scale=1.0), [spc], [s_ac[q]], force=False)
            pm, spm = ps[2], pss[2]
            pe2, spe2 = ps[3], pss[3]
            for q in range(4):
                P.op("pe", lambda h, q=q: h.matmul(pm[:, :], lhsT=cv(ones32, 0, 128, 0, [[1, 128]]), rhs=acv[:, q, :],
                                                   start=(q == 0), stop=(q == 3)), reads=[s_ac[q]], writes=[spm])
            for q in range(4):
                u = q % 2
                ac(lambda h, q=q, u=u: h.activation(out=sqv[u], in_=acv[:, q, :], func=AF.Square), [s_ac[q]], [s_sq[u]], force=False)
                P.op("pe", lambda h, q=q, u=u: h.matmul(pe2[:, :], lhsT=cv(ones32, 0, 128, 0, [[1, 128]]), rhs=sqv[u],
                                                        start=(q == 0), stop=(q == 3)), reads=[s_sq[u]], writes=[spe2])
            dv(lambda h: h.tensor_copy(out=meanv, in_=pm[:, :]), [spm], [s_mean], force=False)
            dv(lambda h: h.tensor_tensor(out=tmpv, in0=meanv, in1=meanv, op=ALU.mult), [s_mean], [s_tv])
            dv(lambda h: h.tensor_tensor(out=tmpv, in0=pe2[:, :], in1=tmpv, op=ALU.subtract), [spe2, s_tv], [s_tv])
            ac(lambda h: h.activation(out=rstdv, in_=tmpv, func=AF.Ln, bias=c.cst[:, 2:3], scale=1.0), [s_tv, c.cst_slot], [s_rstd])
            ac(lambda h: h.activation(out=rstdv, in_=rstdv, func=AF.Exp, scale=-0.5), [s_rstd], [s_rstd])
            for q in range(4):
                u = q % 2
                dv(lambda h, q=q, u=u: h.tensor_tensor(out=ynv[u], in0=acv[:, q, :], in1=meanv, op=ALU.subtract),
                   [s_ac[q], s_mean], [s_yn[u]], force=False)
                dv(lambda h, u=u: h.tensor_tensor(out=ynv[u], in0=ynv[u], in1=rstdv, op=ALU.mult), [s_yn[u], s_rstd], [s_yn[u]])
                ac(lambda h, q=q, u=u, tb=tb: h.activation(out=yaT[:, q, tb * 512:(tb + 1) * 512], in_=ynv[u], func=AF.Silu,
                                                           scale=vec[:, 4 + q:5 + q], bias=vec[:, 8 + q:9 + q]),
                   [s_yn[u]], [s_ya], force=False)
        P.barrier()
        s_UY = [S(f"UY{g}") for g in range(32)]
        s_X = [S("Xf"), S("Xb")]
        s_hist = [S("histf"), S("histb")]
        XSv = lambda p0, ri, g, c0, n: cv(XS, p0, 64, (ri * 32 + g) * NK + c0, [[1, n]])
        dv(lambda h: h.memset(cv(XS, 0, 64, 0, [[NK, 64]]), 0.0), [], [s_hist[0]], force=False)
        dv(lambda h: h.memset(cv(XS, 64, 64, 255, [[NK, 64]]), 0.0), [], [s_hist[1]], force=False)
        for g in range(32):
            q, j, par = g // 8, (g % 8) // 2, g % 2
            u = g % 2
            pu_, su_ = ps[u], pss[u]
            for tp in range(8):
                rhs = cv(uT, 32 * j, 32, q * SEQ + tp * 256, [[1, 256]])
                P.op("pe", lambda h, pu_=pu_, j=j, par=par, tp=tp, rhs=rhs: h.matmul(
                    pu_[:, 0:256], lhsT=selin[32 * j:32 * j + 32, par, tp, :], rhs=rhs, start=(tp == 0), stop=(tp == 7),
                    tile_position=(32 * j, 0)), reads=[s_uTq[q], s_sel], writes=[su_])
            ac(lambda h, pu_=pu_, g=g: h.copy(out=UY[:, g, :], in_=pu_[:, 0:256]), [su_], [s_UY[g]], force=False)
            for ri in range(2):
                px, spx = ps[2 + ri], pss[2 + ri]
                P.op("pe", lambda h, px=px, g=g, ri=ri: h.matmul(px[:, 0:256], lhsT=Ht[:, g, ri, :], rhs=UY[:, g, :],
                                                               start=True, stop=True), reads=[s_H, s_UY[g]], writes=[spx])
                dv(lambda h, px=px, g=g, ri=ri: h.tensor_copy(out=XSv(0, ri, g, 1, 256), in_=px[0:64, 0:256]),
                   [spx], [s_X[0]], force=False)
                dv(lambda h, px=px, g=g, ri=ri: h.tensor_copy(out=XSv(64, ri, g, 0, 255), in_=px[64:128, 1:256]),
                   [spx], [s_X[1]], force=False)
        dirs = []
        for (eng, p0, d, cols) in (("dve", 0, 0, list(range(1, 256))), ("pool", 64, 1, list(range(254, -1, -1)))):
            s_S = [S(f"S{d}_0"), S(f"S{d}_1")]
            s_t1_, s_t2_ = S(f"t1_{d}"), S(f"t2_{d}")
            v2 = lambda a, off=0, p0=p0: cv(a, p0, 64, off, [[32, 2], [1, 32]])
            vsw = lambda a, off=0, p0=p0: cv(a, p0, 64, off + 32, [[-32, 2], [1, 32]])
            P.op(eng, lambda h, p0=p0: h.memset(cv(Sf, p0, 64, 0, [[1, 64]]), 0.0), writes=[s_S[0]])
            dirs.append((eng, p0, d, cols, s_S, s_t1_, s_t2_, v2, vsw))
        for i in range(255):
            for (eng, p0, d, cols, s_S, s_t1_, s_t2_, v2, vsw) in dirs:
                col = cols[i]
                pv_, cu = i % 2, (i + 1) % 2
                xcol = cv(XS, p0, 64, col, [[32 * NK, 2], [NK, 32]])
                P.op(eng, lambda h, pv_=pv_, v2=v2: h.tensor_tensor(out=v2(st1), in0=v2(TA), in1=v2(Sf, pv_ * 64), op=ALU.mult),
                     reads=[s_S[pv_]], writes=[s_t1_], force=True)
                P.op(eng, lambda h, pv_=pv_, v2=v2, vsw=vsw: h.tensor_tensor(out=v2(st2), in0=v2(TB), in1=vsw(Sf, pv_ * 64), op=ALU.mult),
                     reads=[s_S[pv_]], writes=[s_t2_], force=True)
                P.op(eng, lambda h, v2=v2, xcol=xcol: h.tensor_tensor(out=v2(st2), in0=v2(st2), in1=xcol, op=ALU.add),
                     reads=[s_t2_, s_X[d]], writes=[s_t2_], force=True)
                P.op(eng, lambda h, cu=cu, v2=v2: h.tensor_tensor(out=v2(Sf, cu * 64), in0=v2(st1), in1=v2(st2), op=ALU.add),
                     reads=[s_t1_, s_t2_], writes=[s_S[cu]], force=True)
                P.op("act", lambda h, cu=cu, v2=v2, xcol=xcol: h.copy(out=xcol, in_=v2(Sf, cu * 64)),
                     reads=[s_S[cu]], writes=[s_hist[d]])
        for g in range(32):
            u = g % 2
            py, spy = ps[4 + u], pss[4 + u]
            P.op("pe", lambda h, py=py, g=g: h.matmul(py[:, 0:256], lhsT=Mt[:, g, :], rhs=UY[:, g, :], start=True, stop=False),
                 reads=[s_M, s_UY[g]], writes=[spy])
            for ri in range(2):
                P.op("pe", lambda h, py=py, g=g, ri=ri: h.matmul(
                    py[:, 0:256], lhsT=Gt[:, g, ri, :], rhs=cv(XS, 0, 128, (ri * 32 + g) * NK, [[1, 256]]),
                    start=False, stop=(ri == 1)), reads=[s_G, s_hist[0], s_hist[1], s_X[0], s_X[1]], writes=[spy])
            ac(lambda h, py=py, g=g: h.copy(out=UY[:, g, :], in_=py[:, 0:256]), [spy], [s_UY[g]], force=False)
        s_yg = [S(f"yg{q}") for q in range(4)]
        ci = 0
        for q in range(4):
            for tau in range(8):
                u = ci % 2
                ci += 1
                pz, spz = ps[u], pss[u]
                for g8 in range(8):
                    P.op("pe", lambda h, pz=pz, tau=tau, g8=g8, q=q: h.matmul(
                        pz[:, 0:256], lhsT=selout[:, tau, g8, :], rhs=UY[:, 8 * q + g8, :], start=(g8 == 0), stop=(g8 == 7)),
                        reads=[s_sel, s_UY[8 * q + g8]], writes=[spz])
                gelu_evac(c, pz, spz, cv(uT, 0, 128, q * SEQ + tau, [[8, 256]]), s_yg[q], s_uTq[q])
        s_sg2 = [S("sg2_0"), S("sg2_1")]
        for tb in range(4):
            for o in range(4):
                pz, spz = ps[2 + o], pss[2 + o]
                for kq in range(4):
                    P.op("pe", lambda h, pz=pz, kq=kq, o=o, tb=tb: h.matmul(
                        pz[:, :], lhsT=gluw[:, kq, o * 128:(o + 1) * 128], rhs=uT[:, kq, tb * 512:(tb + 1) * 512],
                        start=(kq == 0), stop=(kq == 3)), reads=[s_gluw] + s_yg, writes=[spz])
            for o in range(4):
                u = o % 2
                pz, spz = ps[2 + o], pss[2 + o]
                ac(lambda h, pz=pz, o=o, u=u: h.activation(out=sgm[u], in_=pz[:, :], func=AF.Sigmoid,
                                                           bias=vec[:, 12 + o:13 + o], scale=1.0), [spz], [s_sg2[u]], force=False)
                dv(lambda h, o=o, u=u, tb=tb: h.tensor_tensor(out=uT[:, o, tb * 512:(tb + 1) * 512],
                                                              in0=uT[:, o, tb * 512:(tb + 1) * 512], in1=sgm[u], op=ALU.mult),
                   [s_sg2[u], s_yg[o]], [s_yg[o]], force=False)
        P.barrier()
        s_Wo = S("Wo")
        for q in range(2):
            P.dma("pool", "abWo", lambda h, q=q: h.dma_start(out=Wo[:, q * 4:(q + 1) * 4, :], in_=W["wout"][:, q * 4:(q + 1) * 4, :]),
                  writes=[s_Wo])
        s_xt2 = [S("xt2_0"), S("xt2_1")]
        s_ost = [S("ost0"), S("ost1")]
        for t in range(16):
            j = t % 2
            r0 = tok0 + t * 128
            P.dma("sp", f"abxt{j}", lambda h, j=j, r0=r0: h.dma_start(out=xt[j], in_=xin[r0:r0 + 128, :]),
                  reads=[xin_slot], writes=[s_xt2[j]])
            for half in range(2):
                pd, sd = ps[half], pss[half]
                for k in range(8):
                    src = yaT if k < 4 else uT
                    P.op("pe", lambda h, k=k, t=t, half=half, pd=pd, src=src: h.matmul(
                        pd[:, :], lhsT=src[:, k % 4, t * 128:(t + 1) * 128], rhs=Wo[:, k, half * 512:(half + 1) * 512],
                        start=(k == 0), stop=(k == 7)), reads=[s_Wo], writes=[sd])
                dv(lambda h, j=j, half=half, pd=pd: h.tensor_tensor(
                    out=ost[j][:, half * 512:(half + 1) * 512], in0=pd[:, :], in1=xt[j][:, half * 512:(half + 1) * 512],
                    op=ALU.add), [sd, s_xt2[j]], [s_ost[j]], force=False)
            P.dma("sp", f"abost{j}", lambda h, j=j, r0=r0: h.dma_start(out=xout[r0:r0 + 128, :], in_=ost[j]),
                  reads=[s_ost[j]], writes=[xout_slot])
        P.barrier()


def gelu_evac(c, pz, spz, dst, s_dst, s_dst2):
    P = c.P
    P.op("act", lambda h: h.activation(out=dst, in_=pz[:, 0:256], func=AF.Gelu_apprx_tanh), reads=[spz], writes=[s_dst, s_dst2])

def build_program(plan=None, ntok=TOK):
    if plan is None:
        plan = []
        for l in range(DEPTH):
            plan.append(("ab" if l % 2 == 0 else "na", l))
            plan.append(("ffn", l))
    nc = bass.Bass("TRN2", target_bir_lowering=False)
    c = setup_ctx(nc)
    P = c.P
    x = nc.dram_tensor("x", [ntok, D], F32, kind="ExternalInput").ap()
    out = nc.dram_tensor("out", [ntok, D], F32, kind="ExternalOutput").ap()
    ident = nc.dram_tensor("ident", [2, 128, 128], F32, kind="ExternalInput").ap()
    wg = nc.dram_tensor("ffn_wg", [DEPTH, NM, 128, 8, 128], F32, kind="ExternalInput").ap()
    wu = nc.dram_tensor("ffn_wu", [DEPTH, NM, 128, 8, 128], F32, kind="ExternalInput").ap()
    wd = nc.dram_tensor("ffn_wd", [DEPTH, FF, D], F32, kind="ExternalInput").ap()
    fnorm = nc.dram_tensor("ffn_norm", [DEPTH, D], F32, kind="ExternalInput").ap()
    mnorm = nc.dram_tensor("mix_norm", [DEPTH, D], F32, kind="ExternalInput").ap()
    na_wqkv = nc.dram_tensor("na_wqkv", [2, 4, 128, 8, 768], F32, kind="ExternalInput").ap()
    na_wout = nc.dram_tensor("na_wout", [2, 128, 8, 1024], F32, kind="ExternalInput").ap()
    na_bias = nc.dram_tensor("na_bias", [2, 4, 128, 4, 14, 64], F32, kind="ExternalInput").ap()
    na_qkg = nc.dram_tensor("na_qkg", [2, 128, 2], F32, kind="ExternalInput").ap()
    abd = {}
    for nm, shp in AB_SHAPES.items():
        abd[nm] = nc.dram_tensor("ab_" + nm, list(shp), F32, kind="ExternalInput").ap()
    scr = [nc.dram_tensor(f"scr{i}", [ntok, D], F32, kind="Internal").ap() for i in range(2)]
    if DBG.get('dump_on'):
        DBG['dump'] = nc.dram_tensor("dbg", [128, 16384], F32, kind="ExternalOutput").ap()
    s_scr = [P.slot("scr0"), P.slot("scr1")]
    s_x = P.slot("x_dram")
    s_out = P.slot("out_dram")
    load_ident(c, ident)
    cur, s_cur = x, s_x
    for si, st in enumerate(plan):
        if si == len(plan) - 1:
            dst, s_dst = out, s_out
        else:
            dst, s_dst = scr[si % 2], s_scr[si % 2]
        kind, l = st
        if kind == "ffn":
            ffn_stage(c, cur, s_cur, dst, s_dst, wg[l], wu[l], wd[l], fnorm[l:l + 1, :], ntok=ntok)
        elif kind == "na":
            i = l // 2
            for sq_ in range(ntok // SEQ):
                na_stage(c, cur, s_cur, dst, s_dst, na_wqkv[i], na_wout[i], na_bias[i], na_qkg[i],
                         mnorm[l:l + 1, :], sq_ * SEQ)
        elif kind == "ab":
            i = l // 2
            Wd_ = {k: (v[i] if k in AB_PER_LAYER else v) for k, v in abd.items()}
            ab_stage(c, cur, s_cur, dst, s_dst, Wd_, mnorm[l:l + 1, :], ntok)
        cur, s_cur = dst, s_dst
    fin = Ins("sp", None, False, None, 0)
    for e in ENGS:
        for i in P.streams[e]:
            if i.is_dma:
                fin.deps.append(i)
    fin.idx = len(P.streams["sp"])
    P.streams["sp"].append(fin)
    P.emit()
    return nc


AB_PER_LAYER = ("win", "wout", "gluw", "vec", "cw", "lam", "B", "C", "dcol")
AB_SHAPES = {
    "win": (2, 12, 128, 8, 128), "wout": (2, 128, 8, 1024), "gluw": (2, 128, 4, 512), "vec": (2, 128, 20),
    "cw": (2, 128, 4, 31), "lam": (2, 128, 3, 32), "B": (2, 128, 2, 32, 16), "C": (2, 128, 2, 32, 16),
    "dcol": (2, 128, 32), "selin": (128, 2, 8, 128), "selout": (128, 8, 8, 128), "mask": (128, 2, 128),
    "etab": (128, 32), "ones": (128, 128),
}


def ab_host_layout(inputs):
    g = np.ascontiguousarray
    f = lambda k: np.asarray(inputs[k], dtype=np.float32)
    d = {}
    d["win"] = g(f("ab_w_in").reshape(2, 8, 128, 12, 128).transpose(0, 3, 2, 1, 4))
    d["wout"] = g(f("ab_w_out").reshape(2, 8, 128, 1024).transpose(0, 2, 1, 3))
    d["gluw"] = g(f("ssm_glu_w").reshape(2, 4, 128, 512).transpose(0, 2, 1, 3))
    vec = np.zeros((2, 128, 20), np.float32)
    for j, k in enumerate(("conv_b", "conv_ln_g", "conv_ln_b", "ssm_glu_b")):
        vec[:, :, 4 * j:4 * j + 4] = f(k).reshape(2, 4, 128).transpose(0, 2, 1)
    d["vec"] = vec
    d["cw"] = g(f("conv_w").reshape(2, 31, 4, 128).transpose(0, 3, 2, 1))
    lam = np.empty((2, 2, 64, 3, 32), np.float32)
    lam[:, :, :, 0] = f("ssm_lambda_re").transpose(0, 1, 3, 2)
    lam[:, :, :, 1] = f("ssm_lambda_im").transpose(0, 1, 3, 2)
    lam[:, :, :, 2] = f("ssm_log_step")[:, :, None, :]
    d["lam"] = g(lam.reshape(2, 128, 3, 32))
    B = np.stack([f("ssm_b_re"), f("ssm_b_im")], axis=1)
    d["B"] = g(B.transpose(0, 2, 4, 1, 3, 5).reshape(2, 128, 2, 32, 16))
    C = np.stack([f("ssm_c_re"), f("ssm_c_im")], axis=1)
    d["C"] = g(C.transpose(0, 2, 5, 1, 3, 4).reshape(2, 128, 2, 32, 16))
    dsk = f("ssm_d").reshape(2, 32, 16)
    d["dcol"] = g(np.broadcast_to(dsk.transpose(0, 2, 1)[:, None], (2, 8, 16, 32)).reshape(2, 128, 32))
    selin = np.zeros((4, 2, 16, 2, 8, 8, 16), np.float32)
    selout = np.zeros((8, 16, 8, 8, 8, 16), np.float32)
    for cc in range(16):
        for t in range(8):
            selin[:, 0, cc, 0, t, t, cc] = 1.0
            selin[:, 1, cc, 1, t, t, cc] = 1.0
            for g8 in range(8):
                selout[t, cc, t, g8, g8, cc] = 1.0
    d["selin"] = selin.reshape(128, 2, 8, 128)
    d["selout"] = selout.reshape(128, 8, 8, 128)
    tp = np.repeat(np.arange(8), 16)
    d["mask"] = g(np.stack([(tp[:, None] <= tp[None, :]), (tp[:, None] >= tp[None, :])], axis=1).astype(np.float32))
    tt = np.arange(8, dtype=np.float32)
    ef = np.concatenate([-tt, tt, tt + 1, 7 - tt])
    eb = np.concatenate([tt, -tt, 8 - tt, tt])
    d["etab"] = g(np.concatenate([np.broadcast_to(ef, (64, 32)), np.broadcast_to(eb, (64, 32))], axis=0))
    d["ones"] = np.full((128, 128), 1.0 / 512, np.float32)
    return {"ab_" + k: v for k, v in d.items()}


def host_layout(inputs):
    g = np.ascontiguousarray
    d = {}
    bd = np.zeros((128, 128), np.float32)
    bd[:64, :64] = 1.0
    bd[64:, 64:] = 1.0
    d["ident"] = np.stack([np.eye(128, dtype=np.float32), bd])
    for nm, key in (("ffn_wg", "ffn_w_gate"), ("ffn_wu", "ffn_w_up")):
        w = np.asarray(inputs[key], dtype=np.float32).reshape(DEPTH, 8, 128, NM, 128)
        d[nm] = g(w.transpose(0, 3, 2, 1, 4))
    d["ffn_wd"] = g(np.asarray(inputs["ffn_w_down"], dtype=np.float32))
    d["ffn_norm"] = g(np.asarray(inputs["ffn_norm"], dtype=np.float32))
    d["mix_norm"] = g(np.asarray(inputs["mix_norm"], dtype=np.float32))
    wqkv = np.asarray(inputs["na_w_qkv"], dtype=np.float32).reshape(2, 8, 128, 3, 4, 256)
    d["na_wqkv"] = g(wqkv.transpose(0, 4, 2, 1, 3, 5).reshape(2, 4, 128, 8, 768))
    d["na_wout"] = g(np.asarray(inputs["na_w_out"], dtype=np.float32).reshape(2, 8, 128, 1024).transpose(0, 2, 1, 3))
    rpb = np.asarray(inputs["na_rpb"], dtype=np.float32)
    qc = np.arange(64)[None, :]
    kc = np.arange(64)[:, None]
    cidx = np.clip(kc - qc, -15, 15) + 15
    cstart = np.clip(qc - 8, 0, 48)
    cmask = (kc >= cstart) & (kc < cstart + 16)
    tab = np.empty((2, 16, 14, 2, 64, 64), np.float32)
    for rho in range(14):
        for jj in range(2):
            tab[:, :, rho, jj] = np.where(cmask[None, None], rpb[:, :, rho + jj][:, :, cidx], np.float32(-30000.0))
    tab = tab.reshape(2, 4, 4, 14, 2, 64, 64).transpose(0, 1, 4, 5, 2, 3, 6).reshape(2, 4, 128, 4, 14, 64)
    d["na_bias"] = g(tab)
    qg = np.asarray(inputs["na_q_norm"], dtype=np.float32)
    kg = np.asarray(inputs["na_k_norm"], dtype=np.float32)
    d["na_qkg"] = g(np.stack([np.tile(qg, (1, 2)), np.tile(kg, (1, 2))], axis=-1))
    d.update(ab_host_layout(inputs))
    return d


def kernel(**inputs):
    x = np.asarray(inputs["x"], dtype=np.float32)
    shared = host_layout(inputs)
    nc = build_program()
    in_maps = []
    for i in range(NCORES):
        m = dict(shared)
        m["x"] = np.ascontiguousarray(x[2 * i:2 * i + 2].reshape(TOK, D))
        in_maps.append(m)
    res = run_bass_kernel_spmd(nc, in_maps, core_ids=list(range(NCORES)))
    outs = [res.results[i]["out"].reshape(2, SEQ, D) for i in range(NCORES)]
    return np.concatenate(outs, axis=0).astype(np.float32)
```

```python
import math
import numpy as np
import concourse.bass as bass
import concourse.mybir as mybir
from concourse.bass_utils import run_bass_kernel_spmd

F32 = mybir.dt.float32
BF16 = mybir.dt.bfloat16
I32 = mybir.dt.int32
AF = mybir.ActivationFunctionType
ALU = mybir.AluOpType
AX = mybir.AxisListType

DBG = {}
NCORES = 8
D = 1024
SEQ = 2048
TOK = 2 * SEQ
FF = 2816
NM = FF // 128
DEPTH = 4
EPS = 1e-6


class Slot:
    __slots__ = ("name", "lw", "rs")

    def __init__(self, name):
        self.name = name
        self.lw = None
        self.rs = []


class Ins:
    __slots__ = ("eng", "fn", "deps", "is_dma", "sem", "semval", "inc", "idx", "force")

    def __init__(self, eng, fn, is_dma, sem, semval):
        self.eng = eng
        self.fn = fn
        self.deps = []
        self.is_dma = is_dma
        self.sem = sem
        self.semval = semval
        self.inc = False
        self.idx = -1
        self.force = False


ENGS = ("pe", "act", "dve", "pool", "sp")


class Prog:
    def __init__(self, nc):
        self.nc = nc
        self.streams = {e: [] for e in ENGS}
        self.dma_sems = {}
        self.dma_cnt = {}
        self.slots = []

    def slot(self, name):
        s = Slot(name)
        self.slots.append(s)
        return s

    def slots_n(self, name, n):
        return [self.slot(f"{name}{i}") for i in range(n)]

    def _add(self, ins, reads, writes):
        e = ins.eng
        best = {}
        dl = []

        def need(p):
            if p is None or p is ins:
                return
            if p.is_dma:
                if p not in dl:
                    dl.append(p)
            elif ins.is_dma or p.eng != e or ins.force:
                q = best.get(p.eng)
                if q is None or q.idx < p.idx:
                    best[p.eng] = p

        for s in reads:
            need(s.lw)
        for s in writes:
            need(s.lw)
            for r in s.rs:
                need(r)
        ins.deps = dl + list(best.values())
        for s in reads:
            rs = s.rs
            if rs and (not ins.is_dma) and (not rs[-1].is_dma) and rs[-1].eng == e:
                rs[-1] = ins
            else:
                rs.append(ins)
        for s in writes:
            s.lw = ins
            s.rs = []
        ins.idx = len(self.streams[e])
        self.streams[e].append(ins)
        return ins

    def op(self, eng, fn, reads=(), writes=(), force=False):
        ins = Ins(eng, fn, False, None, 0)
        ins.force = force
        return self._add(ins, reads, writes)

    def dma(self, eng, semname, fn, reads=(), writes=()):
        if semname not in self.dma_sems:
            self.dma_sems[semname] = self.nc.alloc_semaphore(name="d_" + semname)
            self.dma_cnt[semname] = 0
        self.dma_cnt[semname] += 16
        return self._add(Ins(eng, fn, True, self.dma_sems[semname], self.dma_cnt[semname]), reads, writes)

    def barrier(self):
        lasts = []
        for e in ENGS:
            st = self.streams[e]
            if st:
                lasts.append(st[-1])
        dmas = [i for e in ENGS for i in self.streams[e] if i.is_dma and not getattr(i, "_barr", False)]
        for e in ENGS:
            ins = Ins(e, None, False, None, 0)
            for p in lasts:
                if p.eng != e and not p.is_dma and p.fn is not None:
                    ins.deps.append(p)
            for p in dmas:
                ins.deps.append(p)
            ins.idx = len(self.streams[e])
            self.streams[e].append(ins)
        for s in self.slots:
            s.lw = None
            s.rs = []

    def emit(self):
        nc = self.nc
        for e in ENGS:
            for ins in self.streams[e]:
                for p in ins.deps:
                    if not p.is_dma:
                        p.inc = True
        rank = {}
        sems = {}
        for e in ENGS:
            sems[e] = nc.alloc_semaphore(name="s_" + e)
            c = 0
            last_real = None
            for ins in self.streams[e]:
                if ins.fn is not None and not ins.is_dma:
                    last_real = ins
                if ins.inc:
                    assert ins.fn is not None and not ins.is_dma
                    c += 1
                    rank[ins] = c
        handles = {"pe": nc.tensor, "act": nc.scalar, "dve": nc.vector, "pool": nc.gpsimd, "sp": nc.sync}
        streams = self.streams

        def replay(e, h):
            known = {}
            for ins in streams[e]:
                for p in ins.deps:
                    if p.is_dma:
                        key, val, sem = ("d", id(p.sem)), p.semval, p.sem
                    else:
                        key, val, sem = ("c", p.eng), rank[p], sems[p.eng]
                    if known.get(key, 0) >= val:
                        continue
                    known[key] = val
                    h.wait_ge(sem, val)
                if ins.fn is None:
                    continue
                r = ins.fn(h)
                if ins.is_dma:
                    r.then_inc(ins.sem, 16)
                elif ins.inc:
                    r.then_inc(sems[e], 1)

        with nc.Block() as block:
            @block.tensor
            def _(h):
                replay("pe", h)

            @block.scalar
            def _(h):
                replay("act", h)

            @block.vector
            def _(h):
                replay("dve", h)

            @block.gpsimd
            def _(h):
                replay("pool", h)

            @block.sync
            def _(h):
                replay("sp", h)


class Arena:
    def __init__(self, nc, name, nelem, dtype):
        self.t = nc.alloc_sbuf_tensor(name, [128, nelem], dtype)
        self.n = nelem
        self.off = 0

    def reset(self):
        self.off = 0

    def take(self, *shape):
        n = int(np.prod(shape))
        assert self.off + n <= self.n, (self.off, n, self.n)
        ap = self.t[:, self.off:self.off + n]
        self.off += n
        if len(shape) == 2:
            ap = ap.rearrange("p (a b) -> p a b", b=shape[1])
        elif len(shape) == 3:
            ap = ap.rearrange("p (a b c) -> p a b c", b=shape[1], c=shape[2])
        elif len(shape) == 4:
            ap = ap.rearrange("p (a b c d) -> p a b c d", b=shape[1], c=shape[2], d=shape[3])
        return ap


class Ctx:
    pass


def setup_ctx(nc):
    c = Ctx()
    c.nc = nc
    c.P = Prog(nc)
    c.fa = Arena(nc, "fa", 12800, F32)
    c.ba = Arena(nc, "ba", 78848, BF16)
    c.ps = [nc.alloc_psum_tensor(f"ps{i}", [128, 512], F32) for i in range(7)]
    c.psb = nc.alloc_psum_tensor("psb", [128, 1024], BF16)
    c.ps_slots = [c.P.slot(f"ps{i}") for i in range(7)]
    c.psb_slot = c.P.slot("psb")
    c.ident = nc.alloc_sbuf_tensor("ident_sb", [128, 128], BF16)
    c.ident_slot = c.P.slot("ident")
    c.bd = nc.alloc_sbuf_tensor("bd_sb", [128, 128], BF16)
    c.bd_slot = c.P.slot("bd")
    c.cst = nc.alloc_sbuf_tensor("cst_sb", [128, 8], F32)
    c.cst_slot = c.P.slot("cst")
    c.psbs = [(c.psb[:, :], c.psb_slot), (c.ps[6][:, :].bitcast(BF16), c.ps_slots[6])]
    c.nt_count = 0
    return c


def load_ident(c, ident_dram):
    P = c.P
    P.dma("pool", "ident", lambda h: h.dma_start(out=c.ident[:], in_=ident_dram[0]), writes=[c.ident_slot])
    P.dma("pool", "ident", lambda h: h.dma_start(out=c.bd[:], in_=ident_dram[1]), writes=[c.bd_slot])
    for i, v in enumerate((64.0 * EPS, EPS, 1e-5, 0.0, 1.0, math.pi / 2)):
        P.op("dve", lambda h, i=i, v=v: h.memset(c.cst[:, i:i + 1], v), writes=[c.cst_slot])


def norm_a(c, xt, xt_slot, gain_bc, gain_slot, xn, xn_slot, ss, ss_slot):
    P = c.P
    P.op("act", lambda h: h.activation(out=xn, in_=xt, func=AF.Square, accum_out=ss[:, 0:1]),
         reads=[xt_slot], writes=[xn_slot, ss_slot])
    P.op("dve", lambda h: h.tensor_scalar(out=ss[:, 1:2], in0=ss[:, 0:1], scalar1=1.0 / D, scalar2=EPS,
                                           op0=ALU.mult, op1=ALU.add), reads=[ss_slot], writes=[ss_slot])
    P.op("act", lambda h: h.activation(out=ss[:, 2:3], in_=ss[:, 1:2], func=AF.Sqrt), reads=[ss_slot], writes=[ss_slot])
    P.op("dve", lambda h: h.reciprocal(out=ss[:, 3:4], in_=ss[:, 2:3]), reads=[ss_slot], writes=[ss_slot])
    P.op("dve", lambda h: h.scalar_tensor_tensor(out=xn, in0=xt, scalar=ss[:, 3:4], in1=gain_bc,
                                                  op0=ALU.mult, op1=ALU.mult),
         reads=[xt_slot, ss_slot, gain_slot], writes=[xn_slot], force=True)


def norm_b(c, xn, xn_slot, hT, hT_slot, col0):
    P = c.P
    pb_, pb_slot = c.psbs[c.nt_count % 2]
    c.nt_count += 1
    for k in range(8):
        P.op("pe", lambda h, k=k: h.transpose(out=pb_[:, k * 128:(k + 1) * 128], in_=xn[:, k * 128:(k + 1) * 128],
                                              identity=c.ident[:]),
             reads=[xn_slot, c.ident_slot], writes=[pb_slot])
    P.op("act", lambda h: h.copy(out=hT[:, :, col0:col0 + 128],
                                 in_=pb_.rearrange("p (k t) -> p k t", t=128)),
         reads=[pb_slot], writes=[hT_slot])


def norm_transpose(c, xt, xt_slot, gain_bc, gain_slot, xn, xn_slot, ss, ss_slot, hT, hT_slot, col0):
    norm_a(c, xt, xt_slot, gain_bc, gain_slot, xn, xn_slot, ss, ss_slot)
    norm_b(c, xn, xn_slot, hT, hT_slot, col0)


def ffn_stage(c, xin, xin_slot, xout, xout_slot, wg, wu, wd, gain_row, ntok=TOK, TB=1024):
    P = c.P
    fa, ba = c.fa, c.ba
    fa.reset()
    ba.reset()
    NT = TB // 128
    Wd = ba.take(NM, 1024)
    hTs = [ba.take(8, TB) for _ in range(2)]
    actT = ba.take(NM, TB)
    wgb = [ba.take(8, 128) for _ in range(2)]
    wub = [ba.take(8, 128) for _ in range(2)]
    xn = [ba.take(1, 1024)[:, 0, :] for _ in range(2)]
    xt = [fa.take(1, 1024)[:, 0, :] for _ in range(4)]
    ost = [fa.take(1, 1024)[:, 0, :] for _ in range(2)]
    gbc = fa.take(1, 1024)[:, 0, :]
    sg = [fa.take(1, 512)[:, 0, :] for _ in range(2)]
    ss = [fa.take(1, 4)[:, 0, :] for _ in range(2)]

    s_Wd = P.slot("Wd")
    s_hT = [[P.slot(f"hT{j}_{t}") for t in range(NT)] for j in range(2)]
    s_act = [[P.slot(f"act{m}_{h}") for h in range(TB // 512)] for m in range(NM)]
    s_wg = P.slots_n("wg", 2)
    s_wu = P.slots_n("wu", 2)
    s_xn = P.slots_n("xn", 2)
    s_xt = P.slots_n("xt", 4)
    s_ost = P.slots_n("ost", 2)
    s_gbc = P.slot("gbc")
    s_sg = P.slots_n("sg", 2)
    s_ss = P.slots_n("ss", 2)

    P.dma("sp", "gbc", lambda h: h.dma_start(out=gbc, in_=gain_row.partition_broadcast(128)), writes=[s_gbc])
    wdv = wd.rearrange("(m p) o -> p m o", p=128)
    nblk = ntok // TB
    st = {"w": 0, "o": 0, "n": 0, "u": 0, "x": 0}

    pend = {}

    def norm_tile_a(b, t):
        j = st["n"] % 2
        st["n"] += 1
        xi = st["x"] % 2
        st["x"] += 1
        r0 = b * TB + t * 128
        P.dma("sp", f"xt{xi}", lambda h: h.dma_start(out=xt[xi], in_=xin[r0:r0 + 128, :]),
              reads=[xin_slot], writes=[s_xt[xi]])
        norm_a(c, xt[xi], s_xt[xi], gbc, s_gbc, xn[j], s_xn[j], ss[j], s_ss[j])
        pend[(b, t)] = j

    def norm_tile_b(b, t):
        j = pend.pop((b, t))
        norm_b(c, xn[j], s_xn[j], hTs[b % 2], s_hT[b % 2][t], t * 128)

    def norm_tile(b, t):
        norm_tile_a(b, t)
        norm_tile_b(b, t)

    for t in range(NT):
        norm_tile(0, t)
    for q in range(2):
        P.dma("pool", "Wd", lambda h, q=q: h.dma_start(out=Wd[:, q * 11:(q + 1) * 11, :], in_=wdv[:, q * 11:(q + 1) * 11, :]),
              writes=[s_Wd])
    for b in range(nblk):
        hT = hTs[b % 2]
        shT = s_hT[b % 2]
        for m in range(NM):
            j = st["w"] % 2
            st["w"] += 1
            P.dma("pool", f"wg{j}", lambda h, j=j, m=m: h.dma_start(out=wgb[j], in_=wg[m]), writes=[s_wg[j]])
            P.dma("pool", f"wu{j}", lambda h, j=j, m=m: h.dma_start(out=wub[j], in_=wu[m]), writes=[s_wu[j]])
            for hh in range(TB // 512):
                u = st["u"] % 2
                st["u"] += 1
                pg, pu = c.ps[2 * u], c.ps[2 * u + 1]
                sgs, sus = c.ps_slots[2 * u], c.ps_slots[2 * u + 1]
                hs = shT[hh * 4:(hh + 1) * 4]
                for k in range(8):
                    P.op("pe", lambda h, j=j, k=k, hh=hh, pg=pg, hT=hT: h.matmul(
                        pg[:, :], lhsT=wgb[j][:, k, :], rhs=hT[:, k, hh * 512:(hh + 1) * 512],
                        start=(k == 0), stop=(k == 7)), reads=[s_wg[j]] + hs, writes=[sgs])
                for k in range(8):
                    P.op("pe", lambda h, j=j, k=k, hh=hh, pu=pu, hT=hT: h.matmul(
                        pu[:, :], lhsT=wub[j][:, k, :], rhs=hT[:, k, hh * 512:(hh + 1) * 512],
                        start=(k == 0), stop=(k == 7)), reads=[s_wu[j]] + hs, writes=[sus])
                P.op("act", lambda h, u=u, pg=pg: h.activation(out=sg[u], in_=pg[:, :], func=AF.Silu),
                     reads=[sgs], writes=[s_sg[u]])
                P.op("dve", lambda h, u=u, pu=pu, m=m, hh=hh: h.tensor_tensor(
                    out=actT[:, m, hh * 512:(hh + 1) * 512], in0=sg[u], in1=pu[:, :], op=ALU.mult),
                    reads=[s_sg[u], sus], writes=[s_act[m][hh]])
            if b + 1 < nblk and m % 2 == 1:
                i_ = m // 2
                if i_ < NT:
                    norm_tile_a(b + 1, i_)
                if 1 <= i_ <= NT:
                    norm_tile_b(b + 1, i_ - 1)
        for t in range(NT):
            o = st["o"] % 2
            st["o"] += 1
            xi = 2 + (t % 2)
            r0 = b * TB + t * 128
            P.dma("sp", f"xt{xi}", lambda h, xi=xi, r0=r0: h.dma_start(out=xt[xi], in_=xin[r0:r0 + 128, :]),
                  reads=[xin_slot], writes=[s_xt[xi]])
            for half in range(2):
                pd = c.ps[4 + half]
                sd = c.ps_slots[4 + half]
                for m in range(NM):
                    P.op("pe", lambda h, m=m, t=t, half=half, pd=pd: h.matmul(
                        pd[:, :], lhsT=actT[:, m, t * 128:(t + 1) * 128], rhs=Wd[:, m, half * 512:(half + 1) * 512],
                        start=(m == 0), stop=(m == NM - 1)),
                        reads=[s_act[m][t // 4], s_Wd], writes=[sd])
                P.op("dve", lambda h, o=o, xi=xi, half=half, pd=pd: h.tensor_tensor(
                    out=ost[o][:, half * 512:(half + 1) * 512], in0=pd[:, :], in1=xt[xi][:, half * 512:(half + 1) * 512],
                    op=ALU.add), reads=[sd, s_xt[xi]], writes=[s_ost[o]])
            P.dma("sp", f"ost{o}", lambda h, o=o, r0=r0: h.dma_start(out=xout[r0:r0 + 128, :], in_=ost[o]),
                  reads=[s_ost[o]], writes=[xout_slot])
    P.barrier()


def cv_ps(pt, off, dims, p0=0, npart=128):
    a = pt[:, :]
    n = a.ap[0][0]
    return bass.AP(tensor=a.tensor, offset=a.offset + p0 * n + off, ap=[[n, npart]] + [list(d) for d in dims])


def _na_runs():
    out = []
    rs = [min(max(r - 4, 0), 24) for r in range(32)]
    for hf in range(2):
        runs = []
        for par in range(2):
            for ti in range(16):
                row0 = 2 * ti + par
                if row0 + 1 > 31:
                    continue
                for grp in range(2):
                    rows = [r for r in range(16 * hf + 8 * grp, 16 * hf + 8 * grp + 8)
                            if rs[r] % 2 == par and rs[r] <= row0 <= rs[r] + 6]
                    while rows:
                        if len(rows) == 1:
                            run, rows = rows, []
                            st = 1
                        else:
                            st = rows[1] - rows[0]
                            k = 2
                            while k < len(rows) and rows[k] - rows[k - 1] == st:
                                k += 1
                            run, rows = rows[:k], rows[k:]
                        runs.append((par, ti, grp, run[0], st, len(run)))
        out.append(runs)
    return out


NA_RUNS = _na_runs()

def na_stage(c, xin, xin_slot, xout, xout_slot, wqkv, wout, biasT, qkg, gain_row, tok0):
    P = c.P
    fa, ba = c.fa, c.ba
    fa.reset()
    ba.reset()
    hT = ba.take(8, SEQ)
    attnT = ba.take(8, SEQ)
    qT = ba.take(2, SEQ)
    kT = ba.take(2, SEQ)
    Vpad = ba.take(2, 16, 4, 128)
    EB = ba.take(4, 14, 64)
    cpad = ba.take(3, 128)
    zrhs = ba.take(1, 512)[:, 0, :]
    bT = ba.take(4, 14, 64)
    wq = ba.take(8, 768)
    PT = [ba.take(1, 512)[:, 0, :] for _ in range(3)]
    xn = [ba.take(1, 1024)[:, 0, :] for _ in range(2)]
    sq = [ba.take(1, 512)[:, 0, :] for _ in range(2)]
    xt = [fa.take(1, 1024)[:, 0, :] for _ in range(2)]
    ost = [fa.take(1, 1024)[:, 0, :] for _ in range(2)]
    gbc = fa.take(1, 1024)[:, 0, :]
    raw = [fa.take(1, 512)[:, 0, :] for _ in range(2)]
    rinv = [fa.take(1, 512)[:, 0, :] for _ in range(2)]
    ss = [fa.take(1, 4)[:, 0, :] for _ in range(2)]
    rec = [fa.take(1, 2)[:, 0, :] for _ in range(2)]
    gq = fa.take(1, 2)[:, 0, :]

    s_hT = P.slots_n("n_hT", 16)
    s_attnT = P.slots_n("n_attnT", 8)
    s_qT = P.slot("n_qT")
    s_kT = P.slot("n_kT")
    s_V = P.slot("n_V")
    s_bT = P.slot("n_bT")
    s_wq = P.slot("n_wq")
    s_PT = P.slots_n("n_PT", 3)
    s_EB = P.slot("n_EB")
    s_cpad = P.slot("n_cpad")
    s_xn = P.slots_n("n_xn", 2)
    s_sq = P.slots_n("n_sq", 2)
    s_xt = P.slots_n("n_xt", 2)
    s_ost = P.slots_n("n_ost", 2)
    s_gbc = P.slot("n_gbc")
    s_raw = P.slots_n("n_raw", 2)
    s_rinv = P.slots_n("n_rinv", 2)
    s_ss = P.slots_n("n_ss", 2)
    s_rec = P.slots_n("n_rec", 2)
    s_gq = P.slot("n_gq")
    ps, pss = c.ps, c.ps_slots

    P.dma("sp", "gbc", lambda h: h.dma_start(out=gbc, in_=gain_row.partition_broadcast(128)), writes=[s_gbc])
    P.dma("sp", "gq", lambda h: h.dma_start(out=gq, in_=qkg), writes=[s_gq])
    P.op("dve", lambda h: h.memset(Vpad[:, :, :, :, :].rearrange("p a b c d -> p (a b c d)"), 0.0), writes=[s_V])
    P.op("dve", lambda h: h.memset(cpad[:, :, :].rearrange("p a b -> p (a b)"), 0.0), writes=[s_cpad])
    P.op("dve", lambda h: h.memset(cpad[:, 0, 0:64], 1.0), writes=[s_cpad])
    P.op("dve", lambda h: h.memset(cpad[:, 1, 64:128], 1.0), writes=[s_cpad])
    P.op("dve", lambda h: h.memset(zrhs, 0.0), writes=[s_cpad])

    for t in range(16):
        j = t % 2
        r0 = tok0 + t * 128
        P.dma("sp", f"nxt{j}", lambda h, j=j, r0=r0: h.dma_start(out=xt[j], in_=xin[r0:r0 + 128, :]),
              reads=[xin_slot], writes=[s_xt[j]])
        norm_transpose(c, xt[j], s_xt[j], gbc, s_gbc, xn[j], s_xn[j], ss[j], s_ss[j], hT, s_hT[t], t * 128)

    ui = 0
    for hg in DBG.get('hgs', range(4)):
        for part in range(3):
            P.dma("pool", "nwq", lambda h, hg=hg, part=part: h.dma_start(
                out=wq[:, :, part * 256:(part + 1) * 256], in_=wqkv[hg][:, :, part * 256:(part + 1) * 256]),
                writes=[s_wq])
        P.dma("pool", "nbT", lambda h, hg=hg: h.dma_start(out=bT, in_=biasT[hg]), writes=[s_bT])
        for part in DBG.get('qk', range(2)):
            dst, s_dst = (qT, s_qT) if part == 0 else (kT, s_kT)
            for cc in range(2):
                for tb in range(4):
                    u = ui % 2
                    ui += 1
                    pq, spq = ps[u], pss[u]
                    pn, spn = ps[2 + u], pss[2 + u]
                    for k in range(8):
                        P.op("pe", lambda h, k=k, part=part, cc=cc, tb=tb, pq=pq: h.matmul(
                            pq[:, :], lhsT=wq[:, k, part * 256 + cc * 128: part * 256 + cc * 128 + 128],
                            rhs=hT[:, k, tb * 512:(tb + 1) * 512], start=(k == 0), stop=(k == 7)),
                            reads=[s_wq] + s_hT[tb * 4:(tb + 1) * 4], writes=[spq])
                    P.op("act", lambda h, u=u, pq=pq: h.activation(out=sq[u], in_=pq[:, :], func=AF.Square),
                         reads=[spq], writes=[s_sq[u]])
                    P.op("dve", lambda h, u=u, pq=pq: h.tensor_copy(out=raw[u], in_=pq[:, :]),
                         reads=[spq, s_sq[u]], writes=[s_raw[u]])
                    P.op("pe", lambda h, u=u, pn=pn: h.matmul(pn[:, :], lhsT=c.bd[:], rhs=sq[u], start=True, stop=True),
                         reads=[s_sq[u], c.bd_slot], writes=[spn])
                    if part == 0:
                        P.op("act", lambda h, u=u, pn=pn: h.activation(out=rinv[u], in_=pn[:, :], func=AF.Ln,
                                                                       bias=c.cst[:, 0:1], scale=1.0),
                             reads=[spn, c.cst_slot], writes=[s_rinv[u]])
                    else:
                        P.op("act", lambda h, u=u, pn=pn: h.activation(out=rinv[u], in_=pn[:, :], func=AF.Ln,
                                                                       bias=c.cst[:, 1:2], scale=1.0 / 64),
                             reads=[spn, c.cst_slot], writes=[s_rinv[u]])
                    P.op("act", lambda h, u=u: h.activation(out=rinv[u], in_=rinv[u], func=AF.Exp, scale=-0.5),
                         reads=[s_rinv[u]], writes=[s_rinv[u]])
                    P.op("dve", lambda h, u=u, part=part, cc=cc, tb=tb, dst=dst: h.scalar_tensor_tensor(
                        out=dst[:, cc, tb * 512:(tb + 1) * 512], in0=raw[u], scalar=gq[:, part:part + 1], in1=rinv[u],
                        op0=ALU.mult, op1=ALU.mult), reads=[s_raw[u], s_rinv[u], s_gq], writes=[s_dst])
        for par in DBG.get('vpar', range(2)):
            for i in range(16 - par):
                u = ui % 2
                ui += 1
                pv, spv = ps[u], pss[u]
                t0 = par * 64 + i * 128
                for k in range(8):
                    P.op("pe", lambda h, k=k, t0=t0, pv=pv: h.matmul(
                        pv[:, 0:256], lhsT=hT[:, k, t0:t0 + 128], rhs=wq[:, k, 512:768], start=(k == 0), stop=(k == 7)),
                        reads=[s_wq] + s_hT[t0 // 128:(t0 + 127) // 128 + 1], writes=[spv])
                for hp in range(2):
                    P.op("act", lambda h, par=par, i=i, pv=pv, hp=hp: h.copy(
                        out=cv(Vpad, 0, 128, ((par * 16 + i) * 4 + hp) * 128 + hp * 64, [[256, 2], [1, 64]]),
                        in_=cv_ps(pv, hp * 64, [[128, 2], [1, 64]])), reads=[spv], writes=[s_V])
        P.op("act", lambda h: h.activation(out=EB[:, :, :, :].rearrange("p a b c -> p (a b c)"),
                                           in_=bT[:, :, :, :].rearrange("p a b c -> p (a b c)"), func=AF.Exp),
             reads=[s_bT], writes=[s_EB])
        for cc in range(2):
            for hf in range(2):
                for b in range(4):
                    P.op("pe", lambda h, b=b: h.matmul(ps[b][:, :], lhsT=cpad[:, 2, :], rhs=zrhs, start=True, stop=True,
                                                       skip_group_check=True), reads=[s_cpad], writes=[pss[b]])
                units = [(par, ti, grp, r0_, st_, n_, hp) for (par, ti, grp, r0_, st_, n_) in NA_RUNS[hf] for hp in range(2)]
                LA = 2

                def emit_front(idx, cc=cc):
                    (par, ti, grp, r0_, st_, n_, hp) = units[idx]
                    row0 = 2 * ti + par
                    hh = cc * 2 + hp
                    pb = hp * 64
                    w3 = idx % 3
                    sc, ssc = ps[4 + w3], pss[4 + w3]
                    nq = 64 * n_
                    rho0 = row0 - r0_ + 7
                    P.op("pe", lambda h: h.matmul(
                        sc[:, 0:nq], lhsT=kT[pb:pb + 64, cc, row0 * 64: row0 * 64 + 128],
                        rhs=cv(qT, pb, 64, cc * SEQ + r0_ * 64, [[st_ * 64, n_], [1, 64]]), start=True, stop=True),
                        reads=[s_kT, s_qT], writes=[ssc])
                    P.op("act", lambda h: h.activation(out=PT[w3][:, 0:nq], in_=sc[:, 0:nq], func=AF.Exp),
                         reads=[ssc], writes=[s_PT[w3]])
                    P.op("dve", lambda h: h.tensor_tensor(
                        out=cv(PT[w3], 0, 128, 0, [[64, n_], [1, 64]]), in0=cv(PT[w3], 0, 128, 0, [[64, n_], [1, 64]]),
                        in1=cv(EB, 0, 128, hh * 896 + rho0 * 64, [[-st_ * 64, n_], [1, 64]]), op=ALU.mult),
                        reads=[s_PT[w3], s_EB], writes=[s_PT[w3]])

                def emit_back(idx, cc=cc, hf=hf):
                    (par, ti, grp, r0_, st_, n_, hp) = units[idx]
                    hh = cc * 2 + hp
                    w3 = idx % 3
                    nq = 64 * n_
                    c0 = (r0_ - 16 * hf - 8 * grp) * 64
                    for (bank, lw_) in ((grp, Vpad[:, par, ti, hh, :]), (2 + grp, cpad[:, hp, :])):
                        P.op("pe", lambda h, bank=bank, lw_=lw_: h.matmul(
                            cv_ps(ps[bank], c0, [[st_ * 64, n_], [1, 64]]), lhsT=lw_, rhs=PT[w3][:, 0:nq], start=False, stop=True,
                            skip_group_check=True), reads=[s_PT[w3], s_V, s_cpad], writes=[pss[bank]])

                for idx in range(len(units) + LA):
                    if idx < len(units):
                        emit_front(idx)
                    if idx - LA >= 0:
                        emit_back(idx - LA)
                for grp in range(2):
                    u = grp
                    P.op("act", lambda h, u=u, grp=grp: h.activation(out=rinv[u], in_=ps[2 + grp][:, :], func=AF.Ln),
                         reads=[pss[2 + grp]], writes=[s_rinv[u]])
                    P.op("act", lambda h, u=u: h.activation(out=rinv[u], in_=rinv[u], func=AF.Exp, scale=-1.0),
                         reads=[s_rinv[u]], writes=[s_rinv[u]])
                    P.op("dve", lambda h, u=u, grp=grp, hg=hg, cc=cc, hf=hf: h.tensor_tensor(
                        out=attnT[:, hg * 2 + cc, hf * 1024 + grp * 512: hf * 1024 + (grp + 1) * 512], in0=ps[grp][:, :], in1=rinv[u],
                        op=ALU.mult), reads=[pss[grp], s_rinv[u]], writes=[s_attnT[hg * 2 + cc]])
    if DBG.get('dump') is not None:
        dbg = DBG['dump']
        s_dbg = P.slot("dbg")
        P.dma("pool", "dbg", lambda h: h.dma_start(out=dbg[:, 0:4096], in_=qT.rearrange("p a b -> p (a b)")), reads=[s_qT], writes=[s_dbg])
        P.dma("pool", "dbg", lambda h: h.dma_start(out=dbg[:, 4096:8192], in_=kT.rearrange("p a b -> p (a b)")), reads=[s_kT], writes=[s_dbg])
    P.barrier()
    Wo = hT[:, :, 0:1024]
    s_Wo = P.slot("n_Wo")
    for q in range(2):
        P.dma("pool", "nWo", lambda h, q=q: h.dma_start(out=Wo[:, q * 4:(q + 1) * 4, :], in_=wout[:, q * 4:(q + 1) * 4, :]),
              writes=[s_Wo])
    for t in range(16):
        j = t % 2
        r0 = tok0 + t * 128
        P.dma("sp", f"nxt{j}", lambda h, j=j, r0=r0: h.dma_start(out=xt[j], in_=xin[r0:r0 + 128, :]),
              reads=[xin_slot], writes=[s_xt[j]])
        for half in range(2):
            pd, sd = ps[half], pss[half]
            for k in range(8):
                P.op("pe", lambda h, k=k, t=t, half=half, pd=pd: h.matmul(
                    pd[:, :], lhsT=attnT[:, k, t * 128:(t + 1) * 128], rhs=Wo[:, k, half * 512:(half + 1) * 512],
                    start=(k == 0), stop=(k == 7)), reads=[s_attnT[k], s_Wo], writes=[sd])
            P.op("dve", lambda h, j=j, half=half, pd=pd: h.tensor_tensor(
                out=ost[j][:, half * 512:(half + 1) * 512], in0=pd[:, :], in1=xt[j][:, half * 512:(half + 1) * 512],
                op=ALU.add), reads=[sd, s_xt[j]], writes=[s_ost[j]])
        P.dma("sp", f"nost{j}", lambda h, j=j, r0=r0: h.dma_start(out=xout[r0:r0 + 128, :], in_=ost[j]),
              reads=[s_ost[j]], writes=[xout_slot])
    P.barrier()


def cv(ap, p0, npart, off, dims):
    n = ap.ap[0][0]
    return bass.AP(tensor=ap.tensor, offset=ap.offset + p0 * n + off, ap=[[n, npart]] + [list(d) for d in dims])


GELU_FUNC = [None]


def ab_stage(c, xin, xin_slot, xout, xout_slot, W, gain_row, ntok):
    P = c.P
    fa, ba = c.fa, c.ba
    fa.reset()
    ba.reset()
    NK = 257
    R1 = ba.take(1, 16448)[:, 0, :]
    uT = ba.take(4, SEQ)
    aT = ba.take(4, SEQ + 30)
    yaT = ba.take(4, SEQ)
    Ht = ba.take(32, 2, 128)
    Gt = ba.take(32, 2, 128)
    Mt = ba.take(32, 128)
    selin = ba.take(2, 8, 128)
    selout = ba.take(8, 8, 128)
    wsl = [ba.take(8, 128) for _ in range(2)]
    gluw = ba.take(4, 512)
    xn = [ba.take(1, 1024)[:, 0, :] for _ in range(2)]
    hT = R1[:, 0:16384].rearrange("p (a b) -> p a b", b=SEQ)
    cdiag = R1[:, 0:15872].rearrange("p (q k m) -> p q k m", k=31, m=128)
    XS = R1
    Wo = R1[:, 0:8192].rearrange("p (a b) -> p a b", b=1024)
    UY = aT[:, :, :].rearrange("p a b -> p (a b)")[:, 0:8192].rearrange("p (g k) -> p g k", k=256)
    Qt = R1[:, 0:8192].rearrange("p (r e) -> p r e", r=2)
    Pt = R1[:, 8192:16384].rearrange("p (r e) -> p r e", r=2)
    Hp = uT[:, :, :].rearrange("p a b -> p (a b)").rearrange("p (r e) -> p r e", r=2)
    gbc = fa.take(1, 1024)[:, 0, :]
    lam = fa.take(3, 32)
    Apw = fa.take(2, 32, 32)
    msk = fa.take(2, 128)
    dcol = fa.take(1, 32)[:, 0, :]
    cw = fa.take(4, 31)
    vec = fa.take(1, 20)[:, 0, :]
    ones32 = fa.take(1, 128)[:, 0, :]
    etab = fa.take(1, 32)[:, 0, :]
    TA = fa.take(1, 64)[:, 0, :]
    TB = fa.take(1, 64)[:, 0, :]
    Sf = fa.take(2, 128)
    st1 = fa.take(1, 64)[:, 0, :]
    st2 = fa.take(1, 64)[:, 0, :]
    sm = fa.take(12, 32)
    ss = [fa.take(1, 4)[:, 0, :] for _ in range(2)]
    tmpA = fa.take(1, 128)[:, 0, :]
    tmpB = fa.take(1, 128)[:, 0, :]
    Dreg = fa.take(1, 7168)[:, 0, :]
    Bp = Dreg[:, 0:1024]
    Cp = Dreg[:, 1024:2048]
    Bb = Dreg[:, 2048:3072]
    t1 = Dreg[:, 3072:4096]
    t2 = Dreg[:, 4096:5120]
    w5 = [Dreg[:, 5120:6144], Dreg[:, 6144:7168], Dreg[:, 2048:3072]]
    xt = [Dreg[:, 0:1024], Dreg[:, 1024:2048]]
    ost = [Dreg[:, 2048:3072], Dreg[:, 3072:4096]]
    sgm = [Dreg[:, 0:512], Dreg[:, 512:1024]]
    acv = Dreg[:, 0:2048].rearrange("p (q t) -> p q t", t=512)
    sqv = [Dreg[:, 2048:2560], Dreg[:, 2560:3072]]
    meanv = Dreg[:, 3072:3584]
    rstdv = Dreg[:, 3584:4096]
    tmpv = Dreg[:, 4096:4608]
    ynv = [Dreg[:, 4608:5120], Dreg[:, 5120:5632]]
    ps, pss = c.ps, c.ps_slots
    ident = c.ident

    S = lambda n: P.slot("ab_" + n)
    s_tab = S("tab")
    s_D = S("Dreg")
    s_R1 = S("R1")
    s_uT = S("uT")
    s_aT = S("aT")
    s_ya = S("yaT")
    s_H, s_G, s_M = S("H"), S("G"), S("M")
    s_sel = S("sel")
    s_wsl = [S("wsl0"), S("wsl1")]
    s_gluw = S("gluw")
    s_xn = [S("xn0"), S("xn1")]
    s_gbc = S("gbc")
    s_ss = [S("ss0"), S("ss1")]
    s_tmp = S("tmpAB")

    def dv(fn, r, w, force=True):
        P.op("dve", fn, reads=r, writes=w, force=force)

    def ac(fn, r, w, force=True):
        P.op("act", fn, reads=r, writes=w, force=force)

    P.dma("sp", "gbc", lambda h: h.dma_start(out=gbc, in_=gain_row.partition_broadcast(128)), writes=[s_gbc])
    for dst, src in ((lam, W["lam"]), (msk, W["mask"]), (dcol, W["dcol"]), (cw, W["cw"]), (vec, W["vec"]),
                     (ones32, W["ones"]), (etab, W["etab"]), (Bp, W["B"]), (Cp, W["C"])):
        P.dma("sp", "abtab", lambda h, dst=dst, src=src: h.dma_start(out=dst, in_=src), writes=[s_tab])
    P.dma("pool", "absel", lambda h: h.dma_start(out=selin, in_=W["selin"]), writes=[s_sel])
    for q in range(2):
        P.dma("pool", "absel", lambda h, q=q: h.dma_start(out=selout[:, q * 4:(q + 1) * 4], in_=W["selout"][:, q * 4:(q + 1) * 4]),
              writes=[s_sel])
    P.dma("pool", "abglu", lambda h: h.dma_start(out=gluw, in_=W["gluw"]), writes=[s_gluw])

    T = [s_tab]
    sm_ = lambda i: sm[:, i, :]
    m_dt, m_lrd, m_lid, m_pr, m_den, m_cre, m_cim, m_x, m_y = (sm_(i) for i in range(9))
    lr, li, ls = lam[:, 0, :], lam[:, 1, :], lam[:, 2, :]
    ac(lambda h: h.activation(out=m_dt, in_=ls, func=AF.Exp), T, T)
    dv(lambda h: h.tensor_tensor(out=m_lrd, in0=lr, in1=m_dt, op=ALU.mult), T, T)
    dv(lambda h: h.tensor_tensor(out=m_lid, in0=li, in1=m_dt, op=ALU.mult), T, T)
    bc_g = lambda a: cv(a, 0, 128, 0, [[0, 32], [1, 32]])
    bc_e = cv(etab, 0, 128, 0, [[1, 32], [0, 32]])
    v3 = lambda a: cv(a, 0, 128, 0, [[32, 32], [1, 32]])
    argm, ang, kf = w5
    ki = t1[:, 0:1024].bitcast(I32)
    dv(lambda h: h.tensor_tensor(out=v3(argm), in0=bc_g(m_lrd), in1=bc_e, op=ALU.mult), T, T)
    ac(lambda h: h.activation(out=argm, in_=argm, func=AF.Exp), T, T)
    dv(lambda h: h.tensor_tensor(out=v3(ang), in0=bc_g(m_lid), in1=bc_e, op=ALU.mult), T, T)
    trig = t2[:, 0:1024]
    for ri, shift in ((1, 0.0), (0, math.pi / 2)):
        src = ang
        if shift != 0.0:
            dv(lambda h: h.tensor_scalar(out=ang, in0=ang, scalar1=shift, scalar2=None, op0=ALU.add), T, T)
        dv(lambda h: h.tensor_scalar(out=kf, in0=ang, scalar1=1.0 / (2 * math.pi), scalar2=None, op0=ALU.mult), T, T)
        dv(lambda h: h.tensor_copy(out=ki, in_=kf), T, T)
        dv(lambda h: h.tensor_copy(out=kf, in_=ki), T, T)
        dv(lambda h: h.scalar_tensor_tensor(out=kf, in0=kf, scalar=-2 * math.pi, in1=ang, op0=ALU.mult, op1=ALU.add), T, T)
        ac(lambda h: h.activation(out=trig, in_=kf, func=AF.Sin), T, T)
        dv(lambda h, ri=ri: h.tensor_tensor(out=Apw[:, ri, :, :].rearrange("p a b -> p (a b)"), in0=argm, in1=trig, op=ALU.mult), T, T)
    for (p0_, j1) in ((0, 16), (64, 23)):
        hp = lambda a, p0_=p0_: cv(a, p0_, 64, 0, [[1, 32]])
        A1r = cv(Apw, p0_, 64, j1 * 32, [[1, 32]])
        A1i = cv(Apw, p0_, 64, 1024 + j1 * 32, [[1, 32]])
        lr_, li_ = hp(lr), hp(li)
        dv(lambda h, hp=hp, A1r=A1r: h.tensor_scalar(out=hp(m_pr), in0=A1r, scalar1=-1.0, scalar2=None, op0=ALU.add), T, T)
        dv(lambda h, hp=hp, lr_=lr_: h.tensor_tensor(out=hp(m_x), in0=lr_, in1=lr_, op=ALU.mult), T, T)
        dv(lambda h, hp=hp, li_=li_: h.tensor_tensor(out=hp(m_y), in0=li_, in1=li_, op=ALU.mult), T, T)
        dv(lambda h, hp=hp: h.tensor_tensor(out=hp(m_den), in0=hp(m_x), in1=hp(m_y), op=ALU.add), T, T)
        dv(lambda h, hp=hp: h.reciprocal(out=hp(m_den), in_=hp(m_den)), T, T)
        dv(lambda h, hp=hp, lr_=lr_: h.tensor_tensor(out=hp(m_x), in0=hp(m_pr), in1=lr_, op=ALU.mult), T, T)
        dv(lambda h, hp=hp, li_=li_, A1i=A1i: h.tensor_tensor(out=hp(m_y), in0=A1i, in1=li_, op=ALU.mult), T, T)
        dv(lambda h, hp=hp: h.tensor_tensor(out=hp(m_x), in0=hp(m_x), in1=hp(m_y), op=ALU.add), T, T)
        dv(lambda h, hp=hp: h.tensor_tensor(out=hp(m_cre), in0=hp(m_x), in1=hp(m_den), op=ALU.mult), T, T)
        dv(lambda h, hp=hp, lr_=lr_, A1i=A1i: h.tensor_tensor(out=hp(m_x), in0=A1i, in1=lr_, op=ALU.mult), T, T)
        dv(lambda h, hp=hp, li_=li_: h.tensor_tensor(out=hp(m_y), in0=hp(m_pr), in1=li_, op=ALU.mult), T, T)
        dv(lambda h, hp=hp: h.tensor_tensor(out=hp(m_x), in0=hp(m_x), in1=hp(m_y), op=ALU.subtract), T, T)
        dv(lambda h, hp=hp: h.tensor_tensor(out=hp(m_cim), in0=hp(m_x), in1=hp(m_den), op=ALU.mult), T, T)
    bcc = lambda a: cv(a, 0, 128, 0, [[1, 32], [0, 16]])
    g16 = lambda a, off: cv(a, 0, 128, off, [[16, 32], [1, 16]])
    for (o_off, a_, x_off, b_, y_off, op) in ((0, m_cre, 0, m_cim, 512, ALU.subtract), (512, m_cre, 512, m_cim, 0, ALU.add)):
        dv(lambda h, a_=a_, x_off=x_off: h.tensor_tensor(out=g16(t1, 0), in0=bcc(a_), in1=g16(Bp, x_off), op=ALU.mult), T, T)
        dv(lambda h, b_=b_, y_off=y_off: h.tensor_tensor(out=g16(t2, 0), in0=bcc(b_), in1=g16(Bp, y_off), op=ALU.mult), T, T)
        dv(lambda h, o_off=o_off, op=op: h.tensor_tensor(out=Bb[:, o_off:o_off + 512], in0=t1[:, 0:512], in1=t2[:, 0:512], op=op), T, T)

    def fam(dst, d_goff, d_rioff, V, negim, j0):
        np_, p0, js = 128, 0, 1
        for qq in range(4):
            Ar = cv(Apw, p0, np_, j0 * 32 + 8 * qq, [[1, 8], [js * 32, 8], [0, 16]])
            Ai = cv(Apw, p0, np_, 1024 + j0 * 32 + 8 * qq, [[1, 8], [js * 32, 8], [0, 16]])
            Vr = cv(V, p0, np_, 8 * qq * 16, [[16, 8], [0, 8], [1, 16]])
            Vi = cv(V, p0, np_, 512 + 8 * qq * 16, [[16, 8], [0, 8], [1, 16]])
            T1 = cv(t1, p0, np_, 0, [[128, 8], [16, 8], [1, 16]])
            T2 = cv(t2, p0, np_, 0, [[128, 8], [16, 8], [1, 16]])
            dre = cv(dst, p0, np_, 8 * qq * d_goff, [[d_goff, 8], [16, 8], [1, 16]])
            dim_ = cv(dst, p0, np_, 8 * qq * d_goff + d_rioff, [[d_goff, 8], [16, 8], [1, 16]])
            dv(lambda h, Ar=Ar, Vr=Vr, T1=T1: h.tensor_tensor(out=T1, in0=Ar, in1=Vr, op=ALU.mult), T, T)
            dv(lambda h, Ai=Ai, Vi=Vi, T2=T2: h.tensor_tensor(out=T2, in0=Ai, in1=Vi, op=ALU.mult), T, T)
            dv(lambda h, dre=dre, T1=T1, T2=T2: h.tensor_tensor(out=dre, in0=T1, in1=T2, op=ALU.subtract), T, T)
            dv(lambda h, Ar=Ar, Vi=Vi, T1=T1: h.tensor_tensor(out=T1, in0=Ar, in1=Vi, op=ALU.mult), T, T)
            dv(lambda h, Ai=Ai, Vr=Vr, T2=T2: h.tensor_tensor(out=T2, in0=Ai, in1=Vr, op=ALU.mult), T, T)
            if negim:
                dv(lambda h, T1=T1: h.tensor_scalar(out=T1, in0=T1, scalar1=-1.0, scalar2=None, op0=ALU.mult), T, T)
                dv(lambda h, dim_=dim_, T1=T1, T2=T2: h.tensor_tensor(out=dim_, in0=T1, in1=T2, op=ALU.subtract), T, T)
            else:
                dv(lambda h, dim_=dim_, T1=T1, T2=T2: h.tensor_tensor(out=dim_, in0=T1, in1=T2, op=ALU.add), T, T)

    fam(Qt, 128, 4096, Bb, False, 0)
    fam(Pt, 128, 4096, Cp, True, 8)
    fam(Gt, 256, 128, Cp, True, 16)
    fam(Hp, 128, 4096, Bb, False, 24)
    for (p0_, j8) in ((0, 23), (64, 16)):
        Dr = cv(Apw, p0_, 64, j8 * 32, [[1, 32]])
        Di = cv(Apw, p0_, 64, 1024 + j8 * 32, [[1, 32]])
        hq = lambda a, off, p0_=p0_: cv(a, p0_, 64, off, [[1, 32]])
        dv(lambda h, hq=hq, Dr=Dr: h.tensor_copy(out=hq(TA, 0), in_=Dr), T, T)
        dv(lambda h, hq=hq, Dr=Dr: h.tensor_copy(out=hq(TA, 32), in_=Dr), T, T)
        dv(lambda h, hq=hq, Di=Di: h.tensor_scalar(out=hq(TB, 0), in0=Di, scalar1=-1.0, scalar2=None, op0=ALU.mult), T, T)
        dv(lambda h, hq=hq, Di=Di: h.tensor_copy(out=hq(TB, 32), in_=Di), T, T)
    for g in range(32):
        u = g % 2
        pf, pb = ps[2 * u], ps[2 * u + 1]
        sf_, sb_ = pss[2 * u], pss[2 * u + 1]
        for (pp, sp_, p0) in ((pf, sf_, 0), (pb, sb_, 64)):
            for ri in range(2):
                P.op("pe", lambda h, pp=pp, p0=p0, ri=ri, g=g: h.matmul(
                    pp[:, 0:128], lhsT=Qt[p0:p0 + 64, ri, g * 128:(g + 1) * 128], rhs=Pt[p0:p0 + 64, ri, g * 128:(g + 1) * 128],
                    start=(ri == 0), stop=(ri == 1)), reads=T, writes=[sp_])
        dv(lambda h, pf=pf: h.tensor_tensor(out=tmpA, in0=pf[:, 0:128], in1=msk[:, 0, :], op=ALU.mult), [sf_] + T, [s_tmp])
        dv(lambda h, pb=pb: h.tensor_tensor(out=tmpB, in0=pb[:, 0:128], in1=msk[:, 1, :], op=ALU.mult), [sb_] + T, [s_tmp])
        dv(lambda h: h.tensor_tensor(out=tmpA, in0=tmpA, in1=tmpB, op=ALU.add), [s_tmp], [s_tmp])
        dv(lambda h, g=g: h.scalar_tensor_tensor(out=Mt[:, g, :], in0=ident[:], scalar=dcol[:, g:g + 1], in1=tmpA,
                                                  op0=ALU.mult, op1=ALU.add), [s_tmp, c.ident_slot] + T, [s_M])
    for b in range(8):
        for gg in range(4):
            for ri in range(2):
                g = b * 4 + gg
                sl = gg * 2 + ri
                P.op("pe", lambda h, g=g, ri=ri, sl=sl: h.transpose(
                    out=c.psb[:, sl * 128:(sl + 1) * 128], in_=Hp[:, ri, g * 128:(g + 1) * 128], identity=ident[:]),
                    reads=T + [c.ident_slot], writes=[c.psb_slot])
        ac(lambda h, b=b: h.copy(out=Ht[:, b * 4:(b + 1) * 4, :, :].rearrange("p a b c -> p (a b c)"), in_=c.psb[:, :]),
           [c.psb_slot], [s_H])
    P.barrier()

    for sq_ in range(ntok // SEQ):
        tok0 = sq_ * SEQ
        s_hT = [S(f"hT{t}") for t in range(16)]
        s_xt1 = [S("xt1_0"), S("xt1_1")]
        for t in range(16):
            j = t % 2
            r0 = tok0 + t * 128
            P.dma("sp", f"abxt{j}", lambda h, j=j, r0=r0: h.dma_start(out=xt[j], in_=xin[r0:r0 + 128, :]),
                  reads=[xin_slot], writes=[s_xt1[j]])
            norm_transpose(c, xt[j], s_xt1[j], gbc, s_gbc, xn[j], s_xn[j], ss[j], s_ss[j], hT, s_hT[t], t * 128)
        P.barrier()
        s_sg = [S("sg0"), S("sg1")]
        s_aTq = [S(f"aT{q}") for q in range(4)]
        s_uTq = [S(f"uT{q}") for q in range(4)]
        for q in range(4):
            dv(lambda h, q=q: h.memset(aT[:, q, 0:15], 0.0), [], [s_aTq[q]], force=False)
            dv(lambda h, q=q: h.memset(aT[:, q, SEQ + 15:SEQ + 30], 0.0), [], [s_aTq[q]], force=False)
        wi = 0
        ui = 0

        def load_slab(oc):
            nonlocal wi
            j = wi % 2
            wi += 1
            P.dma("pool", f"abw{j}", lambda h, j=j, oc=oc: h.dma_start(out=wsl[j], in_=W["win"][oc]), writes=[s_wsl[j]])
            return j

        for q in range(4):
            ja = load_slab(q)
            jg = load_slab(q + 4)
            for tb in range(4):
                u = ui % 2
                ui += 1
                pa, pg = ps[2 * u], ps[2 * u + 1]
                sa, sg_ = pss[2 * u], pss[2 * u + 1]
                for (pp, sp_, jj) in ((pa, sa, ja), (pg, sg_, jg)):
                    for k in range(8):
                        P.op("pe", lambda h, pp=pp, jj=jj, k=k, tb=tb: h.matmul(
                            pp[:, :], lhsT=wsl[jj][:, k, :], rhs=hT[:, k, tb * 512:(tb + 1) * 512],
                            start=(k == 0), stop=(k == 7)), reads=[s_wsl[jj]] + s_hT[tb * 4:(tb + 1) * 4], writes=[sp_])
                ac(lambda h, u=u, pg=pg: h.activation(out=sgm[u], in_=pg[:, :], func=AF.Sigmoid), [sg_], [s_sg[u]], force=False)
                dv(lambda h, u=u, pa=pa, q=q, tb=tb: h.tensor_tensor(
                    out=aT[:, q, 15 + tb * 512:15 + (tb + 1) * 512], in0=sgm[u], in1=pa[:, :], op=ALU.mult),
                    [s_sg[u], sa], [s_aTq[q]], force=False)
        for q in range(4):
            ju = load_slab(q + 8)
            for tb in range(4):
                u = ui % 2
                ui += 1
                pu_, su_ = ps[4 + u], pss[4 + u]
                for k in range(8):
                    P.op("pe", lambda h, pu_=pu_, ju=ju, k=k, tb=tb: h.matmul(
                        pu_[:, :], lhsT=wsl[ju][:, k, :], rhs=hT[:, k, tb * 512:(tb + 1) * 512],
                        start=(k == 0), stop=(k == 7)), reads=[s_wsl[ju]] + s_hT[tb * 4:(tb + 1) * 4], writes=[su_])
                ac(lambda h, pu_=pu_, q=q, tb=tb: h.copy(
                    out=cv(uT, 0, 128, q * SEQ + tb * 64, [[256, 8], [1, 64]]),
                    in_=pu_[:, :].rearrange("p (k t) -> p t k", t=8)), [su_], [s_uTq[q]], force=False)
        P.barrier()
        s_cd = [S(f"cd{q}") for q in range(4)]
        for q in range(4):
            for k in range(31):
                dv(lambda h, q=q, k=k: h.tensor_scalar(out=cdiag[:, q, k, :], in0=ident[:], scalar1=cw[:, q, k:k + 1],
                                                        scalar2=None, op0=ALU.mult), [c.ident_slot], [s_cd[q]], force=False)
        s_ac = [S(f"ac{q}") for q in range(4)]
        s_sq = [S("sq0"), S("sq1")]
        s_mean, s_rstd, s_tv = S("mean"), S("rstd"), S("tmpv")
        s_yn = [S("yn0"), S("yn1")]
        for tb in range(4):
            for q in range(4):
                pc, spc = ps[q % 2], pss[q % 2]
                for k in range(31):
                    P.op("pe", lambda h, pc=pc, q=q, k=k, tb=tb: h.matmul(
                        pc[:, :], lhsT=cdiag[:, q, k, :], rhs=aT[:, q, tb * 512 + k: tb * 512 + k + 512],
                        start=(k == 0), stop=(k == 30)), reads=[s_cd[q]], writes=[spc])
                ac(lambda h, pc=pc, q=q: h.activation(out=acv[:, q, :], in_=pc[:, :], func=AF.Identity,
                                                      bias=vec[:, q:q + 1], scale=1.0), [spc], [s_ac[q]], force=False)
            pm, spm = ps[2], pss[2]
            pe2, spe2 = ps[3], pss[3]
            for q in range(4):
                P.op("pe", lambda h, q=q: h.matmul(pm[:, :], lhsT=cv(ones32, 0, 128, 0, [[1, 128]]), rhs=acv[:, q, :],
                                                   start=(q == 0), stop=(q == 3)), reads=[s_ac[q]], writes=[spm])
            for q in range(4):
                u = q % 2
                ac(lambda h, q=q, u=u: h.activation(out=sqv[u], in_=acv[:, q, :], func=AF.Square), [s_ac[q]], [s_sq[u]], force=False)
                P.op("pe", lambda h, q=q, u=u: h.matmul(pe2[:, :], lhsT=cv(ones32, 0, 128, 0, [[1, 128]]), rhs=sqv[u],
                                                        start=(q == 0), stop=(q == 3)), reads=[s_sq[u]], writes=[spe2])
            dv(lambda h: h.tensor_copy(out=meanv, in_=pm[:, :]), [spm], [s_mean], force=False)
            dv(lambda h: h.tensor_tensor(out=tmpv, in0=meanv, in1=meanv, op=ALU.mult), [s_mean], [s_tv])
            dv(lambda h: h.tensor_tensor(out=tmpv, in0=pe2[:, :], in1=tmpv, op=ALU.subtract), [spe2, s_tv], [s_tv])
            ac(lambda h: h.activation(out=rstdv, in_=tmpv, func=AF.Ln, bias=c.cst[:, 2:3], scale=1.0), [s_tv, c.cst_slot], [s_rstd])
            ac(lambda h: h.activation(out=rstdv, in_=rstdv, func=AF.Exp, scale=-0.5), [s_rstd], [s_rstd])
            for q in range(4):
                u = q % 2
                dv(lambda h, q=q, u=u: h.tensor_tensor(out=ynv[u], in0=acv[:, q, :], in1=meanv, op=ALU.subtract),
                   [s_ac[q], s_mean], [s_yn[u]], force=False)
                dv(lambda h, u=u: h.tensor_tensor(out=ynv[u], in0=ynv[u], in1=rstdv, op=ALU.mult), [s_yn[u], s_rstd], [s_yn[u]])
                ac(lambda h, q=q, u=u, tb=tb: h.activation(out=yaT[:, q, tb * 512:(tb + 1) * 512], in_=ynv[u], func=AF.Silu,
                                                           scale=vec[:, 4 + q:5 + q], bias=vec[:, 8 + q:9 + q]),
                   [s_yn[u]], [s_ya], force=False)
        P.barrier()
        s_UY = [S(f"UY{g}") for g in range(32)]
        s_X = [S("Xf"), S("Xb")]
        s_hist = [S("histf"), S("histb")]
        XSv = lambda p0, ri, g, c0, n: cv(XS, p0, 64, (ri * 32 + g) * NK + c0, [[1, n]])
        dv(lambda h: h.memset(cv(XS, 0, 64, 0, [[NK, 64]]), 0.0), [], [s_hist[0]], force=False)
        dv(lambda h: h.memset(cv(XS, 64, 64, 255, [[NK, 64]]), 0.0), [], [s_hist[1]], force=False)
        for g in range(32):
            q, j, par = g // 8, (g % 8) // 2, g % 2
            u = g % 2
            pu_, su_ = ps[u], pss[u]
            for tp in range(8):
                rhs = cv(uT, 32 * j, 32, q * SEQ + tp * 256, [[1, 256]])
                P.op("pe", lambda h, pu_=pu_, j=j, par=par, tp=tp, rhs=rhs: h.matmul(
                    pu_[:, 0:256], lhsT=selin[32 * j:32 * j + 32, par, tp, :], rhs=rhs, start=(tp == 0), stop=(tp == 7),
                    tile_position=(32 * j, 0)), reads=[s_uTq[q], s_sel], writes=[su_])
            ac(lambda h, pu_=pu_, g=g: h.copy(out=UY[:, g, :], in_=pu_[:, 0:256]), [su_], [s_UY[g]], force=False)
            for ri in range(2):
                px, spx = ps[2 + ri], pss[2 + ri]
                P.op("pe", lambda h, px=px, g=g, ri=ri: h.matmul(px[:, 0:256], lhsT=Ht[:, g, ri, :], rhs=UY[:, g, :],
                                                               start=True, stop=True), reads=[s_H, s_UY[g]], writes=[spx])
                dv(lambda h, px=px, g=g, ri=ri: h.tensor_copy(out=XSv(0, ri, g, 1, 256), in_=px[0:64, 0:256]),
                   [spx], [s_X[0]], force=False)
                dv(lambda h, px=px, g=g, ri=ri: h.tensor_copy(out=XSv(64, ri, g, 0, 255), in_=px[64:128, 1:256]),
                   [spx], [s_X[1]], force=False)
        dirs = []
        for (eng, p0, d, cols) in (("dve", 0, 0, list(range(1, 256))), ("pool", 64, 1, list(range(254, -1, -1)))):
            s_S = [S(f"S{d}_0"), S(f"S{d}_1")]
            s_w_, s_v_ = S(f"w_{d}"), S(f"v_{d}")
            P.op(eng, lambda h, p0=p0: h.memset(cv(Sf, p0, 64, 0, [[1, 128]]), 0.0), writes=[s_S[0]])
            dirs.append((eng, p0, d, cols, s_S, s_w_, s_v_))
        for i in range(255):
            for (eng, p0, d, cols, s_S, s_w_, s_v_) in dirs:
                col = cols[i]
                pv_, cu = i % 2, (i + 1) % 2
                xcol = cv(XS, p0, 64, col, [[32 * NK, 2], [NK, 32]])
                xcol2 = cv(XS, p0, 64, col, [[0, 2], [32 * NK, 2], [NK, 32]])
                P.op(eng, lambda h, p0=p0, pv_=pv_: h.tensor_tensor(
                    out=cv(st1, p0, 64, 0, [[64, 2], [32, 2], [1, 32]]), in0=cv(TA, p0, 64, 0, [[64, 2], [32, 2], [1, 32]]),
                    in1=cv(Sf, p0, 64, pv_ * 128, [[32, 2], [32, 2], [1, 32]]), op=ALU.mult),
                    reads=[s_S[pv_]], writes=[s_w_], force=True)
                P.op(eng, lambda h, p0=p0: h.tensor_tensor(
                    out=cv(tmpA, p0, 64, 0, [[1, 64]]), in0=cv(st1, p0, 64, 0, [[1, 64]]), in1=cv(st1, p0, 64, 64, [[1, 64]]),
                    op=ALU.add), reads=[s_w_], writes=[s_v_], force=True)
                P.op(eng, lambda h, p0=p0, cu=cu, xcol2=xcol2: h.tensor_tensor(
                    out=cv(Sf, p0, 64, cu * 128, [[64, 2], [32, 2], [1, 32]]), in0=cv(tmpA, p0, 64, 0, [[0, 2], [32, 2], [1, 32]]),
                    in1=xcol2, op=ALU.add), reads=[s_v_, s_X[d]], writes=[s_S[cu]], force=True)
                P.op("act", lambda h, p0=p0, cu=cu, xcol=xcol: h.copy(out=xcol, in_=cv(Sf, p0, 64, cu * 128, [[32, 2], [1, 32]])),
                     reads=[s_S[cu]], writes=[s_hist[d]])
        for g in range(32):
            u = g % 2
            py, spy = ps[4 + u], pss[4 + u]
            P.op("pe", lambda h, py=py, g=g: h.matmul(py[:, 0:256], lhsT=Mt[:, g, :], rhs=UY[:, g, :], start=True, stop=False),
                 reads=[s_M, s_UY[g]], writes=[spy])
            for ri in range(2):
                P.op("pe", lambda h, py=py, g=g, ri=ri: h.matmul(
                    py[:, 0:256], lhsT=Gt[:, g, ri, :], rhs=cv(XS, 0, 128, (ri * 32 + g) * NK, [[1, 256]]),
                    start=False, stop=(ri == 1)), reads=[s_G, s_hist[0], s_hist[1], s_X[0], s_X[1]], writes=[spy])
            ac(lambda h, py=py, g=g: h.copy(out=UY[:, g, :], in_=py[:, 0:256]), [spy], [s_UY[g]], force=False)
        s_yg = [S(f"yg{q}") for q in range(4)]
        ci = 0
        for q in range(4):
            for tau in range(8):
                u = ci % 2
                ci += 1
                pz, spz = ps[u], pss[u]
                for g8 in range(8):
                    P.op("pe", lambda h, pz=pz, tau=tau, g8=g8, q=q: h.matmul(
                        pz[:, 0:256], lhsT=selout[:, tau, g8, :], rhs=UY[:, 8 * q + g8, :], start=(g8 == 0), stop=(g8 == 7)),
                        reads=[s_sel, s_UY[8 * q + g8]], writes=[spz])
                gelu_evac(c, pz, spz, cv(uT, 0, 128, q * SEQ + tau, [[8, 256]]), s_yg[q], s_uTq[q])
        s_sg2 = [S("sg2_0"), S("sg2_1")]
        for tb in range(4):
            for o in range(4):
                pz, spz = ps[2 + o], pss[2 + o]
                for kq in range(4):
                    P.op("pe", lambda h, pz=pz, kq=kq, o=o, tb=tb: h.matmul(
                        pz[:, :], lhsT=gluw[:, kq, o * 128:(o + 1) * 128], rhs=uT[:, kq, tb * 512:(tb + 1) * 512],
                        start=(kq == 0), stop=(kq == 3)), reads=[s_gluw] + s_yg, writes=[spz])
            for o in range(4):
                u = o % 2
                pz, spz = ps[2 + o], pss[2 + o]
                ac(lambda h, pz=pz, o=o, u=u: h.activation(out=sgm[u], in_=pz[:, :], func=AF.Sigmoid,
                                                           bias=vec[:, 12 + o:13 + o], scale=1.0), [spz], [s_sg2[u]], force=False)
                dv(lambda h, o=o, u=u, tb=tb: h.tensor_tensor(out=uT[:, o, tb * 512:(tb + 1) * 512],
                                                              in0=uT[:, o, tb * 512:(tb + 1) * 512], in1=sgm[u], op=ALU.mult),
                   [s_sg2[u], s_yg[o]], [s_yg[o]], force=False)
        P.barrier()
        s_Wo = S("Wo")
        for q in range(2):
            P.dma("pool", "abWo", lambda h, q=q: h.dma_start(out=Wo[:, q * 4:(q + 1) * 4, :], in_=W["wout"][:, q * 4:(q + 1) * 4, :]),
                  writes=[s_Wo])
        s_xt2 = [S("xt2_0"), S("xt2_1")]
        s_ost = [S("ost0"), S("ost1")]
        for t in range(16):
            j = t % 2
            r0 = tok0 + t * 128
            P.dma("sp", f"abxt{j}", lambda h, j=j, r0=r0: h.dma_start(out=xt[j], in_=xin[r0:r0 + 128, :]),
                  reads=[xin_slot], writes=[s_xt2[j]])
            for half in range(2):
                pd, sd = ps[half], pss[half]
                for k in range(8):
                    src = yaT if k < 4 else uT
                    P.op("pe", lambda h, k=k, t=t, half=half, pd=pd, src=src: h.matmul(
                        pd[:, :], lhsT=src[:, k % 4, t * 128:(t + 1) * 128], rhs=Wo[:, k, half * 512:(half + 1) * 512],
                        start=(k == 0), stop=(k == 7)), reads=[s_Wo], writes=[sd])
                dv(lambda h, j=j, half=half, pd=pd: h.tensor_tensor(
                    out=ost[j][:, half * 512:(half + 1) * 512], in0=pd[:, :], in1=xt[j][:, half * 512:(half + 1) * 512],
                    op=ALU.add), [sd, s_xt2[j]], [s_ost[j]], force=False)
            P.dma("sp", f"abost{j}", lambda h, j=j, r0=r0: h.dma_start(out=xout[r0:r0 + 128, :], in_=ost[j]),
                  reads=[s_ost[j]], writes=[xout_slot])
        P.barrier()


def gelu_evac(c, pz, spz, dst, s_dst, s_dst2):
    P = c.P
    P.op("act", lambda h: h.activation(out=dst, in_=pz[:, 0:256], func=AF.Gelu_apprx_tanh), reads=[spz], writes=[s_dst, s_dst2])

def build_program(plan=None, ntok=TOK):
    if plan is None:
        plan = []
        for l in range(DEPTH):
            plan.append(("ab" if l % 2 == 0 else "na", l))
            plan.append(("ffn", l))
    nc = bass.Bass("TRN2", target_bir_lowering=False)
    c = setup_ctx(nc)
    P = c.P
    x = nc.dram_tensor("x", [ntok, D], F32, kind="ExternalInput").ap()
    out = nc.dram_tensor("out", [ntok, D], F32, kind="ExternalOutput").ap()
    ident = nc.dram_tensor("ident", [2, 128, 128], F32, kind="ExternalInput").ap()
    wg = nc.dram_tensor("ffn_wg", [DEPTH, NM, 128, 8, 128], F32, kind="ExternalInput").ap()
    wu = nc.dram_tensor("ffn_wu", [DEPTH, NM, 128, 8, 128], F32, kind="ExternalInput").ap()
    wd = nc.dram_tensor("ffn_wd", [DEPTH, FF, D], F32, kind="ExternalInput").ap()
    fnorm = nc.dram_tensor("ffn_norm", [DEPTH, D], F32, kind="ExternalInput").ap()
    mnorm = nc.dram_tensor("mix_norm", [DEPTH, D], F32, kind="ExternalInput").ap()
    na_wqkv = nc.dram_tensor("na_wqkv", [2, 4, 128, 8, 768], F32, kind="ExternalInput").ap()
    na_wout = nc.dram_tensor("na_wout", [2, 128, 8, 1024], F32, kind="ExternalInput").ap()
    na_bias = nc.dram_tensor("na_bias", [2, 4, 128, 4, 14, 64], F32, kind="ExternalInput").ap()
    na_qkg = nc.dram_tensor("na_qkg", [2, 128, 2], F32, kind="ExternalInput").ap()
    abd = {}
    for nm, shp in AB_SHAPES.items():
        abd[nm] = nc.dram_tensor("ab_" + nm, list(shp), F32, kind="ExternalInput").ap()
    scr = [nc.dram_tensor(f"scr{i}", [ntok, D], F32, kind="Internal").ap() for i in range(2)]
    if DBG.get('dump_on'):
        DBG['dump'] = nc.dram_tensor("dbg", [128, 16384], F32, kind="ExternalOutput").ap()
    s_scr = [P.slot("scr0"), P.slot("scr1")]
    s_x = P.slot("x_dram")
    s_out = P.slot("out_dram")
    load_ident(c, ident)
    cur, s_cur = x, s_x
    for si, st in enumerate(plan):
        if si == len(plan) - 1:
            dst, s_dst = out, s_out
        else:
            dst, s_dst = scr[si % 2], s_scr[si % 2]
        kind, l = st
        if kind == "ffn":
            ffn_stage(c, cur, s_cur, dst, s_dst, wg[l], wu[l], wd[l], fnorm[l:l + 1, :], ntok=ntok)
        elif kind == "na":
            i = l // 2
            for sq_ in range(ntok // SEQ):
                na_stage(c, cur, s_cur, dst, s_dst, na_wqkv[i], na_wout[i], na_bias[i], na_qkg[i],
                         mnorm[l:l + 1, :], sq_ * SEQ)
        elif kind == "ab":
            i = l // 2
            Wd_ = {k: (v[i] if k in AB_PER_LAYER else v) for k, v in abd.items()}
            ab_stage(c, cur, s_cur, dst, s_dst, Wd_, mnorm[l:l + 1, :], ntok)
        cur, s_cur = dst, s_dst
    fin = Ins("sp", None, False, None, 0)
    for e in ENGS:
        for i in P.streams[e]:
            if i.is_dma:
                fin.deps.append(i)
    fin.idx = len(P.streams["sp"])
    P.streams["sp"].append(fin)
    P.emit()
    return nc


AB_PER_LAYER = ("win", "wout", "gluw", "vec", "cw", "lam", "B", "C", "dcol")
AB_SHAPES = {
    "win": (2, 12, 128, 8, 128), "wout": (2, 128, 8, 1024), "gluw": (2, 128, 4, 512), "vec": (2, 128, 20),
    "cw": (2, 128, 4, 31), "lam": (2, 128, 3, 32), "B": (2, 128, 2, 32, 16), "C": (2, 128, 2, 32, 16),
    "dcol": (2, 128, 32), "selin": (128, 2, 8, 128), "selout": (128, 8, 8, 128), "mask": (128, 2, 128),
    "etab": (128, 32), "ones": (128, 128),
}


def ab_host_layout(inputs):
    g = np.ascontiguousarray
    f = lambda k: np.asarray(inputs[k], dtype=np.float32)
    d = {}
    d["win"] = g(f("ab_w_in").reshape(2, 8, 128, 12, 128).transpose(0, 3, 2, 1, 4))
    d["wout"] = g(f("ab_w_out").reshape(2, 8, 128, 1024).transpose(0, 2, 1, 3))
    d["gluw"] = g(f("ssm_glu_w").reshape(2, 4, 128, 512).transpose(0, 2, 1, 3))
    vec = np.zeros((2, 128, 20), np.float32)
    for j, k in enumerate(("conv_b", "conv_ln_g", "conv_ln_b", "ssm_glu_b")):
        vec[:, :, 4 * j:4 * j + 4] = f(k).reshape(2, 4, 128).transpose(0, 2, 1)
    d["vec"] = vec
    d["cw"] = g(f("conv_w").reshape(2, 31, 4, 128).transpose(0, 3, 2, 1))
    lam = np.empty((2, 2, 64, 3, 32), np.float32)
    lam[:, :, :, 0] = f("ssm_lambda_re").transpose(0, 1, 3, 2)
    lam[:, :, :, 1] = f("ssm_lambda_im").transpose(0, 1, 3, 2)
    lam[:, :, :, 2] = f("ssm_log_step")[:, :, None, :]
    d["lam"] = g(lam.reshape(2, 128, 3, 32))
    B = np.stack([f("ssm_b_re"), f("ssm_b_im")], axis=1)
    d["B"] = g(B.transpose(0, 2, 4, 1, 3, 5).reshape(2, 128, 2, 32, 16))
    C = np.stack([f("ssm_c_re"), f("ssm_c_im")], axis=1)
    d["C"] = g(C.transpose(0, 2, 5, 1, 3, 4).reshape(2, 128, 2, 32, 16))
    dsk = f("ssm_d").reshape(2, 32, 16)
    d["dcol"] = g(np.broadcast_to(dsk.transpose(0, 2, 1)[:, None], (2, 8, 16, 32)).reshape(2, 128, 32))
    selin = np.zeros((4, 2, 16, 2, 8, 8, 16), np.float32)
    selout = np.zeros((8, 16, 8, 8, 8, 16), np.float32)
    for cc in range(16):
        for t in range(8):
            selin[:, 0, cc, 0, t, t, cc] = 1.0
            selin[:, 1, cc, 1, t, t, cc] = 1.0
            for g8 in range(8):
                selout[t, cc, t, g8, g8, cc] = 1.0
    d["selin"] = selin.reshape(128, 2, 8, 128)
    d["selout"] = selout.reshape(128, 8, 8, 128)
    tp = np.repeat(np.arange(8), 16)
    d["mask"] = g(np.stack([(tp[:, None] <= tp[None, :]), (tp[:, None] >= tp[None, :])], axis=1).astype(np.float32))
    tt = np.arange(8, dtype=np.float32)
    ef = np.concatenate([-tt, tt, tt + 1, 7 - tt])
    eb = np.concatenate([tt, -tt, 8 - tt, tt])
    d["etab"] = g(np.concatenate([np.broadcast_to(ef, (64, 32)), np.broadcast_to(eb, (64, 32))], axis=0))
    d["ones"] = np.full((128, 128), 1.0 / 512, np.float32)
    return {"ab_" + k: v for k, v in d.items()}


def host_layout(inputs):
    g = np.ascontiguousarray
    d = {}
    bd = np.zeros((128, 128), np.float32)
    bd[:64, :64] = 1.0
    bd[64:, 64:] = 1.0
    d["ident"] = np.stack([np.eye(128, dtype=np.float32), bd])
    for nm, key in (("ffn_wg", "ffn_w_gate"), ("ffn_wu", "ffn_w_up")):
        w = np.asarray(inputs[key], dtype=np.float32).reshape(DEPTH, 8, 128, NM, 128)
        d[nm] = g(w.transpose(0, 3, 2, 1, 4))
    d["ffn_wd"] = g(np.asarray(inputs["ffn_w_down"], dtype=np.float32))
    d["ffn_norm"] = g(np.asarray(inputs["ffn_norm"], dtype=np.float32))
    d["mix_norm"] = g(np.asarray(inputs["mix_norm"], dtype=np.float32))
    wqkv = np.asarray(inputs["na_w_qkv"], dtype=np.float32).reshape(2, 8, 128, 3, 4, 256)
    d["na_wqkv"] = g(wqkv.transpose(0, 4, 2, 1, 3, 5).reshape(2, 4, 128, 8, 768))
    d["na_wout"] = g(np.asarray(inputs["na_w_out"], dtype=np.float32).reshape(2, 8, 128, 1024).transpose(0, 2, 1, 3))
    rpb = np.asarray(inputs["na_rpb"], dtype=np.float32)
    qc = np.arange(64)[None, :]
    kc = np.arange(64)[:, None]
    cidx = np.clip(kc - qc, -15, 15) + 15
    cstart = np.clip(qc - 8, 0, 48)
    cmask = (kc >= cstart) & (kc < cstart + 16)
    tab = np.empty((2, 16, 14, 2, 64, 64), np.float32)
    for rho in range(14):
        for jj in range(2):
            tab[:, :, rho, jj] = np.where(cmask[None, None], rpb[:, :, rho + jj][:, :, cidx], np.float32(-30000.0))
    tab = tab.reshape(2, 4, 4, 14, 2, 64, 64).transpose(0, 1, 4, 5, 2, 3, 6).reshape(2, 4, 128, 4, 14, 64)
    d["na_bias"] = g(tab)
    qg = np.asarray(inputs["na_q_norm"], dtype=np.float32)
    kg = np.asarray(inputs["na_k_norm"], dtype=np.float32)
    d["na_qkg"] = g(np.stack([np.tile(qg, (1, 2)), np.tile(kg, (1, 2))], axis=-1))
    d.update(ab_host_layout(inputs))
    return d


def kernel(**inputs):
    x = np.asarray(inputs["x"], dtype=np.float32)
    shared = host_layout(inputs)
    nc = build_program()
    in_maps = []
    for i in range(NCORES):
        m = dict(shared)
        m["x"] = np.ascontiguousarray(x[2 * i:2 * i + 2].reshape(TOK, D))
        in_maps.append(m)
    res = run_bass_kernel_spmd(nc, in_maps, core_ids=list(range(NCORES)))
    outs = [res.results[i]["out"].reshape(2, SEQ, D) for i in range(NCORES)]
    return np.concatenate(outs, axis=0).astype(np.float32)
```

```python
import math
import numpy as np
import concourse.bass as bass
import concourse.mybir as mybir
from concourse.bass_utils import run_bass_kernel_spmd

F32 = mybir.dt.float32
BF16 = mybir.dt.bfloat16
I32 = mybir.dt.int32
AF = mybir.ActivationFunctionType
ALU = mybir.AluOpType
AX = mybir.AxisListType

DBG = {}
NCORES = 8
D = 1024
SEQ = 2048
TOK = 2 * SEQ
FF = 2816
NM = FF // 128
DEPTH = 4
EPS = 1e-6


class Slot:
    __slots__ = ("name", "lw", "rs")

    def __init__(self, name):
        self.name = name
        self.lw = None
        self.rs = []


class Ins:
    __slots__ = ("eng", "fn", "deps", "is_dma", "sem", "semval", "inc", "idx", "force")

    def __init__(self, eng, fn, is_dma, sem, semval):
        self.eng = eng
        self.fn = fn
        self.deps = []
        self.is_dma = is_dma
        self.sem = sem
        self.semval = semval
        self.inc = False
        self.idx = -1
        self.force = False


ENGS = ("pe", "act", "dve", "pool", "sp")


class Prog:
    def __init__(self, nc):
        self.nc = nc
        self.streams = {e: [] for e in ENGS}
        self.dma_sems = {}
        self.dma_cnt = {}
        self.slots = []

    def slot(self, name):
        s = Slot(name)
        self.slots.append(s)
        return s

    def slots_n(self, name, n):
        return [self.slot(f"{name}{i}") for i in range(n)]

    def _add(self, ins, reads, writes):
        e = ins.eng
        best = {}
        dl = []

        def need(p):
            if p is None or p is ins:
                return
            if p.is_dma:
                if p not in dl:
                    dl.append(p)
            elif ins.is_dma or p.eng != e or ins.force:
                q = best.get(p.eng)
                if q is None or q.idx < p.idx:
                    best[p.eng] = p

        for s in reads:
            need(s.lw)
        for s in writes:
            need(s.lw)
            for r in s.rs:
                need(r)
        ins.deps = dl + list(best.values())
        for s in reads:
            rs = s.rs
            if rs and (not ins.is_dma) and (not rs[-1].is_dma) and rs[-1].eng == e:
                rs[-1] = ins
            else:
                rs.append(ins)
        for s in writes:
            s.lw = ins
            s.rs = []
        ins.idx = len(self.streams[e])
        self.streams[e].append(ins)
        return ins

    def op(self, eng, fn, reads=(), writes=(), force=False):
        ins = Ins(eng, fn, False, None, 0)
        ins.force = force
        return self._add(ins, reads, writes)

    def dma(self, eng, semname, fn, reads=(), writes=()):
        if semname not in self.dma_sems:
            self.dma_sems[semname] = self.nc.alloc_semaphore(name="d_" + semname)
            self.dma_cnt[semname] = 0
        self.dma_cnt[semname] += 16
        return self._add(Ins(eng, fn, True, self.dma_sems[semname], self.dma_cnt[semname]), reads, writes)

    def barrier(self):
        lasts = []
        for e in ENGS:
            st = self.streams[e]
            if st:
                lasts.append(st[-1])
        dmas = [i for e in ENGS for i in self.streams[e] if i.is_dma and not getattr(i, "_barr", False)]
        for e in ENGS:
            ins = Ins(e, None, False, None, 0)
            for p in lasts:
                if p.eng != e and not p.is_dma and p.fn is not None:
                    ins.deps.append(p)
            for p in dmas:
                ins.deps.append(p)
            ins.idx = len(self.streams[e])
            self.streams[e].append(ins)
        for s in self.slots:
            s.lw = None
            s.rs = []

    def emit(self):
        nc = self.nc
        for e in ENGS:
            for ins in self.streams[e]:
                for p in ins.deps:
                    if not p.is_dma:
                        p.inc = True
        rank = {}
        sems = {}
        for e in ENGS:
            sems[e] = nc.alloc_semaphore(name="s_" + e)
            c = 0
            last_real = None
            for ins in self.streams[e]:
                if ins.fn is not None and not ins.is_dma:
                    last_real = ins
                if ins.inc:
                    assert ins.fn is not None and not ins.is_dma
                    c += 1
                    rank[ins] = c
        handles = {"pe": nc.tensor, "act": nc.scalar, "dve": nc.vector, "pool": nc.gpsimd, "sp": nc.sync}
        streams = self.streams

        def replay(e, h):
            known = {}
            for ins in streams[e]:
                for p in ins.deps:
                    if p.is_dma:
                        key, val, sem = ("d", id(p.sem)), p.semval, p.sem
                    else:
                        key, val, sem = ("c", p.eng), rank[p], sems[p.eng]
                    if known.get(key, 0) >= val:
                        continue
                    known[key] = val
                    h.wait_ge(sem, val)
                if ins.fn is None:
                    continue
                r = ins.fn(h)
                if ins.is_dma:
                    r.then_inc(ins.sem, 16)
                elif ins.inc:
                    r.then_inc(sems[e], 1)

        with nc.Block() as block:
            @block.tensor
            def _(h):
                replay("pe", h)

            @block.scalar
            def _(h):
                replay("act", h)

            @block.vector
            def _(h):
                replay("dve", h)

            @block.gpsimd
            def _(h):
                replay("pool", h)

            @block.sync
            def _(h):
                replay("sp", h)


class Arena:
    def __init__(self, nc, name, nelem, dtype):
        self.t = nc.alloc_sbuf_tensor(name, [128, nelem], dtype)
        self.n = nelem
        self.off = 0

    def reset(self):
        self.off = 0

    def take(self, *shape):
        n = int(np.prod(shape))
        assert self.off + n <= self.n, (self.off, n, self.n)
        ap = self.t[:, self.off:self.off + n]
        self.off += n
        if len(shape) == 2:
            ap = ap.rearrange("p (a b) -> p a b", b=shape[1])
        elif len(shape) == 3:
            ap = ap.rearrange("p (a b c) -> p a b c", b=shape[1], c=shape[2])
        elif len(shape) == 4:
            ap = ap.rearrange("p (a b c d) -> p a b c d", b=shape[1], c=shape[2], d=shape[3])
        return ap


class Ctx:
    pass


def setup_ctx(nc):
    c = Ctx()
    c.nc = nc
    c.P = Prog(nc)
    c.fa = Arena(nc, "fa", 12800, F32)
    c.ba = Arena(nc, "ba", 78848, BF16)
    c.ps = [nc.alloc_psum_tensor(f"ps{i}", [128, 512], F32) for i in range(7)]
    c.psb = nc.alloc_psum_tensor("psb", [128, 1024], BF16)
    c.ps_slots = [c.P.slot(f"ps{i}") for i in range(7)]
    c.psb_slot = c.P.slot("psb")
    c.ident = nc.alloc_sbuf_tensor("ident_sb", [128, 128], BF16)
    c.ident_slot = c.P.slot("ident")
    c.bd = nc.alloc_sbuf_tensor("bd_sb", [128, 128], BF16)
    c.bd_slot = c.P.slot("bd")
    c.cst = nc.alloc_sbuf_tensor("cst_sb", [128, 8], F32)
    c.cst_slot = c.P.slot("cst")
    c.psbs = [(c.psb[:, :], c.psb_slot), (c.ps[6][:, :].bitcast(BF16), c.ps_slots[6])]
    c.nt_count = 0
    return c


def load_ident(c, ident_dram):
    P = c.P
    P.dma("pool", "ident", lambda h: h.dma_start(out=c.ident[:], in_=ident_dram[0]), writes=[c.ident_slot])
    P.dma("pool", "ident", lambda h: h.dma_start(out=c.bd[:], in_=ident_dram[1]), writes=[c.bd_slot])
    for i, v in enumerate((64.0 * EPS, EPS, 1e-5, 0.0, 1.0, math.pi / 2)):
        P.op("dve", lambda h, i=i, v=v: h.memset(c.cst[:, i:i + 1], v), writes=[c.cst_slot])


def norm_a(c, xt, xt_slot, gain_bc, gain_slot, xn, xn_slot, ss, ss_slot):
    P = c.P
    P.op("act", lambda h: h.activation(out=xn, in_=xt, func=AF.Square, accum_out=ss[:, 0:1]),
         reads=[xt_slot], writes=[xn_slot, ss_slot])
    P.op("dve", lambda h: h.tensor_scalar(out=ss[:, 1:2], in0=ss[:, 0:1], scalar1=1.0 / D, scalar2=EPS,
                                           op0=ALU.mult, op1=ALU.add), reads=[ss_slot], writes=[ss_slot])
    P.op("act", lambda h: h.activation(out=ss[:, 2:3], in_=ss[:, 1:2], func=AF.Sqrt), reads=[ss_slot], writes=[ss_slot])
    P.op("dve", lambda h: h.reciprocal(out=ss[:, 3:4], in_=ss[:, 2:3]), reads=[ss_slot], writes=[ss_slot])
    P.op("dve", lambda h: h.scalar_tensor_tensor(out=xn, in0=xt, scalar=ss[:, 3:4], in1=gain_bc,
                                                  op0=ALU.mult, op1=ALU.mult),
         reads=[xt_slot, ss_slot, gain_slot], writes=[xn_slot], force=True)


def norm_b(c, xn, xn_slot, hT, hT_slot, col0):
    P = c.P
    pb_, pb_slot = c.psbs[c.nt_count % 2]
    c.nt_count += 1
    for k in range(8):
        P.op("pe", lambda h, k=k: h.transpose(out=pb_[:, k * 128:(k + 1) * 128], in_=xn[:, k * 128:(k + 1) * 128],
                                              identity=c.ident[:]),
             reads=[xn_slot, c.ident_slot], writes=[pb_slot])
    P.op("act", lambda h: h.copy(out=hT[:, :, col0:col0 + 128],
                                 in_=pb_.rearrange("p (k t) -> p k t", t=128)),
         reads=[pb_slot], writes=[hT_slot])


def norm_transpose(c, xt, xt_slot, gain_bc, gain_slot, xn, xn_slot, ss, ss_slot, hT, hT_slot, col0):
    norm_a(c, xt, xt_slot, gain_bc, gain_slot, xn, xn_slot, ss, ss_slot)
    norm_b(c, xn, xn_slot, hT, hT_slot, col0)


def ffn_stage(c, xin, xin_slot, xout, xout_slot, wg, wu, wd, gain_row, ntok=TOK, TB=1024):
    P = c.P
    fa, ba = c.fa, c.ba
    fa.reset()
    ba.reset()
    NT = TB // 128
    Wd = ba.take(NM, 1024)
    hTs = [ba.take(8, TB) for _ in range(2)]
    actT = ba.take(NM, TB)
    wgb = [ba.take(8, 128) for _ in range(2)]
    wub = [ba.take(8, 128) for _ in range(2)]
    xn = [ba.take(1, 1024)[:, 0, :] for _ in range(2)]
    xt = [fa.take(1, 1024)[:, 0, :] for _ in range(4)]
    ost = [fa.take(1, 1024)[:, 0, :] for _ in range(2)]
    gbc = fa.take(1, 1024)[:, 0, :]
    sg = [fa.take(1, 512)[:, 0, :] for _ in range(2)]
    ss = [fa.take(1, 4)[:, 0, :] for _ in range(2)]

    s_Wd = P.slot("Wd")
    s_hT = [[P.slot(f"hT{j}_{t}") for t in range(NT)] for j in range(2)]
    s_act = [[P.slot(f"act{m}_{h}") for h in range(TB // 512)] for m in range(NM)]
    s_wg = P.slots_n("wg", 2)
    s_wu = P.slots_n("wu", 2)
    s_xn = P.slots_n("xn", 2)
    s_xt = P.slots_n("xt", 4)
    s_ost = P.slots_n("ost", 2)
    s_gbc = P.slot("gbc")
    s_sg = P.slots_n("sg", 2)
    s_ss = P.slots_n("ss", 2)

    P.dma("sp", "gbc", lambda h: h.dma_start(out=gbc, in_=gain_row.partition_broadcast(128)), writes=[s_gbc])
    wdv = wd.rearrange("(m p) o -> p m o", p=128)
    nblk = ntok // TB
    st = {"w": 0, "o": 0, "n": 0, "u": 0, "x": 0}

    pend = {}

    def norm_tile_a(b, t):
        j = st["n"] % 2
        st["n"] += 1
        xi = st["x"] % 2
        st["x"] += 1
        r0 = b * TB + t * 128
        P.dma("sp", f"xt{xi}", lambda h: h.dma_start(out=xt[xi], in_=xin[r0:r0 + 128, :]),
              reads=[xin_slot], writes=[s_xt[xi]])
        norm_a(c, xt[xi], s_xt[xi], gbc, s_gbc, xn[j], s_xn[j], ss[j], s_ss[j])
        pend[(b, t)] = j

    def norm_tile_b(b, t):
        j = pend.pop((b, t))
        norm_b(c, xn[j], s_xn[j], hTs[b % 2], s_hT[b % 2][t], t * 128)

    def norm_tile(b, t):
        norm_tile_a(b, t)
        norm_tile_b(b, t)

    for t in range(NT):
        norm_tile(0, t)
    for q in range(2):
        P.dma("pool", "Wd", lambda h, q=q: h.dma_start(out=Wd[:, q * 11:(q + 1) * 11, :], in_=wdv[:, q * 11:(q + 1) * 11, :]),
              writes=[s_Wd])
    for b in range(nblk):
        hT = hTs[b % 2]
        shT = s_hT[b % 2]
        for m in range(NM):
            j = st["w"] % 2
            st["w"] += 1
            P.dma("pool", f"wg{j}", lambda h, j=j, m=m: h.dma_start(out=wgb[j], in_=wg[m]), writes=[s_wg[j]])
            P.dma("pool", f"wu{j}", lambda h, j=j, m=m: h.dma_start(out=wub[j], in_=wu[m]), writes=[s_wu[j]])
            for hh in range(TB // 512):
                u = st["u"] % 2
                st["u"] += 1
                pg, pu = c.ps[2 * u], c.ps[2 * u + 1]
                sgs, sus = c.ps_slots[2 * u], c.ps_slots[2 * u + 1]
                hs = shT[hh * 4:(hh + 1) * 4]
                for k in range(8):
                    P.op("pe", lambda h, j=j, k=k, hh=hh, pg=pg, hT=hT: h.matmul(
                        pg[:, :], lhsT=wgb[j][:, k, :], rhs=hT[:, k, hh * 512:(hh + 1) * 512],
                        start=(k == 0), stop=(k == 7)), reads=[s_wg[j]] + hs, writes=[sgs])
                for k in range(8):
                    P.op("pe", lambda h, j=j, k=k, hh=hh, pu=pu, hT=hT: h.matmul(
                        pu[:, :], lhsT=wub[j][:, k, :], rhs=hT[:, k, hh * 512:(hh + 1) * 512],
                        start=(k == 0), stop=(k == 7)), reads=[s_wu[j]] + hs, writes=[sus])
                P.op("act", lambda h, u=u, pg=pg: h.activation(out=sg[u], in_=pg[:, :], func=AF.Silu),
                     reads=[sgs], writes=[s_sg[u]])
                P.op("dve", lambda h, u=u, pu=pu, m=m, hh=hh: h.tensor_tensor(
                    out=actT[:, m, hh * 512:(hh + 1) * 512], in0=sg[u], in1=pu[:, :], op=ALU.mult),
                    reads=[s_sg[u], sus], writes=[s_act[m][hh]])
            if b + 1 < nblk and m % 2 == 1:
                i_ = m // 2
                if i_ < NT:
                    norm_tile_a(b + 1, i_)
                if 1 <= i_ <= NT:
                    norm_tile_b(b + 1, i_ - 1)
        for t in range(NT):
            o = st["o"] % 2
            st["o"] += 1
            xi = 2 + (t % 2)
            r0 = b * TB + t * 128
            P.dma("sp", f"xt{xi}", lambda h, xi=xi, r0=r0: h.dma_start(out=xt[xi], in_=xin[r0:r0 + 128, :]),
                  reads=[xin_slot], writes=[s_xt[xi]])
            for half in range(2):
                pd = c.ps[4 + half]
                sd = c.ps_slots[4 + half]
                for m in range(NM):
                    P.op("pe", lambda h, m=m, t=t, half=half, pd=pd: h.matmul(
                        pd[:, :], lhsT=actT[:, m, t * 128:(t + 1) * 128], rhs=Wd[:, m, half * 512:(half + 1) * 512],
                        start=(m == 0), stop=(m == NM - 1)),
                        reads=[s_act[m][t // 4], s_Wd], writes=[sd])
                P.op("dve", lambda h, o=o, xi=xi, half=half, pd=pd: h.tensor_tensor(
                    out=ost[o][:, half * 512:(half + 1) * 512], in0=pd[:, :], in1=xt[xi][:, half * 512:(half + 1) * 512],
                    op=ALU.add), reads=[sd, s_xt[xi]], writes=[s_ost[o]])
            P.dma("sp", f"ost{o}", lambda h, o=o, r0=r0: h.dma_start(out=xout[r0:r0 + 128, :], in_=ost[o]),
                  reads=[s_ost[o]], writes=[xout_slot])
    P.barrier()


def cv_ps(pt, off, dims, p0=0, npart=128):
    a = pt[:, :]
    n = a.ap[0][0]
    return bass.AP(tensor=a.tensor, offset=a.offset + p0 * n + off, ap=[[n, npart]] + [list(d) for d in dims])


def _na_runs():
    out = []
    rs = [min(max(r - 4, 0), 24) for r in range(32)]
    for hf in range(2):
        runs = []
        for par in range(2):
            for ti in range(16):
                row0 = 2 * ti + par
                if row0 + 1 > 31:
                    continue
                for grp in range(2):
                    rows = [r for r in range(16 * hf + 8 * grp, 16 * hf + 8 * grp + 8)
                            if rs[r] % 2 == par and rs[r] <= row0 <= rs[r] + 6]
                    while rows:
                        if len(rows) == 1:
                            run, rows = rows, []
                            st = 1
                        else:
                            st = rows[1] - rows[0]
                            k = 2
                            while k < len(rows) and rows[k] - rows[k - 1] == st:
                                k += 1
                            run, rows = rows[:k], rows[k:]
                        runs.append((par, ti, grp, run[0], st, len(run)))
        out.append(runs)
    return out


NA_RUNS = _na_runs()

def na_stage(c, xin, xin_slot, xout, xout_slot, wqkv, wout, biasT, qkg, gain_row, tok0):
    P = c.P
    fa, ba = c.fa, c.ba
    fa.reset()
    ba.reset()
    hT = ba.take(8, SEQ)
    attnT = ba.take(8, SEQ)
    qT = ba.take(2, SEQ)
    kT = ba.take(2, SEQ)
    Vpad = ba.take(2, 16, 4, 128)
    EB = ba.take(4, 14, 64)
    cpad = ba.take(3, 128)
    zrhs = ba.take(1, 512)[:, 0, :]
    bT = ba.take(4, 14, 64)
    wq = ba.take(8, 768)
    PT = [ba.take(1, 512)[:, 0, :] for _ in range(3)]
    xn = [ba.take(1, 1024)[:, 0, :] for _ in range(2)]
    sq = [ba.take(1, 512)[:, 0, :] for _ in range(2)]
    xt = [fa.take(1, 1024)[:, 0, :] for _ in range(2)]
    ost = [fa.take(1, 1024)[:, 0, :] for _ in range(2)]
    gbc = fa.take(1, 1024)[:, 0, :]
    raw = [fa.take(1, 512)[:, 0, :] for _ in range(2)]
    rinv = [fa.take(1, 512)[:, 0, :] for _ in range(2)]
    ss = [fa.take(1, 4)[:, 0, :] for _ in range(2)]
    rec = [fa.take(1, 2)[:, 0, :] for _ in range(2)]
    gq = fa.take(1, 2)[:, 0, :]

    s_hT = P.slots_n("n_hT", 16)
    s_attnT = P.slots_n("n_attnT", 8)
    s_qT = P.slot("n_qT")
    s_kT = P.slot("n_kT")
    s_V = P.slot("n_V")
    s_bT = P.slot("n_bT")
    s_wq = P.slot("n_wq")
    s_PT = P.slots_n("n_PT", 3)
    s_EB = P.slot("n_EB")
    s_cpad = P.slot("n_cpad")
    s_xn = P.slots_n("n_xn", 2)
    s_sq = P.slots_n("n_sq", 2)
    s_xt = P.slots_n("n_xt", 2)
    s_ost = P.slots_n("n_ost", 2)
    s_gbc = P.slot("n_gbc")
    s_raw = P.slots_n("n_raw", 2)
    s_rinv = P.slots_n("n_rinv", 2)
    s_ss = P.slots_n("n_ss", 2)
    s_rec = P.slots_n("n_rec", 2)
    s_gq = P.slot("n_gq")
    ps, pss = c.ps, c.ps_slots

    P.dma("sp", "gbc", lambda h: h.dma_start(out=gbc, in_=gain_row.partition_broadcast(128)), writes=[s_gbc])
    P.dma("sp", "gq", lambda h: h.dma_start(out=gq, in_=qkg), writes=[s_gq])
    P.op("dve", lambda h: h.memset(Vpad[:, :, :, :, :].rearrange("p a b c d -> p (a b c d)"), 0.0), writes=[s_V])
    P.op("dve", lambda h: h.memset(cpad[:, :, :].rearrange("p a b -> p (a b)"), 0.0), writes=[s_cpad])
    P.op("dve", lambda h: h.memset(cpad[:, 0, 0:64], 1.0), writes=[s_cpad])
    P.op("dve", lambda h: h.memset(cpad[:, 1, 64:128], 1.0), writes=[s_cpad])
    P.op("dve", lambda h: h.memset(zrhs, 0.0), writes=[s_cpad])

    for t in range(16):
        j = t % 2
        r0 = tok0 + t * 128
        P.dma("sp", f"nxt{j}", lambda h, j=j, r0=r0: h.dma_start(out=xt[j], in_=xin[r0:r0 + 128, :]),
              reads=[xin_slot], writes=[s_xt[j]])
        norm_transpose(c, xt[j], s_xt[j], gbc, s_gbc, xn[j], s_xn[j], ss[j], s_ss[j], hT, s_hT[t], t * 128)

    ui = 0
    for hg in DBG.get('hgs', range(4)):
        for part in range(3):
            P.dma("pool", "nwq", lambda h, hg=hg, part=part: h.dma_start(
                out=wq[:, :, part * 256:(part + 1) * 256], in_=wqkv[hg][:, :, part * 256:(part + 1) * 256]),
                writes=[s_wq])
        P.dma("pool", "nbT", lambda h, hg=hg: h.dma_start(out=bT, in_=biasT[hg]), writes=[s_bT])
        for part in DBG.get('qk', range(2)):
            dst, s_dst = (qT, s_qT) if part == 0 else (kT, s_kT)
            for cc in range(2):
                for tb in range(4):
                    u = ui % 2
                    ui += 1
                    pq, spq = ps[u], pss[u]
                    pn, spn = ps[2 + u], pss[2 + u]
                    for k in range(8):
                        P.op("pe", lambda h, k=k, part=part, cc=cc, tb=tb, pq=pq: h.matmul(
                            pq[:, :], lhsT=wq[:, k, part * 256 + cc * 128: part * 256 + cc * 128 + 128],
                            rhs=hT[:, k, tb * 512:(tb + 1) * 512], start=(k == 0), stop=(k == 7)),
                            reads=[s_wq] + s_hT[tb * 4:(tb + 1) * 4], writes=[spq])
                    P.op("act", lambda h, u=u, pq=pq: h.activation(out=sq[u], in_=pq[:, :], func=AF.Square),
                         reads=[spq], writes=[s_sq[u]])
                    P.op("dve", lambda h, u=u, pq=pq: h.tensor_copy(out=raw[u], in_=pq[:, :]),
                         reads=[spq, s_sq[u]], writes=[s_raw[u]])
                    P.op("pe", lambda h, u=u, pn=pn: h.matmul(pn[:, :], lhsT=c.bd[:], rhs=sq[u], start=True, stop=True),
                         reads=[s_sq[u], c.bd_slot], writes=[spn])
                    if part == 0:
                        P.op("act", lambda h, u=u, pn=pn: h.activation(out=rinv[u], in_=pn[:, :], func=AF.Ln,
                                                                       bias=c.cst[:, 0:1], scale=1.0),
                             reads=[spn, c.cst_slot], writes=[s_rinv[u]])
                    else:
                        P.op("act", lambda h, u=u, pn=pn: h.activation(out=rinv[u], in_=pn[:, :], func=AF.Ln,
                                                                       bias=c.cst[:, 1:2], scale=1.0 / 64),
                             reads=[spn, c.cst_slot], writes=[s_rinv[u]])
                    P.op("act", lambda h, u=u: h.activation(out=rinv[u], in_=rinv[u], func=AF.Exp, scale=-0.5),
                         reads=[s_rinv[u]], writes=[s_rinv[u]])
                    P.op("dve", lambda h, u=u, part=part, cc=cc, tb=tb, dst=dst: h.scalar_tensor_tensor(
                        out=dst[:, cc, tb * 512:(tb + 1) * 512], in0=raw[u], scalar=gq[:, part:part + 1], in1=rinv[u],
                        op0=ALU.mult, op1=ALU.mult), reads=[s_raw[u], s_rinv[u], s_gq], writes=[s_dst])
        for par in DBG.get('vpar', range(2)):
            for i in range(16 - par):
                u = ui % 2
                ui += 1
                pv, spv = ps[u], pss[u]
                t0 = par * 64 + i * 128
                for k in range(8):
                    P.op("pe", lambda h, k=k, t0=t0, pv=pv: h.matmul(
                        pv[:, 0:256], lhsT=hT[:, k, t0:t0 + 128], rhs=wq[:, k, 512:768], start=(k == 0), stop=(k == 7)),
                        reads=[s_wq] + s_hT[t0 // 128:(t0 + 127) // 128 + 1], writes=[spv])
                for hp in range(2):
                    P.op("act", lambda h, par=par, i=i, pv=pv, hp=hp: h.copy(
                        out=cv(Vpad, 0, 128, ((par * 16 + i) * 4 + hp) * 128 + hp * 64, [[256, 2], [1, 64]]),
                        in_=cv_ps(pv, hp * 64, [[128, 2], [1, 64]])), reads=[spv], writes=[s_V])
        P.op("act", lambda h: h.activation(out=EB[:, :, :, :].rearrange("p a b c -> p (a b c)"),
                                           in_=bT[:, :, :, :].rearrange("p a b c -> p (a b c)"), func=AF.Exp),
             reads=[s_bT], writes=[s_EB])
        for cc in range(2):
            for hf in range(2):
                for b in range(4):
                    P.op("pe", lambda h, b=b: h.matmul(ps[b][:, :], lhsT=cpad[:, 2, :], rhs=zrhs, start=True, stop=True,
                                                       skip_group_check=True), reads=[s_cpad], writes=[pss[b]])
                units = [(par, ti, grp, r0_, st_, n_, hp) for (par, ti, grp, r0_, st_, n_) in NA_RUNS[hf] for hp in range(2)]
                LA = 2

                def emit_front(idx, cc=cc):
                    (par, ti, grp, r0_, st_, n_, hp) = units[idx]
                    row0 = 2 * ti + par
                    hh = cc * 2 + hp
                    pb = hp * 64
                    w3 = idx % 3
                    sc, ssc = ps[4 + w3], pss[4 + w3]
                    nq = 64 * n_
                    rho0 = row0 - r0_ + 7
                    P.op("pe", lambda h: h.matmul(
                        sc[:, 0:nq], lhsT=kT[pb:pb + 64, cc, row0 * 64: row0 * 64 + 128],
                        rhs=cv(qT, pb, 64, cc * SEQ + r0_ * 64, [[st_ * 64, n_], [1, 64]]), start=True, stop=True),
                        reads=[s_kT, s_qT], writes=[ssc])
                    P.op("act", lambda h: h.activation(out=PT[w3][:, 0:nq], in_=sc[:, 0:nq], func=AF.Exp),
                         reads=[ssc], writes=[s_PT[w3]])
                    P.op("dve", lambda h: h.tensor_tensor(
                        out=cv(PT[w3], 0, 128, 0, [[64, n_], [1, 64]]), in0=cv(PT[w3], 0, 128, 0, [[64, n_], [1, 64]]),
                        in1=cv(EB, 0, 128, hh * 896 + rho0 * 64, [[-st_ * 64, n_], [1, 64]]), op=ALU.mult),
                        reads=[s_PT[w3], s_EB], writes=[s_PT[w3]])

                def emit_back(idx, cc=cc, hf=hf):
                    (par, ti, grp, r0_, st_, n_, hp) = units[idx]
                    hh = cc * 2 + hp
                    w3 = idx % 3
                    nq = 64 * n_
                    c0 = (r0_ - 16 * hf - 8 * grp) * 64
                    for (bank, lw_) in ((grp, Vpad[:, par, ti, hh, :]), (2 + grp, cpad[:, hp, :])):
                        P.op("pe", lambda h, bank=bank, lw_=lw_: h.matmul(
                            cv_ps(ps[bank], c0, [[st_ * 64, n_], [1, 64]]), lhsT=lw_, rhs=PT[w3][:, 0:nq], start=False, stop=True,
                            skip_group_check=True), reads=[s_PT[w3], s_V, s_cpad], writes=[pss[bank]])

                for idx in range(len(units) + LA):
                    if idx < len(units):
                        emit_front(idx)
                    if idx - LA >= 0:
                        emit_back(idx - LA)
                for grp in range(2):
                    u = grp
                    P.op("act", lambda h, u=u, grp=grp: h.activation(out=rinv[u], in_=ps[2 + grp][:, :], func=AF.Ln),
                         reads=[pss[2 + grp]], writes=[s_rinv[u]])
                    P.op("act", lambda h, u=u: h.activation(out=rinv[u], in_=rinv[u], func=AF.Exp, scale=-1.0),
                         reads=[s_rinv[u]], writes=[s_rinv[u]])
                    P.op("dve", lambda h, u=u, grp=grp, hg=hg, cc=cc, hf=hf: h.tensor_tensor(
                        out=attnT[:, hg * 2 + cc, hf * 1024 + grp * 512: hf * 1024 + (grp + 1) * 512], in0=ps[grp][:, :], in1=rinv[u],
                        op=ALU.mult), reads=[pss[grp], s_rinv[u]], writes=[s_attnT[hg * 2 + cc]])
    if DBG.get('dump') is not None:
        dbg = DBG['dump']
        s_dbg = P.slot("dbg")
        P.dma("pool", "dbg", lambda h: h.dma_start(out=dbg[:, 0:4096], in_=qT.rearrange("p a b -> p (a b)")), reads=[s_qT], writes=[s_dbg])
        P.dma("pool", "dbg", lambda h: h.dma_start(out=dbg[:, 4096:8192], in_=kT.rearrange("p a b -> p (a b)")), reads=[s_kT], writes=[s_dbg])
    P.barrier()
    Wo = hT[:, :, 0:1024]
    s_Wo = P.slot("n_Wo")
    for q in range(2):
        P.dma("pool", "nWo", lambda h, q=q: h.dma_start(out=Wo[:, q * 4:(q + 1) * 4, :], in_=wout[:, q * 4:(q + 1) * 4, :]),
              writes=[s_Wo])
    for t in range(16):
        j = t % 2
        r0 = tok0 + t * 128
        P.dma("sp", f"nxt{j}", lambda h, j=j, r0=r0: h.dma_start(out=xt[j], in_=xin[r0:r0 + 128, :]),
              reads=[xin_slot], writes=[s_xt[j]])
        for half in range(2):
            pd, sd = ps[half], pss[half]
            for k in range(8):
                P.op("pe", lambda h, k=k, t=t, half=half, pd=pd: h.matmul(
                    pd[:, :], lhsT=attnT[:, k, t * 128:(t + 1) * 128], rhs=Wo[:, k, half * 512:(half + 1) * 512],
                    start=(k == 0), stop=(k == 7)), reads=[s_attnT[k], s_Wo], writes=[sd])
            P.op("dve", lambda h, j=j, half=half, pd=pd: h.tensor_tensor(
                out=ost[j][:, half * 512:(half + 1) * 512], in0=pd[:, :], in1=xt[j][:, half * 512:(half + 1) * 512],
                op=ALU.add), reads=[sd, s_xt[j]], writes=[s_ost[j]])
        P.dma("sp", f"nost{j}", lambda h, j=j, r0=r0: h.dma_start(out=xout[r0:r0 + 128, :], in_=ost[j]),
              reads=[s_ost[j]], writes=[xout_slot])
    P.barrier()


def cv(ap, p0, npart, off, dims):
    n = ap.ap[0][0]
    return bass.AP(tensor=ap.tensor, offset=ap.offset + p0 * n + off, ap=[[n, npart]] + [list(d) for d in dims])


GELU_FUNC = [None]


def ab_stage(c, xin, xin_slot, xout, xout_slot, W, gain_row, ntok):
    P = c.P
    fa, ba = c.fa, c.ba
    fa.reset()
    ba.reset()
    NK = 257
    R1 = ba.take(1, 16448)[:, 0, :]
    uT = ba.take(4, SEQ)
    aT = ba.take(4, SEQ + 30)
    yaT = ba.take(4, SEQ)
    Ht = ba.take(32, 2, 128)
    Gt = ba.take(32, 2, 128)
    Mt = ba.take(32, 128)
    selin = ba.take(2, 8, 128)
    selout = ba.take(8, 8, 128)
    wsl = [ba.take(8, 128) for _ in range(2)]
    gluw = ba.take(4, 512)
    xn = [ba.take(1, 1024)[:, 0, :] for _ in range(2)]
    hT = R1[:, 0:16384].rearrange("p (a b) -> p a b", b=SEQ)
    cdiag = R1[:, 0:15872].rearrange("p (q k m) -> p q k m", k=31, m=128)
    XS = R1
    Wo = R1[:, 0:8192].rearrange("p (a b) -> p a b", b=1024)
    UY = aT[:, :, :].rearrange("p a b -> p (a b)")[:, 0:8192].rearrange("p (g k) -> p g k", k=256)
    Qt = R1[:, 0:8192].rearrange("p (r e) -> p r e", r=2)
    Pt = R1[:, 8192:16384].rearrange("p (r e) -> p r e", r=2)
    Hp = uT[:, :, :].rearrange("p a b -> p (a b)").rearrange("p (r e) -> p r e", r=2)
    gbc = fa.take(1, 1024)[:, 0, :]
    lam = fa.take(3, 32)
    Apw = fa.take(2, 32, 32)
    msk = fa.take(2, 128)
    dcol = fa.take(1, 32)[:, 0, :]
    cw = fa.take(4, 31)
    vec = fa.take(1, 20)[:, 0, :]
    ones32 = fa.take(1, 128)[:, 0, :]
    etab = fa.take(1, 32)[:, 0, :]
    TA = fa.take(1, 64)[:, 0, :]
    TB = fa.take(1, 64)[:, 0, :]
    Sf = fa.take(2, 128)
    st1 = fa.take(1, 64)[:, 0, :]
    st2 = fa.take(1, 64)[:, 0, :]
    sm = fa.take(12, 32)
    ss = [fa.take(1, 4)[:, 0, :] for _ in range(2)]
    tmpA = fa.take(1, 128)[:, 0, :]
    tmpB = fa.take(1, 128)[:, 0, :]
    Dreg = fa.take(1, 7168)[:, 0, :]
    Bp = Dreg[:, 0:1024]
    Cp = Dreg[:, 1024:2048]
    Bb = Dreg[:, 2048:3072]
    t1 = Dreg[:, 3072:4096]
    t2 = Dreg[:, 4096:5120]
    w5 = [Dreg[:, 5120:6144], Dreg[:, 6144:7168], Dreg[:, 2048:3072]]
    xt = [Dreg[:, 0:1024], Dreg[:, 1024:2048]]
    ost = [Dreg[:, 2048:3072], Dreg[:, 3072:4096]]
    sgm = [Dreg[:, 0:512], Dreg[:, 512:1024]]
    acv = Dreg[:, 0:2048].rearrange("p (q t) -> p q t", t=512)
    sqv = [Dreg[:, 2048:2560], Dreg[:, 2560:3072]]
    meanv = Dreg[:, 3072:3584]
    rstdv = Dreg[:, 3584:4096]
    tmpv = Dreg[:, 4096:4608]
    ynv = [Dreg[:, 4608:5120], Dreg[:, 5120:5632]]
    ps, pss = c.ps, c.ps_slots
    ident = c.ident

    S = lambda n: P.slot("ab_" + n)
    s_tab = S("tab")
    s_D = S("Dreg")
    s_R1 = S("R1")
    s_uT = S("uT")
    s_aT = S("aT")
    s_ya = S("yaT")
    s_H, s_G, s_M = S("H"), S("G"), S("M")
    s_sel = S("sel")
    s_wsl = [S("wsl0"), S("wsl1")]
    s_gluw = S("gluw")
    s_xn = [S("xn0"), S("xn1")]
    s_gbc = S("gbc")
    s_ss = [S("ss0"), S("ss1")]
    s_tmp = S("tmpAB")

    def dv(fn, r, w, force=True):
        P.op("dve", fn, reads=r, writes=w, force=force)

    def ac(fn, r, w, force=True):
        P.op("act", fn, reads=r, writes=w, force=force)

    P.dma("sp", "gbc", lambda h: h.dma_start(out=gbc, in_=gain_row.partition_broadcast(128)), writes=[s_gbc])
    for dst, src in ((lam, W["lam"]), (msk, W["mask"]), (dcol, W["dcol"]), (cw, W["cw"]), (vec, W["vec"]),
                     (ones32, W["ones"]), (etab, W["etab"]), (Bp, W["B"]), (Cp, W["C"])):
        P.dma("sp", "abtab", lambda h, dst=dst, src=src: h.dma_start(out=dst, in_=src), writes=[s_tab])
    P.dma("pool", "absel", lambda h: h.dma_start(out=selin, in_=W["selin"]), writes=[s_sel])
    for q in range(2):
        P.dma("pool", "absel", lambda h, q=q: h.dma_start(out=selout[:, q * 4:(q + 1) * 4], in_=W["selout"][:, q * 4:(q + 1) * 4]),
              writes=[s_sel])
    P.dma("pool", "abglu", lambda h: h.dma_start(out=gluw, in_=W["gluw"]), writes=[s_gluw])

    T = [s_tab]
    sm_ = lambda i: sm[:, i, :]
    m_dt, m_lrd, m_lid, m_pr, m_den, m_cre, m_cim, m_x, m_y = (sm_(i) for i in range(9))
    lr, li, ls = lam[:, 0, :], lam[:, 1, :], lam[:, 2, :]
    ac(lambda h: h.activation(out=m_dt, in_=ls, func=AF.Exp), T, T)
    dv(lambda h: h.tensor_tensor(out=m_lrd, in0=lr, in1=m_dt, op=ALU.mult), T, T)
    dv(lambda h: h.tensor_tensor(out=m_lid, in0=li, in1=m_dt, op=ALU.mult), T, T)
    bc_g = lambda a: cv(a, 0, 128, 0, [[0, 32], [1, 32]])
    bc_e = cv(etab, 0, 128, 0, [[1, 32], [0, 32]])
    v3 = lambda a: cv(a, 0, 128, 0, [[32, 32], [1, 32]])
    argm, ang, kf = w5
    ki = t1[:, 0:1024].bitcast(I32)
    dv(lambda h: h.tensor_tensor(out=v3(argm), in0=bc_g(m_lrd), in1=bc_e, op=ALU.mult), T, T)
    ac(lambda h: h.activation(out=argm, in_=argm, func=AF.Exp), T, T)
    dv(lambda h: h.tensor_tensor(out=v3(ang), in0=bc_g(m_lid), in1=bc_e, op=ALU.mult), T, T)
    trig = t2[:, 0:1024]
    for ri, shift in ((1, 0.0), (0, math.pi / 2)):
        src = ang
        if shift != 0.0:
            dv(lambda h: h.tensor_scalar(out=ang, in0=ang, scalar1=shift, scalar2=None, op0=ALU.add), T, T)
        dv(lambda h: h.tensor_scalar(out=kf, in0=ang, scalar1=1.0 / (2 * math.pi), scalar2=None, op0=ALU.mult), T, T)
        dv(lambda h: h.tensor_copy(out=ki, in_=kf), T, T)
        dv(lambda h: h.tensor_copy(out=kf, in_=ki), T, T)
        dv(lambda h: h.scalar_tensor_tensor(out=kf, in0=kf, scalar=-2 * math.pi, in1=ang, op0=ALU.mult, op1=ALU.add), T, T)
        ac(lambda h: h.activation(out=trig, in_=kf, func=AF.Sin), T, T)
        dv(lambda h, ri=ri: h.tensor_tensor(out=Apw[:, ri, :, :].rearrange("p a b -> p (a b)"), in0=argm, in1=trig, op=ALU.mult), T, T)
    for (p0_, j1) in ((0, 16), (64, 23)):
        hp = lambda a, p0_=p0_: cv(a, p0_, 64, 0, [[1, 32]])
        A1r = cv(Apw, p0_, 64, j1 * 32, [[1, 32]])
        A1i = cv(Apw, p0_, 64, 1024 + j1 * 32, [[1, 32]])
        lr_, li_ = hp(lr), hp(li)
        dv(lambda h, hp=hp, A1r=A1r: h.tensor_scalar(out=hp(m_pr), in0=A1r, scalar1=-1.0, scalar2=None, op0=ALU.add), T, T)
        dv(lambda h, hp=hp, lr_=lr_: h.tensor_tensor(out=hp(m_x), in0=lr_, in1=lr_, op=ALU.mult), T, T)
        dv(lambda h, hp=hp, li_=li_: h.tensor_tensor(out=hp(m_y), in0=li_, in1=li_, op=ALU.mult), T, T)
        dv(lambda h, hp=hp: h.tensor_tensor(out=hp(m_den), in0=hp(m_x), in1=hp(m_y), op=ALU.add), T, T)
        dv(lambda h, hp=hp: h.reciprocal(out=hp(m_den), in_=hp(m_den)), T, T)
        dv(lambda h, hp=hp, lr_=lr_: h.tensor_tensor(out=hp(m_x), in0=hp(m_pr), in1=lr_, op=ALU.mult), T, T)
        dv(lambda h, hp=hp, li_=li_, A1i=A1i: h.tensor_tensor(out=hp(m_y), in0=A1i, in1=li_, op=ALU.mult), T, T)
        dv(lambda h, hp=hp: h.tensor_tensor(out=hp(m_x), in0=hp(m_x), in1=hp(m_y), op=ALU.add), T, T)
        dv(lambda h, hp=hp: h.tensor_tensor(out=hp(m_cre), in0=hp(m_x), in1=hp(m_den), op=ALU.mult), T, T)
        dv(lambda h, hp=hp, lr_=lr_, A1i=A1i: h.tensor_tensor(out=hp(m_x), in0=A1i, in1=lr_, op=ALU.mult), T, T)
        dv(lambda h, hp=hp, li_=li_: h.tensor_tensor(out=hp(m_y), in0=hp(m_pr), in1=li_, op=ALU.mult), T, T)
        dv(lambda h, hp=hp: h.tensor_tensor(out=hp(m_x), in0=hp(m_x), in1=hp(m_y), op=ALU.subtract), T, T)
        dv(lambda h, hp=hp: h.tensor_tensor(out=hp(m_cim), in0=hp(m_x), in1=hp(m_den), op=ALU.mult), T, T)
    bcc = lambda a: cv(a, 0, 128, 0, [[1, 32], [0, 16]])
    g16 = lambda a, off: cv(a, 0, 128, off, [[16, 32], [1, 16]])
    for (o_off, a_, x_off, b_, y_off, op) in ((0, m_cre, 0, m_cim, 512, ALU.subtract), (512, m_cre, 512, m_cim, 0, ALU.add)):
        dv(lambda h, a_=a_, x_off=x_off: h.tensor_tensor(out=g16(t1, 0), in0=bcc(a_), in1=g16(Bp, x_off), op=ALU.mult), T, T)
        dv(lambda h, b_=b_, y_off=y_off: h.tensor_tensor(out=g16(t2, 0), in0=bcc(b_), in1=g16(Bp, y_off), op=ALU.mult), T, T)
        dv(lambda h, o_off=o_off, op=op: h.tensor_tensor(out=Bb[:, o_off:o_off + 512], in0=t1[:, 0:512], in1=t2[:, 0:512], op=op), T, T)

    def fam(dst, d_goff, d_rioff, V, negim, j0):
        np_, p0, js = 128, 0, 1
        for qq in range(4):
            Ar = cv(Apw, p0, np_, j0 * 32 + 8 * qq, [[1, 8], [js * 32, 8], [0, 16]])
            Ai = cv(Apw, p0, np_, 1024 + j0 * 32 + 8 * qq, [[1, 8], [js * 32, 8], [0, 16]])
            Vr = cv(V, p0, np_, 8 * qq * 16, [[16, 8], [0, 8], [1, 16]])
            Vi = cv(V, p0, np_, 512 + 8 * qq * 16, [[16, 8], [0, 8], [1, 16]])
            T1 = cv(t1, p0, np_, 0, [[128, 8], [16, 8], [1, 16]])
            T2 = cv(t2, p0, np_, 0, [[128, 8], [16, 8], [1, 16]])
            dre = cv(dst, p0, np_, 8 * qq * d_goff, [[d_goff, 8], [16, 8], [1, 16]])
            dim_ = cv(dst, p0, np_, 8 * qq * d_goff + d_rioff, [[d_goff, 8], [16, 8], [1, 16]])
            dv(lambda h, Ar=Ar, Vr=Vr, T1=T1: h.tensor_tensor(out=T1, in0=Ar, in1=Vr, op=ALU.mult), T, T)
            dv(lambda h, Ai=Ai, Vi=Vi, T2=T2: h.tensor_tensor(out=T2, in0=Ai, in1=Vi, op=ALU.mult), T, T)
            dv(lambda h, dre=dre, T1=T1, T2=T2: h.tensor_tensor(out=dre, in0=T1, in1=T2, op=ALU.subtract), T, T)
            dv(lambda h, Ar=Ar, Vi=Vi, T1=T1: h.tensor_tensor(out=T1, in0=Ar, in1=Vi, op=ALU.mult), T, T)
            dv(lambda h, Ai=Ai, Vr=Vr, T2=T2: h.tensor_tensor(out=T2, in0=Ai, in1=Vr, op=ALU.mult), T, T)
            if negim:
                dv(lambda h, T1=T1: h.tensor_scalar(out=T1, in0=T1, scalar1=-1.0, scalar2=None, op0=ALU.mult), T, T)
                dv(lambda h, dim_=dim_, T1=T1, T2=T2: h.tensor_tensor(out=dim_, in0=T1, in1=T2, op=ALU.subtract), T, T)
            else:
                dv(lambda h, dim_=dim_, T1=T1, T2=T2: h.tensor_tensor(out=dim_, in0=T1, in1=T2, op=ALU.add), T, T)

    fam(Qt, 128, 4096, Bb, False, 0)
    fam(Pt, 128, 4096, Cp, True, 8)
    fam(Gt, 256, 128, Cp, True, 16)
    fam(Hp, 128, 4096, Bb, False, 24)
    for (p0_, j8) in ((0, 23), (64, 16)):
        Dr = cv(Apw, p0_, 64, j8 * 32, [[1, 32]])
        Di = cv(Apw, p0_, 64, 1024 + j8 * 32, [[1, 32]])
        hq = lambda a, off, p0_=p0_: cv(a, p0_, 64, off, [[1, 32]])
        dv(lambda h, hq=hq, Dr=Dr: h.tensor_copy(out=hq(TA, 0), in_=Dr), T, T)
        dv(lambda h, hq=hq, Dr=Dr: h.tensor_copy(out=hq(TA, 32), in_=Dr), T, T)
        dv(lambda h, hq=hq, Di=Di: h.tensor_scalar(out=hq(TB, 0), in0=Di, scalar1=-1.0, scalar2=None, op0=ALU.mult), T, T)
        dv(lambda h, hq=hq, Di=Di: h.tensor_copy(out=hq(TB, 32), in_=Di), T, T)
    for g in range(32):
        u = g % 2
        pf, pb = ps[2 * u], ps[2 * u + 1]
        sf_, sb_ = pss[2 * u], pss[2 * u + 1]
        for (pp, sp_, p0) in ((pf, sf_, 0), (pb, sb_, 64)):
            for ri in range(2):
                P.op("pe", lambda h, pp=pp, p0=p0, ri=ri, g=g: h.matmul(
                    pp[:, 0:128], lhsT=Qt[p0:p0 + 64, ri, g * 128:(g + 1) * 128], rhs=Pt[p0:p0 + 64, ri, g * 128:(g + 1) * 128],
                    start=(ri == 0), stop=(ri == 1)), reads=T, writes=[sp_])
        dv(lambda h, pf=pf: h.tensor_tensor(out=tmpA, in0=pf[:, 0:128], in1=msk[:, 0, :], op=ALU.mult), [sf_] + T, [s_tmp])
        dv(lambda h, pb=pb: h.tensor_tensor(out=tmpB, in0=pb[:, 0:128], in1=msk[:, 1, :], op=ALU.mult), [sb_] + T, [s_tmp])
        dv(lambda h: h.tensor_tensor(out=tmpA, in0=tmpA, in1=tmpB, op=ALU.add), [s_tmp], [s_tmp])
        dv(lambda h, g=g: h.scalar_tensor_tensor(out=Mt[:, g, :], in0=ident[:], scalar=dcol[:, g:g + 1], in1=tmpA,
                                                  op0=ALU.mult, op1=ALU.add), [s_tmp, c.ident_slot] + T, [s_M])
    for b in range(8):
        for gg in range(4):
            for ri in range(2):
                g = b * 4 + gg
                sl = gg * 2 + ri
                P.op("pe", lambda h, g=g, ri=ri, sl=sl: h.transpose(
                    out=c.psb[:, sl * 128:(sl + 1) * 128], in_=Hp[:, ri, g * 128:(g + 1) * 128], identity=ident[:]),
                    reads=T + [c.ident_slot], writes=[c.psb_slot])
        ac(lambda h, b=b: h.copy(out=Ht[:, b * 4:(b + 1) * 4, :, :].rearrange("p a b c -> p (a b c)"), in_=c.psb[:, :]),
           [c.psb_slot], [s_H])
    P.barrier()

    for sq_ in range(ntok // SEQ):
        tok0 = sq_ * SEQ
        s_hT = [S(f"hT{t}") for t in range(16)]
        s_xt1 = [S("xt1_0"), S("xt1_1")]
        for t in range(16):
            j = t % 2
            r0 = tok0 + t * 128
            P.dma("sp", f"abxt{j}", lambda h, j=j, r0=r0: h.dma_start(out=xt[j], in_=xin[r0:r0 + 128, :]),
                  reads=[xin_slot], writes=[s_xt1[j]])
            norm_transpose(c, xt[j], s_xt1[j], gbc, s_gbc, xn[j], s_xn[j], ss[j], s_ss[j], hT, s_hT[t], t * 128)
        P.barrier()
        s_sg = [S("sg0"), S("sg1")]
        s_aTq = [S(f"aT{q}") for q in range(4)]
        s_uTq = [S(f"uT{q}") for q in range(4)]
        for q in range(4):
            dv(lambda h, q=q: h.memset(aT[:, q, 0:15], 0.0), [], [s_aTq[q]], force=False)
            dv(lambda h, q=q: h.memset(aT[:, q, SEQ + 15:SEQ + 30], 0.0), [], [s_aTq[q]], force=False)
        wi = 0
        ui = 0

        def load_slab(oc):
            nonlocal wi
            j = wi % 2
            wi += 1
            P.dma("pool", f"abw{j}", lambda h, j=j, oc=oc: h.dma_start(out=wsl[j], in_=W["win"][oc]), writes=[s_wsl[j]])
            return j

        for q in range(4):
            ja = load_slab(q)
            jg = load_slab(q + 4)
            for tb in range(4):
                u = ui % 2
                ui += 1
                pa, pg = ps[2 * u], ps[2 * u + 1]
                sa, sg_ = pss[2 * u], pss[2 * u + 1]
                for (pp, sp_, jj) in ((pa, sa, ja), (pg, sg_, jg)):
                    for k in range(8):
                        P.op("pe", lambda h, pp=pp, jj=jj, k=k, tb=tb: h.matmul(
                            pp[:, :], lhsT=wsl[jj][:, k, :], rhs=hT[:, k, tb * 512:(tb + 1) * 512],
                            start=(k == 0), stop=(k == 7)), reads=[s_wsl[jj]] + s_hT[tb * 4:(tb + 1) * 4], writes=[sp_])
                ac(lambda h, u=u, pg=pg: h.activation(out=sgm[u], in_=pg[:, :], func=AF.Sigmoid), [sg_], [s_sg[u]], force=False)
                dv(lambda h, u=u, pa=pa, q=q, tb=tb: h.tensor_tensor(
                    out=aT[:, q, 15 + tb * 512:15 + (tb + 1) * 512], in0=sgm[u], in1=pa[:, :], op=ALU.mult),
                    [s_sg[u], sa], [s_aTq[q]], force=False)
        for q in range(4):
            ju = load_slab(q + 8)
            for tb in range(4):
                u = ui % 2
                ui += 1
                pu_, su_ = ps[4 + u], pss[4 + u]
                for k in range(8):
                    P.op("pe", lambda h, pu_=pu_, ju=ju, k=k, tb=tb: h.matmul(
                        pu_[:, :], lhsT=wsl[ju][:, k, :], rhs=hT[:, k, tb * 512:(tb + 1) * 512],
                        start=(k == 0), stop=(k == 7)), reads=[s_wsl[ju]] + s_hT[tb * 4:(tb + 1) * 4], writes=[su_])
                ac(lambda h, pu_=pu_, q=q, tb=tb: h.copy(
                    out=cv(uT, 0, 128, q * SEQ + tb * 64, [[256, 8], [1, 64]]),
                    in_=pu_[:, :].rearrange("p (k t) -> p t k", t=8)), [su_], [s_uTq[q]], force=False)
        P.barrier()
        s_cd = [S(f"cd{q}") for q in range(4)]
        for q in range(4):
            for k in range(31):
                dv(lambda h, q=q, k=k: h.tensor_scalar(out=cdiag[:, q, k, :], in0=ident[:], scalar1=cw[:, q, k:k + 1],
                                                        scalar2=None, op0=ALU.mult), [c.ident_slot], [s_cd[q]], force=False)
        s_ac = [S(f"ac{q}") for q in range(4)]
        s_sq = [S("sq0"), S("sq1")]
        s_mean, s_rstd, s_tv = S("mean"), S("rstd"), S("tmpv")
        s_yn = [S("yn0"), S("yn1")]
        for tb in range(4):
            for q in range(4):
                pc, spc = ps[q % 2], pss[q % 2]
                for k in range(31):
                    P.op("pe", lambda h, pc=pc, q=q, k=k, tb=tb: h.matmul(
                        pc[:, :], lhsT=cdiag[:, q, k, :], rhs=aT[:, q, tb * 512 + k: tb * 512 + k + 512],
                        start=(k == 0), stop=(k == 30)), reads=[s_cd[q]], writes=[spc])
                ac(lambda h, pc=pc, q=q: h.activation(out=acv[:, q, :], in_=pc[:, :], func=AF.Identity,
                                                      bias=vec[:, q:q + 1], scale=1.0), [spc], [s_ac[q]], force=False)
            pm, spm = ps[2], pss[2]
            pe2, spe2 = ps[3], pss[3]
            for q in range(4):
                P.op("pe", lambda h, q=q: h.matmul(pm[:, :], lhsT=cv(ones32, 0, 128, 0, [[1, 128]]), rhs=acv[:, q, :],
                                                   start=(q == 0), stop=(q == 3)), reads=[s_ac[q]], writes=[spm])
            for q in range(4):
                u = q % 2
                ac(lambda h, q=q, u=u: h.activation(out=sqv[u], in_=acv[:, q, :], func=AF.Square), [s_ac[q]], [s_sq[u]], force=False)
                P.op("pe", lambda h, q=q, u=u: h.matmul(pe2[:, :], lhsT=cv(ones32, 0, 128, 0, [[1, 128]]), rhs=sqv[u],
                                                        start=(q == 0), stop=(q == 3)), reads=[s_sq[u]], writes=[spe2])
            dv(lambda h: h.tensor_copy(out=meanv, in_=pm[:, :]), [spm], [s_mean], force=False)
            dv(lambda h: h.tensor_tensor(out=tmpv, in0=meanv, in1=meanv, op=ALU.mult), [s_mean], [s_tv])
            dv(lambda h: h.tensor_tensor(out=tmpv, in0=pe2[:, :], in1=tmpv, op=ALU.subtract), [spe2, s_tv], [s_tv])
            ac(lambda h: h.activation(out=rstdv, in_=tmpv, func=AF.Ln, bias=c.cst[:, 2:3], scale=1.0), [s_tv, c.cst_slot], [s_rstd])
            ac(lambda h: h.activation(out=rstdv, in_=rstdv, func=AF.Exp, scale=-0.5), [s_rstd], [s_rstd])
            for q in range(4):
                u = q % 2
                dv(lambda h, q=q, u=u: h.tensor_tensor(out=ynv[u], in0=acv[:, q, :], in1=meanv, op=ALU.subtract),
                   [s_ac[q], s_mean], [s_yn[u]], force=False)
                dv(lambda h, u=u: h.tensor_tensor(out=ynv[u], in0=ynv[u], in1=rstdv, op=ALU.mult), [s_yn[u], s_rstd], [s_yn[u]])
                ac(lambda h, q=q, u=u, tb=tb: h.activation(out=yaT[:, q, tb * 512:(tb + 1) * 512], in_=ynv[u], func=AF.Silu,
                                                           scale=vec[:, 4 + q:5 + q], bias=vec[:, 8 + q:9 + q]),
                   [s_yn[u]], [s_ya], force=False)
        P.barrier()
        s_UY = [S(f"UY{g}") for g in range(32)]
        s_X = [S("Xf"), S("Xb")]
        s_hist = [S("histf"), S("histb")]
        XSv = lambda p0, ri, g, c0, n: cv(XS, p0, 64, (ri * 32 + g) * NK + c0, [[1, n]])
        dv(lambda h: h.memset(cv(XS, 0, 64, 0, [[NK, 64]]), 0.0), [], [s_hist[0]], force=False)
        dv(lambda h: h.memset(cv(XS, 64, 64, 255, [[NK, 64]]), 0.0), [], [s_hist[1]], force=False)
        for g in range(32):
            q, j, par = g // 8, (g % 8) // 2, g % 2
            u = g % 2
            pu_, su_ = ps[u], pss[u]
            for tp in range(8):
                rhs = cv(uT, 32 * j, 32, q * SEQ + tp * 256, [[1, 256]])
                P.op("pe", lambda h, pu_=pu_, j=j, par=par, tp=tp, rhs=rhs: h.matmul(
                    pu_[:, 0:256], lhsT=selin[32 * j:32 * j + 32, par, tp, :], rhs=rhs, start=(tp == 0), stop=(tp == 7),
                    tile_position=(32 * j, 0)), reads=[s_uTq[q], s_sel], writes=[su_])
            ac(lambda h, pu_=pu_, g=g: h.copy(out=UY[:, g, :], in_=pu_[:, 0:256]), [su_], [s_UY[g]], force=False)
            for ri in range(2):
                px, spx = ps[2 + ri], pss[2 + ri]
                P.op("pe", lambda h, px=px, g=g, ri=ri: h.matmul(px[:, 0:256], lhsT=Ht[:, g, ri, :], rhs=UY[:, g, :],
                                                               start=True, stop=True), reads=[s_H, s_UY[g]], writes=[spx])
                dv(lambda h, px=px, g=g, ri=ri: h.tensor_copy(out=XSv(0, ri, g, 1, 256), in_=px[0:64, 0:256]),
                   [spx], [s_X[0]], force=False)
                dv(lambda h, px=px, g=g, ri=ri: h.tensor_copy(out=XSv(64, ri, g, 0, 255), in_=px[64:128, 1:256]),
                   [spx], [s_X[1]], force=False)
        dirs = []
        for (eng, p0, d, cols) in ((DBG.get("scan_f", "dve"), 0, 0, list(range(1, 256))),
                                   (DBG.get("scan_b", "dve"), 64, 1, list(range(254, -1, -1)))):
            s_S = [S(f"S{d}_0"), S(f"S{d}_1")]
            s_w_, s_v_ = S(f"w_{d}"), S(f"v_{d}")
            P.op(eng, lambda h, p0=p0: h.memset(cv(Sf, p0, 64, 0, [[1, 128]]), 0.0), writes=[s_S[0]])
            dirs.append((eng, p0, d, cols, s_S, s_w_, s_v_))
        for i in range(255):
            for (eng, p0, d, cols, s_S, s_w_, s_v_) in dirs:
                col = cols[i]
                pv_, cu = i % 2, (i + 1) % 2
                xcol = cv(XS, p0, 64, col, [[32 * NK, 2], [NK, 32]])
                xcol2 = cv(XS, p0, 64, col, [[0, 2], [32 * NK, 2], [NK, 32]])
                P.op(eng, lambda h, p0=p0, pv_=pv_: h.tensor_tensor(
                    out=cv(st1, p0, 64, 0, [[64, 2], [32, 2], [1, 32]]), in0=cv(TA, p0, 64, 0, [[64, 2], [32, 2], [1, 32]]),
                    in1=cv(Sf, p0, 64, pv_ * 128, [[32, 2], [32, 2], [1, 32]]), op=ALU.mult),
                    reads=[s_S[pv_]], writes=[s_w_], force=DBG.get('scanforce', False))
                P.op(eng, lambda h, p0=p0: h.tensor_tensor(
                    out=cv(tmpA, p0, 64, 0, [[1, 64]]), in0=cv(st1, p0, 64, 0, [[1, 64]]), in1=cv(st1, p0, 64, 64, [[1, 64]]),
                    op=ALU.add), reads=[s_w_], writes=[s_v_], force=DBG.get('scanforce', False))
                P.op(eng, lambda h, p0=p0, cu=cu, xcol2=xcol2: h.tensor_tensor(
                    out=cv(Sf, p0, 64, cu * 128, [[64, 2], [32, 2], [1, 32]]), in0=cv(tmpA, p0, 64, 0, [[0, 2], [32, 2], [1, 32]]),
                    in1=xcol2, op=ALU.add), reads=[s_v_, s_X[d]], writes=[s_S[cu]], force=DBG.get('scanforce', False))
                P.op("act", lambda h, p0=p0, cu=cu, xcol=xcol: h.copy(out=xcol, in_=cv(Sf, p0, 64, cu * 128, [[32, 2], [1, 32]])),
                     reads=[s_S[cu]], writes=[s_hist[d]])
        for g in range(32):
            u = g % 2
            py, spy = ps[4 + u], pss[4 + u]
            P.op("pe", lambda h, py=py, g=g: h.matmul(py[:, 0:256], lhsT=Mt[:, g, :], rhs=UY[:, g, :], start=True, stop=False),
                 reads=[s_M, s_UY[g]], writes=[spy])
            for ri in range(2):
                P.op("pe", lambda h, py=py, g=g, ri=ri: h.matmul(
                    py[:, 0:256], lhsT=Gt[:, g, ri, :], rhs=cv(XS, 0, 128, (ri * 32 + g) * NK, [[1, 256]]),
                    start=False, stop=(ri == 1)), reads=[s_G, s_hist[0], s_hist[1], s_X[0], s_X[1]], writes=[spy])
            ac(lambda h, py=py, g=g: h.copy(out=UY[:, g, :], in_=py[:, 0:256]), [spy], [s_UY[g]], force=False)
        s_yg = [S(f"yg{q}") for q in range(4)]
        ci = 0
        for q in range(4):
            for tau in range(8):
                u = ci % 2
                ci += 1
                pz, spz = ps[u], pss[u]
                for g8 in range(8):
                    P.op("pe", lambda h, pz=pz, tau=tau, g8=g8, q=q: h.matmul(
                        pz[:, 0:256], lhsT=selout[:, tau, g8, :], rhs=UY[:, 8 * q + g8, :], start=(g8 == 0), stop=(g8 == 7)),
                        reads=[s_sel, s_UY[8 * q + g8]], writes=[spz])
                gelu_evac(c, pz, spz, cv(uT, 0, 128, q * SEQ + tau, [[8, 256]]), s_yg[q], s_uTq[q])
        s_sg2 = [S("sg2_0"), S("sg2_1")]
        for tb in range(4):
            for o in range(4):
                pz, spz = ps[2 + o], pss[2 + o]
                for kq in range(4):
                    P.op("pe", lambda h, pz=pz, kq=kq, o=o, tb=tb: h.matmul(
                        pz[:, :], lhsT=gluw[:, kq, o * 128:(o + 1) * 128], rhs=uT[:, kq, tb * 512:(tb + 1) * 512],
                        start=(kq == 0), stop=(kq == 3)), reads=[s_gluw] + s_yg, writes=[spz])
            for o in range(4):
                u = o % 2
                pz, spz = ps[2 + o], pss[2 + o]
                ac(lambda h, pz=pz, o=o, u=u: h.activation(out=sgm[u], in_=pz[:, :], func=AF.Sigmoid,
                                                           bias=vec[:, 12 + o:13 + o], scale=1.0), [spz], [s_sg2[u]], force=False)
                dv(lambda h, o=o, u=u, tb=tb: h.tensor_tensor(out=uT[:, o, tb * 512:(tb + 1) * 512],
                                                              in0=uT[:, o, tb * 512:(tb + 1) * 512], in1=sgm[u], op=ALU.mult),
                   [s_sg2[u], s_yg[o]], [s_yg[o]], force=False)
        P.barrier()
        s_Wo = S("Wo")
        for q in range(2):
            P.dma("pool", "abWo", lambda h, q=q: h.dma_start(out=Wo[:, q * 4:(q + 1) * 4, :], in_=W["wout"][:, q * 4:(q + 1) * 4, :]),
                  writes=[s_Wo])
        s_xt2 = [S("xt2_0"), S("xt2_1")]
        s_ost = [S("ost0"), S("ost1")]
        for t in range(16):
            j = t % 2
            r0 = tok0 + t * 128
            P.dma("sp", f"abxt{j}", lambda h, j=j, r0=r0: h.dma_start(out=xt[j], in_=xin[r0:r0 + 128, :]),
                  reads=[xin_slot], writes=[s_xt2[j]])
            for half in range(2):
                pd, sd = ps[half], pss[half]
                for k in range(8):
                    src = yaT if k < 4 else uT
                    P.op("pe", lambda h, k=k, t=t, half=half, pd=pd, src=src: h.matmul(
                        pd[:, :], lhsT=src[:, k % 4, t * 128:(t + 1) * 128], rhs=Wo[:, k, half * 512:(half + 1) * 512],
                        start=(k == 0), stop=(k == 7)), reads=[s_Wo], writes=[sd])
                dv(lambda h, j=j, half=half, pd=pd: h.tensor_tensor(
                    out=ost[j][:, half * 512:(half + 1) * 512], in0=pd[:, :], in1=xt[j][:, half * 512:(half + 1) * 512],
                    op=ALU.add), [sd, s_xt2[j]], [s_ost[j]], force=False)
            P.dma("sp", f"abost{j}", lambda h, j=j, r0=r0: h.dma_start(out=xout[r0:r0 + 128, :], in_=ost[j]),
                  reads=[s_ost[j]], writes=[xout_slot])
        P.barrier()


def gelu_evac(c, pz, spz, dst, s_dst, s_dst2):
    P = c.P
    P.op("act", lambda h: h.activation(out=dst, in_=pz[:, 0:256], func=AF.Gelu_apprx_tanh), reads=[spz], writes=[s_dst, s_dst2])

def build_program(plan=None, ntok=TOK):
    if plan is None:
        plan = []
        for l in range(DEPTH):
            plan.append(("ab" if l % 2 == 0 else "na", l))
            plan.append(("ffn", l))
    nc = bass.Bass("TRN2", target_bir_lowering=False)
    c = setup_ctx(nc)
    P = c.P
    x = nc.dram_tensor("x", [ntok, D], F32, kind="ExternalInput").ap()
    out = nc.dram_tensor("out", [ntok, D], F32, kind="ExternalOutput").ap()
    ident = nc.dram_tensor("ident", [2, 128, 128], F32, kind="ExternalInput").ap()
    wg = nc.dram_tensor("ffn_wg", [DEPTH, NM, 128, 8, 128], F32, kind="ExternalInput").ap()
    wu = nc.dram_tensor("ffn_wu", [DEPTH, NM, 128, 8, 128], F32, kind="ExternalInput").ap()
    wd = nc.dram_tensor("ffn_wd", [DEPTH, FF, D], F32, kind="ExternalInput").ap()
    fnorm = nc.dram_tensor("ffn_norm", [DEPTH, D], F32, kind="ExternalInput").ap()
    mnorm = nc.dram_tensor("mix_norm", [DEPTH, D], F32, kind="ExternalInput").ap()
    na_wqkv = nc.dram_tensor("na_wqkv", [2, 4, 128, 8, 768], F32, kind="ExternalInput").ap()
    na_wout = nc.dram_tensor("na_wout", [2, 128, 8, 1024], F32, kind="ExternalInput").ap()
    na_bias = nc.dram_tensor("na_bias", [2, 4, 128, 4, 14, 64], F32, kind="ExternalInput").ap()
    na_qkg = nc.dram_tensor("na_qkg", [2, 128, 2], F32, kind="ExternalInput").ap()
    abd = {}
    for nm, shp in AB_SHAPES.items():
        abd[nm] = nc.dram_tensor("ab_" + nm, list(shp), F32, kind="ExternalInput").ap()
    scr = [nc.dram_tensor(f"scr{i}", [ntok, D], F32, kind="Internal").ap() for i in range(2)]
    if DBG.get('dump_on'):
        DBG['dump'] = nc.dram_tensor("dbg", [128, 16384], F32, kind="ExternalOutput").ap()
    s_scr = [P.slot("scr0"), P.slot("scr1")]
    s_x = P.slot("x_dram")
    s_out = P.slot("out_dram")
    load_ident(c, ident)
    cur, s_cur = x, s_x
    for si, st in enumerate(plan):
        if si == len(plan) - 1:
            dst, s_dst = out, s_out
        else:
            dst, s_dst = scr[si % 2], s_scr[si % 2]
        kind, l = st
        if kind == "ffn":
            ffn_stage(c, cur, s_cur, dst, s_dst, wg[l], wu[l], wd[l], fnorm[l:l + 1, :], ntok=ntok)
        elif kind == "na":
            i = l // 2
            for sq_ in range(ntok // SEQ):
                na_stage(c, cur, s_cur, dst, s_dst, na_wqkv[i], na_wout[i], na_bias[i], na_qkg[i],
                         mnorm[l:l + 1, :], sq_ * SEQ)
        elif kind == "ab":
            i = l // 2
            Wd_ = {k: (v[i] if k in AB_PER_LAYER else v) for k, v in abd.items()}
            ab_stage(c, cur, s_cur, dst, s_dst, Wd_, mnorm[l:l + 1, :], ntok)
        cur, s_cur = dst, s_dst
    fin = Ins("sp", None, False, None, 0)
    for e in ENGS:
        for i in P.streams[e]:
            if i.is_dma:
                fin.deps.append(i)
    fin.idx = len(P.streams["sp"])
    P.streams["sp"].append(fin)
    P.emit()
    return nc


AB_PER_LAYER = ("win", "wout", "gluw", "vec", "cw", "lam", "B", "C", "dcol")
AB_SHAPES = {
    "win": (2, 12, 128, 8, 128), "wout": (2, 128, 8, 1024), "gluw": (2, 128, 4, 512), "vec": (2, 128, 20),
    "cw": (2, 128, 4, 31), "lam": (2, 128, 3, 32), "B": (2, 128, 2, 32, 16), "C": (2, 128, 2, 32, 16),
    "dcol": (2, 128, 32), "selin": (128, 2, 8, 128), "selout": (128, 8, 8, 128), "mask": (128, 2, 128),
    "etab": (128, 32), "ones": (128, 128),
}


def ab_host_layout(inputs):
    g = np.ascontiguousarray
    f = lambda k: np.asarray(inputs[k], dtype=np.float32)
    d = {}
    d["win"] = g(f("ab_w_in").reshape(2, 8, 128, 12, 128).transpose(0, 3, 2, 1, 4))
    d["wout"] = g(f("ab_w_out").reshape(2, 8, 128, 1024).transpose(0, 2, 1, 3))
    d["gluw"] = g(f("ssm_glu_w").reshape(2, 4, 128, 512).transpose(0, 2, 1, 3))
    vec = np.zeros((2, 128, 20), np.float32)
    for j, k in enumerate(("conv_b", "conv_ln_g", "conv_ln_b", "ssm_glu_b")):
        vec[:, :, 4 * j:4 * j + 4] = f(k).reshape(2, 4, 128).transpose(0, 2, 1)
    d["vec"] = vec
    d["cw"] = g(f("conv_w").reshape(2, 31, 4, 128).transpose(0, 3, 2, 1))
    lam = np.empty((2, 2, 64, 3, 32), np.float32)
    lam[:, :, :, 0] = f("ssm_lambda_re").transpose(0, 1, 3, 2)
    lam[:, :, :, 1] = f("ssm_lambda_im").transpose(0, 1, 3, 2)
    lam[:, :, :, 2] = f("ssm_log_step")[:, :, None, :]
    d["lam"] = g(lam.reshape(2, 128, 3, 32))
    B = np.stack([f("ssm_b_re"), f("ssm_b_im")], axis=1)
    d["B"] = g(B.transpose(0, 2, 4, 1, 3, 5).reshape(2, 128, 2, 32, 16))
    C = np.stack([f("ssm_c_re"), f("ssm_c_im")], axis=1)
    d["C"] = g(C.transpose(0, 2, 5, 1, 3, 4).reshape(2, 128, 2, 32, 16))
    dsk = f("ssm_d").reshape(2, 32, 16)
    d["dcol"] = g(np.broadcast_to(dsk.transpose(0, 2, 1)[:, None], (2, 8, 16, 32)).reshape(2, 128, 32))
    selin = np.zeros((4, 2, 16, 2, 8, 8, 16), np.float32)
    selout = np.zeros((8, 16, 8, 8, 8, 16), np.float32)
    for cc in range(16):
        for t in range(8):
            selin[:, 0, cc, 0, t, t, cc] = 1.0
            selin[:, 1, cc, 1, t, t, cc] = 1.0
            for g8 in range(8):
                selout[t, cc, t, g8, g8, cc] = 1.0
    d["selin"] = selin.reshape(128, 2, 8, 128)
    d["selout"] = selout.reshape(128, 8, 8, 128)
    tp = np.repeat(np.arange(8), 16)
    d["mask"] = g(np.stack([(tp[:, None] <= tp[None, :]), (tp[:, None] >= tp[None, :])], axis=1).astype(np.float32))
    tt = np.arange(8, dtype=np.float32)
    ef = np.concatenate([-tt, tt, tt + 1, 7 - tt])
    eb = np.concatenate([tt, -tt, 8 - tt, tt])
    d["etab"] = g(np.concatenate([np.broadcast_to(ef, (64, 32)), np.broadcast_to(eb, (64, 32))], axis=0))
    d["ones"] = np.full((128, 128), 1.0 / 512, np.float32)
    return {"ab_" + k: v for k, v in d.items()}


def host_layout(inputs):
    g = np.ascontiguousarray
    d = {}
    bd = np.zeros((128, 128), np.float32)
    bd[:64, :64] = 1.0
    bd[64:, 64:] = 1.0
    d["ident"] = np.stack([np.eye(128, dtype=np.float32), bd])
    for nm, key in (("ffn_wg", "ffn_w_gate"), ("ffn_wu", "ffn_w_up")):
        w = np.asarray(inputs[key], dtype=np.float32).reshape(DEPTH, 8, 128, NM, 128)
        d[nm] = g(w.transpose(0, 3, 2, 1, 4))
    d["ffn_wd"] = g(np.asarray(inputs["ffn_w_down"], dtype=np.float32))
    d["ffn_norm"] = g(np.asarray(inputs["ffn_norm"], dtype=np.float32))
    d["mix_norm"] = g(np.asarray(inputs["mix_norm"], dtype=np.float32))
    wqkv = np.asarray(inputs["na_w_qkv"], dtype=np.float32).reshape(2, 8, 128, 3, 4, 256)
    d["na_wqkv"] = g(wqkv.transpose(0, 4, 2, 1, 3, 5).reshape(2, 4, 128, 8, 768))
    d["na_wout"] = g(np.asarray(inputs["na_w_out"], dtype=np.float32).reshape(2, 8, 128, 1024).transpose(0, 2, 1, 3))
    rpb = np.asarray(inputs["na_rpb"], dtype=np.float32)
    qc = np.arange(64)[None, :]
    kc = np.arange(64)[:, None]
    cidx = np.clip(kc - qc, -15, 15) + 15
    cstart = np.clip(qc - 8, 0, 48)
    cmask = (kc >= cstart) & (kc < cstart + 16)
    tab = np.empty((2, 16, 14, 2, 64, 64), np.float32)
    for rho in range(14):
        for jj in range(2):
            tab[:, :, rho, jj] = np.where(cmask[None, None], rpb[:, :, rho + jj][:, :, cidx], np.float32(-30000.0))
    tab = tab.reshape(2, 4, 4, 14, 2, 64, 64).transpose(0, 1, 4, 5, 2, 3, 6).reshape(2, 4, 128, 4, 14, 64)
    d["na_bias"] = g(tab)
    qg = np.asarray(inputs["na_q_norm"], dtype=np.float32)
    kg = np.asarray(inputs["na_k_norm"], dtype=np.float32)
    d["na_qkg"] = g(np.stack([np.tile(qg, (1, 2)), np.tile(kg, (1, 2))], axis=-1))
    d.update(ab_host_layout(inputs))
    return d


def kernel(**inputs):
    x = np.asarray(inputs["x"], dtype=np.float32)
    shared = host_layout(inputs)
    nc = build_program()
    in_maps = []
    for i in range(NCORES):
        m = dict(shared)
        m["x"] = np.ascontiguousarray(x[2 * i:2 * i + 2].reshape(TOK, D))
        in_maps.append(m)
    res = run_bass_kernel_spmd(nc, in_maps, core_ids=list(range(NCORES)))
    outs = [res.results[i]["out"].reshape(2, SEQ, D) for i in range(NCORES)]
    return np.concatenate(outs, axis=0).astype(np.float32)
```

```python
import math
import numpy as np
import concourse.bass as bass
import concourse.mybir as mybir
from concourse.bass_utils import run_bass_kernel_spmd

F32 = mybir.dt.float32
BF16 = mybir.dt.bfloat16
I32 = mybir.dt.int32
AF = mybir.ActivationFunctionType
ALU = mybir.AluOpType
AX = mybir.AxisListType

DBG = {}
NCORES = 8
D = 1024
SEQ = 2048
TOK = 2 * SEQ
FF = 2816
NM = FF // 128
DEPTH = 4
EPS = 1e-6


class Slot:
    __slots__ = ("name", "lw", "rs")

    def __init__(self, name):
        self.name = name
        self.lw = None
        self.rs = []


class Ins:
    __slots__ = ("eng", "fn", "deps", "is_dma", "sem", "semval", "inc", "idx", "force")

    def __init__(self, eng, fn, is_dma, sem, semval):
        self.eng = eng
        self.fn = fn
        self.deps = []
        self.is_dma = is_dma
        self.sem = sem
        self.semval = semval
        self.inc = False
        self.idx = -1
        self.force = False


ENGS = ("pe", "act", "dve", "pool", "sp")


class Prog:
    def __init__(self, nc):
        self.nc = nc
        self.streams = {e: [] for e in ENGS}
        self.dma_sems = {}
        self.dma_cnt = {}
        self.slots = []

    def slot(self, name):
        s = Slot(name)
        self.slots.append(s)
        return s

    def slots_n(self, name, n):
        return [self.slot(f"{name}{i}") for i in range(n)]

    def _add(self, ins, reads, writes):
        e = ins.eng
        best = {}
        dl = []

        def need(p):
            if p is None or p is ins:
                return
            if p.is_dma:
                if p not in dl:
                    dl.append(p)
            elif ins.is_dma or p.eng != e or ins.force:
                q = best.get(p.eng)
                if q is None or q.idx < p.idx:
                    best[p.eng] = p

        for s in reads:
            need(s.lw)
        for s in writes:
            need(s.lw)
            for r in s.rs:
                need(r)
        ins.deps = dl + list(best.values())
        for s in reads:
            rs = s.rs
            if rs and (not ins.is_dma) and (not rs[-1].is_dma) and rs[-1].eng == e:
                rs[-1] = ins
            else:
                rs.append(ins)
        for s in writes:
            s.lw = ins
            s.rs = []
        ins.idx = len(self.streams[e])
        self.streams[e].append(ins)
        return ins

    def op(self, eng, fn, reads=(), writes=(), force=False):
        ins = Ins(eng, fn, False, None, 0)
        ins.force = force
        return self._add(ins, reads, writes)

    def dma(self, eng, semname, fn, reads=(), writes=()):
        if semname not in self.dma_sems:
            self.dma_sems[semname] = self.nc.alloc_semaphore(name="d_" + semname)
            self.dma_cnt[semname] = 0
        self.dma_cnt[semname] += 16
        return self._add(Ins(eng, fn, True, self.dma_sems[semname], self.dma_cnt[semname]), reads, writes)

    def barrier(self):
        lasts = []
        for e in ENGS:
            st = self.streams[e]
            if st:
                lasts.append(st[-1])
        dmas = [i for e in ENGS for i in self.streams[e] if i.is_dma and not getattr(i, "_barr", False)]
        for e in ENGS:
            ins = Ins(e, None, False, None, 0)
            for p in lasts:
                if p.eng != e and not p.is_dma and p.fn is not None:
                    ins.deps.append(p)
            for p in dmas:
                ins.deps.append(p)
            ins.idx = len(self.streams[e])
            self.streams[e].append(ins)
        for s in self.slots:
            s.lw = None
            s.rs = []

    def emit(self):
        nc = self.nc
        for e in ENGS:
            for ins in self.streams[e]:
                for p in ins.deps:
                    if not p.is_dma:
                        p.inc = True
        rank = {}
        sems = {}
        for e in ENGS:
            sems[e] = nc.alloc_semaphore(name="s_" + e)
            c = 0
            last_real = None
            for ins in self.streams[e]:
                if ins.fn is not None and not ins.is_dma:
                    last_real = ins
                if ins.inc:
                    assert ins.fn is not None and not ins.is_dma
                    c += 1
                    rank[ins] = c
        handles = {"pe": nc.tensor, "act": nc.scalar, "dve": nc.vector, "pool": nc.gpsimd, "sp": nc.sync}
        streams = self.streams

        def replay(e, h):
            known = {}
            for ins in streams[e]:
                for p in ins.deps:
                    if p.is_dma:
                        key, val, sem = ("d", id(p.sem)), p.semval, p.sem
                    else:
                        key, val, sem = ("c", p.eng), rank[p], sems[p.eng]
                    if known.get(key, 0) >= val:
                        continue
                    known[key] = val
                    h.wait_ge(sem, val)
                if ins.fn is None:
                    continue
                r = ins.fn(h)
                if ins.is_dma:
                    r.then_inc(ins.sem, 16)
                elif ins.inc:
                    r.then_inc(sems[e], 1)

        with nc.Block() as block:
            @block.tensor
            def _(h):
                replay("pe", h)

            @block.scalar
            def _(h):
                replay("act", h)

            @block.vector
            def _(h):
                replay("dve", h)

            @block.gpsimd
            def _(h):
                replay("pool", h)

            @block.sync
            def _(h):
                replay("sp", h)


class Arena:
    def __init__(self, nc, name, nelem, dtype):
        self.t = nc.alloc_sbuf_tensor(name, [128, nelem], dtype)
        self.n = nelem
        self.off = 0

    def reset(self):
        self.off = 0

    def take(self, *shape):
        n = int(np.prod(shape))
        assert self.off + n <= self.n, (self.off, n, self.n)
        ap = self.t[:, self.off:self.off + n]
        self.off += n
        if len(shape) == 2:
            ap = ap.rearrange("p (a b) -> p a b", b=shape[1])
        elif len(shape) == 3:
            ap = ap.rearrange("p (a b c) -> p a b c", b=shape[1], c=shape[2])
        elif len(shape) == 4:
            ap = ap.rearrange("p (a b c d) -> p a b c d", b=shape[1], c=shape[2], d=shape[3])
        return ap


class Ctx:
    pass


def setup_ctx(nc):
    c = Ctx()
    c.nc = nc
    c.P = Prog(nc)
    c.fa = Arena(nc, "fa", 12800, F32)
    c.ba = Arena(nc, "ba", 78848, BF16)
    c.ps = [nc.alloc_psum_tensor(f"ps{i}", [128, 512], F32) for i in range(7)]
    c.psb = nc.alloc_psum_tensor("psb", [128, 1024], BF16)
    c.ps_slots = [c.P.slot(f"ps{i}") for i in range(7)]
    c.psb_slot = c.P.slot("psb")
    c.ident = nc.alloc_sbuf_tensor("ident_sb", [128, 128], BF16)
    c.ident_slot = c.P.slot("ident")
    c.bd = nc.alloc_sbuf_tensor("bd_sb", [128, 128], BF16)
    c.bd_slot = c.P.slot("bd")
    c.cst = nc.alloc_sbuf_tensor("cst_sb", [128, 8], F32)
    c.cst_slot = c.P.slot("cst")
    c.psbs = [(c.psb[:, :], c.psb_slot), (c.ps[6][:, :].bitcast(BF16), c.ps_slots[6])]
    c.nt_count = 0
    return c


def load_ident(c, ident_dram):
    P = c.P
    P.dma("pool", "ident", lambda h: h.dma_start(out=c.ident[:], in_=ident_dram[0]), writes=[c.ident_slot])
    P.dma("pool", "ident", lambda h: h.dma_start(out=c.bd[:], in_=ident_dram[1]), writes=[c.bd_slot])
    for i, v in enumerate((64.0 * EPS, EPS, 1e-5, 0.0, 1.0, math.pi / 2)):
        P.op("dve", lambda h, i=i, v=v: h.memset(c.cst[:, i:i + 1], v), writes=[c.cst_slot])


def norm_a(c, xt, xt_slot, gain_bc, gain_slot, xn, xn_slot, ss, ss_slot):
    P = c.P
    P.op("act", lambda h: h.activation(out=xn, in_=xt, func=AF.Square, accum_out=ss[:, 0:1]),
         reads=[xt_slot], writes=[xn_slot, ss_slot])
    P.op("dve", lambda h: h.tensor_scalar(out=ss[:, 1:2], in0=ss[:, 0:1], scalar1=1.0 / D, scalar2=EPS,
                                           op0=ALU.mult, op1=ALU.add), reads=[ss_slot], writes=[ss_slot])
    P.op("act", lambda h: h.activation(out=ss[:, 2:3], in_=ss[:, 1:2], func=AF.Sqrt), reads=[ss_slot], writes=[ss_slot])
    P.op("dve", lambda h: h.reciprocal(out=ss[:, 3:4], in_=ss[:, 2:3]), reads=[ss_slot], writes=[ss_slot])
    P.op("dve", lambda h: h.scalar_tensor_tensor(out=xn, in0=xt, scalar=ss[:, 3:4], in1=gain_bc,
                                                  op0=ALU.mult, op1=ALU.mult),
         reads=[xt_slot, ss_slot, gain_slot], writes=[xn_slot], force=True)


def norm_b(c, xn, xn_slot, hT, hT_slot, col0):
    P = c.P
    pb_, pb_slot = c.psbs[c.nt_count % 2]
    c.nt_count += 1
    for k in range(8):
        P.op("pe", lambda h, k=k: h.transpose(out=pb_[:, k * 128:(k + 1) * 128], in_=xn[:, k * 128:(k + 1) * 128],
                                              identity=c.ident[:]),
             reads=[xn_slot, c.ident_slot], writes=[pb_slot])
    P.op("act", lambda h: h.copy(out=hT[:, :, col0:col0 + 128],
                                 in_=pb_.rearrange("p (k t) -> p k t", t=128)),
         reads=[pb_slot], writes=[hT_slot])


def norm_transpose(c, xt, xt_slot, gain_bc, gain_slot, xn, xn_slot, ss, ss_slot, hT, hT_slot, col0):
    norm_a(c, xt, xt_slot, gain_bc, gain_slot, xn, xn_slot, ss, ss_slot)
    norm_b(c, xn, xn_slot, hT, hT_slot, col0)


def ffn_stage(c, xin, xin_slot, xout, xout_slot, wg, wu, wd, gain_row, ntok=TOK, TB=1024):
    P = c.P
    fa, ba = c.fa, c.ba
    fa.reset()
    ba.reset()
    NT = TB // 128
    Wd = ba.take(NM, 1024)
    hTs = [ba.take(8, TB) for _ in range(2)]
    actT = ba.take(NM, TB)
    wgb = [ba.take(8, 128) for _ in range(2)]
    wub = [ba.take(8, 128) for _ in range(2)]
    xn = [ba.take(1, 1024)[:, 0, :] for _ in range(2)]
    xt = [fa.take(1, 1024)[:, 0, :] for _ in range(4)]
    ost = [fa.take(1, 1024)[:, 0, :] for _ in range(2)]
    gbc = fa.take(1, 1024)[:, 0, :]
    sg = [fa.take(1, 512)[:, 0, :] for _ in range(2)]
    ss = [fa.take(1, 4)[:, 0, :] for _ in range(2)]

    s_Wd = P.slot("Wd")
    s_hT = [[P.slot(f"hT{j}_{t}") for t in range(NT)] for j in range(2)]
    s_act = [[P.slot(f"act{m}_{h}") for h in range(TB // 512)] for m in range(NM)]
    s_wg = P.slots_n("wg", 2)
    s_wu = P.slots_n("wu", 2)
    s_xn = P.slots_n("xn", 2)
    s_xt = P.slots_n("xt", 4)
    s_ost = P.slots_n("ost", 2)
    s_gbc = P.slot("gbc")
    s_sg = P.slots_n("sg", 2)
    s_ss = P.slots_n("ss", 2)

    P.dma("sp", "gbc", lambda h: h.dma_start(out=gbc, in_=gain_row.partition_broadcast(128)), writes=[s_gbc])
    wdv = wd.rearrange("(m p) o -> p m o", p=128)
    nblk = ntok // TB
    st = {"w": 0, "o": 0, "n": 0, "u": 0, "x": 0}

    pend = {}

    def norm_tile_a(b, t):
        j = st["n"] % 2
        st["n"] += 1
        xi = st["x"] % 2
        st["x"] += 1
        r0 = b * TB + t * 128
        P.dma("sp", f"xt{xi}", lambda h: h.dma_start(out=xt[xi], in_=xin[r0:r0 + 128, :]),
              reads=[xin_slot], writes=[s_xt[xi]])
        norm_a(c, xt[xi], s_xt[xi], gbc, s_gbc, xn[j], s_xn[j], ss[j], s_ss[j])
        pend[(b, t)] = j

    def norm_tile_b(b, t):
        j = pend.pop((b, t))
        norm_b(c, xn[j], s_xn[j], hTs[b % 2], s_hT[b % 2][t], t * 128)

    def norm_tile(b, t):
        norm_tile_a(b, t)
        norm_tile_b(b, t)

    for t in range(NT):
        norm_tile(0, t)
    for q in range(2):
        P.dma("pool", "Wd", lambda h, q=q: h.dma_start(out=Wd[:, q * 11:(q + 1) * 11, :], in_=wdv[:, q * 11:(q + 1) * 11, :]),
              writes=[s_Wd])
    for b in range(nblk):
        hT = hTs[b % 2]
        shT = s_hT[b % 2]
        for m in range(NM):
            j = st["w"] % 2
            st["w"] += 1
            P.dma("pool", f"wg{j}", lambda h, j=j, m=m: h.dma_start(out=wgb[j], in_=wg[m]), writes=[s_wg[j]])
            P.dma("pool", f"wu{j}", lambda h, j=j, m=m: h.dma_start(out=wub[j], in_=wu[m]), writes=[s_wu[j]])
            for hh in range(TB // 512):
                u = st["u"] % 2
                st["u"] += 1
                pg, pu = c.ps[2 * u], c.ps[2 * u + 1]
                sgs, sus = c.ps_slots[2 * u], c.ps_slots[2 * u + 1]
                hs = shT[hh * 4:(hh + 1) * 4]
                for k in range(8):
                    P.op("pe", lambda h, j=j, k=k, hh=hh, pg=pg, hT=hT: h.matmul(
                        pg[:, :], lhsT=wgb[j][:, k, :], rhs=hT[:, k, hh * 512:(hh + 1) * 512],
                        start=(k == 0), stop=(k == 7)), reads=[s_wg[j]] + hs, writes=[sgs])
                for k in range(8):
                    P.op("pe", lambda h, j=j, k=k, hh=hh, pu=pu, hT=hT: h.matmul(
                        pu[:, :], lhsT=wub[j][:, k, :], rhs=hT[:, k, hh * 512:(hh + 1) * 512],
                        start=(k == 0), stop=(k == 7)), reads=[s_wu[j]] + hs, writes=[sus])
                P.op("act", lambda h, u=u, pg=pg: h.activation(out=sg[u], in_=pg[:, :], func=AF.Silu),
                     reads=[sgs], writes=[s_sg[u]])
                P.op("dve", lambda h, u=u, pu=pu, m=m, hh=hh: h.tensor_tensor(
                    out=actT[:, m, hh * 512:(hh + 1) * 512], in0=sg[u], in1=pu[:, :], op=ALU.mult),
                    reads=[s_sg[u], sus], writes=[s_act[m][hh]])
            if b + 1 < nblk and m % 2 == 1:
                i_ = m // 2
                if i_ < NT:
                    norm_tile_a(b + 1, i_)
                if 1 <= i_ <= NT:
                    norm_tile_b(b + 1, i_ - 1)
        for t in range(NT):
            o = st["o"] % 2
            st["o"] += 1
            xi = 2 + (t % 2)
            r0 = b * TB + t * 128
            P.dma("sp", f"xt{xi}", lambda h, xi=xi, r0=r0: h.dma_start(out=xt[xi], in_=xin[r0:r0 + 128, :]),
                  reads=[xin_slot], writes=[s_xt[xi]])
            for half in range(2):
                pd = c.ps[4 + half]
                sd = c.ps_slots[4 + half]
                for m in range(NM):
                    P.op("pe", lambda h, m=m, t=t, half=half, pd=pd: h.matmul(
                        pd[:, :], lhsT=actT[:, m, t * 128:(t + 1) * 128], rhs=Wd[:, m, half * 512:(half + 1) * 512],
                        start=(m == 0), stop=(m == NM - 1)),
                        reads=[s_act[m][t // 4], s_Wd], writes=[sd])
                P.op("dve", lambda h, o=o, xi=xi, half=half, pd=pd: h.tensor_tensor(
                    out=ost[o][:, half * 512:(half + 1) * 512], in0=pd[:, :], in1=xt[xi][:, half * 512:(half + 1) * 512],
                    op=ALU.add), reads=[sd, s_xt[xi]], writes=[s_ost[o]])
            P.dma("sp", f"ost{o}", lambda h, o=o, r0=r0: h.dma_start(out=xout[r0:r0 + 128, :], in_=ost[o]),
                  reads=[s_ost[o]], writes=[xout_slot])
    P.barrier()


def cv_ps(pt, off, dims, p0=0, npart=128):
    a = pt[:, :]
    n = a.ap[0][0]
    return bass.AP(tensor=a.tensor, offset=a.offset + p0 * n + off, ap=[[n, npart]] + [list(d) for d in dims])


def _na_runs():
    out = []
    rs = [min(max(r - 4, 0), 24) for r in range(32)]
    for hf in range(2):
        runs = []
        for par in range(2):
            for ti in range(16):
                row0 = 2 * ti + par
                if row0 + 1 > 31:
                    continue
                for grp in range(2):
                    rows = [r for r in range(16 * hf + 8 * grp, 16 * hf + 8 * grp + 8)
                            if rs[r] % 2 == par and rs[r] <= row0 <= rs[r] + 6]
                    while rows:
                        if len(rows) == 1:
                            run, rows = rows, []
                            st = 1
                        else:
                            st = rows[1] - rows[0]
                            k = 2
                            while k < len(rows) and rows[k] - rows[k - 1] == st:
                                k += 1
                            run, rows = rows[:k], rows[k:]
                        runs.append((par, ti, grp, run[0], st, len(run)))
        out.append(runs)
    return out


NA_RUNS = _na_runs()

def na_stage(c, xin, xin_slot, xout, xout_slot, wqkv, wout, biasT, qkg, gain_row, tok0):
    P = c.P
    fa, ba = c.fa, c.ba
    fa.reset()
    ba.reset()
    hT = ba.take(8, SEQ)
    attnT = ba.take(8, SEQ)
    qT = ba.take(2, SEQ)
    kT = ba.take(2, SEQ)
    Vpad = ba.take(2, 16, 4, 128)
    EB = ba.take(4, 14, 64)
    cpad = ba.take(3, 128)
    zrhs = ba.take(1, 512)[:, 0, :]
    bT = ba.take(4, 14, 64)
    wq = ba.take(8, 768)
    PT = [ba.take(1, 512)[:, 0, :] for _ in range(3)]
    xn = [ba.take(1, 1024)[:, 0, :] for _ in range(2)]
    sq = [ba.take(1, 512)[:, 0, :] for _ in range(2)]
    xt = [fa.take(1, 1024)[:, 0, :] for _ in range(2)]
    ost = [fa.take(1, 1024)[:, 0, :] for _ in range(2)]
    gbc = fa.take(1, 1024)[:, 0, :]
    raw = [fa.take(1, 512)[:, 0, :] for _ in range(2)]
    rinv = [fa.take(1, 512)[:, 0, :] for _ in range(2)]
    ss = [fa.take(1, 4)[:, 0, :] for _ in range(2)]
    rec = [fa.take(1, 2)[:, 0, :] for _ in range(2)]
    gq = fa.take(1, 2)[:, 0, :]

    s_hT = P.slots_n("n_hT", 16)
    s_attnT = P.slots_n("n_attnT", 8)
    s_qT = P.slot("n_qT")
    s_kT = P.slot("n_kT")
    s_V = P.slot("n_V")
    s_bT = P.slot("n_bT")
    s_wq = P.slot("n_wq")
    s_PT = P.slots_n("n_PT", 3)
    s_EB = P.slot("n_EB")
    s_cpad = P.slot("n_cpad")
    s_xn = P.slots_n("n_xn", 2)
    s_sq = P.slots_n("n_sq", 2)
    s_xt = P.slots_n("n_xt", 2)
    s_ost = P.slots_n("n_ost", 2)
    s_gbc = P.slot("n_gbc")
    s_raw = P.slots_n("n_raw", 2)
    s_rinv = P.slots_n("n_rinv", 2)
    s_ss = P.slots_n("n_ss", 2)
    s_rec = P.slots_n("n_rec", 2)
    s_gq = P.slot("n_gq")
    ps, pss = c.ps, c.ps_slots

    P.dma("sp", "gbc", lambda h: h.dma_start(out=gbc, in_=gain_row.partition_broadcast(128)), writes=[s_gbc])
    P.dma("sp", "gq", lambda h: h.dma_start(out=gq, in_=qkg), writes=[s_gq])
    P.op("dve", lambda h: h.memset(Vpad[:, :, :, :, :].rearrange("p a b c d -> p (a b c d)"), 0.0), writes=[s_V])
    P.op("dve", lambda h: h.memset(cpad[:, :, :].rearrange("p a b -> p (a b)"), 0.0), writes=[s_cpad])
    P.op("dve", lambda h: h.memset(cpad[:, 0, 0:64], 1.0), writes=[s_cpad])
    P.op("dve", lambda h: h.memset(cpad[:, 1, 64:128], 1.0), writes=[s_cpad])
    P.op("dve", lambda h: h.memset(zrhs, 0.0), writes=[s_cpad])

    for t in range(16):
        j = t % 2
        r0 = tok0 + t * 128
        P.dma("sp", f"nxt{j}", lambda h, j=j, r0=r0: h.dma_start(out=xt[j], in_=xin[r0:r0 + 128, :]),
              reads=[xin_slot], writes=[s_xt[j]])
        norm_a(c, xt[j], s_xt[j], gbc, s_gbc, xn[j], s_xn[j], ss[j], s_ss[j])
        if t >= 1:
            norm_b(c, xn[1 - j], s_xn[1 - j], hT, s_hT[t - 1], (t - 1) * 128)
    norm_b(c, xn[1], s_xn[1], hT, s_hT[15], 15 * 128)

    Wo = hT[:, :, 0:1024]
    s_Wo = P.slot("n_Wo")
    ui = 0
    for hg in DBG.get('hgs', range(4)):
        for part in range(3):
            P.dma("pool", "nwq", lambda h, hg=hg, part=part: h.dma_start(
                out=wq[:, :, part * 256:(part + 1) * 256], in_=wqkv[hg][:, :, part * 256:(part + 1) * 256]),
                writes=[s_wq])
        P.dma("pool", "nbT", lambda h, hg=hg: h.dma_start(out=bT, in_=biasT[hg]), writes=[s_bT])
        for part in DBG.get('qk', range(2)):
            dst, s_dst = (qT, s_qT) if part == 0 else (kT, s_kT)
            for cc in range(2):
                for tb in range(4):
                    u = ui % 2
                    ui += 1
                    pq, spq = ps[u], pss[u]
                    pn, spn = ps[2 + u], pss[2 + u]
                    for k in range(8):
                        P.op("pe", lambda h, k=k, part=part, cc=cc, tb=tb, pq=pq: h.matmul(
                            pq[:, :], lhsT=wq[:, k, part * 256 + cc * 128: part * 256 + cc * 128 + 128],
                            rhs=hT[:, k, tb * 512:(tb + 1) * 512], start=(k == 0), stop=(k == 7)),
                            reads=[s_wq] + s_hT[tb * 4:(tb + 1) * 4], writes=[spq])
                    P.op("act", lambda h, u=u, pq=pq: h.activation(out=sq[u], in_=pq[:, :], func=AF.Square),
                         reads=[spq], writes=[s_sq[u]])
                    P.op("dve", lambda h, u=u, pq=pq: h.tensor_copy(out=raw[u], in_=pq[:, :]),
                         reads=[spq, s_sq[u]], writes=[s_raw[u]])
                    P.op("pe", lambda h, u=u, pn=pn: h.matmul(pn[:, :], lhsT=c.bd[:], rhs=sq[u], start=True, stop=True),
                         reads=[s_sq[u], c.bd_slot], writes=[spn])
                    if part == 0:
                        P.op("act", lambda h, u=u, pn=pn: h.activation(out=rinv[u], in_=pn[:, :], func=AF.Ln,
                                                                       bias=c.cst[:, 0:1], scale=1.0),
                             reads=[spn, c.cst_slot], writes=[s_rinv[u]])
                    else:
                        P.op("act", lambda h, u=u, pn=pn: h.activation(out=rinv[u], in_=pn[:, :], func=AF.Ln,
                                                                       bias=c.cst[:, 1:2], scale=1.0 / 64),
                             reads=[spn, c.cst_slot], writes=[s_rinv[u]])
                    P.op("act", lambda h, u=u: h.activation(out=rinv[u], in_=rinv[u], func=AF.Exp, scale=-0.5),
                         reads=[s_rinv[u]], writes=[s_rinv[u]])
                    P.op("dve", lambda h, u=u, part=part, cc=cc, tb=tb, dst=dst: h.scalar_tensor_tensor(
                        out=dst[:, cc, tb * 512:(tb + 1) * 512], in0=raw[u], scalar=gq[:, part:part + 1], in1=rinv[u],
                        op0=ALU.mult, op1=ALU.mult), reads=[s_raw[u], s_rinv[u], s_gq], writes=[s_dst])
        for par in DBG.get('vpar', range(2)):
            for i in range(16 - par):
                u = ui % 2
                ui += 1
                pv, spv = ps[u], pss[u]
                t0 = par * 64 + i * 128
                for k in range(8):
                    P.op("pe", lambda h, k=k, t0=t0, pv=pv: h.matmul(
                        pv[:, 0:256], lhsT=hT[:, k, t0:t0 + 128], rhs=wq[:, k, 512:768], start=(k == 0), stop=(k == 7)),
                        reads=[s_wq] + s_hT[t0 // 128:(t0 + 127) // 128 + 1], writes=[spv])
                for hp in range(2):
                    P.op("act", lambda h, par=par, i=i, pv=pv, hp=hp: h.copy(
                        out=cv(Vpad, 0, 128, ((par * 16 + i) * 4 + hp) * 128 + hp * 64, [[256, 2], [1, 64]]),
                        in_=cv_ps(pv, hp * 64, [[128, 2], [1, 64]])), reads=[spv], writes=[s_V])
        if hg == 3:
            for q in range(4):
                P.dma("pool", "nWo", lambda h, q=q: h.dma_start(out=Wo[:, q * 2:(q + 1) * 2, :], in_=wout[:, q * 2:(q + 1) * 2, :]),
                      writes=[s_Wo] + s_hT)
        P.op("act", lambda h: h.activation(out=EB[:, :, :, :].rearrange("p a b c -> p (a b c)"),
                                           in_=bT[:, :, :, :].rearrange("p a b c -> p (a b c)"), func=AF.Exp),
             reads=[s_bT], writes=[s_EB])
        for cc in range(2):
            for hf in range(2):
                for b in range(4):
                    P.op("pe", lambda h, b=b: h.matmul(ps[b][:, :], lhsT=cpad[:, 2, :], rhs=zrhs, start=True, stop=True,
                                                       skip_group_check=True), reads=[s_cpad], writes=[pss[b]])
                units = [(par, ti, grp, r0_, st_, n_, hp) for (par, ti, grp, r0_, st_, n_) in NA_RUNS[hf] for hp in range(2)]
                LA = 2

                def emit_front(idx, cc=cc):
                    (par, ti, grp, r0_, st_, n_, hp) = units[idx]
                    row0 = 2 * ti + par
                    hh = cc * 2 + hp
                    pb = hp * 64
                    w3 = idx % 3
                    sc, ssc = ps[4 + w3], pss[4 + w3]
                    nq = 64 * n_
                    rho0 = row0 - r0_ + 7
                    P.op("pe", lambda h: h.matmul(
                        sc[:, 0:nq], lhsT=kT[pb:pb + 64, cc, row0 * 64: row0 * 64 + 128],
                        rhs=cv(qT, pb, 64, cc * SEQ + r0_ * 64, [[st_ * 64, n_], [1, 64]]), start=True, stop=True),
                        reads=[s_kT, s_qT], writes=[ssc])
                    P.op("act", lambda h: h.activation(out=PT[w3][:, 0:nq], in_=sc[:, 0:nq], func=AF.Exp),
                         reads=[ssc], writes=[s_PT[w3]])
                    P.op("dve", lambda h: h.tensor_tensor(
                        out=cv(PT[w3], 0, 128, 0, [[64, n_], [1, 64]]), in0=cv(PT[w3], 0, 128, 0, [[64, n_], [1, 64]]),
                        in1=cv(EB, 0, 128, hh * 896 + rho0 * 64, [[-st_ * 64, n_], [1, 64]]), op=ALU.mult),
                        reads=[s_PT[w3], s_EB], writes=[s_PT[w3]])

                def emit_back(idx, cc=cc, hf=hf):
                    (par, ti, grp, r0_, st_, n_, hp) = units[idx]
                    hh = cc * 2 + hp
                    w3 = idx % 3
                    nq = 64 * n_
                    c0 = (r0_ - 16 * hf - 8 * grp) * 64
                    for (bank, lw_) in ((grp, Vpad[:, par, ti, hh, :]), (2 + grp, cpad[:, hp, :])):
                        P.op("pe", lambda h, bank=bank, lw_=lw_: h.matmul(
                            cv_ps(ps[bank], c0, [[st_ * 64, n_], [1, 64]]), lhsT=lw_, rhs=PT[w3][:, 0:nq], start=False, stop=True,
                            skip_group_check=True), reads=[s_PT[w3], s_V, s_cpad], writes=[pss[bank]])

                for idx in range(len(units) + LA):
                    if idx < len(units):
                        emit_front(idx)
                    if idx - LA >= 0:
                        emit_back(idx - LA)
                for grp in range(2):
                    u = grp
                    P.op("act", lambda h, u=u, grp=grp: h.activation(out=rinv[u], in_=ps[2 + grp][:, :], func=AF.Ln),
                         reads=[pss[2 + grp]], writes=[s_rinv[u]])
                    P.op("act", lambda h, u=u: h.activation(out=rinv[u], in_=rinv[u], func=AF.Exp, scale=-1.0),
                         reads=[s_rinv[u]], writes=[s_rinv[u]])
                    P.op("dve", lambda h, u=u, grp=grp, hg=hg, cc=cc, hf=hf: h.tensor_tensor(
                        out=attnT[:, hg * 2 + cc, hf * 1024 + grp * 512: hf * 1024 + (grp + 1) * 512], in0=ps[grp][:, :], in1=rinv[u],
                        op=ALU.mult), reads=[pss[grp], s_rinv[u]], writes=[s_attnT[hg * 2 + cc]])
    if DBG.get('dump') is not None:
        dbg = DBG['dump']
        s_dbg = P.slot("dbg")
        P.dma("pool", "dbg", lambda h: h.dma_start(out=dbg[:, 0:4096], in_=qT.rearrange("p a b -> p (a b)")), reads=[s_qT], writes=[s_dbg])
        P.dma("pool", "dbg", lambda h: h.dma_start(out=dbg[:, 4096:8192], in_=kT.rearrange("p a b -> p (a b)")), reads=[s_kT], writes=[s_dbg])
    P.barrier()
    for t in range(16):
        j = t % 2
        r0 = tok0 + t * 128
        P.dma("sp", f"nxt{j}", lambda h, j=j, r0=r0: h.dma_start(out=xt[j], in_=xin[r0:r0 + 128, :]),
              reads=[xin_slot], writes=[s_xt[j]])
        for half in range(2):
            pd, sd = ps[half], pss[half]
            for k in range(8):
                P.op("pe", lambda h, k=k, t=t, half=half, pd=pd: h.matmul(
                    pd[:, :], lhsT=attnT[:, k, t * 128:(t + 1) * 128], rhs=Wo[:, k, half * 512:(half + 1) * 512],
                    start=(k == 0), stop=(k == 7)), reads=[s_attnT[k], s_Wo], writes=[sd])
            P.op("dve", lambda h, j=j, half=half, pd=pd: h.tensor_tensor(
                out=ost[j][:, half * 512:(half + 1) * 512], in0=pd[:, :], in1=xt[j][:, half * 512:(half + 1) * 512],
                op=ALU.add), reads=[sd, s_xt[j]], writes=[s_ost[j]])
        P.dma("sp", f"nost{j}", lambda h, j=j, r0=r0: h.dma_start(out=xout[r0:r0 + 128, :], in_=ost[j]),
              reads=[s_ost[j]], writes=[xout_slot])
    P.barrier()


def cv(ap, p0, npart, off, dims):
    n = ap.ap[0][0]
    return bass.AP(tensor=ap.tensor, offset=ap.offset + p0 * n + off, ap=[[n, npart]] + [list(d) for d in dims])


GELU_FUNC = [None]


def ab_stage(c, xin, xin_slot, xout, xout_slot, W, gain_row, ntok):
    P = c.P
    fa, ba = c.fa, c.ba
    fa.reset()
    ba.reset()
    NK = 257
    R1 = ba.take(1, 16448)[:, 0, :]
    uT = ba.take(4, SEQ)
    aT = ba.take(4, SEQ + 30)
    yaT = ba.take(4, SEQ)
    Ht = ba.take(32, 2, 128)
    Gt = ba.take(32, 2, 128)
    Mt = ba.take(32, 128)
    selin = ba.take(2, 8, 128)
    selout = ba.take(8, 8, 128)
    wsl = [ba.take(8, 128) for _ in range(2)]
    gluw = ba.take(4, 512)
    xn = [ba.take(1, 1024)[:, 0, :] for _ in range(2)]
    hT = R1[:, 0:16384].rearrange("p (a b) -> p a b", b=SEQ)
    cdiag = R1[:, 0:15872].rearrange("p (q k m) -> p q k m", k=31, m=128)
    XS = R1
    Wo = R1[:, 0:8192].rearrange("p (a b) -> p a b", b=1024)
    UY = aT[:, :, :].rearrange("p a b -> p (a b)")[:, 0:8192].rearrange("p (g k) -> p g k", k=256)
    Qt = R1[:, 0:8192].rearrange("p (r e) -> p r e", r=2)
    Pt = R1[:, 8192:16384].rearrange("p (r e) -> p r e", r=2)
    Hp = uT[:, :, :].rearrange("p a b -> p (a b)").rearrange("p (r e) -> p r e", r=2)
    gbc = fa.take(1, 1024)[:, 0, :]
    lam = fa.take(3, 32)
    Apw = fa.take(2, 32, 32)
    msk = fa.take(2, 128)
    dcol = fa.take(1, 32)[:, 0, :]
    cw = fa.take(4, 31)
    vec = fa.take(1, 20)[:, 0, :]
    ones32 = fa.take(1, 128)[:, 0, :]
    etab = fa.take(1, 32)[:, 0, :]
    TA = fa.take(1, 64)[:, 0, :]
    TB = fa.take(1, 64)[:, 0, :]
    Sf = fa.take(2, 128)
    st1 = fa.take(1, 64)[:, 0, :]
    st2 = fa.take(1, 64)[:, 0, :]
    sm = fa.take(12, 32)
    ss = [fa.take(1, 4)[:, 0, :] for _ in range(2)]
    tmpA = fa.take(1, 128)[:, 0, :]
    tmpB = fa.take(1, 128)[:, 0, :]
    Dreg = fa.take(1, 7168)[:, 0, :]
    Bp = Dreg[:, 0:1024]
    Cp = Dreg[:, 1024:2048]
    Bb = Dreg[:, 2048:3072]
    t1 = Dreg[:, 3072:4096]
    t2 = Dreg[:, 4096:5120]
    w5 = [Dreg[:, 5120:6144], Dreg[:, 6144:7168], Dreg[:, 2048:3072]]
    xt = [Dreg[:, 0:1024], Dreg[:, 1024:2048]]
    ost = [Dreg[:, 2048:3072], Dreg[:, 3072:4096]]
    sgm = [Dreg[:, 0:512], Dreg[:, 512:1024]]
    acv = Dreg[:, 0:2048].rearrange("p (q t) -> p q t", t=512)
    sqv = [Dreg[:, 2048:2560], Dreg[:, 2560:3072]]
    meanv = Dreg[:, 3072:3584]
    rstdv = Dreg[:, 3584:4096]
    tmpv = Dreg[:, 4096:4608]
    ynv = [Dreg[:, 4608:5120], Dreg[:, 5120:5632]]
    ps, pss = c.ps, c.ps_slots
    ident = c.ident

    S = lambda n: P.slot("ab_" + n)
    s_tab = S("tab")
    s_D = S("Dreg")
    s_R1 = S("R1")
    s_uT = S("uT")
    s_aT = S("aT")
    s_ya = S("yaT")
    s_H, s_G, s_M = S("H"), S("G"), S("M")
    s_sel = S("sel")
    s_wsl = [S("wsl0"), S("wsl1")]
    s_gluw = S("gluw")
    s_xn = [S("xn0"), S("xn1")]
    s_gbc = S("gbc")
    s_ss = [S("ss0"), S("ss1")]
    s_tmp = S("tmpAB")

    def dv(fn, r, w, force=True):
        P.op("dve", fn, reads=r, writes=w, force=force)

    def ac(fn, r, w, force=True):
        P.op("act", fn, reads=r, writes=w, force=force)

    P.dma("sp", "gbc", lambda h: h.dma_start(out=gbc, in_=gain_row.partition_broadcast(128)), writes=[s_gbc])
    for dst, src in ((lam, W["lam"]), (msk, W["mask"]), (dcol, W["dcol"]), (cw, W["cw"]), (vec, W["vec"]),
                     (ones32, W["ones"]), (etab, W["etab"]), (Bp, W["B"]), (Cp, W["C"])):
        P.dma("sp", "abtab", lambda h, dst=dst, src=src: h.dma_start(out=dst, in_=src), writes=[s_tab])
    P.dma("pool", "absel", lambda h: h.dma_start(out=selin, in_=W["selin"]), writes=[s_sel])
    for q in range(2):
        P.dma("pool", "absel", lambda h, q=q: h.dma_start(out=selout[:, q * 4:(q + 1) * 4], in_=W["selout"][:, q * 4:(q + 1) * 4]),
              writes=[s_sel])
    P.dma("pool", "abglu", lambda h: h.dma_start(out=gluw, in_=W["gluw"]), writes=[s_gluw])

    T = [s_tab]
    sm_ = lambda i: sm[:, i, :]
    m_dt, m_lrd, m_lid, m_pr, m_den, m_cre, m_cim, m_x, m_y = (sm_(i) for i in range(9))
    lr, li, ls = lam[:, 0, :], lam[:, 1, :], lam[:, 2, :]
    ac(lambda h: h.activation(out=m_dt, in_=ls, func=AF.Exp), T, T)
    dv(lambda h: h.tensor_tensor(out=m_lrd, in0=lr, in1=m_dt, op=ALU.mult), T, T)
    dv(lambda h: h.tensor_tensor(out=m_lid, in0=li, in1=m_dt, op=ALU.mult), T, T)
    bc_g = lambda a: cv(a, 0, 128, 0, [[0, 32], [1, 32]])
    bc_e = cv(etab, 0, 128, 0, [[1, 32], [0, 32]])
    v3 = lambda a: cv(a, 0, 128, 0, [[32, 32], [1, 32]])
    argm, ang, kf = w5
    ki = t1[:, 0:1024].bitcast(I32)
    dv(lambda h: h.tensor_tensor(out=v3(argm), in0=bc_g(m_lrd), in1=bc_e, op=ALU.mult), T, T)
    ac(lambda h: h.activation(out=argm, in_=argm, func=AF.Exp), T, T)
    dv(lambda h: h.tensor_tensor(out=v3(ang), in0=bc_g(m_lid), in1=bc_e, op=ALU.mult), T, T)
    trig = t2[:, 0:1024]
    for ri, shift in ((1, 0.0), (0, math.pi / 2)):
        src = ang
        if shift != 0.0:
            dv(lambda h: h.tensor_scalar(out=ang, in0=ang, scalar1=shift, scalar2=None, op0=ALU.add), T, T)
        dv(lambda h: h.tensor_scalar(out=kf, in0=ang, scalar1=1.0 / (2 * math.pi), scalar2=None, op0=ALU.mult), T, T)
        dv(lambda h: h.tensor_copy(out=ki, in_=kf), T, T)
        dv(lambda h: h.tensor_copy(out=kf, in_=ki), T, T)
        dv(lambda h: h.scalar_tensor_tensor(out=kf, in0=kf, scalar=-2 * math.pi, in1=ang, op0=ALU.mult, op1=ALU.add), T, T)
        ac(lambda h: h.activation(out=trig, in_=kf, func=AF.Sin), T, T)
        dv(lambda h, ri=ri: h.tensor_tensor(out=Apw[:, ri, :, :].rearrange("p a b -> p (a b)"), in0=argm, in1=trig, op=ALU.mult), T, T)
    for (p0_, j1) in ((0, 16), (64, 23)):
        hp = lambda a, p0_=p0_: cv(a, p0_, 64, 0, [[1, 32]])
        A1r = cv(Apw, p0_, 64, j1 * 32, [[1, 32]])
        A1i = cv(Apw, p0_, 64, 1024 + j1 * 32, [[1, 32]])
        lr_, li_ = hp(lr), hp(li)
        dv(lambda h, hp=hp, A1r=A1r: h.tensor_scalar(out=hp(m_pr), in0=A1r, scalar1=-1.0, scalar2=None, op0=ALU.add), T, T)
        dv(lambda h, hp=hp, lr_=lr_: h.tensor_tensor(out=hp(m_x), in0=lr_, in1=lr_, op=ALU.mult), T, T)
        dv(lambda h, hp=hp, li_=li_: h.tensor_tensor(out=hp(m_y), in0=li_, in1=li_, op=ALU.mult), T, T)
        dv(lambda h, hp=hp: h.tensor_tensor(out=hp(m_den), in0=hp(m_x), in1=hp(m_y), op=ALU.add), T, T)
        dv(lambda h, hp=hp: h.reciprocal(out=hp(m_den), in_=hp(m_den)), T, T)
        dv(lambda h, hp=hp, lr_=lr_: h.tensor_tensor(out=hp(m_x), in0=hp(m_pr), in1=lr_, op=ALU.mult), T, T)
        dv(lambda h, hp=hp, li_=li_, A1i=A1i: h.tensor_tensor(out=hp(m_y), in0=A1i, in1=li_, op=ALU.mult), T, T)
        dv(lambda h, hp=hp: h.tensor_tensor(out=hp(m_x), in0=hp(m_x), in1=hp(m_y), op=ALU.add), T, T)
        dv(lambda h, hp=hp: h.tensor_tensor(out=hp(m_cre), in0=hp(m_x), in1=hp(m_den), op=ALU.mult), T, T)
        dv(lambda h, hp=hp, lr_=lr_, A1i=A1i: h.tensor_tensor(out=hp(m_x), in0=A1i, in1=lr_, op=ALU.mult), T, T)
        dv(lambda h, hp=hp, li_=li_: h.tensor_tensor(out=hp(m_y), in0=hp(m_pr), in1=li_, op=ALU.mult), T, T)
        dv(lambda h, hp=hp: h.tensor_tensor(out=hp(m_x), in0=hp(m_x), in1=hp(m_y), op=ALU.subtract), T, T)
        dv(lambda h, hp=hp: h.tensor_tensor(out=hp(m_cim), in0=hp(m_x), in1=hp(m_den), op=ALU.mult), T, T)
    bcc = lambda a: cv(a, 0, 128, 0, [[1, 32], [0, 16]])
    g16 = lambda a, off: cv(a, 0, 128, off, [[16, 32], [1, 16]])
    for (o_off, a_, x_off, b_, y_off, op) in ((0, m_cre, 0, m_cim, 512, ALU.subtract), (512, m_cre, 512, m_cim, 0, ALU.add)):
        dv(lambda h, a_=a_, x_off=x_off: h.tensor_tensor(out=g16(t1, 0), in0=bcc(a_), in1=g16(Bp, x_off), op=ALU.mult), T, T)
        dv(lambda h, b_=b_, y_off=y_off: h.tensor_tensor(out=g16(t2, 0), in0=bcc(b_), in1=g16(Bp, y_off), op=ALU.mult), T, T)
        dv(lambda h, o_off=o_off, op=op: h.tensor_tensor(out=Bb[:, o_off:o_off + 512], in0=t1[:, 0:512], in1=t2[:, 0:512], op=op), T, T)

    def fam(dst, d_goff, d_rioff, V, negim, j0):
        np_, p0, js = 128, 0, 1
        for qq in range(4):
            Ar = cv(Apw, p0, np_, j0 * 32 + 8 * qq, [[1, 8], [js * 32, 8], [0, 16]])
            Ai = cv(Apw, p0, np_, 1024 + j0 * 32 + 8 * qq, [[1, 8], [js * 32, 8], [0, 16]])
            Vr = cv(V, p0, np_, 8 * qq * 16, [[16, 8], [0, 8], [1, 16]])
            Vi = cv(V, p0, np_, 512 + 8 * qq * 16, [[16, 8], [0, 8], [1, 16]])
            T1 = cv(t1, p0, np_, 0, [[128, 8], [16, 8], [1, 16]])
            T2 = cv(t2, p0, np_, 0, [[128, 8], [16, 8], [1, 16]])
            dre = cv(dst, p0, np_, 8 * qq * d_goff, [[d_goff, 8], [16, 8], [1, 16]])
            dim_ = cv(dst, p0, np_, 8 * qq * d_goff + d_rioff, [[d_goff, 8], [16, 8], [1, 16]])
            dv(lambda h, Ar=Ar, Vr=Vr, T1=T1: h.tensor_tensor(out=T1, in0=Ar, in1=Vr, op=ALU.mult), T, T)
            dv(lambda h, Ai=Ai, Vi=Vi, T2=T2: h.tensor_tensor(out=T2, in0=Ai, in1=Vi, op=ALU.mult), T, T)
            dv(lambda h, dre=dre, T1=T1, T2=T2: h.tensor_tensor(out=dre, in0=T1, in1=T2, op=ALU.subtract), T, T)
            dv(lambda h, Ar=Ar, Vi=Vi, T1=T1: h.tensor_tensor(out=T1, in0=Ar, in1=Vi, op=ALU.mult), T, T)
            dv(lambda h, Ai=Ai, Vr=Vr, T2=T2: h.tensor_tensor(out=T2, in0=Ai, in1=Vr, op=ALU.mult), T, T)
            if negim:
                dv(lambda h, T1=T1: h.tensor_scalar(out=T1, in0=T1, scalar1=-1.0, scalar2=None, op0=ALU.mult), T, T)
                dv(lambda h, dim_=dim_, T1=T1, T2=T2: h.tensor_tensor(out=dim_, in0=T1, in1=T2, op=ALU.subtract), T, T)
            else:
                dv(lambda h, dim_=dim_, T1=T1, T2=T2: h.tensor_tensor(out=dim_, in0=T1, in1=T2, op=ALU.add), T, T)

    fam(Qt, 128, 4096, Bb, False, 0)
    fam(Pt, 128, 4096, Cp, True, 8)
    fam(Gt, 256, 128, Cp, True, 16)
    fam(Hp, 128, 4096, Bb, False, 24)
    for (p0_, j8) in ((0, 23), (64, 16)):
        Dr = cv(Apw, p0_, 64, j8 * 32, [[1, 32]])
        Di = cv(Apw, p0_, 64, 1024 + j8 * 32, [[1, 32]])
        hq = lambda a, off, p0_=p0_: cv(a, p0_, 64, off, [[1, 32]])
        dv(lambda h, hq=hq, Dr=Dr: h.tensor_copy(out=hq(TA, 0), in_=Dr), T, T)
        dv(lambda h, hq=hq, Dr=Dr: h.tensor_copy(out=hq(TA, 32), in_=Dr), T, T)
        dv(lambda h, hq=hq, Di=Di: h.tensor_scalar(out=hq(TB, 0), in0=Di, scalar1=-1.0, scalar2=None, op0=ALU.mult), T, T)
        dv(lambda h, hq=hq, Di=Di: h.tensor_copy(out=hq(TB, 32), in_=Di), T, T)
    for g in range(32):
        u = g % 2
        pf, pb = ps[2 * u], ps[2 * u + 1]
        sf_, sb_ = pss[2 * u], pss[2 * u + 1]
        for (pp, sp_, p0) in ((pf, sf_, 0), (pb, sb_, 64)):
            for ri in range(2):
                P.op("pe", lambda h, pp=pp, p0=p0, ri=ri, g=g: h.matmul(
                    pp[:, 0:128], lhsT=Qt[p0:p0 + 64, ri, g * 128:(g + 1) * 128], rhs=Pt[p0:p0 + 64, ri, g * 128:(g + 1) * 128],
                    start=(ri == 0), stop=(ri == 1)), reads=T, writes=[sp_])
        dv(lambda h, pf=pf: h.tensor_tensor(out=tmpA, in0=pf[:, 0:128], in1=msk[:, 0, :], op=ALU.mult), [sf_] + T, [s_tmp])
        dv(lambda h, pb=pb: h.tensor_tensor(out=tmpB, in0=pb[:, 0:128], in1=msk[:, 1, :], op=ALU.mult), [sb_] + T, [s_tmp])
        dv(lambda h: h.tensor_tensor(out=tmpA, in0=tmpA, in1=tmpB, op=ALU.add), [s_tmp], [s_tmp])
        dv(lambda h, g=g: h.scalar_tensor_tensor(out=Mt[:, g, :], in0=ident[:], scalar=dcol[:, g:g + 1], in1=tmpA,
                                                  op0=ALU.mult, op1=ALU.add), [s_tmp, c.ident_slot] + T, [s_M])
    for b in range(8):
        for gg in range(4):
            for ri in range(2):
                g = b * 4 + gg
                sl = gg * 2 + ri
                P.op("pe", lambda h, g=g, ri=ri, sl=sl: h.transpose(
                    out=c.psb[:, sl * 128:(sl + 1) * 128], in_=Hp[:, ri, g * 128:(g + 1) * 128], identity=ident[:]),
                    reads=T + [c.ident_slot], writes=[c.psb_slot])
        ac(lambda h, b=b: h.copy(out=Ht[:, b * 4:(b + 1) * 4, :, :].rearrange("p a b c -> p (a b c)"), in_=c.psb[:, :]),
           [c.psb_slot], [s_H])
    P.barrier()

    for sq_ in range(ntok // SEQ):
        tok0 = sq_ * SEQ
        s_hT = [S(f"hT{t}") for t in range(16)]
        s_xt1 = [S("xt1_0"), S("xt1_1")]
        for t in range(16):
            j = t % 2
            r0 = tok0 + t * 128
            P.dma("sp", f"abxt{j}", lambda h, j=j, r0=r0: h.dma_start(out=xt[j], in_=xin[r0:r0 + 128, :]),
                  reads=[xin_slot], writes=[s_xt1[j]])
            norm_a(c, xt[j], s_xt1[j], gbc, s_gbc, xn[j], s_xn[j], ss[j], s_ss[j])
            if t >= 1:
                norm_b(c, xn[1 - j], s_xn[1 - j], hT, s_hT[t - 1], (t - 1) * 128)
        norm_b(c, xn[1], s_xn[1], hT, s_hT[15], 15 * 128)
        P.barrier()
        s_sg = [S("sg0"), S("sg1")]
        s_aTq = [S(f"aT{q}") for q in range(4)]
        s_uTq = [S(f"uT{q}") for q in range(4)]
        for q in range(4):
            dv(lambda h, q=q: h.memset(aT[:, q, 0:15], 0.0), [], [s_aTq[q]], force=False)
            dv(lambda h, q=q: h.memset(aT[:, q, SEQ + 15:SEQ + 30], 0.0), [], [s_aTq[q]], force=False)
        wi = 0
        ui = 0

        def load_slab(oc):
            nonlocal wi
            j = wi % 2
            wi += 1
            P.dma("pool", f"abw{j}", lambda h, j=j, oc=oc: h.dma_start(out=wsl[j], in_=W["win"][oc]), writes=[s_wsl[j]])
            return j

        for q in range(4):
            ja = load_slab(q)
            jg = load_slab(q + 4)
            for tb in range(4):
                u = ui % 2
                ui += 1
                pa, pg = ps[2 * u], ps[2 * u + 1]
                sa, sg_ = pss[2 * u], pss[2 * u + 1]
                for (pp, sp_, jj) in ((pa, sa, ja), (pg, sg_, jg)):
                    for k in range(8):
                        P.op("pe", lambda h, pp=pp, jj=jj, k=k, tb=tb: h.matmul(
                            pp[:, :], lhsT=wsl[jj][:, k, :], rhs=hT[:, k, tb * 512:(tb + 1) * 512],
                            start=(k == 0), stop=(k == 7)), reads=[s_wsl[jj]] + s_hT[tb * 4:(tb + 1) * 4], writes=[sp_])
                ac(lambda h, u=u, pg=pg: h.activation(out=sgm[u], in_=pg[:, :], func=AF.Sigmoid), [sg_], [s_sg[u]], force=False)
                dv(lambda h, u=u, pa=pa, q=q, tb=tb: h.tensor_tensor(
                    out=aT[:, q, 15 + tb * 512:15 + (tb + 1) * 512], in0=sgm[u], in1=pa[:, :], op=ALU.mult),
                    [s_sg[u], sa], [s_aTq[q]], force=False)
        for q in range(4):
            ju = load_slab(q + 8)
            for tb in range(4):
                u = ui % 2
                ui += 1
                pu_, su_ = ps[4 + u], pss[4 + u]
                for k in range(8):
                    P.op("pe", lambda h, pu_=pu_, ju=ju, k=k, tb=tb: h.matmul(
                        pu_[:, :], lhsT=wsl[ju][:, k, :], rhs=hT[:, k, tb * 512:(tb + 1) * 512],
                        start=(k == 0), stop=(k == 7)), reads=[s_wsl[ju]] + s_hT[tb * 4:(tb + 1) * 4], writes=[su_])
                ac(lambda h, pu_=pu_, q=q, tb=tb: h.copy(
                    out=cv(uT, 0, 128, q * SEQ + tb * 64, [[256, 8], [1, 64]]),
                    in_=pu_[:, :].rearrange("p (k t) -> p t k", t=8)), [su_], [s_uTq[q]], force=False)
        P.barrier()
        s_cd = [S(f"cd{q}") for q in range(4)]
        for q in range(4):
            for k in range(31):
                dv(lambda h, q=q, k=k: h.tensor_scalar(out=cdiag[:, q, k, :], in0=ident[:], scalar1=cw[:, q, k:k + 1],
                                                        scalar2=None, op0=ALU.mult), [c.ident_slot], [s_cd[q]], force=False)
        s_ac = [S(f"ac{q}") for q in range(4)]
        s_sq = [S("sq0"), S("sq1")]
        s_mean, s_rstd, s_tv = S("mean"), S("rstd"), S("tmpv")
        s_yn = [S("yn0"), S("yn1")]
        for tb in range(4):
            for q in range(4):
                pc, spc = ps[q % 2], pss[q % 2]
                for k in range(31):
                    P.op("pe", lambda h, pc=pc, q=q, k=k, tb=tb: h.matmul(
                        pc[:, :], lhsT=cdiag[:, q, k, :], rhs=aT[:, q, tb * 512 + k: tb * 512 + k + 512],
                        start=(k == 0), stop=(k == 30)), reads=[s_cd[q]], writes=[spc])
                ac(lambda h, pc=pc, q=q: h.activation(out=acv[:, q, :], in_=pc[:, :], func=AF.Identity,
                                                      bias=vec[:, q:q + 1], scale=1.0), [spc], [s_ac[q]], force=False)
            pm, spm = ps[2], pss[2]
            pe2, spe2 = ps[3], pss[3]
            for q in range(4):
                P.op("pe", lambda h, q=q: h.matmul(pm[:, :], lhsT=cv(ones32, 0, 128, 0, [[1, 128]]), rhs=acv[:, q, :],
                                                   start=(q == 0), stop=(q == 3)), reads=[s_ac[q]], writes=[spm])
            for q in range(4):
                u = q % 2
                ac(lambda h, q=q, u=u: h.activation(out=sqv[u], in_=acv[:, q, :], func=AF.Square), [s_ac[q]], [s_sq[u]], force=False)
                P.op("pe", lambda h, q=q, u=u: h.matmul(pe2[:, :], lhsT=cv(ones32, 0, 128, 0, [[1, 128]]), rhs=sqv[u],
                                                        start=(q == 0), stop=(q == 3)), reads=[s_sq[u]], writes=[spe2])
            dv(lambda h: h.tensor_copy(out=meanv, in_=pm[:, :]), [spm], [s_mean], force=False)
            dv(lambda h: h.tensor_tensor(out=tmpv, in0=meanv, in1=meanv, op=ALU.mult), [s_mean], [s_tv])
            dv(lambda h: h.tensor_tensor(out=tmpv, in0=pe2[:, :], in1=tmpv, op=ALU.subtract), [spe2, s_tv], [s_tv])
            ac(lambda h: h.activation(out=rstdv, in_=tmpv, func=AF.Ln, bias=c.cst[:, 2:3], scale=1.0), [s_tv, c.cst_slot], [s_rstd])
            ac(lambda h: h.activation(out=rstdv, in_=rstdv, func=AF.Exp, scale=-0.5), [s_rstd], [s_rstd])
            for q in range(4):
                u = q % 2
                dv(lambda h, q=q, u=u: h.tensor_tensor(out=ynv[u], in0=acv[:, q, :], in1=meanv, op=ALU.subtract),
                   [s_ac[q], s_mean], [s_yn[u]], force=False)
                dv(lambda h, u=u: h.tensor_tensor(out=ynv[u], in0=ynv[u], in1=rstdv, op=ALU.mult), [s_yn[u], s_rstd], [s_yn[u]])
                ac(lambda h, q=q, u=u, tb=tb: h.activation(out=yaT[:, q, tb * 512:(tb + 1) * 512], in_=ynv[u], func=AF.Silu,
                                                           scale=vec[:, 4 + q:5 + q], bias=vec[:, 8 + q:9 + q]),
                   [s_yn[u]], [s_ya], force=False)
        P.barrier()
        s_UY = [S(f"UY{g}") for g in range(32)]
        s_X = [S("Xf"), S("Xb")]
        s_hist = [S("histf"), S("histb")]
        XSv = lambda p0, ri, g, c0, n: cv(XS, p0, 64, (ri * 32 + g) * NK + c0, [[1, n]])
        dv(lambda h: h.memset(cv(XS, 0, 64, 0, [[NK, 64]]), 0.0), [], [s_hist[0]], force=False)
        dv(lambda h: h.memset(cv(XS, 64, 64, 255, [[NK, 64]]), 0.0), [], [s_hist[1]], force=False)
        for g in range(32):
            q, j, par = g // 8, (g % 8) // 2, g % 2
            u = g % 2
            pu_, su_ = ps[u], pss[u]
            for tp in range(8):
                rhs = cv(uT, 32 * j, 32, q * SEQ + tp * 256, [[1, 256]])
                P.op("pe", lambda h, pu_=pu_, j=j, par=par, tp=tp, rhs=rhs: h.matmul(
                    pu_[:, 0:256], lhsT=selin[32 * j:32 * j + 32, par, tp, :], rhs=rhs, start=(tp == 0), stop=(tp == 7),
                    tile_position=(32 * j, 0)), reads=[s_uTq[q], s_sel], writes=[su_])
            ac(lambda h, pu_=pu_, g=g: h.copy(out=UY[:, g, :], in_=pu_[:, 0:256]), [su_], [s_UY[g]], force=False)
            for ri in range(2):
                px, spx = ps[2 + ri], pss[2 + ri]
                P.op("pe", lambda h, px=px, g=g, ri=ri: h.matmul(px[:, 0:256], lhsT=Ht[:, g, ri, :], rhs=UY[:, g, :],
                                                               start=True, stop=True), reads=[s_H, s_UY[g]], writes=[spx])
                dv(lambda h, px=px, g=g, ri=ri: h.tensor_copy(out=XSv(0, ri, g, 1, 256), in_=px[0:64, 0:256]),
                   [spx], [s_X[0]], force=False)
                dv(lambda h, px=px, g=g, ri=ri: h.tensor_copy(out=XSv(64, ri, g, 0, 255), in_=px[64:128, 1:256]),
                   [spx], [s_X[1]], force=False)
        dirs = []
        for (eng, p0, d, cols) in ((DBG.get("scan_f", "dve"), 0, 0, list(range(1, 256))),
                                   (DBG.get("scan_b", "dve"), 64, 1, list(range(254, -1, -1)))):
            s_S = [S(f"S{d}_0"), S(f"S{d}_1")]
            s_w_, s_v_ = S(f"w_{d}"), S(f"v_{d}")
            P.op(eng, lambda h, p0=p0: h.memset(cv(Sf, p0, 64, 0, [[1, 128]]), 0.0), writes=[s_S[0]])
            dirs.append((eng, p0, d, cols, s_S, s_w_, s_v_))
        for i in range(255):
            for (eng, p0, d, cols, s_S, s_w_, s_v_) in dirs:
                col = cols[i]
                pv_, cu = i % 2, (i + 1) % 2
                xcol = cv(XS, p0, 64, col, [[32 * NK, 2], [NK, 32]])
                xcol2 = cv(XS, p0, 64, col, [[0, 2], [32 * NK, 2], [NK, 32]])
                P.op(eng, lambda h, p0=p0, pv_=pv_: h.tensor_tensor(
                    out=cv(st1, p0, 64, 0, [[64, 2], [32, 2], [1, 32]]), in0=cv(TA, p0, 64, 0, [[64, 2], [32, 2], [1, 32]]),
                    in1=cv(Sf, p0, 64, pv_ * 128, [[32, 2], [32, 2], [1, 32]]), op=ALU.mult),
                    reads=[s_S[pv_]], writes=[s_w_], force=DBG.get('scanforce', False))
                P.op(eng, lambda h, p0=p0: h.tensor_tensor(
                    out=cv(tmpA, p0, 64, 0, [[1, 64]]), in0=cv(st1, p0, 64, 0, [[1, 64]]), in1=cv(st1, p0, 64, 64, [[1, 64]]),
                    op=ALU.add), reads=[s_w_], writes=[s_v_], force=DBG.get('scanforce', False))
                P.op(eng, lambda h, p0=p0, cu=cu, xcol2=xcol2: h.tensor_tensor(
                    out=cv(Sf, p0, 64, cu * 128, [[64, 2], [32, 2], [1, 32]]), in0=cv(tmpA, p0, 64, 0, [[0, 2], [32, 2], [1, 32]]),
                    in1=xcol2, op=ALU.add), reads=[s_v_, s_X[d]], writes=[s_S[cu]], force=DBG.get('scanforce', False))
                P.op("act", lambda h, p0=p0, cu=cu, xcol=xcol: h.copy(out=xcol, in_=cv(Sf, p0, 64, cu * 128, [[32, 2], [1, 32]])),
                     reads=[s_S[cu]], writes=[s_hist[d]])
        for g in range(32):
            u = g % 2
            py, spy = ps[4 + u], pss[4 + u]
            P.op("pe", lambda h, py=py, g=g: h.matmul(py[:, 0:256], lhsT=Mt[:, g, :], rhs=UY[:, g, :], start=True, stop=False),
                 reads=[s_M, s_UY[g]], writes=[spy])
            for ri in range(2):
                P.op("pe", lambda h, py=py, g=g, ri=ri: h.matmul(
                    py[:, 0:256], lhsT=Gt[:, g, ri, :], rhs=cv(XS, 0, 128, (ri * 32 + g) * NK, [[1, 256]]),
                    start=False, stop=(ri == 1)), reads=[s_G, s_hist[0], s_hist[1], s_X[0], s_X[1]], writes=[spy])
            ac(lambda h, py=py, g=g: h.copy(out=UY[:, g, :], in_=py[:, 0:256]), [spy], [s_UY[g]], force=False)
        s_Wo = S("Wo")
        for q in range(4):
            P.dma("pool", "abWo", lambda h, q=q: h.dma_start(out=Wo[:, q * 2:(q + 1) * 2, :], in_=W["wout"][:, q * 2:(q + 1) * 2, :]),
                  reads=[], writes=[s_Wo, s_X[0], s_X[1], s_hist[0], s_hist[1]])
        s_yg = [S(f"yg{q}") for q in range(4)]
        ci = 0
        for q in range(4):
            for tau in range(8):
                u = ci % 2
                ci += 1
                pz, spz = ps[u], pss[u]
                for g8 in range(8):
                    P.op("pe", lambda h, pz=pz, tau=tau, g8=g8, q=q: h.matmul(
                        pz[:, 0:256], lhsT=selout[:, tau, g8, :], rhs=UY[:, 8 * q + g8, :], start=(g8 == 0), stop=(g8 == 7)),
                        reads=[s_sel, s_UY[8 * q + g8]], writes=[spz])
                gelu_evac(c, pz, spz, cv(uT, 0, 128, q * SEQ + tau, [[8, 256]]), s_yg[q], s_uTq[q])
        s_sg2 = [S("sg2_0"), S("sg2_1")]
        for tb in range(4):
            for o in range(4):
                pz, spz = ps[2 + o], pss[2 + o]
                for kq in range(4):
                    P.op("pe", lambda h, pz=pz, kq=kq, o=o, tb=tb: h.matmul(
                        pz[:, :], lhsT=gluw[:, kq, o * 128:(o + 1) * 128], rhs=uT[:, kq, tb * 512:(tb + 1) * 512],
                        start=(kq == 0), stop=(kq == 3)), reads=[s_gluw] + s_yg, writes=[spz])
            for o in range(4):
                u = o % 2
                pz, spz = ps[2 + o], pss[2 + o]
                ac(lambda h, pz=pz, o=o, u=u: h.activation(out=sgm[u], in_=pz[:, :], func=AF.Sigmoid,
                                                           bias=vec[:, 12 + o:13 + o], scale=1.0), [spz], [s_sg2[u]], force=False)
                dv(lambda h, o=o, u=u, tb=tb: h.tensor_tensor(out=uT[:, o, tb * 512:(tb + 1) * 512],
                                                              in0=uT[:, o, tb * 512:(tb + 1) * 512], in1=sgm[u], op=ALU.mult),
                   [s_sg2[u], s_yg[o]], [s_yg[o]], force=False)
        P.barrier()
        s_xt2 = [S("xt2_0"), S("xt2_1")]
        s_ost = [S("ost0"), S("ost1")]
        for t in range(16):
            j = t % 2
            r0 = tok0 + t * 128
            P.dma("sp", f"abxt{j}", lambda h, j=j, r0=r0: h.dma_start(out=xt[j], in_=xin[r0:r0 + 128, :]),
                  reads=[xin_slot], writes=[s_xt2[j]])
            for half in range(2):
                pd, sd = ps[half], pss[half]
                for k in range(8):
                    src = yaT if k < 4 else uT
                    P.op("pe", lambda h, k=k, t=t, half=half, pd=pd, src=src: h.matmul(
                        pd[:, :], lhsT=src[:, k % 4, t * 128:(t + 1) * 128], rhs=Wo[:, k, half * 512:(half + 1) * 512],
                        start=(k == 0), stop=(k == 7)), reads=[s_Wo], writes=[sd])
                dv(lambda h, j=j, half=half, pd=pd: h.tensor_tensor(
                    out=ost[j][:, half * 512:(half + 1) * 512], in0=pd[:, :], in1=xt[j][:, half * 512:(half + 1) * 512],
                    op=ALU.add), [sd, s_xt2[j]], [s_ost[j]], force=False)
            P.dma("sp", f"abost{j}", lambda h, j=j, r0=r0: h.dma_start(out=xout[r0:r0 + 128, :], in_=ost[j]),
                  reads=[s_ost[j]], writes=[xout_slot])
        P.barrier()


def gelu_evac(c, pz, spz, dst, s_dst, s_dst2):
    P = c.P
    P.op("act", lambda h: h.activation(out=dst, in_=pz[:, 0:256], func=AF.Gelu_apprx_tanh), reads=[spz], writes=[s_dst, s_dst2])

def build_program(plan=None, ntok=TOK):
    if plan is None:
        plan = []
        for l in range(DEPTH):
            plan.append(("ab" if l % 2 == 0 else "na", l))
            plan.append(("ffn", l))
    nc = bass.Bass("TRN2", target_bir_lowering=False)
    c = setup_ctx(nc)
    P = c.P
    x = nc.dram_tensor("x", [ntok, D], F32, kind="ExternalInput").ap()
    out = nc.dram_tensor("out", [ntok, D], F32, kind="ExternalOutput").ap()
    ident = nc.dram_tensor("ident", [2, 128, 128], F32, kind="ExternalInput").ap()
    wg = nc.dram_tensor("ffn_wg", [DEPTH, NM, 128, 8, 128], F32, kind="ExternalInput").ap()
    wu = nc.dram_tensor("ffn_wu", [DEPTH, NM, 128, 8, 128], F32, kind="ExternalInput").ap()
    wd = nc.dram_tensor("ffn_wd", [DEPTH, FF, D], F32, kind="ExternalInput").ap()
    fnorm = nc.dram_tensor("ffn_norm", [DEPTH, D], F32, kind="ExternalInput").ap()
    mnorm = nc.dram_tensor("mix_norm", [DEPTH, D], F32, kind="ExternalInput").ap()
    na_wqkv = nc.dram_tensor("na_wqkv", [2, 4, 128, 8, 768], F32, kind="ExternalInput").ap()
    na_wout = nc.dram_tensor("na_wout", [2, 128, 8, 1024], F32, kind="ExternalInput").ap()
    na_bias = nc.dram_tensor("na_bias", [2, 4, 128, 4, 14, 64], F32, kind="ExternalInput").ap()
    na_qkg = nc.dram_tensor("na_qkg", [2, 128, 2], F32, kind="ExternalInput").ap()
    abd = {}
    for nm, shp in AB_SHAPES.items():
        abd[nm] = nc.dram_tensor("ab_" + nm, list(shp), F32, kind="ExternalInput").ap()
    scr = [nc.dram_tensor(f"scr{i}", [ntok, D], F32, kind="Internal").ap() for i in range(2)]
    if DBG.get('dump_on'):
        DBG['dump'] = nc.dram_tensor("dbg", [128, 16384], F32, kind="ExternalOutput").ap()
    s_scr = [P.slot("scr0"), P.slot("scr1")]
    s_x = P.slot("x_dram")
    s_out = P.slot("out_dram")
    load_ident(c, ident)
    cur, s_cur = x, s_x
    for si, st in enumerate(plan):
        if si == len(plan) - 1:
            dst, s_dst = out, s_out
        else:
            dst, s_dst = scr[si % 2], s_scr[si % 2]
        kind, l = st
        if kind == "ffn":
            ffn_stage(c, cur, s_cur, dst, s_dst, wg[l], wu[l], wd[l], fnorm[l:l + 1, :], ntok=ntok)
        elif kind == "na":
            i = l // 2
            for sq_ in range(ntok // SEQ):
                na_stage(c, cur, s_cur, dst, s_dst, na_wqkv[i], na_wout[i], na_bias[i], na_qkg[i],
                         mnorm[l:l + 1, :], sq_ * SEQ)
        elif kind == "ab":
            i = l // 2
            Wd_ = {k: (v[i] if k in AB_PER_LAYER else v) for k, v in abd.items()}
            ab_stage(c, cur, s_cur, dst, s_dst, Wd_, mnorm[l:l + 1, :], ntok)
        cur, s_cur = dst, s_dst
    fin = Ins("sp", None, False, None, 0)
    for e in ENGS:
        for i in P.streams[e]:
            if i.is_dma:
                fin.deps.append(i)
    fin.idx = len(P.streams["sp"])
    P.streams["sp"].append(fin)
    P.emit()
    return nc


AB_PER_LAYER = ("win", "wout", "gluw", "vec", "cw", "lam", "B", "C", "dcol")
AB_SHAPES = {
    "win": (2, 12, 128, 8, 128), "wout": (2, 128, 8, 1024), "gluw": (2, 128, 4, 512), "vec": (2, 128, 20),
    "cw": (2, 128, 4, 31), "lam": (2, 128, 3, 32), "B": (2, 128, 2, 32, 16), "C": (2, 128, 2, 32, 16),
    "dcol": (2, 128, 32), "selin": (128, 2, 8, 128), "selout": (128, 8, 8, 128), "mask": (128, 2, 128),
    "etab": (128, 32), "ones": (128, 128),
}


def ab_host_layout(inputs):
    g = np.ascontiguousarray
    f = lambda k: np.asarray(inputs[k], dtype=np.float32)
    d = {}
    d["win"] = g(f("ab_w_in").reshape(2, 8, 128, 12, 128).transpose(0, 3, 2, 1, 4))
    d["wout"] = g(f("ab_w_out").reshape(2, 8, 128, 1024).transpose(0, 2, 1, 3))
    d["gluw"] = g(f("ssm_glu_w").reshape(2, 4, 128, 512).transpose(0, 2, 1, 3))
    vec = np.zeros((2, 128, 20), np.float32)
    for j, k in enumerate(("conv_b", "conv_ln_g", "conv_ln_b", "ssm_glu_b")):
        vec[:, :, 4 * j:4 * j + 4] = f(k).reshape(2, 4, 128).transpose(0, 2, 1)
    d["vec"] = vec
    d["cw"] = g(f("conv_w").reshape(2, 31, 4, 128).transpose(0, 3, 2, 1))
    lam = np.empty((2, 2, 64, 3, 32), np.float32)
    lam[:, :, :, 0] = f("ssm_lambda_re").transpose(0, 1, 3, 2)
    lam[:, :, :, 1] = f("ssm_lambda_im").transpose(0, 1, 3, 2)
    lam[:, :, :, 2] = f("ssm_log_step")[:, :, None, :]
    d["lam"] = g(lam.reshape(2, 128, 3, 32))
    B = np.stack([f("ssm_b_re"), f("ssm_b_im")], axis=1)
    d["B"] = g(B.transpose(0, 2, 4, 1, 3, 5).reshape(2, 128, 2, 32, 16))
    C = np.stack([f("ssm_c_re"), f("ssm_c_im")], axis=1)
    d["C"] = g(C.transpose(0, 2, 5, 1, 3, 4).reshape(2, 128, 2, 32, 16))
    dsk = f("ssm_d").reshape(2, 32, 16)
    d["dcol"] = g(np.broadcast_to(dsk.transpose(0, 2, 1)[:, None], (2, 8, 16, 32)).reshape(2, 128, 32))
    selin = np.zeros((4, 2, 16, 2, 8, 8, 16), np.float32)
    selout = np.zeros((8, 16, 8, 8, 8, 16), np.float32)
    for cc in range(16):
        for t in range(8):
            selin[:, 0, cc, 0, t, t, cc] = 1.0
            selin[:, 1, cc, 1, t, t, cc] = 1.0
            for g8 in range(8):
                selout[t, cc, t, g8, g8, cc] = 1.0
    d["selin"] = selin.reshape(128, 2, 8, 128)
    d["selout"] = selout.reshape(128, 8, 8, 128)
    tp = np.repeat(np.arange(8), 16)
    d["mask"] = g(np.stack([(tp[:, None] <= tp[None, :]), (tp[:, None] >= tp[None, :])], axis=1).astype(np.float32))
    tt = np.arange(8, dtype=np.float32)
    ef = np.concatenate([-tt, tt, tt + 1, 7 - tt])
    eb = np.concatenate([tt, -tt, 8 - tt, tt])
    d["etab"] = g(np.concatenate([np.broadcast_to(ef, (64, 32)), np.broadcast_to(eb, (64, 32))], axis=0))
    d["ones"] = np.full((128, 128), 1.0 / 512, np.float32)
    return {"ab_" + k: v for k, v in d.items()}


def host_layout(inputs):
    g = np.ascontiguousarray
    d = {}
    bd = np.zeros((128, 128), np.float32)
    bd[:64, :64] = 1.0
    bd[64:, 64:] = 1.0
    d["ident"] = np.stack([np.eye(128, dtype=np.float32), bd])
    for nm, key in (("ffn_wg", "ffn_w_gate"), ("ffn_wu", "ffn_w_up")):
        w = np.asarray(inputs[key], dtype=np.float32).reshape(DEPTH, 8, 128, NM, 128)
        d[nm] = g(w.transpose(0, 3, 2, 1, 4))
    d["ffn_wd"] = g(np.asarray(inputs["ffn_w_down"], dtype=np.float32))
    d["ffn_norm"] = g(np.asarray(inputs["ffn_norm"], dtype=np.float32))
    d["mix_norm"] = g(np.asarray(inputs["mix_norm"], dtype=np.float32))
    wqkv = np.asarray(inputs["na_w_qkv"], dtype=np.float32).reshape(2, 8, 128, 3, 4, 256)
    d["na_wqkv"] = g(wqkv.transpose(0, 4, 2, 1, 3, 5).reshape(2, 4, 128, 8, 768))
    d["na_wout"] = g(np.asarray(inputs["na_w_out"], dtype=np.float32).reshape(2, 8, 128, 1024).transpose(0, 2, 1, 3))
    rpb = np.asarray(inputs["na_rpb"], dtype=np.float32)
    qc = np.arange(64)[None, :]
    kc = np.arange(64)[:, None]
    cidx = np.clip(kc - qc, -15, 15) + 15
    cstart = np.clip(qc - 8, 0, 48)
    cmask = (kc >= cstart) & (kc < cstart + 16)
    tab = np.empty((2, 16, 14, 2, 64, 64), np.float32)
    for rho in range(14):
        for jj in range(2):
            tab[:, :, rho, jj] = np.where(cmask[None, None], rpb[:, :, rho + jj][:, :, cidx], np.float32(-30000.0))
    tab = tab.reshape(2, 4, 4, 14, 2, 64, 64).transpose(0, 1, 4, 5, 2, 3, 6).reshape(2, 4, 128, 4, 14, 64)
    d["na_bias"] = g(tab)
    qg = np.asarray(inputs["na_q_norm"], dtype=np.float32)
    kg = np.asarray(inputs["na_k_norm"], dtype=np.float32)
    d["na_qkg"] = g(np.stack([np.tile(qg, (1, 2)), np.tile(kg, (1, 2))], axis=-1))
    d.update(ab_host_layout(inputs))
    return d


def kernel(**inputs):
    x = np.asarray(inputs["x"], dtype=np.float32)
    shared = host_layout(inputs)
    nc = build_program()
    in_maps = []
    for i in range(NCORES):
        m = dict(shared)
        m["x"] = np.ascontiguousarray(x[2 * i:2 * i + 2].reshape(TOK, D))
        in_maps.append(m)
    res = run_bass_kernel_spmd(nc, in_maps, core_ids=list(range(NCORES)))
    outs = [res.results[i]["out"].reshape(2, SEQ, D) for i in range(NCORES)]
    return np.concatenate(outs, axis=0).astype(np.float32)
```

```python
import math
import numpy as np
import concourse.bass as bass
import concourse.mybir as mybir
from concourse.bass_utils import run_bass_kernel_spmd

F32 = mybir.dt.float32
BF16 = mybir.dt.bfloat16
I32 = mybir.dt.int32
AF = mybir.ActivationFunctionType
ALU = mybir.AluOpType
AX = mybir.AxisListType

DBG = {}
NCORES = 8
D = 1024
SEQ = 2048
TOK = 2 * SEQ
FF = 2816
NM = FF // 128
DEPTH = 4
EPS = 1e-6


class Slot:
    __slots__ = ("name", "lw", "rs")

    def __init__(self, name):
        self.name = name
        self.lw = None
        self.rs = []


class Ins:
    __slots__ = ("eng", "fn", "deps", "is_dma", "sem", "semval", "inc", "idx", "force")

    def __init__(self, eng, fn, is_dma, sem, semval):
        self.eng = eng
        self.fn = fn
        self.deps = []
        self.is_dma = is_dma
        self.sem = sem
        self.semval = semval
        self.inc = False
        self.idx = -1
        self.force = False


ENGS = ("pe", "act", "dve", "pool", "sp")


class Prog:
    def __init__(self, nc):
        self.nc = nc
        self.streams = {e: [] for e in ENGS}
        self.dma_sems = {}
        self.dma_cnt = {}
        self.slots = []

    def slot(self, name):
        s = Slot(name)
        self.slots.append(s)
        return s

    def slots_n(self, name, n):
        return [self.slot(f"{name}{i}") for i in range(n)]

    def _add(self, ins, reads, writes):
        e = ins.eng
        best = {}
        dl = []

        def need(p):
            if p is None or p is ins:
                return
            if p.is_dma:
                if p not in dl:
                    dl.append(p)
            elif ins.is_dma or p.eng != e or ins.force:
                q = best.get(p.eng)
                if q is None or q.idx < p.idx:
                    best[p.eng] = p

        for s in reads:
            need(s.lw)
        for s in writes:
            need(s.lw)
            for r in s.rs:
                need(r)
        ins.deps = dl + list(best.values())
        for s in reads:
            rs = s.rs
            if rs and (not ins.is_dma) and (not rs[-1].is_dma) and rs[-1].eng == e:
                rs[-1] = ins
            else:
                rs.append(ins)
        for s in writes:
            s.lw = ins
            s.rs = []
        ins.idx = len(self.streams[e])
        self.streams[e].append(ins)
        return ins

    def op(self, eng, fn, reads=(), writes=(), force=False):
        ins = Ins(eng, fn, False, None, 0)
        ins.force = force
        return self._add(ins, reads, writes)

    def dma(self, eng, semname, fn, reads=(), writes=()):
        if semname not in self.dma_sems:
            self.dma_sems[semname] = self.nc.alloc_semaphore(name="d_" + semname)
            self.dma_cnt[semname] = 0
        self.dma_cnt[semname] += 16
        return self._add(Ins(eng, fn, True, self.dma_sems[semname], self.dma_cnt[semname]), reads, writes)

    def barrier(self):
        lasts = []
        for e in ENGS:
            st = self.streams[e]
            if st:
                lasts.append(st[-1])
        dmas = [i for e in ENGS for i in self.streams[e] if i.is_dma and not getattr(i, "_barr", False)]
        for e in ENGS:
            ins = Ins(e, None, False, None, 0)
            for p in lasts:
                if p.eng != e and not p.is_dma and p.fn is not None:
                    ins.deps.append(p)
            for p in dmas:
                ins.deps.append(p)
            ins.idx = len(self.streams[e])
            self.streams[e].append(ins)
        for s in self.slots:
            s.lw = None
            s.rs = []

    def emit(self):
        nc = self.nc
        for e in ENGS:
            for ins in self.streams[e]:
                for p in ins.deps:
                    if not p.is_dma:
                        p.inc = True
        rank = {}
        sems = {}
        for e in ENGS:
            sems[e] = nc.alloc_semaphore(name="s_" + e)
            c = 0
            last_real = None
            for ins in self.streams[e]:
                if ins.fn is not None and not ins.is_dma:
                    last_real = ins
                if ins.inc:
                    assert ins.fn is not None and not ins.is_dma
                    c += 1
                    rank[ins] = c
        handles = {"pe": nc.tensor, "act": nc.scalar, "dve": nc.vector, "pool": nc.gpsimd, "sp": nc.sync}
        streams = self.streams

        def replay(e, h):
            known = {}
            for ins in streams[e]:
                for p in ins.deps:
                    if p.is_dma:
                        key, val, sem = ("d", id(p.sem)), p.semval, p.sem
                    else:
                        key, val, sem = ("c", p.eng), rank[p], sems[p.eng]
                    if known.get(key, 0) >= val:
                        continue
                    known[key] = val
                    h.wait_ge(sem, val)
                if ins.fn is None:
                    continue
                r = ins.fn(h)
                if ins.is_dma:
                    r.then_inc(ins.sem, 16)
                elif ins.inc:
                    r.then_inc(sems[e], 1)

        with nc.Block() as block:
            @block.tensor
            def _(h):
                replay("pe", h)

            @block.scalar
            def _(h):
                replay("act", h)

            @block.vector
            def _(h):
                replay("dve", h)

            @block.gpsimd
            def _(h):
                replay("pool", h)

            @block.sync
            def _(h):
                replay("sp", h)


class Arena:
    def __init__(self, nc, name, nelem, dtype):
        self.t = nc.alloc_sbuf_tensor(name, [128, nelem], dtype)
        self.n = nelem
        self.off = 0

    def reset(self):
        self.off = 0

    def take(self, *shape):
        n = int(np.prod(shape))
        assert self.off + n <= self.n, (self.off, n, self.n)
        ap = self.t[:, self.off:self.off + n]
        self.off += n
        if len(shape) == 2:
            ap = ap.rearrange("p (a b) -> p a b", b=shape[1])
        elif len(shape) == 3:
            ap = ap.rearrange("p (a b c) -> p a b c", b=shape[1], c=shape[2])
        elif len(shape) == 4:
            ap = ap.rearrange("p (a b c d) -> p a b c d", b=shape[1], c=shape[2], d=shape[3])
        return ap


class Ctx:
    pass


def setup_ctx(nc):
    c = Ctx()
    c.nc = nc
    c.P = Prog(nc)
    c.fa = Arena(nc, "fa", 12800, F32)
    c.ba = Arena(nc, "ba", 78848, BF16)
    c.ps = [nc.alloc_psum_tensor(f"ps{i}", [128, 512], F32) for i in range(7)]
    c.psb = nc.alloc_psum_tensor("psb", [128, 1024], BF16)
    c.ps_slots = [c.P.slot(f"ps{i}") for i in range(7)]
    c.psb_slot = c.P.slot("psb")
    c.ident = nc.alloc_sbuf_tensor("ident_sb", [128, 128], BF16)
    c.ident_slot = c.P.slot("ident")
    c.bd = nc.alloc_sbuf_tensor("bd_sb", [128, 128], BF16)
    c.bd_slot = c.P.slot("bd")
    c.cst = nc.alloc_sbuf_tensor("cst_sb", [128, 8], F32)
    c.cst_slot = c.P.slot("cst")
    c.psbs = [(c.psb[:, :], c.psb_slot), (c.ps[6][:, :].bitcast(BF16), c.ps_slots[6])]
    c.nt_count = 0
    return c


def load_ident(c, ident_dram):
    P = c.P
    P.dma("pool", "ident", lambda h: h.dma_start(out=c.ident[:], in_=ident_dram[0]), writes=[c.ident_slot])
    P.dma("pool", "ident", lambda h: h.dma_start(out=c.bd[:], in_=ident_dram[1]), writes=[c.bd_slot])
    for i, v in enumerate((64.0 * EPS, EPS, 1e-5, 0.0, 1.0, math.pi / 2)):
        P.op("dve", lambda h, i=i, v=v: h.memset(c.cst[:, i:i + 1], v), writes=[c.cst_slot])


def norm_a(c, xt, xt_slot, gain_bc, gain_slot, xn, xn_slot, ss, ss_slot):
    P = c.P
    P.op("act", lambda h: h.activation(out=xn, in_=xt, func=AF.Square, accum_out=ss[:, 0:1]),
         reads=[xt_slot], writes=[xn_slot, ss_slot])
    P.op("dve", lambda h: h.tensor_scalar(out=ss[:, 1:2], in0=ss[:, 0:1], scalar1=1.0 / D, scalar2=EPS,
                                           op0=ALU.mult, op1=ALU.add), reads=[ss_slot], writes=[ss_slot])
    P.op("act", lambda h: h.activation(out=ss[:, 2:3], in_=ss[:, 1:2], func=AF.Sqrt), reads=[ss_slot], writes=[ss_slot])
    P.op("dve", lambda h: h.reciprocal(out=ss[:, 3:4], in_=ss[:, 2:3]), reads=[ss_slot], writes=[ss_slot])
    P.op("dve", lambda h: h.scalar_tensor_tensor(out=xn, in0=xt, scalar=ss[:, 3:4], in1=gain_bc,
                                                  op0=ALU.mult, op1=ALU.mult),
         reads=[xt_slot, ss_slot, gain_slot], writes=[xn_slot], force=True)


def norm_b(c, xn, xn_slot, hT, hT_slot, col0):
    P = c.P
    pb_, pb_slot = c.psbs[c.nt_count % 2]
    c.nt_count += 1
    for k in range(8):
        P.op("pe", lambda h, k=k: h.transpose(out=pb_[:, k * 128:(k + 1) * 128], in_=xn[:, k * 128:(k + 1) * 128],
                                              identity=c.ident[:]),
             reads=[xn_slot, c.ident_slot], writes=[pb_slot])
    P.op("act", lambda h: h.copy(out=hT[:, :, col0:col0 + 128],
                                 in_=pb_.rearrange("p (k t) -> p k t", t=128)),
         reads=[pb_slot], writes=[hT_slot])


def norm_transpose(c, xt, xt_slot, gain_bc, gain_slot, xn, xn_slot, ss, ss_slot, hT, hT_slot, col0):
    norm_a(c, xt, xt_slot, gain_bc, gain_slot, xn, xn_slot, ss, ss_slot)
    norm_b(c, xn, xn_slot, hT, hT_slot, col0)


def ffn_stage(c, xin, xin_slot, xout, xout_slot, wg, wu, wd, gain_row, ntok=TOK, TB=1024):
    P = c.P
    fa, ba = c.fa, c.ba
    fa.reset()
    ba.reset()
    NT = TB // 128
    Wd = ba.take(NM, 1024)
    hTs = [ba.take(8, TB) for _ in range(2)]
    actT = ba.take(NM, TB)
    wgb = [ba.take(8, 128) for _ in range(2)]
    wub = [ba.take(8, 128) for _ in range(2)]
    xn = [ba.take(1, 1024)[:, 0, :] for _ in range(2)]
    xt = [fa.take(1, 1024)[:, 0, :] for _ in range(4)]
    ost = [fa.take(1, 1024)[:, 0, :] for _ in range(2)]
    gbc = fa.take(1, 1024)[:, 0, :]
    sg = [fa.take(1, 512)[:, 0, :] for _ in range(2)]
    ss = [fa.take(1, 4)[:, 0, :] for _ in range(2)]

    s_Wd = P.slot("Wd")
    s_hT = [[P.slot(f"hT{j}_{t}") for t in range(NT)] for j in range(2)]
    s_act = [[P.slot(f"act{m}_{h}") for h in range(TB // 512)] for m in range(NM)]
    s_wg = P.slots_n("wg", 2)
    s_wu = P.slots_n("wu", 2)
    s_xn = P.slots_n("xn", 2)
    s_xt = P.slots_n("xt", 4)
    s_ost = P.slots_n("ost", 2)
    s_gbc = P.slot("gbc")
    s_sg = P.slots_n("sg", 2)
    s_ss = P.slots_n("ss", 2)

    P.dma("sp", "gbc", lambda h: h.dma_start(out=gbc, in_=gain_row.partition_broadcast(128)), writes=[s_gbc])
    wdv = wd.rearrange("(m p) o -> p m o", p=128)
    nblk = ntok // TB
    st = {"w": 0, "o": 0, "n": 0, "u": 0, "x": 0}

    pend = {}

    def norm_tile_a(b, t):
        j = st["n"] % 2
        st["n"] += 1
        xi = st["x"] % 2
        st["x"] += 1
        r0 = b * TB + t * 128
        P.dma("sp", f"xt{xi}", lambda h: h.dma_start(out=xt[xi], in_=xin[r0:r0 + 128, :]),
              reads=[xin_slot], writes=[s_xt[xi]])
        norm_a(c, xt[xi], s_xt[xi], gbc, s_gbc, xn[j], s_xn[j], ss[j], s_ss[j])
        pend[(b, t)] = j

    def norm_tile_b(b, t):
        j = pend.pop((b, t))
        norm_b(c, xn[j], s_xn[j], hTs[b % 2], s_hT[b % 2][t], t * 128)

    def norm_tile(b, t):
        norm_tile_a(b, t)
        norm_tile_b(b, t)

    for t in range(NT):
        norm_tile(0, t)
    for q in range(2):
        P.dma("pool", "Wd", lambda h, q=q: h.dma_start(out=Wd[:, q * 11:(q + 1) * 11, :], in_=wdv[:, q * 11:(q + 1) * 11, :]),
              writes=[s_Wd])
    for b in range(nblk):
        hT = hTs[b % 2]
        shT = s_hT[b % 2]
        for m in range(NM):
            j = st["w"] % 2
            st["w"] += 1
            P.dma("pool", f"wg{j}", lambda h, j=j, m=m: h.dma_start(out=wgb[j], in_=wg[m]), writes=[s_wg[j]])
            P.dma("pool", f"wu{j}", lambda h, j=j, m=m: h.dma_start(out=wub[j], in_=wu[m]), writes=[s_wu[j]])
            for hh in range(TB // 512):
                u = st["u"] % 2
                st["u"] += 1
                pg, pu = c.ps[2 * u], c.ps[2 * u + 1]
                sgs, sus = c.ps_slots[2 * u], c.ps_slots[2 * u + 1]
                hs = shT[hh * 4:(hh + 1) * 4]
                for k in range(8):
                    P.op("pe", lambda h, j=j, k=k, hh=hh, pg=pg, hT=hT: h.matmul(
                        pg[:, :], lhsT=wgb[j][:, k, :], rhs=hT[:, k, hh * 512:(hh + 1) * 512],
                        start=(k == 0), stop=(k == 7)), reads=[s_wg[j]] + hs, writes=[sgs])
                for k in range(8):
                    P.op("pe", lambda h, j=j, k=k, hh=hh, pu=pu, hT=hT: h.matmul(
                        pu[:, :], lhsT=wub[j][:, k, :], rhs=hT[:, k, hh * 512:(hh + 1) * 512],
                        start=(k == 0), stop=(k == 7)), reads=[s_wu[j]] + hs, writes=[sus])
                P.op("act", lambda h, u=u, pg=pg: h.activation(out=sg[u], in_=pg[:, :], func=AF.Silu),
                     reads=[sgs], writes=[s_sg[u]])
                P.op("dve", lambda h, u=u, pu=pu, m=m, hh=hh: h.tensor_tensor(
                    out=actT[:, m, hh * 512:(hh + 1) * 512], in0=sg[u], in1=pu[:, :], op=ALU.mult),
                    reads=[s_sg[u], sus], writes=[s_act[m][hh]])
            if b + 1 < nblk and m % 2 == 1:
                i_ = m // 2
                if i_ < NT:
                    norm_tile_a(b + 1, i_)
                if 1 <= i_ <= NT:
                    norm_tile_b(b + 1, i_ - 1)
        for t in range(NT):
            o = st["o"] % 2
            st["o"] += 1
            xi = 2 + (t % 2)
            r0 = b * TB + t * 128
            P.dma("sp", f"xt{xi}", lambda h, xi=xi, r0=r0: h.dma_start(out=xt[xi], in_=xin[r0:r0 + 128, :]),
                  reads=[xin_slot], writes=[s_xt[xi]])
            for half in range(2):
                pd = c.ps[4 + half]
                sd = c.ps_slots[4 + half]
                for m in range(NM):
                    P.op("pe", lambda h, m=m, t=t, half=half, pd=pd: h.matmul(
                        pd[:, :], lhsT=actT[:, m, t * 128:(t + 1) * 128], rhs=Wd[:, m, half * 512:(half + 1) * 512],
                        start=(m == 0), stop=(m == NM - 1)),
                        reads=[s_act[m][t // 4], s_Wd], writes=[sd])
                P.op("dve", lambda h, o=o, xi=xi, half=half, pd=pd: h.tensor_tensor(
                    out=ost[o][:, half * 512:(half + 1) * 512], in0=pd[:, :], in1=xt[xi][:, half * 512:(half + 1) * 512],
                    op=ALU.add), reads=[sd, s_xt[xi]], writes=[s_ost[o]])
            P.dma("sp", f"ost{o}", lambda h, o=o, r0=r0: h.dma_start(out=xout[r0:r0 + 128, :], in_=ost[o]),
                  reads=[s_ost[o]], writes=[xout_slot])
    P.barrier()


def cv_ps(pt, off, dims, p0=0, npart=128):
    a = pt[:, :]
    n = a.ap[0][0]
    return bass.AP(tensor=a.tensor, offset=a.offset + p0 * n + off, ap=[[n, npart]] + [list(d) for d in dims])


def _na_runs():
    out = []
    rs = [min(max(r - 4, 0), 24) for r in range(32)]
    for hf in range(2):
        runs = []
        for par in range(2):
            for ti in range(16):
                row0 = 2 * ti + par
                if row0 + 1 > 31:
                    continue
                for grp in range(2):
                    rows = [r for r in range(16 * hf + 8 * grp, 16 * hf + 8 * grp + 8)
                            if rs[r] % 2 == par and rs[r] <= row0 <= rs[r] + 6]
                    while rows:
                        if len(rows) == 1:
                            run, rows = rows, []
                            st = 1
                        else:
                            st = rows[1] - rows[0]
                            k = 2
                            while k < len(rows) and rows[k] - rows[k - 1] == st:
                                k += 1
                            run, rows = rows[:k], rows[k:]
                        runs.append((par, ti, grp, run[0], st, len(run)))
        out.append(runs)
    return out


NA_RUNS = _na_runs()

def na_stage(c, xin, xin_slot, xout, xout_slot, wqkv, wout, biasT, qkg, gain_row, tok0):
    P = c.P
    fa, ba = c.fa, c.ba
    fa.reset()
    ba.reset()
    hT = ba.take(8, SEQ)
    attnT = ba.take(8, SEQ)
    qT = ba.take(2, SEQ)
    kT = ba.take(2, SEQ)
    Vpad = ba.take(2, 16, 4, 128)
    EB = ba.take(4, 14, 64)
    cpad = ba.take(3, 128)
    zrhs = ba.take(1, 512)[:, 0, :]
    bT = ba.take(4, 14, 64)
    wq = ba.take(8, 768)
    PT = [ba.take(1, 512)[:, 0, :] for _ in range(3)]
    xn = [ba.take(1, 1024)[:, 0, :] for _ in range(2)]
    sq = [ba.take(1, 512)[:, 0, :] for _ in range(2)]
    xt = [fa.take(1, 1024)[:, 0, :] for _ in range(2)]
    ost = [fa.take(1, 1024)[:, 0, :] for _ in range(2)]
    gbc = fa.take(1, 1024)[:, 0, :]
    raw = [fa.take(1, 512)[:, 0, :] for _ in range(2)]
    rinv = [fa.take(1, 512)[:, 0, :] for _ in range(2)]
    ss = [fa.take(1, 4)[:, 0, :] for _ in range(2)]
    rec = [fa.take(1, 2)[:, 0, :] for _ in range(2)]
    gq = fa.take(1, 2)[:, 0, :]

    s_hT = P.slots_n("n_hT", 16)
    s_attnT = P.slots_n("n_attnT", 8)
    s_qT = P.slot("n_qT")
    s_kT = P.slot("n_kT")
    s_V = P.slot("n_V")
    s_bT = P.slot("n_bT")
    s_wq = P.slot("n_wq")
    s_PT = P.slots_n("n_PT", 3)
    s_EB = P.slot("n_EB")
    s_cpad = P.slot("n_cpad")
    s_xn = P.slots_n("n_xn", 2)
    s_sq = P.slots_n("n_sq", 2)
    s_xt = P.slots_n("n_xt", 2)
    s_ost = P.slots_n("n_ost", 2)
    s_gbc = P.slot("n_gbc")
    s_raw = P.slots_n("n_raw", 2)
    s_rinv = P.slots_n("n_rinv", 2)
    s_ss = P.slots_n("n_ss", 2)
    s_rec = P.slots_n("n_rec", 2)
    s_gq = P.slot("n_gq")
    ps, pss = c.ps, c.ps_slots

    P.dma("sp", "gbc", lambda h: h.dma_start(out=gbc, in_=gain_row.partition_broadcast(128)), writes=[s_gbc])
    P.dma("sp", "gq", lambda h: h.dma_start(out=gq, in_=qkg), writes=[s_gq])
    P.op("dve", lambda h: h.memset(Vpad[:, :, :, :, :].rearrange("p a b c d -> p (a b c d)"), 0.0), writes=[s_V])
    P.op("dve", lambda h: h.memset(cpad[:, :, :].rearrange("p a b -> p (a b)"), 0.0), writes=[s_cpad])
    P.op("dve", lambda h: h.memset(cpad[:, 0, 0:64], 1.0), writes=[s_cpad])
    P.op("dve", lambda h: h.memset(cpad[:, 1, 64:128], 1.0), writes=[s_cpad])
    P.op("dve", lambda h: h.memset(zrhs, 0.0), writes=[s_cpad])

    for t in range(16):
        j = t % 2
        r0 = tok0 + t * 128
        P.dma("sp", f"nxt{j}", lambda h, j=j, r0=r0: h.dma_start(out=xt[j], in_=xin[r0:r0 + 128, :]),
              reads=[xin_slot], writes=[s_xt[j]])
        norm_a(c, xt[j], s_xt[j], gbc, s_gbc, xn[j], s_xn[j], ss[j], s_ss[j])
        if t >= 1:
            norm_b(c, xn[1 - j], s_xn[1 - j], hT, s_hT[t - 1], (t - 1) * 128)
    norm_b(c, xn[1], s_xn[1], hT, s_hT[15], 15 * 128)

    Wo = hT[:, :, 0:1024]
    s_Wo = P.slot("n_Wo")
    ui = 0
    for hg in DBG.get('hgs', range(4)):
        for part in range(3):
            P.dma("pool", "nwq", lambda h, hg=hg, part=part: h.dma_start(
                out=wq[:, :, part * 256:(part + 1) * 256], in_=wqkv[hg][:, :, part * 256:(part + 1) * 256]),
                writes=[s_wq])
        P.dma("pool", "nbT", lambda h, hg=hg: h.dma_start(out=bT, in_=biasT[hg]), writes=[s_bT])
        qk_units = [(part, cc, tb) for part in DBG.get('qk', range(2)) for cc in range(2) for tb in range(4)]
        ubase = ui

        def qk_front(n):
            part, cc, tb = qk_units[n]
            u = (ubase + n) % 2
            pq, spq = ps[u], pss[u]
            for k in range(8):
                P.op("pe", lambda h, k=k: h.matmul(
                    pq[:, :], lhsT=wq[:, k, part * 256 + cc * 128: part * 256 + cc * 128 + 128],
                    rhs=hT[:, k, tb * 512:(tb + 1) * 512], start=(k == 0), stop=(k == 7)),
                    reads=[s_wq] + s_hT[tb * 4:(tb + 1) * 4], writes=[spq])
            P.op("act", lambda h: h.activation(out=sq[u], in_=pq[:, :], func=AF.Square), reads=[spq], writes=[s_sq[u]])
            P.op("dve", lambda h: h.tensor_copy(out=raw[u], in_=pq[:, :]), reads=[spq, s_sq[u]], writes=[s_raw[u]])

        def qk_back(n):
            part, cc, tb = qk_units[n]
            dst, s_dst = (qT, s_qT) if part == 0 else (kT, s_kT)
            u = (ubase + n) % 2
            pn, spn = ps[2 + u], pss[2 + u]
            P.op("pe", lambda h: h.matmul(pn[:, :], lhsT=c.bd[:], rhs=sq[u], start=True, stop=True),
                 reads=[s_sq[u], c.bd_slot], writes=[spn])
            if part == 0:
                P.op("act", lambda h: h.activation(out=rinv[u], in_=pn[:, :], func=AF.Ln, bias=c.cst[:, 0:1], scale=1.0),
                     reads=[spn, c.cst_slot], writes=[s_rinv[u]])
            else:
                P.op("act", lambda h: h.activation(out=rinv[u], in_=pn[:, :], func=AF.Ln, bias=c.cst[:, 1:2], scale=1.0 / 64),
                     reads=[spn, c.cst_slot], writes=[s_rinv[u]])
            P.op("act", lambda h: h.activation(out=rinv[u], in_=rinv[u], func=AF.Exp, scale=-0.5),
                 reads=[s_rinv[u]], writes=[s_rinv[u]])
            P.op("dve", lambda h: h.scalar_tensor_tensor(
                out=dst[:, cc, tb * 512:(tb + 1) * 512], in0=raw[u], scalar=gq[:, part:part + 1], in1=rinv[u],
                op0=ALU.mult, op1=ALU.mult), reads=[s_raw[u], s_rinv[u], s_gq], writes=[s_dst])

        for n in range(len(qk_units) + 1):
            if n < len(qk_units):
                qk_front(n)
            if n >= 1:
                qk_back(n - 1)
        ui += len(qk_units)
        for par in DBG.get('vpar', range(2)):
            for i in range(16 - par):
                u = ui % 2
                ui += 1
                pv, spv = ps[u], pss[u]
                t0 = par * 64 + i * 128
                for k in range(8):
                    P.op("pe", lambda h, k=k, t0=t0, pv=pv: h.matmul(
                        pv[:, 0:256], lhsT=hT[:, k, t0:t0 + 128], rhs=wq[:, k, 512:768], start=(k == 0), stop=(k == 7)),
                        reads=[s_wq] + s_hT[t0 // 128:(t0 + 127) // 128 + 1], writes=[spv])
                for hp in range(2):
                    P.op("act", lambda h, par=par, i=i, pv=pv, hp=hp: h.copy(
                        out=cv(Vpad, 0, 128, ((par * 16 + i) * 4 + hp) * 128 + hp * 64, [[256, 2], [1, 64]]),
                        in_=cv_ps(pv, hp * 64, [[128, 2], [1, 64]])), reads=[spv], writes=[s_V])
        if hg == 3:
            for q in range(4):
                P.dma("pool", "nWo", lambda h, q=q: h.dma_start(out=Wo[:, q * 2:(q + 1) * 2, :], in_=wout[:, q * 2:(q + 1) * 2, :]),
                      writes=[s_Wo] + s_hT)
        P.op("act", lambda h: h.activation(out=EB[:, :, :, :].rearrange("p a b c -> p (a b c)"),
                                           in_=bT[:, :, :, :].rearrange("p a b c -> p (a b c)"), func=AF.Exp),
             reads=[s_bT], writes=[s_EB])
        for cc in range(2):
            for hf in range(2):
                for b in range(4):
                    P.op("pe", lambda h, b=b: h.matmul(ps[b][:, :], lhsT=cpad[:, 2, :], rhs=zrhs, start=True, stop=True,
                                                       skip_group_check=True), reads=[s_cpad], writes=[pss[b]])
                units = [(par, ti, grp, r0_, st_, n_, hp) for (par, ti, grp, r0_, st_, n_) in NA_RUNS[hf] for hp in range(2)]
                LA = 2

                def emit_front(idx, cc=cc):
                    (par, ti, grp, r0_, st_, n_, hp) = units[idx]
                    row0 = 2 * ti + par
                    hh = cc * 2 + hp
                    pb = hp * 64
                    w3 = idx % 3
                    sc, ssc = ps[4 + w3], pss[4 + w3]
                    nq = 64 * n_
                    rho0 = row0 - r0_ + 7
                    P.op("pe", lambda h: h.matmul(
                        sc[:, 0:nq], lhsT=kT[pb:pb + 64, cc, row0 * 64: row0 * 64 + 128],
                        rhs=cv(qT, pb, 64, cc * SEQ + r0_ * 64, [[st_ * 64, n_], [1, 64]]), start=True, stop=True),
                        reads=[s_kT, s_qT], writes=[ssc])
                    P.op("act", lambda h: h.activation(out=PT[w3][:, 0:nq], in_=sc[:, 0:nq], func=AF.Exp),
                         reads=[ssc], writes=[s_PT[w3]])
                    P.op("dve", lambda h: h.tensor_tensor(
                        out=cv(PT[w3], 0, 128, 0, [[64, n_], [1, 64]]), in0=cv(PT[w3], 0, 128, 0, [[64, n_], [1, 64]]),
                        in1=cv(EB, 0, 128, hh * 896 + rho0 * 64, [[-st_ * 64, n_], [1, 64]]), op=ALU.mult),
                        reads=[s_PT[w3], s_EB], writes=[s_PT[w3]])

                def emit_back(idx, cc=cc, hf=hf):
                    (par, ti, grp, r0_, st_, n_, hp) = units[idx]
                    hh = cc * 2 + hp
                    w3 = idx % 3
                    nq = 64 * n_
                    c0 = (r0_ - 16 * hf - 8 * grp) * 64
                    for (bank, lw_) in ((grp, Vpad[:, par, ti, hh, :]), (2 + grp, cpad[:, hp, :])):
                        P.op("pe", lambda h, bank=bank, lw_=lw_: h.matmul(
                            cv_ps(ps[bank], c0, [[st_ * 64, n_], [1, 64]]), lhsT=lw_, rhs=PT[w3][:, 0:nq], start=False, stop=True,
                            skip_group_check=True), reads=[s_PT[w3], s_V, s_cpad], writes=[pss[bank]])

                for idx in range(len(units) + LA):
                    if idx < len(units):
                        emit_front(idx)
                    if idx - LA >= 0:
                        emit_back(idx - LA)
                for grp in range(2):
                    u = grp
                    P.op("act", lambda h, u=u, grp=grp: h.activation(out=rinv[u], in_=ps[2 + grp][:, :], func=AF.Ln),
                         reads=[pss[2 + grp]], writes=[s_rinv[u]])
                    P.op("act", lambda h, u=u: h.activation(out=rinv[u], in_=rinv[u], func=AF.Exp, scale=-1.0),
                         reads=[s_rinv[u]], writes=[s_rinv[u]])
                    P.op("dve", lambda h, u=u, grp=grp, hg=hg, cc=cc, hf=hf: h.tensor_tensor(
                        out=attnT[:, hg * 2 + cc, hf * 1024 + grp * 512: hf * 1024 + (grp + 1) * 512], in0=ps[grp][:, :], in1=rinv[u],
                        op=ALU.mult), reads=[pss[grp], s_rinv[u]], writes=[s_attnT[hg * 2 + cc]])
    if DBG.get('dump') is not None:
        dbg = DBG['dump']
        s_dbg = P.slot("dbg")
        P.dma("pool", "dbg", lambda h: h.dma_start(out=dbg[:, 0:4096], in_=qT.rearrange("p a b -> p (a b)")), reads=[s_qT], writes=[s_dbg])
        P.dma("pool", "dbg", lambda h: h.dma_start(out=dbg[:, 4096:8192], in_=kT.rearrange("p a b -> p (a b)")), reads=[s_kT], writes=[s_dbg])
    P.barrier()
    for t in range(16):
        j = t % 2
        r0 = tok0 + t * 128
        P.dma("sp", f"nxt{j}", lambda h, j=j, r0=r0: h.dma_start(out=xt[j], in_=xin[r0:r0 + 128, :]),
              reads=[xin_slot], writes=[s_xt[j]])
        for half in range(2):
            pd, sd = ps[half], pss[half]
            for k in range(8):
                P.op("pe", lambda h, k=k, t=t, half=half, pd=pd: h.matmul(
                    pd[:, :], lhsT=attnT[:, k, t * 128:(t + 1) * 128], rhs=Wo[:, k, half * 512:(half + 1) * 512],
                    start=(k == 0), stop=(k == 7)), reads=[s_attnT[k], s_Wo], writes=[sd])
            P.op("dve", lambda h, j=j, half=half, pd=pd: h.tensor_tensor(
                out=ost[j][:, half * 512:(half + 1) * 512], in0=pd[:, :], in1=xt[j][:, half * 512:(half + 1) * 512],
                op=ALU.add), reads=[sd, s_xt[j]], writes=[s_ost[j]])
        P.dma("sp", f"nost{j}", lambda h, j=j, r0=r0: h.dma_start(out=xout[r0:r0 + 128, :], in_=ost[j]),
              reads=[s_ost[j]], writes=[xout_slot])
    P.barrier()


def cv(ap, p0, npart, off, dims):
    n = ap.ap[0][0]
    return bass.AP(tensor=ap.tensor, offset=ap.offset + p0 * n + off, ap=[[n, npart]] + [list(d) for d in dims])


GELU_FUNC = [None]


def ab_stage(c, xin, xin_slot, xout, xout_slot, W, gain_row, ntok):
    P = c.P
    fa, ba = c.fa, c.ba
    fa.reset()
    ba.reset()
    NK = 257
    R1 = ba.take(1, 16448)[:, 0, :]
    uT = ba.take(4, SEQ)
    aT = ba.take(4, SEQ + 30)
    yaT = ba.take(4, SEQ)
    Ht = ba.take(32, 2, 128)
    Gt = ba.take(32, 2, 128)
    Mt = ba.take(32, 128)
    selin = ba.take(2, 8, 128)
    selout = ba.take(8, 8, 128)
    wsl = [ba.take(8, 128) for _ in range(2)]
    gluw = ba.take(4, 512)
    xn = [ba.take(1, 1024)[:, 0, :] for _ in range(2)]
    hT = R1[:, 0:16384].rearrange("p (a b) -> p a b", b=SEQ)
    cdiag = R1[:, 0:15872].rearrange("p (q k m) -> p q k m", k=31, m=128)
    XS = R1
    Wo = R1[:, 0:8192].rearrange("p (a b) -> p a b", b=1024)
    UY = aT[:, :, :].rearrange("p a b -> p (a b)")[:, 0:8192].rearrange("p (g k) -> p g k", k=256)
    Qt = R1[:, 0:8192].rearrange("p (r e) -> p r e", r=2)
    Pt = R1[:, 8192:16384].rearrange("p (r e) -> p r e", r=2)
    Hp = uT[:, :, :].rearrange("p a b -> p (a b)").rearrange("p (r e) -> p r e", r=2)
    gbc = fa.take(1, 1024)[:, 0, :]
    lam = fa.take(3, 32)
    Apw = fa.take(2, 32, 32)
    msk = fa.take(2, 128)
    dcol = fa.take(1, 32)[:, 0, :]
    cw = fa.take(4, 31)
    vec = fa.take(1, 20)[:, 0, :]
    ones32 = fa.take(1, 128)[:, 0, :]
    etab = fa.take(1, 32)[:, 0, :]
    TA = fa.take(1, 64)[:, 0, :]
    TB = fa.take(1, 64)[:, 0, :]
    Sf = fa.take(2, 128)
    st1 = fa.take(1, 64)[:, 0, :]
    st2 = fa.take(1, 64)[:, 0, :]
    sm = fa.take(12, 32)
    ss = [fa.take(1, 4)[:, 0, :] for _ in range(2)]
    tmpA = fa.take(1, 128)[:, 0, :]
    tmpB = fa.take(1, 128)[:, 0, :]
    Dreg = fa.take(1, 7168)[:, 0, :]
    Bp = Dreg[:, 0:1024]
    Cp = Dreg[:, 1024:2048]
    Bb = Dreg[:, 2048:3072]
    t1 = Dreg[:, 3072:4096]
    t2 = Dreg[:, 4096:5120]
    w5 = [Dreg[:, 5120:6144], Dreg[:, 6144:7168], Dreg[:, 2048:3072]]
    xt = [Dreg[:, 0:1024], Dreg[:, 1024:2048]]
    ost = [Dreg[:, 2048:3072], Dreg[:, 3072:4096]]
    sgm = [Dreg[:, 0:512], Dreg[:, 512:1024]]
    acv = Dreg[:, 0:2048].rearrange("p (q t) -> p q t", t=512)
    sqv = [Dreg[:, 2048:2560], Dreg[:, 2560:3072]]
    meanv = Dreg[:, 3072:3584]
    rstdv = Dreg[:, 3584:4096]
    tmpv = Dreg[:, 4096:4608]
    ynv = [Dreg[:, 4608:5120], Dreg[:, 5120:5632]]
    ps, pss = c.ps, c.ps_slots
    ident = c.ident

    S = lambda n: P.slot("ab_" + n)
    s_tab = S("tab")
    s_D = S("Dreg")
    s_R1 = S("R1")
    s_uT = S("uT")
    s_aT = S("aT")
    s_ya = S("yaT")
    s_H, s_G, s_M = S("H"), S("G"), S("M")
    s_sel = S("sel")
    s_wsl = [S("wsl0"), S("wsl1")]
    s_gluw = S("gluw")
    s_xn = [S("xn0"), S("xn1")]
    s_gbc = S("gbc")
    s_ss = [S("ss0"), S("ss1")]
    s_tmp = S("tmpAB")

    def dv(fn, r, w, force=True):
        P.op("dve", fn, reads=r, writes=w, force=force)

    def ac(fn, r, w, force=True):
        P.op("act", fn, reads=r, writes=w, force=force)

    P.dma("sp", "gbc", lambda h: h.dma_start(out=gbc, in_=gain_row.partition_broadcast(128)), writes=[s_gbc])
    for dst, src in ((lam, W["lam"]), (msk, W["mask"]), (dcol, W["dcol"]), (cw, W["cw"]), (vec, W["vec"]),
                     (ones32, W["ones"]), (etab, W["etab"]), (Bp, W["B"]), (Cp, W["C"])):
        P.dma("sp", "abtab", lambda h, dst=dst, src=src: h.dma_start(out=dst, in_=src), writes=[s_tab])
    P.dma("pool", "absel", lambda h: h.dma_start(out=selin, in_=W["selin"]), writes=[s_sel])
    for q in range(2):
        P.dma("pool", "absel", lambda h, q=q: h.dma_start(out=selout[:, q * 4:(q + 1) * 4], in_=W["selout"][:, q * 4:(q + 1) * 4]),
              writes=[s_sel])
    P.dma("pool", "abglu", lambda h: h.dma_start(out=gluw, in_=W["gluw"]), writes=[s_gluw])

    T = [s_tab]
    sm_ = lambda i: sm[:, i, :]
    m_dt, m_lrd, m_lid, m_pr, m_den, m_cre, m_cim, m_x, m_y = (sm_(i) for i in range(9))
    lr, li, ls = lam[:, 0, :], lam[:, 1, :], lam[:, 2, :]
    ac(lambda h: h.activation(out=m_dt, in_=ls, func=AF.Exp), T, T)
    dv(lambda h: h.tensor_tensor(out=m_lrd, in0=lr, in1=m_dt, op=ALU.mult), T, T)
    dv(lambda h: h.tensor_tensor(out=m_lid, in0=li, in1=m_dt, op=ALU.mult), T, T)
    bc_g = lambda a: cv(a, 0, 128, 0, [[0, 32], [1, 32]])
    bc_e = cv(etab, 0, 128, 0, [[1, 32], [0, 32]])
    v3 = lambda a: cv(a, 0, 128, 0, [[32, 32], [1, 32]])
    argm, ang, kf = w5
    ki = t1[:, 0:1024].bitcast(I32)
    dv(lambda h: h.tensor_tensor(out=v3(argm), in0=bc_g(m_lrd), in1=bc_e, op=ALU.mult), T, T)
    ac(lambda h: h.activation(out=argm, in_=argm, func=AF.Exp), T, T)
    dv(lambda h: h.tensor_tensor(out=v3(ang), in0=bc_g(m_lid), in1=bc_e, op=ALU.mult), T, T)
    trig = t2[:, 0:1024]
    for ri, shift in ((1, 0.0), (0, math.pi / 2)):
        src = ang
        if shift != 0.0:
            dv(lambda h: h.tensor_scalar(out=ang, in0=ang, scalar1=shift, scalar2=None, op0=ALU.add), T, T)
        dv(lambda h: h.tensor_scalar(out=kf, in0=ang, scalar1=1.0 / (2 * math.pi), scalar2=None, op0=ALU.mult), T, T)
        dv(lambda h: h.tensor_copy(out=ki, in_=kf), T, T)
        dv(lambda h: h.tensor_copy(out=kf, in_=ki), T, T)
        dv(lambda h: h.scalar_tensor_tensor(out=kf, in0=kf, scalar=-2 * math.pi, in1=ang, op0=ALU.mult, op1=ALU.add), T, T)
        ac(lambda h: h.activation(out=trig, in_=kf, func=AF.Sin), T, T)
        dv(lambda h, ri=ri: h.tensor_tensor(out=Apw[:, ri, :, :].rearrange("p a b -> p (a b)"), in0=argm, in1=trig, op=ALU.mult), T, T)
    for (p0_, j1) in ((0, 16), (64, 23)):
        hp = lambda a, p0_=p0_: cv(a, p0_, 64, 0, [[1, 32]])
        A1r = cv(Apw, p0_, 64, j1 * 32, [[1, 32]])
        A1i = cv(Apw, p0_, 64, 1024 + j1 * 32, [[1, 32]])
        lr_, li_ = hp(lr), hp(li)
        dv(lambda h, hp=hp, A1r=A1r: h.tensor_scalar(out=hp(m_pr), in0=A1r, scalar1=-1.0, scalar2=None, op0=ALU.add), T, T)
        dv(lambda h, hp=hp, lr_=lr_: h.tensor_tensor(out=hp(m_x), in0=lr_, in1=lr_, op=ALU.mult), T, T)
        dv(lambda h, hp=hp, li_=li_: h.tensor_tensor(out=hp(m_y), in0=li_, in1=li_, op=ALU.mult), T, T)
        dv(lambda h, hp=hp: h.tensor_tensor(out=hp(m_den), in0=hp(m_x), in1=hp(m_y), op=ALU.add), T, T)
        dv(lambda h, hp=hp: h.reciprocal(out=hp(m_den), in_=hp(m_den)), T, T)
        dv(lambda h, hp=hp, lr_=lr_: h.tensor_tensor(out=hp(m_x), in0=hp(m_pr), in1=lr_, op=ALU.mult), T, T)
        dv(lambda h, hp=hp, li_=li_, A1i=A1i: h.tensor_tensor(out=hp(m_y), in0=A1i, in1=li_, op=ALU.mult), T, T)
        dv(lambda h, hp=hp: h.tensor_tensor(out=hp(m_x), in0=hp(m_x), in1=hp(m_y), op=ALU.add), T, T)
        dv(lambda h, hp=hp: h.tensor_tensor(out=hp(m_cre), in0=hp(m_x), in1=hp(m_den), op=ALU.mult), T, T)
        dv(lambda h, hp=hp, lr_=lr_, A1i=A1i: h.tensor_tensor(out=hp(m_x), in0=A1i, in1=lr_, op=ALU.mult), T, T)
        dv(lambda h, hp=hp, li_=li_: h.tensor_tensor(out=hp(m_y), in0=hp(m_pr), in1=li_, op=ALU.mult), T, T)
        dv(lambda h, hp=hp: h.tensor_tensor(out=hp(m_x), in0=hp(m_x), in1=hp(m_y), op=ALU.subtract), T, T)
        dv(lambda h, hp=hp: h.tensor_tensor(out=hp(m_cim), in0=hp(m_x), in1=hp(m_den), op=ALU.mult), T, T)
    bcc = lambda a: cv(a, 0, 128, 0, [[1, 32], [0, 16]])
    g16 = lambda a, off: cv(a, 0, 128, off, [[16, 32], [1, 16]])
    for (o_off, a_, x_off, b_, y_off, op) in ((0, m_cre, 0, m_cim, 512, ALU.subtract), (512, m_cre, 512, m_cim, 0, ALU.add)):
        dv(lambda h, a_=a_, x_off=x_off: h.tensor_tensor(out=g16(t1, 0), in0=bcc(a_), in1=g16(Bp, x_off), op=ALU.mult), T, T)
        dv(lambda h, b_=b_, y_off=y_off: h.tensor_tensor(out=g16(t2, 0), in0=bcc(b_), in1=g16(Bp, y_off), op=ALU.mult), T, T)
        dv(lambda h, o_off=o_off, op=op: h.tensor_tensor(out=Bb[:, o_off:o_off + 512], in0=t1[:, 0:512], in1=t2[:, 0:512], op=op), T, T)

    def fam(dst, d_goff, d_rioff, V, negim, j0):
        np_, p0, js = 128, 0, 1
        for qq in range(4):
            Ar = cv(Apw, p0, np_, j0 * 32 + 8 * qq, [[1, 8], [js * 32, 8], [0, 16]])
            Ai = cv(Apw, p0, np_, 1024 + j0 * 32 + 8 * qq, [[1, 8], [js * 32, 8], [0, 16]])
            Vr = cv(V, p0, np_, 8 * qq * 16, [[16, 8], [0, 8], [1, 16]])
            Vi = cv(V, p0, np_, 512 + 8 * qq * 16, [[16, 8], [0, 8], [1, 16]])
            T1 = cv(t1, p0, np_, 0, [[128, 8], [16, 8], [1, 16]])
            T2 = cv(t2, p0, np_, 0, [[128, 8], [16, 8], [1, 16]])
            dre = cv(dst, p0, np_, 8 * qq * d_goff, [[d_goff, 8], [16, 8], [1, 16]])
            dim_ = cv(dst, p0, np_, 8 * qq * d_goff + d_rioff, [[d_goff, 8], [16, 8], [1, 16]])
            dv(lambda h, Ar=Ar, Vr=Vr, T1=T1: h.tensor_tensor(out=T1, in0=Ar, in1=Vr, op=ALU.mult), T, T)
            dv(lambda h, Ai=Ai, Vi=Vi, T2=T2: h.tensor_tensor(out=T2, in0=Ai, in1=Vi, op=ALU.mult), T, T)
            dv(lambda h, dre=dre, T1=T1, T2=T2: h.tensor_tensor(out=dre, in0=T1, in1=T2, op=ALU.subtract), T, T)
            dv(lambda h, Ar=Ar, Vi=Vi, T1=T1: h.tensor_tensor(out=T1, in0=Ar, in1=Vi, op=ALU.mult), T, T)
            dv(lambda h, Ai=Ai, Vr=Vr, T2=T2: h.tensor_tensor(out=T2, in0=Ai, in1=Vr, op=ALU.mult), T, T)
            if negim:
                dv(lambda h, T1=T1: h.tensor_scalar(out=T1, in0=T1, scalar1=-1.0, scalar2=None, op0=ALU.mult), T, T)
                dv(lambda h, dim_=dim_, T1=T1, T2=T2: h.tensor_tensor(out=dim_, in0=T1, in1=T2, op=ALU.subtract), T, T)
            else:
                dv(lambda h, dim_=dim_, T1=T1, T2=T2: h.tensor_tensor(out=dim_, in0=T1, in1=T2, op=ALU.add), T, T)

    fam(Qt, 128, 4096, Bb, False, 0)
    fam(Pt, 128, 4096, Cp, True, 8)
    fam(Gt, 256, 128, Cp, True, 16)
    fam(Hp, 128, 4096, Bb, False, 24)
    for (p0_, j8) in ((0, 23), (64, 16)):
        Dr = cv(Apw, p0_, 64, j8 * 32, [[1, 32]])
        Di = cv(Apw, p0_, 64, 1024 + j8 * 32, [[1, 32]])
        hq = lambda a, off, p0_=p0_: cv(a, p0_, 64, off, [[1, 32]])
        dv(lambda h, hq=hq, Dr=Dr: h.tensor_copy(out=hq(TA, 0), in_=Dr), T, T)
        dv(lambda h, hq=hq, Dr=Dr: h.tensor_copy(out=hq(TA, 32), in_=Dr), T, T)
        dv(lambda h, hq=hq, Di=Di: h.tensor_scalar(out=hq(TB, 0), in0=Di, scalar1=-1.0, scalar2=None, op0=ALU.mult), T, T)
        dv(lambda h, hq=hq, Di=Di: h.tensor_copy(out=hq(TB, 32), in_=Di), T, T)
    for g in range(32):
        u = g % 2
        pf, pb = ps[2 * u], ps[2 * u + 1]
        sf_, sb_ = pss[2 * u], pss[2 * u + 1]
        for (pp, sp_, p0) in ((pf, sf_, 0), (pb, sb_, 64)):
            for ri in range(2):
                P.op("pe", lambda h, pp=pp, p0=p0, ri=ri, g=g: h.matmul(
                    pp[:, 0:128], lhsT=Qt[p0:p0 + 64, ri, g * 128:(g + 1) * 128], rhs=Pt[p0:p0 + 64, ri, g * 128:(g + 1) * 128],
                    start=(ri == 0), stop=(ri == 1)), reads=T, writes=[sp_])
        dv(lambda h, pf=pf: h.tensor_tensor(out=tmpA, in0=pf[:, 0:128], in1=msk[:, 0, :], op=ALU.mult), [sf_] + T, [s_tmp])
        dv(lambda h, pb=pb: h.tensor_tensor(out=tmpB, in0=pb[:, 0:128], in1=msk[:, 1, :], op=ALU.mult), [sb_] + T, [s_tmp])
        dv(lambda h: h.tensor_tensor(out=tmpA, in0=tmpA, in1=tmpB, op=ALU.add), [s_tmp], [s_tmp])
        dv(lambda h, g=g: h.scalar_tensor_tensor(out=Mt[:, g, :], in0=ident[:], scalar=dcol[:, g:g + 1], in1=tmpA,
                                                  op0=ALU.mult, op1=ALU.add), [s_tmp, c.ident_slot] + T, [s_M])
    for b in range(8):
        for gg in range(4):
            for ri in range(2):
                g = b * 4 + gg
                sl = gg * 2 + ri
                P.op("pe", lambda h, g=g, ri=ri, sl=sl: h.transpose(
                    out=c.psb[:, sl * 128:(sl + 1) * 128], in_=Hp[:, ri, g * 128:(g + 1) * 128], identity=ident[:]),
                    reads=T + [c.ident_slot], writes=[c.psb_slot])
        ac(lambda h, b=b: h.copy(out=Ht[:, b * 4:(b + 1) * 4, :, :].rearrange("p a b c -> p (a b c)"), in_=c.psb[:, :]),
           [c.psb_slot], [s_H])
    P.barrier()

    for sq_ in range(ntok // SEQ):
        tok0 = sq_ * SEQ
        s_hT = [S(f"hT{t}") for t in range(16)]
        s_xt1 = [S("xt1_0"), S("xt1_1")]
        for t in range(16):
            j = t % 2
            r0 = tok0 + t * 128
            P.dma("sp", f"abxt{j}", lambda h, j=j, r0=r0: h.dma_start(out=xt[j], in_=xin[r0:r0 + 128, :]),
                  reads=[xin_slot], writes=[s_xt1[j]])
            norm_a(c, xt[j], s_xt1[j], gbc, s_gbc, xn[j], s_xn[j], ss[j], s_ss[j])
            if t >= 1:
                norm_b(c, xn[1 - j], s_xn[1 - j], hT, s_hT[t - 1], (t - 1) * 128)
        norm_b(c, xn[1], s_xn[1], hT, s_hT[15], 15 * 128)
        P.barrier()
        s_sg = [S("sg0"), S("sg1")]
        s_aTq = [S(f"aT{q}") for q in range(4)]
        s_uTq = [S(f"uT{q}") for q in range(4)]
        for q in range(4):
            dv(lambda h, q=q: h.memset(aT[:, q, 0:15], 0.0), [], [s_aTq[q]], force=False)
            dv(lambda h, q=q: h.memset(aT[:, q, SEQ + 15:SEQ + 30], 0.0), [], [s_aTq[q]], force=False)
        wi = 0
        ui = 0

        def load_slab(oc):
            nonlocal wi
            j = wi % 2
            wi += 1
            P.dma("pool", f"abw{j}", lambda h, j=j, oc=oc: h.dma_start(out=wsl[j], in_=W["win"][oc]), writes=[s_wsl[j]])
            return j

        for q in range(4):
            ja = load_slab(q)
            jg = load_slab(q + 4)
            for tb in range(4):
                u = ui % 2
                ui += 1
                pa, pg = ps[2 * u], ps[2 * u + 1]
                sa, sg_ = pss[2 * u], pss[2 * u + 1]
                for (pp, sp_, jj) in ((pa, sa, ja), (pg, sg_, jg)):
                    for k in range(8):
                        P.op("pe", lambda h, pp=pp, jj=jj, k=k, tb=tb: h.matmul(
                            pp[:, :], lhsT=wsl[jj][:, k, :], rhs=hT[:, k, tb * 512:(tb + 1) * 512],
                            start=(k == 0), stop=(k == 7)), reads=[s_wsl[jj]] + s_hT[tb * 4:(tb + 1) * 4], writes=[sp_])
                ac(lambda h, u=u, pg=pg: h.activation(out=sgm[u], in_=pg[:, :], func=AF.Sigmoid), [sg_], [s_sg[u]], force=False)
                dv(lambda h, u=u, pa=pa, q=q, tb=tb: h.tensor_tensor(
                    out=aT[:, q, 15 + tb * 512:15 + (tb + 1) * 512], in0=sgm[u], in1=pa[:, :], op=ALU.mult),
                    [s_sg[u], sa], [s_aTq[q]], force=False)
        for q in range(4):
            ju = load_slab(q + 8)
            for tb in range(4):
                u = ui % 2
                ui += 1
                pu_, su_ = ps[4 + u], pss[4 + u]
                for k in range(8):
                    P.op("pe", lambda h, pu_=pu_, ju=ju, k=k, tb=tb: h.matmul(
                        pu_[:, :], lhsT=wsl[ju][:, k, :], rhs=hT[:, k, tb * 512:(tb + 1) * 512],
                        start=(k == 0), stop=(k == 7)), reads=[s_wsl[ju]] + s_hT[tb * 4:(tb + 1) * 4], writes=[su_])
                ac(lambda h, pu_=pu_, q=q, tb=tb: h.copy(
                    out=cv(uT, 0, 128, q * SEQ + tb * 64, [[256, 8], [1, 64]]),
                    in_=pu_[:, :].rearrange("p (k t) -> p t k", t=8)), [su_], [s_uTq[q]], force=False)
        P.barrier()
        s_cd = [S(f"cd{q}") for q in range(4)]
        for q in range(4):
            for k in range(31):
                dv(lambda h, q=q, k=k: h.tensor_scalar(out=cdiag[:, q, k, :], in0=ident[:], scalar1=cw[:, q, k:k + 1],
                                                        scalar2=None, op0=ALU.mult), [c.ident_slot], [s_cd[q]], force=False)
        s_ac = [S(f"ac{q}") for q in range(4)]
        s_sq = [S("sq0"), S("sq1")]
        s_mean, s_rstd, s_tv = S("mean"), S("rstd"), S("tmpv")
        s_yn = [S("yn0"), S("yn1")]
        for tb in range(4):
            for q in range(4):
                pc, spc = ps[q % 2], pss[q % 2]
                for k in range(31):
                    P.op("pe", lambda h, pc=pc, q=q, k=k, tb=tb: h.matmul(
                        pc[:, :], lhsT=cdiag[:, q, k, :], rhs=aT[:, q, tb * 512 + k: tb * 512 + k + 512],
                        start=(k == 0), stop=(k == 30)), reads=[s_cd[q]], writes=[spc])
                ac(lambda h, pc=pc, q=q: h.activation(out=acv[:, q, :], in_=pc[:, :], func=AF.Identity,
                                                      bias=vec[:, q:q + 1], scale=1.0), [spc], [s_ac[q]], force=False)
            pm, spm = ps[2], pss[2]
            pe2, spe2 = ps[3], pss[3]
            for q in range(4):
                P.op("pe", lambda h, q=q: h.matmul(pm[:, :], lhsT=cv(ones32, 0, 128, 0, [[1, 128]]), rhs=acv[:, q, :],
                                                   start=(q == 0), stop=(q == 3)), reads=[s_ac[q]], writes=[spm])
            for q in range(4):
                u = q % 2
                ac(lambda h, q=q, u=u: h.activation(out=sqv[u], in_=acv[:, q, :], func=AF.Square), [s_ac[q]], [s_sq[u]], force=False)
                P.op("pe", lambda h, q=q, u=u: h.matmul(pe2[:, :], lhsT=cv(ones32, 0, 128, 0, [[1, 128]]), rhs=sqv[u],
                                                        start=(q == 0), stop=(q == 3)), reads=[s_sq[u]], writes=[spe2])
            dv(lambda h: h.tensor_copy(out=meanv, in_=pm[:, :]), [spm], [s_mean], force=False)
            dv(lambda h: h.tensor_tensor(out=tmpv, in0=meanv, in1=meanv, op=ALU.mult), [s_mean], [s_tv])
            dv(lambda h: h.tensor_tensor(out=tmpv, in0=pe2[:, :], in1=tmpv, op=ALU.subtract), [spe2, s_tv], [s_tv])
            ac(lambda h: h.activation(out=rstdv, in_=tmpv, func=AF.Ln, bias=c.cst[:, 2:3], scale=1.0), [s_tv, c.cst_slot], [s_rstd])
            ac(lambda h: h.activation(out=rstdv, in_=rstdv, func=AF.Exp, scale=-0.5), [s_rstd], [s_rstd])
            for q in range(4):
                u = q % 2
                dv(lambda h, q=q, u=u: h.tensor_tensor(out=ynv[u], in0=acv[:, q, :], in1=meanv, op=ALU.subtract),
                   [s_ac[q], s_mean], [s_yn[u]], force=False)
                dv(lambda h, u=u: h.tensor_tensor(out=ynv[u], in0=ynv[u], in1=rstdv, op=ALU.mult), [s_yn[u], s_rstd], [s_yn[u]])
                ac(lambda h, q=q, u=u, tb=tb: h.activation(out=yaT[:, q, tb * 512:(tb + 1) * 512], in_=ynv[u], func=AF.Silu,
                                                           scale=vec[:, 4 + q:5 + q], bias=vec[:, 8 + q:9 + q]),
                   [s_yn[u]], [s_ya], force=False)
        P.barrier()
        s_UY = [S(f"UY{g}") for g in range(32)]
        s_X = [S("Xf"), S("Xb")]
        s_hist = [S("histf"), S("histb")]
        XSv = lambda p0, ri, g, c0, n: cv(XS, p0, 64, (ri * 32 + g) * NK + c0, [[1, n]])
        dv(lambda h: h.memset(cv(XS, 0, 64, 0, [[NK, 64]]), 0.0), [], [s_hist[0]], force=False)
        dv(lambda h: h.memset(cv(XS, 64, 64, 255, [[NK, 64]]), 0.0), [], [s_hist[1]], force=False)
        for g in range(32):
            q, j, par = g // 8, (g % 8) // 2, g % 2
            u = g % 2
            pu_, su_ = ps[u], pss[u]
            for tp in range(8):
                rhs = cv(uT, 32 * j, 32, q * SEQ + tp * 256, [[1, 256]])
                P.op("pe", lambda h, pu_=pu_, j=j, par=par, tp=tp, rhs=rhs: h.matmul(
                    pu_[:, 0:256], lhsT=selin[32 * j:32 * j + 32, par, tp, :], rhs=rhs, start=(tp == 0), stop=(tp == 7),
                    tile_position=(32 * j, 0)), reads=[s_uTq[q], s_sel], writes=[su_])
            ac(lambda h, pu_=pu_, g=g: h.copy(out=UY[:, g, :], in_=pu_[:, 0:256]), [su_], [s_UY[g]], force=False)
            for ri in range(2):
                px, spx = ps[2 + ri], pss[2 + ri]
                P.op("pe", lambda h, px=px, g=g, ri=ri: h.matmul(px[:, 0:256], lhsT=Ht[:, g, ri, :], rhs=UY[:, g, :],
                                                               start=True, stop=True), reads=[s_H, s_UY[g]], writes=[spx])
                dv(lambda h, px=px, g=g, ri=ri: h.tensor_copy(out=XSv(0, ri, g, 1, 256), in_=px[0:64, 0:256]),
                   [spx], [s_X[0]], force=False)
                dv(lambda h, px=px, g=g, ri=ri: h.tensor_copy(out=XSv(64, ri, g, 0, 255), in_=px[64:128, 1:256]),
                   [spx], [s_X[1]], force=False)
        dirs = []
        for (eng, p0, d, cols) in ((DBG.get("scan_f", "dve"), 0, 0, list(range(1, 256))),
                                   (DBG.get("scan_b", "dve"), 64, 1, list(range(254, -1, -1)))):
            s_S = [S(f"S{d}_0"), S(f"S{d}_1")]
            s_w_, s_v_ = S(f"w_{d}"), S(f"v_{d}")
            P.op(eng, lambda h, p0=p0: h.memset(cv(Sf, p0, 64, 0, [[1, 128]]), 0.0), writes=[s_S[0]])
            dirs.append((eng, p0, d, cols, s_S, s_w_, s_v_))
        for i in range(255):
            for (eng, p0, d, cols, s_S, s_w_, s_v_) in dirs:
                col = cols[i]
                pv_, cu = i % 2, (i + 1) % 2
                xcol = cv(XS, p0, 64, col, [[32 * NK, 2], [NK, 32]])
                xcol2 = cv(XS, p0, 64, col, [[0, 2], [32 * NK, 2], [NK, 32]])
                P.op(eng, lambda h, p0=p0, pv_=pv_: h.tensor_tensor(
                    out=cv(st1, p0, 64, 0, [[64, 2], [32, 2], [1, 32]]), in0=cv(TA, p0, 64, 0, [[64, 2], [32, 2], [1, 32]]),
                    in1=cv(Sf, p0, 64, pv_ * 128, [[32, 2], [32, 2], [1, 32]]), op=ALU.mult),
                    reads=[s_S[pv_]], writes=[s_w_], force=DBG.get('scanforce', False))
                P.op(eng, lambda h, p0=p0: h.tensor_tensor(
                    out=cv(tmpA, p0, 64, 0, [[1, 64]]), in0=cv(st1, p0, 64, 0, [[1, 64]]), in1=cv(st1, p0, 64, 64, [[1, 64]]),
                    op=ALU.add), reads=[s_w_], writes=[s_v_], force=DBG.get('scanforce', False))
                P.op(eng, lambda h, p0=p0, cu=cu, xcol2=xcol2: h.tensor_tensor(
                    out=cv(Sf, p0, 64, cu * 128, [[64, 2], [32, 2], [1, 32]]), in0=cv(tmpA, p0, 64, 0, [[0, 2], [32, 2], [1, 32]]),
                    in1=xcol2, op=ALU.add), reads=[s_v_, s_X[d]], writes=[s_S[cu]], force=DBG.get('scanforce', False))
                P.op("act", lambda h, p0=p0, cu=cu, xcol=xcol: h.copy(out=xcol, in_=cv(Sf, p0, 64, cu * 128, [[32, 2], [1, 32]])),
                     reads=[s_S[cu]], writes=[s_hist[d]])
        for g in range(32):
            u = g % 2
            py, spy = ps[4 + u], pss[4 + u]
            P.op("pe", lambda h, py=py, g=g: h.matmul(py[:, 0:256], lhsT=Mt[:, g, :], rhs=UY[:, g, :], start=True, stop=False),
                 reads=[s_M, s_UY[g]], writes=[spy])
            for ri in range(2):
                P.op("pe", lambda h, py=py, g=g, ri=ri: h.matmul(
                    py[:, 0:256], lhsT=Gt[:, g, ri, :], rhs=cv(XS, 0, 128, (ri * 32 + g) * NK, [[1, 256]]),
                    start=False, stop=(ri == 1)), reads=[s_G, s_hist[0], s_hist[1], s_X[0], s_X[1]], writes=[spy])
            ac(lambda h, py=py, g=g: h.copy(out=UY[:, g, :], in_=py[:, 0:256]), [spy], [s_UY[g]], force=False)
        s_Wo = S("Wo")
        for q in range(4):
            P.dma("pool", "abWo", lambda h, q=q: h.dma_start(out=Wo[:, q * 2:(q + 1) * 2, :], in_=W["wout"][:, q * 2:(q + 1) * 2, :]),
                  reads=[], writes=[s_Wo, s_X[0], s_X[1], s_hist[0], s_hist[1]])
        s_yg = [S(f"yg{q}") for q in range(4)]
        ci = 0
        for q in range(4):
            for tau in range(8):
                u = ci % 2
                ci += 1
                pz, spz = ps[u], pss[u]
                for g8 in range(8):
                    P.op("pe", lambda h, pz=pz, tau=tau, g8=g8, q=q: h.matmul(
                        pz[:, 0:256], lhsT=selout[:, tau, g8, :], rhs=UY[:, 8 * q + g8, :], start=(g8 == 0), stop=(g8 == 7)),
                        reads=[s_sel, s_UY[8 * q + g8]], writes=[spz])
                gelu_evac(c, pz, spz, cv(uT, 0, 128, q * SEQ + tau * 256, [[1, 256]]), s_yg[q], s_uTq[q])
        s_sg2 = [S("sg2_0"), S("sg2_1")]
        for tb in range(4):
            for o in range(4):
                pz, spz = ps[2 + o], pss[2 + o]
                for kq in range(4):
                    P.op("pe", lambda h, pz=pz, kq=kq, o=o, tb=tb: h.matmul(
                        pz[:, :], lhsT=gluw[:, kq, o * 128:(o + 1) * 128], rhs=uT[:, kq, tb * 512:(tb + 1) * 512],
                        start=(kq == 0), stop=(kq == 3)), reads=[s_gluw] + s_yg, writes=[spz])
            for o in range(4):
                u = o % 2
                pz, spz = ps[2 + o], pss[2 + o]
                ac(lambda h, pz=pz, o=o, u=u: h.activation(out=sgm[u], in_=pz[:, :], func=AF.Sigmoid,
                                                           bias=vec[:, 12 + o:13 + o], scale=1.0), [spz], [s_sg2[u]], force=False)
                dv(lambda h, o=o, u=u, tb=tb: h.tensor_tensor(out=uT[:, o, tb * 512:(tb + 1) * 512],
                                                              in0=uT[:, o, tb * 512:(tb + 1) * 512], in1=sgm[u], op=ALU.mult),
                   [s_sg2[u], s_yg[o]], [s_yg[o]], force=False)
        P.barrier()
        s_xt2 = [S("xt2_0"), S("xt2_1")]
        s_ost = [S("ost0"), S("ost1")]
        xin_v = xin[tok0:tok0 + SEQ, :].rearrange("(k t) d -> t k d", t=8)
        xout_v = xout[tok0:tok0 + SEQ, :].rearrange("(k t) d -> t k d", t=8)
        for t in range(16):
            j = t % 2
            tau, kb = t // 2, t % 2
            P.dma("sp", f"abxt{j}", lambda h, j=j, tau=tau, kb=kb, xin_v=xin_v: h.dma_start(out=xt[j], in_=xin_v[tau, 128 * kb:128 * kb + 128, :]),
                  reads=[xin_slot], writes=[s_xt2[j]])
            for half in range(2):
                pd, sd = ps[half], pss[half]
                for k in range(8):
                    if k < 4:
                        lw_ = cv(yaT, 0, 128, k * SEQ + 1024 * kb + tau, [[8, 128]])
                    else:
                        lw_ = cv(uT, 0, 128, (k - 4) * SEQ + tau * 256 + 128 * kb, [[1, 128]])
                    P.op("pe", lambda h, k=k, half=half, pd=pd, lw_=lw_: h.matmul(
                        pd[:, :], lhsT=lw_, rhs=Wo[:, k, half * 512:(half + 1) * 512],
                        start=(k == 0), stop=(k == 7)), reads=[s_Wo], writes=[sd])
                dv(lambda h, j=j, half=half, pd=pd: h.tensor_tensor(
                    out=ost[j][:, half * 512:(half + 1) * 512], in0=pd[:, :], in1=xt[j][:, half * 512:(half + 1) * 512],
                    op=ALU.add), [sd, s_xt2[j]], [s_ost[j]], force=False)
            P.dma("sp", f"abost{j}", lambda h, j=j, tau=tau, kb=kb, xout_v=xout_v: h.dma_start(out=xout_v[tau, 128 * kb:128 * kb + 128, :], in_=ost[j]),
                  reads=[s_ost[j]], writes=[xout_slot])
        P.barrier()


def gelu_evac(c, pz, spz, dst, s_dst, s_dst2):
    P = c.P
    P.op("act", lambda h: h.activation(out=dst, in_=pz[:, 0:256], func=AF.Gelu_apprx_tanh), reads=[spz], writes=[s_dst, s_dst2])

def build_program(plan=None, ntok=TOK):
    if plan is None:
        plan = []
        for l in range(DEPTH):
            plan.append(("ab" if l % 2 == 0 else "na", l))
            plan.append(("ffn", l))
    nc = bass.Bass("TRN2", target_bir_lowering=False)
    c = setup_ctx(nc)
    P = c.P
    x = nc.dram_tensor("x", [ntok, D], F32, kind="ExternalInput").ap()
    out = nc.dram_tensor("out", [ntok, D], F32, kind="ExternalOutput").ap()
    ident = nc.dram_tensor("ident", [2, 128, 128], F32, kind="ExternalInput").ap()
    wg = nc.dram_tensor("ffn_wg", [DEPTH, NM, 128, 8, 128], F32, kind="ExternalInput").ap()
    wu = nc.dram_tensor("ffn_wu", [DEPTH, NM, 128, 8, 128], F32, kind="ExternalInput").ap()
    wd = nc.dram_tensor("ffn_wd", [DEPTH, FF, D], F32, kind="ExternalInput").ap()
    fnorm = nc.dram_tensor("ffn_norm", [DEPTH, D], F32, kind="ExternalInput").ap()
    mnorm = nc.dram_tensor("mix_norm", [DEPTH, D], F32, kind="ExternalInput").ap()
    na_wqkv = nc.dram_tensor("na_wqkv", [2, 4, 128, 8, 768], F32, kind="ExternalInput").ap()
    na_wout = nc.dram_tensor("na_wout", [2, 128, 8, 1024], F32, kind="ExternalInput").ap()
    na_bias = nc.dram_tensor("na_bias", [2, 4, 128, 4, 14, 64], F32, kind="ExternalInput").ap()
    na_qkg = nc.dram_tensor("na_qkg", [2, 128, 2], F32, kind="ExternalInput").ap()
    abd = {}
    for nm, shp in AB_SHAPES.items():
        abd[nm] = nc.dram_tensor("ab_" + nm, list(shp), F32, kind="ExternalInput").ap()
    scr = [nc.dram_tensor(f"scr{i}", [ntok, D], F32, kind="Internal").ap() for i in range(2)]
    if DBG.get('dump_on'):
        DBG['dump'] = nc.dram_tensor("dbg", [128, 16384], F32, kind="ExternalOutput").ap()
    s_scr = [P.slot("scr0"), P.slot("scr1")]
    s_x = P.slot("x_dram")
    s_out = P.slot("out_dram")
    load_ident(c, ident)
    cur, s_cur = x, s_x
    for si, st in enumerate(plan):
        if si == len(plan) - 1:
            dst, s_dst = out, s_out
        else:
            dst, s_dst = scr[si % 2], s_scr[si % 2]
        kind, l = st
        if kind == "ffn":
            ffn_stage(c, cur, s_cur, dst, s_dst, wg[l], wu[l], wd[l], fnorm[l:l + 1, :], ntok=ntok)
        elif kind == "na":
            i = l // 2
            for sq_ in range(ntok // SEQ):
                na_stage(c, cur, s_cur, dst, s_dst, na_wqkv[i], na_wout[i], na_bias[i], na_qkg[i],
                         mnorm[l:l + 1, :], sq_ * SEQ)
        elif kind == "ab":
            i = l // 2
            Wd_ = {k: (v[i] if k in AB_PER_LAYER else v) for k, v in abd.items()}
            ab_stage(c, cur, s_cur, dst, s_dst, Wd_, mnorm[l:l + 1, :], ntok)
        cur, s_cur = dst, s_dst
    fin = Ins("sp", None, False, None, 0)
    for e in ENGS:
        for i in P.streams[e]:
            if i.is_dma:
                fin.deps.append(i)
    fin.idx = len(P.streams["sp"])
    P.streams["sp"].append(fin)
    P.emit()
    return nc


AB_PER_LAYER = ("win", "wout", "gluw", "vec", "cw", "lam", "B", "C", "dcol")
AB_SHAPES = {
    "win": (2, 12, 128, 8, 128), "wout": (2, 128, 8, 1024), "gluw": (2, 128, 4, 512), "vec": (2, 128, 20),
    "cw": (2, 128, 4, 31), "lam": (2, 128, 3, 32), "B": (2, 128, 2, 32, 16), "C": (2, 128, 2, 32, 16),
    "dcol": (2, 128, 32), "selin": (128, 2, 8, 128), "selout": (128, 8, 8, 128), "mask": (128, 2, 128),
    "etab": (128, 32), "ones": (128, 128),
}


def ab_host_layout(inputs):
    g = np.ascontiguousarray
    f = lambda k: np.asarray(inputs[k], dtype=np.float32)
    d = {}
    d["win"] = g(f("ab_w_in").reshape(2, 8, 128, 12, 128).transpose(0, 3, 2, 1, 4))
    d["wout"] = g(f("ab_w_out").reshape(2, 8, 128, 1024).transpose(0, 2, 1, 3))
    d["gluw"] = g(f("ssm_glu_w").reshape(2, 4, 128, 512).transpose(0, 2, 1, 3))
    vec = np.zeros((2, 128, 20), np.float32)
    for j, k in enumerate(("conv_b", "conv_ln_g", "conv_ln_b", "ssm_glu_b")):
        vec[:, :, 4 * j:4 * j + 4] = f(k).reshape(2, 4, 128).transpose(0, 2, 1)
    d["vec"] = vec
    d["cw"] = g(f("conv_w").reshape(2, 31, 4, 128).transpose(0, 3, 2, 1))
    lam = np.empty((2, 2, 64, 3, 32), np.float32)
    lam[:, :, :, 0] = f("ssm_lambda_re").transpose(0, 1, 3, 2)
    lam[:, :, :, 1] = f("ssm_lambda_im").transpose(0, 1, 3, 2)
    lam[:, :, :, 2] = f("ssm_log_step")[:, :, None, :]
    d["lam"] = g(lam.reshape(2, 128, 3, 32))
    B = np.stack([f("ssm_b_re"), f("ssm_b_im")], axis=1)
    d["B"] = g(B.transpose(0, 2, 4, 1, 3, 5).reshape(2, 128, 2, 32, 16))
    C = np.stack([f("ssm_c_re"), f("ssm_c_im")], axis=1)
    d["C"] = g(C.transpose(0, 2, 5, 1, 3, 4).reshape(2, 128, 2, 32, 16))
    dsk = f("ssm_d").reshape(2, 32, 16)
    d["dcol"] = g(np.broadcast_to(dsk.transpose(0, 2, 1)[:, None], (2, 8, 16, 32)).reshape(2, 128, 32))
    selin = np.zeros((4, 2, 16, 2, 8, 8, 16), np.float32)
    selout = np.zeros((8, 16, 8, 8, 8, 16), np.float32)
    for cc in range(16):
        for t in range(8):
            selin[:, 0, cc, 0, t, t, cc] = 1.0
            selin[:, 1, cc, 1, t, t, cc] = 1.0
            for g8 in range(8):
                selout[t, cc, t, g8, g8, cc] = 1.0
    d["selin"] = selin.reshape(128, 2, 8, 128)
    d["selout"] = selout.reshape(128, 8, 8, 128)
    tp = np.repeat(np.arange(8), 16)
    d["mask"] = g(np.stack([(tp[:, None] <= tp[None, :]), (tp[:, None] >= tp[None, :])], axis=1).astype(np.float32))
    tt = np.arange(8, dtype=np.float32)
    ef = np.concatenate([-tt, tt, tt + 1, 7 - tt])
    eb = np.concatenate([tt, -tt, 8 - tt, tt])
    d["etab"] = g(np.concatenate([np.broadcast_to(ef, (64, 32)), np.broadcast_to(eb, (64, 32))], axis=0))
    d["ones"] = np.full((128, 128), 1.0 / 512, np.float32)
    return {"ab_" + k: v for k, v in d.items()}


def host_layout(inputs):
    g = np.ascontiguousarray
    d = {}
    bd = np.zeros((128, 128), np.float32)
    bd[:64, :64] = 1.0
    bd[64:, 64:] = 1.0
    d["ident"] = np.stack([np.eye(128, dtype=np.float32), bd])
    for nm, key in (("ffn_wg", "ffn_w_gate"), ("ffn_wu", "ffn_w_up")):
        w = np.asarray(inputs[key], dtype=np.float32).reshape(DEPTH, 8, 128, NM, 128)
        d[nm] = g(w.transpose(0, 3, 2, 1, 4))
    d["ffn_wd"] = g(np.asarray(inputs["ffn_w_down"], dtype=np.float32))
    d["ffn_norm"] = g(np.asarray(inputs["ffn_norm"], dtype=np.float32))
    d["mix_norm"] = g(np.asarray(inputs["mix_norm"], dtype=np.float32))
    wqkv = np.asarray(inputs["na_w_qkv"], dtype=np.float32).reshape(2, 8, 128, 3, 4, 256)
    d["na_wqkv"] = g(wqkv.transpose(0, 4, 2, 1, 3, 5).reshape(2, 4, 128, 8, 768))
    d["na_wout"] = g(np.asarray(inputs["na_w_out"], dtype=np.float32).reshape(2, 8, 128, 1024).transpose(0, 2, 1, 3))
    rpb = np.asarray(inputs["na_rpb"], dtype=np.float32)
    qc = np.arange(64)[None, :]
    kc = np.arange(64)[:, None]
    cidx = np.clip(kc - qc, -15, 15) + 15
    cstart = np.clip(qc - 8, 0, 48)
    cmask = (kc >= cstart) & (kc < cstart + 16)
    tab = np.empty((2, 16, 14, 2, 64, 64), np.float32)
    for rho in range(14):
        for jj in range(2):
            tab[:, :, rho, jj] = np.where(cmask[None, None], rpb[:, :, rho + jj][:, :, cidx], np.float32(-30000.0))
    tab = tab.reshape(2, 4, 4, 14, 2, 64, 64).transpose(0, 1, 4, 5, 2, 3, 6).reshape(2, 4, 128, 4, 14, 64)
    d["na_bias"] = g(tab)
    qg = np.asarray(inputs["na_q_norm"], dtype=np.float32)
    kg = np.asarray(inputs["na_k_norm"], dtype=np.float32)
    d["na_qkg"] = g(np.stack([np.tile(qg, (1, 2)), np.tile(kg, (1, 2))], axis=-1))
    d.update(ab_host_layout(inputs))
    return d


def kernel(**inputs):
    x = np.asarray(inputs["x"], dtype=np.float32)
    shared = host_layout(inputs)
    nc = build_program()
    in_maps = []
    for i in range(NCORES):
        m = dict(shared)
        m["x"] = np.ascontiguousarray(x[2 * i:2 * i + 2].reshape(TOK, D))
        in_maps.append(m)
    res = run_bass_kernel_spmd(nc, in_maps, core_ids=list(range(NCORES)))
    outs = [res.results[i]["out"].reshape(2, SEQ, D) for i in range(NCORES)]
    return np.concatenate(outs, axis=0).astype(np.float32)
```

```python
import math
import numpy as np
import concourse.bass as bass
import concourse.mybir as mybir
from concourse.bass_utils import run_bass_kernel_spmd

F32 = mybir.dt.float32
BF16 = mybir.dt.bfloat16
I32 = mybir.dt.int32
AF = mybir.ActivationFunctionType
ALU = mybir.AluOpType
AX = mybir.AxisListType

DBG = {}
NCORES = 8
D = 1024
SEQ = 2048
TOK = 2 * SEQ
FF = 2816
NM = FF // 128
DEPTH = 4
EPS = 1e-6


class Slot:
    __slots__ = ("name", "lw", "rs")

    def __init__(self, name):
        self.name = name
        self.lw = None
        self.rs = []


class Ins:
    __slots__ = ("eng", "fn", "deps", "is_dma", "sem", "semval", "inc", "idx", "force")

    def __init__(self, eng, fn, is_dma, sem, semval):
        self.eng = eng
        self.fn = fn
        self.deps = []
        self.is_dma = is_dma
        self.sem = sem
        self.semval = semval
        self.inc = False
        self.idx = -1
        self.force = False


ENGS = ("pe", "act", "dve", "pool", "sp")


class Prog:
    def __init__(self, nc):
        self.nc = nc
        self.streams = {e: [] for e in ENGS}
        self.dma_sems = {}
        self.dma_cnt = {}
        self.slots = []

    def slot(self, name):
        s = Slot(name)
        self.slots.append(s)
        return s

    def slots_n(self, name, n):
        return [self.slot(f"{name}{i}") for i in range(n)]

    def _add(self, ins, reads, writes):
        e = ins.eng
        best = {}
        dl = []

        def need(p):
            if p is None or p is ins:
                return
            if p.is_dma:
                if p not in dl:
                    dl.append(p)
            elif ins.is_dma or p.eng != e or ins.force:
                q = best.get(p.eng)
                if q is None or q.idx < p.idx:
                    best[p.eng] = p

        for s in reads:
            need(s.lw)
        for s in writes:
            need(s.lw)
            for r in s.rs:
                need(r)
        ins.deps = dl + list(best.values())
        for s in reads:
            rs = s.rs
            if rs and (not ins.is_dma) and (not rs[-1].is_dma) and rs[-1].eng == e:
                rs[-1] = ins
            else:
                rs.append(ins)
        for s in writes:
            s.lw = ins
            s.rs = []
        ins.idx = len(self.streams[e])
        self.streams[e].append(ins)
        return ins

    def op(self, eng, fn, reads=(), writes=(), force=False):
        ins = Ins(eng, fn, False, None, 0)
        ins.force = force
        return self._add(ins, reads, writes)

    def dma(self, eng, semname, fn, reads=(), writes=()):
        if semname not in self.dma_sems:
            self.dma_sems[semname] = self.nc.alloc_semaphore(name="d_" + semname)
            self.dma_cnt[semname] = 0
        self.dma_cnt[semname] += 16
        return self._add(Ins(eng, fn, True, self.dma_sems[semname], self.dma_cnt[semname]), reads, writes)

    def barrier(self):
        lasts = []
        for e in ENGS:
            st = self.streams[e]
            if st:
                lasts.append(st[-1])
        dmas = [i for e in ENGS for i in self.streams[e] if i.is_dma and not getattr(i, "_barr", False)]
        for e in ENGS:
            ins = Ins(e, None, False, None, 0)
            for p in lasts:
                if p.eng != e and not p.is_dma and p.fn is not None:
                    ins.deps.append(p)
            for p in dmas:
                ins.deps.append(p)
            ins.idx = len(self.streams[e])
            self.streams[e].append(ins)
        for s in self.slots:
            s.lw = None
            s.rs = []

    def emit(self):
        nc = self.nc
        for e in ENGS:
            for ins in self.streams[e]:
                for p in ins.deps:
                    if not p.is_dma:
                        p.inc = True
        rank = {}
        sems = {}
        for e in ENGS:
            sems[e] = nc.alloc_semaphore(name="s_" + e)
            c = 0
            last_real = None
            for ins in self.streams[e]:
                if ins.fn is not None and not ins.is_dma:
                    last_real = ins
                if ins.inc:
                    assert ins.fn is not None and not ins.is_dma
                    c += 1
                    rank[ins] = c
        handles = {"pe": nc.tensor, "act": nc.scalar, "dve": nc.vector, "pool": nc.gpsimd, "sp": nc.sync}
        streams = self.streams

        def replay(e, h):
            known = {}
            for ins in streams[e]:
                for p in ins.deps:
                    if p.is_dma:
                        key, val, sem = ("d", id(p.sem)), p.semval, p.sem
                    else:
                        key, val, sem = ("c", p.eng), rank[p], sems[p.eng]
                    if known.get(key, 0) >= val:
                        continue
                    known[key] = val
                    h.wait_ge(sem, val)
                if ins.fn is None:
                    continue
                r = ins.fn(h)
                if ins.is_dma:
                    r.then_inc(ins.sem, 16)
                elif ins.inc:
                    r.then_inc(sems[e], 1)

        with nc.Block() as block:
            @block.tensor
            def _(h):
                replay("pe", h)

            @block.scalar
            def _(h):
                replay("act", h)

            @block.vector
            def _(h):
                replay("dve", h)

            @block.gpsimd
            def _(h):
                replay("pool", h)

            @block.sync
            def _(h):
                replay("sp", h)


class Arena:
    def __init__(self, nc, name, nelem, dtype):
        self.t = nc.alloc_sbuf_tensor(name, [128, nelem], dtype)
        self.n = nelem
        self.off = 0

    def reset(self):
        self.off = 0

    def take(self, *shape):
        n = int(np.prod(shape))
        assert self.off + n <= self.n, (self.off, n, self.n)
        ap = self.t[:, self.off:self.off + n]
        self.off += n
        if len(shape) == 2:
            ap = ap.rearrange("p (a b) -> p a b", b=shape[1])
        elif len(shape) == 3:
            ap = ap.rearrange("p (a b c) -> p a b c", b=shape[1], c=shape[2])
        elif len(shape) == 4:
            ap = ap.rearrange("p (a b c d) -> p a b c d", b=shape[1], c=shape[2], d=shape[3])
        return ap


class Ctx:
    pass


def setup_ctx(nc):
    c = Ctx()
    c.nc = nc
    c.P = Prog(nc)
    c.fa = Arena(nc, "fa", 12800, F32)
    c.ba = Arena(nc, "ba", 78848, BF16)
    c.ps = [nc.alloc_psum_tensor(f"ps{i}", [128, 512], F32) for i in range(7)]
    c.psb = nc.alloc_psum_tensor("psb", [128, 1024], BF16)
    c.ps_slots = [c.P.slot(f"ps{i}") for i in range(7)]
    c.psb_slot = c.P.slot("psb")
    c.ident = nc.alloc_sbuf_tensor("ident_sb", [128, 128], BF16)
    c.ident_slot = c.P.slot("ident")
    c.bd = nc.alloc_sbuf_tensor("bd_sb", [128, 128], BF16)
    c.bd_slot = c.P.slot("bd")
    c.cst = nc.alloc_sbuf_tensor("cst_sb", [128, 8], F32)
    c.cst_slot = c.P.slot("cst")
    c.psbs = [(c.psb[:, :], c.psb_slot), (c.ps[6][:, :].bitcast(BF16), c.ps_slots[6])]
    c.nt_count = 0
    return c


def load_ident(c, ident_dram):
    P = c.P
    P.dma("pool", "ident", lambda h: h.dma_start(out=c.ident[:], in_=ident_dram[0]), writes=[c.ident_slot])
    P.dma("pool", "bdones", lambda h: h.dma_start(out=c.bd[:], in_=ident_dram[1]), writes=[c.bd_slot])
    for i, v in enumerate((64.0 * EPS, EPS, 1e-5, 0.0, 1.0, math.pi / 2)):
        P.op("dve", lambda h, i=i, v=v: h.memset(c.cst[:, i:i + 1], v), writes=[c.cst_slot])


def norm_a(c, xt, xt_slot, gain_bc, gain_slot, xn, xn_slot, ss, ss_slot):
    P = c.P
    P.op("act", lambda h: h.activation(out=xn, in_=xt, func=AF.Square, accum_out=ss[:, 0:1]),
         reads=[xt_slot], writes=[xn_slot, ss_slot])
    P.op("dve", lambda h: h.tensor_scalar(out=ss[:, 1:2], in0=ss[:, 0:1], scalar1=1.0 / D, scalar2=EPS,
                                           op0=ALU.mult, op1=ALU.add), reads=[ss_slot], writes=[ss_slot])
    P.op("act", lambda h: h.activation(out=ss[:, 2:3], in_=ss[:, 1:2], func=AF.Sqrt), reads=[ss_slot], writes=[ss_slot])
    P.op("dve", lambda h: h.reciprocal(out=ss[:, 3:4], in_=ss[:, 2:3]), reads=[ss_slot], writes=[ss_slot])
    P.op("dve", lambda h: h.scalar_tensor_tensor(out=xn, in0=xt, scalar=ss[:, 3:4], in1=gain_bc,
                                                  op0=ALU.mult, op1=ALU.mult),
         reads=[xt_slot, ss_slot, gain_slot], writes=[xn_slot], force=True)


def norm_b(c, xn, xn_slot, hT, hT_slot, col0):
    P = c.P
    pb_, pb_slot = c.psbs[c.nt_count % 2]
    c.nt_count += 1
    for k in range(8):
        P.op("pe", lambda h, k=k: h.transpose(out=pb_[:, k * 128:(k + 1) * 128], in_=xn[:, k * 128:(k + 1) * 128],
                                              identity=c.ident[:]),
             reads=[xn_slot, c.ident_slot], writes=[pb_slot])
    P.op("act", lambda h: h.copy(out=hT[:, :, col0:col0 + 128],
                                 in_=pb_.rearrange("p (k t) -> p k t", t=128)),
         reads=[pb_slot], writes=[hT_slot])


def norm_transpose(c, xt, xt_slot, gain_bc, gain_slot, xn, xn_slot, ss, ss_slot, hT, hT_slot, col0):
    norm_a(c, xt, xt_slot, gain_bc, gain_slot, xn, xn_slot, ss, ss_slot)
    norm_b(c, xn, xn_slot, hT, hT_slot, col0)


def ffn_stage(c, xin, xin_slot, xout, xout_slot, wg, wu, wd, gain_row, ntok=TOK, TB=1024):
    P = c.P
    fa, ba = c.fa, c.ba
    fa.reset()
    ba.reset()
    NT = TB // 128
    Wd = ba.take(NM, 1024)
    hTs = [ba.take(8, TB) for _ in range(2)]
    actT = ba.take(NM, TB)
    wgb = [ba.take(8, 128) for _ in range(2)]
    wub = [ba.take(8, 128) for _ in range(2)]
    xn = [ba.take(1, 1024)[:, 0, :] for _ in range(2)]
    xt = [fa.take(1, 1024)[:, 0, :] for _ in range(4)]
    ost = [fa.take(1, 1024)[:, 0, :] for _ in range(2)]
    gbc = fa.take(1, 1024)[:, 0, :]
    sg = [fa.take(1, 512)[:, 0, :] for _ in range(2)]
    ss = [fa.take(1, 4)[:, 0, :] for _ in range(2)]

    s_Wd = P.slot("Wd")
    s_hT = [[P.slot(f"hT{j}_{t}") for t in range(NT)] for j in range(2)]
    s_act = [[P.slot(f"act{m}_{h}") for h in range(TB // 512)] for m in range(NM)]
    s_wg = P.slots_n("wg", 2)
    s_wu = P.slots_n("wu", 2)
    s_xn = P.slots_n("xn", 2)
    s_xt = P.slots_n("xt", 4)
    s_ost = P.slots_n("ost", 2)
    s_gbc = P.slot("gbc")
    s_sg = P.slots_n("sg", 2)
    s_ss = P.slots_n("ss", 2)

    P.dma("sp", "gbc", lambda h: h.dma_start(out=gbc, in_=gain_row.partition_broadcast(128)), writes=[s_gbc])
    wdv = wd.rearrange("(m p) o -> p m o", p=128)
    nblk = ntok // TB
    st = {"w": 0, "o": 0, "n": 0, "u": 0, "x": 0}

    pend = {}

    def norm_tile_a(b, t):
        j = st["n"] % 2
        st["n"] += 1
        xi = st["x"] % 2
        st["x"] += 1
        r0 = b * TB + t * 128
        P.dma("sp", f"xt{xi}", lambda h: h.dma_start(out=xt[xi], in_=xin[r0:r0 + 128, :]),
              reads=[xin_slot], writes=[s_xt[xi]])
        norm_a(c, xt[xi], s_xt[xi], gbc, s_gbc, xn[j], s_xn[j], ss[j], s_ss[j])
        pend[(b, t)] = j

    def norm_tile_b(b, t):
        j = pend.pop((b, t))
        norm_b(c, xn[j], s_xn[j], hTs[b % 2], s_hT[b % 2][t], t * 128)

    def norm_tile(b, t):
        norm_tile_a(b, t)
        norm_tile_b(b, t)

    for t in range(NT):
        norm_tile(0, t)
    for q in range(2):
        P.dma("pool", "Wd", lambda h, q=q: h.dma_start(out=Wd[:, q * 11:(q + 1) * 11, :], in_=wdv[:, q * 11:(q + 1) * 11, :]),
              writes=[s_Wd])
    for b in range(nblk):
        hT = hTs[b % 2]
        shT = s_hT[b % 2]
        for m in range(NM):
            j = st["w"] % 2
            st["w"] += 1
            P.dma("pool", f"wg{j}", lambda h, j=j, m=m: h.dma_start(out=wgb[j], in_=wg[m]), writes=[s_wg[j]])
            P.dma("pool", f"wu{j}", lambda h, j=j, m=m: h.dma_start(out=wub[j], in_=wu[m]), writes=[s_wu[j]])
            for hh in range(TB // 512):
                u = st["u"] % 2
                st["u"] += 1
                pg, pu = c.ps[2 * u], c.ps[2 * u + 1]
                sgs, sus = c.ps_slots[2 * u], c.ps_slots[2 * u + 1]
                hs = shT[hh * 4:(hh + 1) * 4]
                for k in range(8):
                    P.op("pe", lambda h, j=j, k=k, hh=hh, pg=pg, hT=hT: h.matmul(
                        pg[:, :], lhsT=wgb[j][:, k, :], rhs=hT[:, k, hh * 512:(hh + 1) * 512],
                        start=(k == 0), stop=(k == 7)), reads=[s_wg[j]] + hs, writes=[sgs])
                for k in range(8):
                    P.op("pe", lambda h, j=j, k=k, hh=hh, pu=pu, hT=hT: h.matmul(
                        pu[:, :], lhsT=wub[j][:, k, :], rhs=hT[:, k, hh * 512:(hh + 1) * 512],
                        start=(k == 0), stop=(k == 7)), reads=[s_wu[j]] + hs, writes=[sus])
                P.op("act", lambda h, u=u, pg=pg: h.activation(out=sg[u], in_=pg[:, :], func=AF.Silu),
                     reads=[sgs], writes=[s_sg[u]])
                P.op("dve", lambda h, u=u, pu=pu, m=m, hh=hh: h.tensor_tensor(
                    out=actT[:, m, hh * 512:(hh + 1) * 512], in0=sg[u], in1=pu[:, :], op=ALU.mult),
                    reads=[s_sg[u], sus], writes=[s_act[m][hh]])
            if b + 1 < nblk and m % 2 == 1:
                i_ = m // 2
                if i_ < NT:
                    norm_tile_a(b + 1, i_)
                if 1 <= i_ <= NT:
                    norm_tile_b(b + 1, i_ - 1)
        for t in range(NT):
            o = st["o"] % 2
            st["o"] += 1
            xi = 2 + (t % 2)
            r0 = b * TB + t * 128
            P.dma("sp", f"xt{xi}", lambda h, xi=xi, r0=r0: h.dma_start(out=xt[xi], in_=xin[r0:r0 + 128, :]),
                  reads=[xin_slot], writes=[s_xt[xi]])
            for half in range(2):
                pd = c.ps[4 + half]
                sd = c.ps_slots[4 + half]
                for m in range(NM):
                    P.op("pe", lambda h, m=m, t=t, half=half, pd=pd: h.matmul(
                        pd[:, :], lhsT=actT[:, m, t * 128:(t + 1) * 128], rhs=Wd[:, m, half * 512:(half + 1) * 512],
                        start=(m == 0), stop=(m == NM - 1)),
                        reads=[s_act[m][t // 4], s_Wd], writes=[sd])
                P.op("dve", lambda h, o=o, xi=xi, half=half, pd=pd: h.tensor_tensor(
                    out=ost[o][:, half * 512:(half + 1) * 512], in0=pd[:, :], in1=xt[xi][:, half * 512:(half + 1) * 512],
                    op=ALU.add), reads=[sd, s_xt[xi]], writes=[s_ost[o]])
            P.dma("sp", f"ost{o}", lambda h, o=o, r0=r0: h.dma_start(out=xout[r0:r0 + 128, :], in_=ost[o]),
                  reads=[s_ost[o]], writes=[xout_slot])
    P.barrier()


def cv_ps(pt, off, dims, p0=0, npart=128):
    a = pt[:, :]
    n = a.ap[0][0]
    return bass.AP(tensor=a.tensor, offset=a.offset + p0 * n + off, ap=[[n, npart]] + [list(d) for d in dims])


def _na_runs():
    out = []
    rs = [min(max(r - 4, 0), 24) for r in range(32)]
    for hf in range(2):
        runs = []
        for par in range(2):
            for ti in range(16):
                row0 = 2 * ti + par
                if row0 + 1 > 31:
                    continue
                for grp in range(2):
                    rows = [r for r in range(16 * hf + 8 * grp, 16 * hf + 8 * grp + 8)
                            if rs[r] % 2 == par and rs[r] <= row0 <= rs[r] + 6]
                    while rows:
                        if len(rows) == 1:
                            run, rows = rows, []
                            st = 1
                        else:
                            st = rows[1] - rows[0]
                            k = 2
                            while k < len(rows) and rows[k] - rows[k - 1] == st:
                                k += 1
                            run, rows = rows[:k], rows[k:]
                        runs.append((par, ti, grp, run[0], st, len(run)))
        out.append(runs)
    return out


NA_RUNS = _na_runs()

def na_stage(c, xin, xin_slot, xout, xout_slot, wqkv, wout, biasT, qkg, gain_row, tok0):
    P = c.P
    fa, ba = c.fa, c.ba
    fa.reset()
    ba.reset()
    hT = ba.take(8, SEQ)
    attnT = ba.take(8, SEQ)
    qT = ba.take(2, SEQ)
    kT = ba.take(2, SEQ)
    Vpad = ba.take(2, 16, 4, 128)
    EB = ba.take(4, 14, 64)
    cpad = ba.take(3, 128)
    zrhs = ba.take(1, 512)[:, 0, :]
    bT = ba.take(4, 14, 64)
    wq = ba.take(8, 768)
    PT = [ba.take(1, 512)[:, 0, :] for _ in range(3)]
    xn = [ba.take(1, 1024)[:, 0, :] for _ in range(2)]
    sq = [ba.take(1, 512)[:, 0, :] for _ in range(2)]
    xt = [fa.take(1, 1024)[:, 0, :] for _ in range(2)]
    ost = [fa.take(1, 1024)[:, 0, :] for _ in range(2)]
    gbc = fa.take(1, 1024)[:, 0, :]
    raw = [fa.take(1, 512)[:, 0, :] for _ in range(2)]
    rinv = [fa.take(1, 512)[:, 0, :] for _ in range(2)]
    ss = [fa.take(1, 4)[:, 0, :] for _ in range(2)]
    rec = [fa.take(1, 2)[:, 0, :] for _ in range(2)]
    gq = fa.take(1, 2)[:, 0, :]

    s_hT = P.slots_n("n_hT", 16)
    s_attnT = P.slots_n("n_attnT", 8)
    s_qT = P.slot("n_qT")
    s_kT = P.slot("n_kT")
    s_V = P.slot("n_V")
    s_bT = P.slot("n_bT")
    s_wq = P.slot("n_wq")
    s_PT = P.slots_n("n_PT", 3)
    s_EB = P.slot("n_EB")
    s_cpad = P.slot("n_cpad")
    s_xn = P.slots_n("n_xn", 2)
    s_sq = P.slots_n("n_sq", 2)
    s_xt = P.slots_n("n_xt", 2)
    s_ost = P.slots_n("n_ost", 2)
    s_gbc = P.slot("n_gbc")
    s_raw = P.slots_n("n_raw", 2)
    s_rinv = P.slots_n("n_rinv", 2)
    s_ss = P.slots_n("n_ss", 2)
    s_rec = P.slots_n("n_rec", 2)
    s_gq = P.slot("n_gq")
    ps, pss = c.ps, c.ps_slots

    P.dma("sp", "gbc", lambda h: h.dma_start(out=gbc, in_=gain_row.partition_broadcast(128)), writes=[s_gbc])
    P.dma("sp", "gq", lambda h: h.dma_start(out=gq, in_=qkg), writes=[s_gq])
    P.op("dve", lambda h: h.memset(Vpad[:, :, :, :, :].rearrange("p a b c d -> p (a b c d)"), 0.0), writes=[s_V])
    P.op("dve", lambda h: h.memset(cpad[:, :, :].rearrange("p a b -> p (a b)"), 0.0), writes=[s_cpad])
    P.op("dve", lambda h: h.memset(cpad[:, 0, 0:64], 1.0), writes=[s_cpad])
    P.op("dve", lambda h: h.memset(cpad[:, 1, 64:128], 1.0), writes=[s_cpad])
    P.op("dve", lambda h: h.memset(zrhs, 0.0), writes=[s_cpad])

    for t in range(16):
        j = t % 2
        r0 = tok0 + t * 128
        P.dma("sp", f"nxt{j}", lambda h, j=j, r0=r0: h.dma_start(out=xt[j], in_=xin[r0:r0 + 128, :]),
              reads=[xin_slot], writes=[s_xt[j]])
        norm_a(c, xt[j], s_xt[j], gbc, s_gbc, xn[j], s_xn[j], ss[j], s_ss[j])
        if t >= 1:
            norm_b(c, xn[1 - j], s_xn[1 - j], hT, s_hT[t - 1], (t - 1) * 128)
    norm_b(c, xn[1], s_xn[1], hT, s_hT[15], 15 * 128)

    Wo = hT[:, :, 0:1024]
    s_Wo = P.slot("n_Wo")
    ui = 0
    for hg in DBG.get('hgs', range(4)):
        for part in range(3):
            P.dma("pool", "nwq", lambda h, hg=hg, part=part: h.dma_start(
                out=wq[:, :, part * 256:(part + 1) * 256], in_=wqkv[hg][:, :, part * 256:(part + 1) * 256]),
                writes=[s_wq])
        P.dma("pool", "nbT", lambda h, hg=hg: h.dma_start(out=bT, in_=biasT[hg]), writes=[s_bT])
        qk_units = [(part, cc, tb) for part in DBG.get('qk', range(2)) for cc in range(2) for tb in range(4)]
        ubase = ui

        def qk_front(n):
            part, cc, tb = qk_units[n]
            u = (ubase + n) % 2
            pq, spq = ps[u], pss[u]
            for k in range(8):
                P.op("pe", lambda h, k=k: h.matmul(
                    pq[:, :], lhsT=wq[:, k, part * 256 + cc * 128: part * 256 + cc * 128 + 128],
                    rhs=hT[:, k, tb * 512:(tb + 1) * 512], start=(k == 0), stop=(k == 7)),
                    reads=[s_wq] + s_hT[tb * 4:(tb + 1) * 4], writes=[spq])
            P.op("act", lambda h: h.activation(out=sq[u], in_=pq[:, :], func=AF.Square), reads=[spq], writes=[s_sq[u]])
            P.op("dve", lambda h: h.tensor_copy(out=raw[u], in_=pq[:, :]), reads=[spq, s_sq[u]], writes=[s_raw[u]])

        def qk_back(n):
            part, cc, tb = qk_units[n]
            dst, s_dst = (qT, s_qT) if part == 0 else (kT, s_kT)
            u = (ubase + n) % 2
            pn, spn = ps[2 + u], pss[2 + u]
            P.op("pe", lambda h: h.matmul(pn[:, :], lhsT=c.bd[:], rhs=sq[u], start=True, stop=True),
                 reads=[s_sq[u], c.bd_slot], writes=[spn])
            if part == 0:
                P.op("act", lambda h: h.activation(out=rinv[u], in_=pn[:, :], func=AF.Ln, bias=c.cst[:, 0:1], scale=1.0),
                     reads=[spn, c.cst_slot], writes=[s_rinv[u]])
            else:
                P.op("act", lambda h: h.activation(out=rinv[u], in_=pn[:, :], func=AF.Ln, bias=c.cst[:, 1:2], scale=1.0 / 64),
                     reads=[spn, c.cst_slot], writes=[s_rinv[u]])
            P.op("act", lambda h: h.activation(out=rinv[u], in_=rinv[u], func=AF.Exp, scale=-0.5),
                 reads=[s_rinv[u]], writes=[s_rinv[u]])
            P.op("dve", lambda h: h.scalar_tensor_tensor(
                out=dst[:, cc, tb * 512:(tb + 1) * 512], in0=raw[u], scalar=gq[:, part:part + 1], in1=rinv[u],
                op0=ALU.mult, op1=ALU.mult), reads=[s_raw[u], s_rinv[u], s_gq], writes=[s_dst])

        for n in range(len(qk_units) + 1):
            if n < len(qk_units):
                qk_front(n)
            if n >= 1:
                qk_back(n - 1)
        ui += len(qk_units)
        for par in DBG.get('vpar', range(2)):
            for i in range(16 - par):
                u = ui % 2
                ui += 1
                pv, spv = ps[u], pss[u]
                t0 = par * 64 + i * 128
                for k in range(8):
                    P.op("pe", lambda h, k=k, t0=t0, pv=pv: h.matmul(
                        pv[:, 0:256], lhsT=hT[:, k, t0:t0 + 128], rhs=wq[:, k, 512:768], start=(k == 0), stop=(k == 7)),
                        reads=[s_wq] + s_hT[t0 // 128:(t0 + 127) // 128 + 1], writes=[spv])
                for hp in range(2):
                    P.op("act", lambda h, par=par, i=i, pv=pv, hp=hp: h.copy(
                        out=cv(Vpad, 0, 128, ((par * 16 + i) * 4 + hp) * 128 + hp * 64, [[256, 2], [1, 64]]),
                        in_=cv_ps(pv, hp * 64, [[128, 2], [1, 64]])), reads=[spv], writes=[s_V])
        if hg == 3:
            for q in range(4):
                P.dma("pool", "nWo", lambda h, q=q: h.dma_start(out=Wo[:, q * 2:(q + 1) * 2, :], in_=wout[:, q * 2:(q + 1) * 2, :]),
                      writes=[s_Wo] + s_hT)
        P.op("act", lambda h: h.activation(out=EB[:, :, :, :].rearrange("p a b c -> p (a b c)"),
                                           in_=bT[:, :, :, :].rearrange("p a b c -> p (a b c)"), func=AF.Exp),
             reads=[s_bT], writes=[s_EB])
        for cc in range(2):
            for hf in range(2):
                for b in range(4):
                    P.op("pe", lambda h, b=b: h.matmul(ps[b][:, :], lhsT=cpad[:, 2, :], rhs=zrhs, start=True, stop=True,
                                                       skip_group_check=True), reads=[s_cpad], writes=[pss[b]])
                units = [(par, ti, grp, r0_, st_, n_, hp) for (par, ti, grp, r0_, st_, n_) in NA_RUNS[hf] for hp in range(2)]
                LA = 2

                def emit_front(idx, cc=cc):
                    (par, ti, grp, r0_, st_, n_, hp) = units[idx]
                    row0 = 2 * ti + par
                    hh = cc * 2 + hp
                    pb = hp * 64
                    w3 = idx % 3
                    sc, ssc = ps[4 + w3], pss[4 + w3]
                    nq = 64 * n_
                    rho0 = row0 - r0_ + 7
                    P.op("pe", lambda h: h.matmul(
                        sc[:, 0:nq], lhsT=kT[pb:pb + 64, cc, row0 * 64: row0 * 64 + 128],
                        rhs=cv(qT, pb, 64, cc * SEQ + r0_ * 64, [[st_ * 64, n_], [1, 64]]), start=True, stop=True),
                        reads=[s_kT, s_qT], writes=[ssc])
                    P.op("act", lambda h: h.activation(out=PT[w3][:, 0:nq], in_=sc[:, 0:nq], func=AF.Exp),
                         reads=[ssc], writes=[s_PT[w3]])
                    P.op("dve", lambda h: h.tensor_tensor(
                        out=cv(PT[w3], 0, 128, 0, [[64, n_], [1, 64]]), in0=cv(PT[w3], 0, 128, 0, [[64, n_], [1, 64]]),
                        in1=cv(EB, 0, 128, hh * 896 + rho0 * 64, [[-st_ * 64, n_], [1, 64]]), op=ALU.mult),
                        reads=[s_PT[w3], s_EB], writes=[s_PT[w3]])

                def emit_back(idx, cc=cc, hf=hf):
                    (par, ti, grp, r0_, st_, n_, hp) = units[idx]
                    hh = cc * 2 + hp
                    w3 = idx % 3
                    nq = 64 * n_
                    c0 = (r0_ - 16 * hf - 8 * grp) * 64
                    for (bank, lw_) in ((grp, Vpad[:, par, ti, hh, :]), (2 + grp, cpad[:, hp, :])):
                        P.op("pe", lambda h, bank=bank, lw_=lw_: h.matmul(
                            cv_ps(ps[bank], c0, [[st_ * 64, n_], [1, 64]]), lhsT=lw_, rhs=PT[w3][:, 0:nq], start=False, stop=True,
                            skip_group_check=True), reads=[s_PT[w3], s_V, s_cpad], writes=[pss[bank]])

                for idx in range(len(units) + LA):
                    if idx < len(units):
                        emit_front(idx)
                    if idx - LA >= 0:
                        emit_back(idx - LA)
                for grp in range(2):
                    u = grp
                    P.op("act", lambda h, u=u, grp=grp: h.activation(out=rinv[u], in_=ps[2 + grp][:, :], func=AF.Ln),
                         reads=[pss[2 + grp]], writes=[s_rinv[u]])
                    P.op("act", lambda h, u=u: h.activation(out=rinv[u], in_=rinv[u], func=AF.Exp, scale=-1.0),
                         reads=[s_rinv[u]], writes=[s_rinv[u]])
                    P.op("dve", lambda h, u=u, grp=grp, hg=hg, cc=cc, hf=hf: h.tensor_tensor(
                        out=attnT[:, hg * 2 + cc, hf * 1024 + grp * 512: hf * 1024 + (grp + 1) * 512], in0=ps[grp][:, :], in1=rinv[u],
                        op=ALU.mult), reads=[pss[grp], s_rinv[u]], writes=[s_attnT[hg * 2 + cc]])
    if DBG.get('dump') is not None:
        dbg = DBG['dump']
        s_dbg = P.slot("dbg")
        P.dma("pool", "dbg", lambda h: h.dma_start(out=dbg[:, 0:4096], in_=qT.rearrange("p a b -> p (a b)")), reads=[s_qT], writes=[s_dbg])
        P.dma("pool", "dbg", lambda h: h.dma_start(out=dbg[:, 4096:8192], in_=kT.rearrange("p a b -> p (a b)")), reads=[s_kT], writes=[s_dbg])
    P.barrier()
    for t in range(16):
        j = t % 2
        r0 = tok0 + t * 128
        P.dma("sp", f"nxt{j}", lambda h, j=j, r0=r0: h.dma_start(out=xt[j], in_=xin[r0:r0 + 128, :]),
              reads=[xin_slot], writes=[s_xt[j]])
        for half in range(2):
            pd, sd = ps[half], pss[half]
            for k in range(8):
                P.op("pe", lambda h, k=k, t=t, half=half, pd=pd: h.matmul(
                    pd[:, :], lhsT=attnT[:, k, t * 128:(t + 1) * 128], rhs=Wo[:, k, half * 512:(half + 1) * 512],
                    start=(k == 0), stop=(k == 7)), reads=[s_attnT[k], s_Wo], writes=[sd])
            P.op("dve", lambda h, j=j, half=half, pd=pd: h.tensor_tensor(
                out=ost[j][:, half * 512:(half + 1) * 512], in0=pd[:, :], in1=xt[j][:, half * 512:(half + 1) * 512],
                op=ALU.add), reads=[sd, s_xt[j]], writes=[s_ost[j]])
        P.dma("sp", f"nost{j}", lambda h, j=j, r0=r0: h.dma_start(out=xout[r0:r0 + 128, :], in_=ost[j]),
              reads=[s_ost[j]], writes=[xout_slot])
    P.barrier()


def cv(ap, p0, npart, off, dims):
    n = ap.ap[0][0]
    return bass.AP(tensor=ap.tensor, offset=ap.offset + p0 * n + off, ap=[[n, npart]] + [list(d) for d in dims])


GELU_FUNC = [None]


def ab_stage(c, xin, xin_slot, xout, xout_slot, W, gain_row, ntok):
    P = c.P
    fa, ba = c.fa, c.ba
    fa.reset()
    ba.reset()
    NK = 257
    R1 = ba.take(1, 16448)[:, 0, :]
    uT = ba.take(4, SEQ)
    aT = ba.take(4, SEQ + 30)
    yaT = ba.take(4, SEQ)
    Ht = ba.take(32, 2, 128)
    Gt = ba.take(32, 2, 128)
    Mt = ba.take(32, 128)
    selin = ba.take(2, 8, 128)
    selout = ba.take(8, 8, 128)
    wsl = [ba.take(8, 128) for _ in range(2)]
    gluw = ba.take(4, 512)
    xn = [ba.take(1, 1024)[:, 0, :] for _ in range(2)]
    hT = R1[:, 0:16384].rearrange("p (a b) -> p a b", b=SEQ)
    cdiag = R1[:, 0:15872].rearrange("p (q k m) -> p q k m", k=31, m=128)
    XS = R1
    Wo = R1[:, 0:8192].rearrange("p (a b) -> p a b", b=1024)
    UY = aT[:, :, :].rearrange("p a b -> p (a b)")[:, 0:8192].rearrange("p (g k) -> p g k", k=256)
    Qt = R1[:, 0:8192].rearrange("p (r e) -> p r e", r=2)
    Pt = R1[:, 8192:16384].rearrange("p (r e) -> p r e", r=2)
    Hp = uT[:, :, :].rearrange("p a b -> p (a b)").rearrange("p (r e) -> p r e", r=2)
    gbc = fa.take(1, 1024)[:, 0, :]
    lam = fa.take(3, 32)
    Apw = fa.take(2, 32, 32)
    msk = fa.take(2, 128)
    dcol = fa.take(1, 32)[:, 0, :]
    cw = fa.take(4, 31)
    vec = fa.take(1, 20)[:, 0, :]
    ones32 = fa.take(1, 128)[:, 0, :]
    etab = fa.take(1, 32)[:, 0, :]
    TA = fa.take(1, 64)[:, 0, :]
    TB = fa.take(1, 64)[:, 0, :]
    Sf = fa.take(2, 128)
    st1 = fa.take(1, 64)[:, 0, :]
    st2 = fa.take(1, 64)[:, 0, :]
    sm = fa.take(12, 32)
    ss = [fa.take(1, 4)[:, 0, :] for _ in range(2)]
    tmpA = fa.take(1, 128)[:, 0, :]
    tmpB = fa.take(1, 128)[:, 0, :]
    Dreg = fa.take(1, 7168)[:, 0, :]
    Bp = Dreg[:, 0:1024]
    Cp = Dreg[:, 1024:2048]
    Bb = Dreg[:, 2048:3072]
    t1 = Dreg[:, 3072:4096]
    t2 = Dreg[:, 4096:5120]
    w5 = [Dreg[:, 5120:6144], Dreg[:, 6144:7168], Dreg[:, 2048:3072]]
    xt = [Dreg[:, 0:1024], Dreg[:, 1024:2048]]
    ost = [Dreg[:, 2048:3072], Dreg[:, 3072:4096]]
    sgm = [Dreg[:, 0:512], Dreg[:, 512:1024]]
    acv = Dreg[:, 0:2048].rearrange("p (q t) -> p q t", t=512)
    sqv = [Dreg[:, 2048:2560], Dreg[:, 2560:3072]]
    meanv = Dreg[:, 3072:3584]
    rstdv = Dreg[:, 3584:4096]
    tmpv = Dreg[:, 4096:4608]
    ynv = [Dreg[:, 4608:5120], Dreg[:, 5120:5632]]
    ps, pss = c.ps, c.ps_slots
    ident = c.ident

    S = lambda n: P.slot("ab_" + n)
    s_tab = S("tab")
    s_D = S("Dreg")
    s_R1 = S("R1")
    s_uT = S("uT")
    s_aT = S("aT")
    s_ya = S("yaT")
    s_H, s_G, s_M = S("H"), S("G"), S("M")
    s_sel = S("sel")
    s_wsl = [S("wsl0"), S("wsl1")]
    s_gluw = S("gluw")
    s_xn = [S("xn0"), S("xn1")]
    s_gbc = S("gbc")
    s_ss = [S("ss0"), S("ss1")]
    s_tmp = S("tmpAB")

    def dv(fn, r, w, force=True):
        P.op("dve", fn, reads=r, writes=w, force=force)

    def ac(fn, r, w, force=True):
        P.op("act", fn, reads=r, writes=w, force=force)

    P.dma("sp", "gbc", lambda h: h.dma_start(out=gbc, in_=gain_row.partition_broadcast(128)), writes=[s_gbc])
    for dst, src in ((lam, W["lam"]), (msk, W["mask"]), (dcol, W["dcol"]), (cw, W["cw"]), (vec, W["vec"]),
                     (ones32, W["ones"]), (etab, W["etab"]), (Bp, W["B"]), (Cp, W["C"])):
        P.dma("sp", "abtab", lambda h, dst=dst, src=src: h.dma_start(out=dst, in_=src), writes=[s_tab])
    P.dma("pool", "absel", lambda h: h.dma_start(out=selin, in_=W["selin"]), writes=[s_sel])
    for q in range(2):
        P.dma("pool", "absel", lambda h, q=q: h.dma_start(out=selout[:, q * 4:(q + 1) * 4], in_=W["selout"][:, q * 4:(q + 1) * 4]),
              writes=[s_sel])
    P.dma("pool", "abglu", lambda h: h.dma_start(out=gluw, in_=W["gluw"]), writes=[s_gluw])

    T = [s_tab]
    sm_ = lambda i: sm[:, i, :]
    m_dt, m_lrd, m_lid, m_pr, m_den, m_cre, m_cim, m_x, m_y = (sm_(i) for i in range(9))
    lr, li, ls = lam[:, 0, :], lam[:, 1, :], lam[:, 2, :]
    ac(lambda h: h.activation(out=m_dt, in_=ls, func=AF.Exp), T, T)
    dv(lambda h: h.tensor_tensor(out=m_lrd, in0=lr, in1=m_dt, op=ALU.mult), T, T)
    dv(lambda h: h.tensor_tensor(out=m_lid, in0=li, in1=m_dt, op=ALU.mult), T, T)
    bc_g = lambda a: cv(a, 0, 128, 0, [[0, 32], [1, 32]])
    bc_e = cv(etab, 0, 128, 0, [[1, 32], [0, 32]])
    v3 = lambda a: cv(a, 0, 128, 0, [[32, 32], [1, 32]])
    argm, ang, kf = w5
    ki = t1[:, 0:1024].bitcast(I32)
    dv(lambda h: h.tensor_tensor(out=v3(argm), in0=bc_g(m_lrd), in1=bc_e, op=ALU.mult), T, T)
    ac(lambda h: h.activation(out=argm, in_=argm, func=AF.Exp), T, T)
    dv(lambda h: h.tensor_tensor(out=v3(ang), in0=bc_g(m_lid), in1=bc_e, op=ALU.mult), T, T)
    trig = t2[:, 0:1024]
    for ri, shift in ((1, 0.0), (0, math.pi / 2)):
        src = ang
        if shift != 0.0:
            dv(lambda h: h.tensor_scalar(out=ang, in0=ang, scalar1=shift, scalar2=None, op0=ALU.add), T, T)
        dv(lambda h: h.tensor_scalar(out=kf, in0=ang, scalar1=1.0 / (2 * math.pi), scalar2=None, op0=ALU.mult), T, T)
        dv(lambda h: h.tensor_copy(out=ki, in_=kf), T, T)
        dv(lambda h: h.tensor_copy(out=kf, in_=ki), T, T)
        dv(lambda h: h.scalar_tensor_tensor(out=kf, in0=kf, scalar=-2 * math.pi, in1=ang, op0=ALU.mult, op1=ALU.add), T, T)
        ac(lambda h: h.activation(out=trig, in_=kf, func=AF.Sin), T, T)
        dv(lambda h, ri=ri: h.tensor_tensor(out=Apw[:, ri, :, :].rearrange("p a b -> p (a b)"), in0=argm, in1=trig, op=ALU.mult), T, T)
    for (p0_, j1) in ((0, 16), (64, 23)):
        hp = lambda a, p0_=p0_: cv(a, p0_, 64, 0, [[1, 32]])
        A1r = cv(Apw, p0_, 64, j1 * 32, [[1, 32]])
        A1i = cv(Apw, p0_, 64, 1024 + j1 * 32, [[1, 32]])
        lr_, li_ = hp(lr), hp(li)
        dv(lambda h, hp=hp, A1r=A1r: h.tensor_scalar(out=hp(m_pr), in0=A1r, scalar1=-1.0, scalar2=None, op0=ALU.add), T, T)
        dv(lambda h, hp=hp, lr_=lr_: h.tensor_tensor(out=hp(m_x), in0=lr_, in1=lr_, op=ALU.mult), T, T)
        dv(lambda h, hp=hp, li_=li_: h.tensor_tensor(out=hp(m_y), in0=li_, in1=li_, op=ALU.mult), T, T)
        dv(lambda h, hp=hp: h.tensor_tensor(out=hp(m_den), in0=hp(m_x), in1=hp(m_y), op=ALU.add), T, T)
        dv(lambda h, hp=hp: h.reciprocal(out=hp(m_den), in_=hp(m_den)), T, T)
        dv(lambda h, hp=hp, lr_=lr_: h.tensor_tensor(out=hp(m_x), in0=hp(m_pr), in1=lr_, op=ALU.mult), T, T)
        dv(lambda h, hp=hp, li_=li_, A1i=A1i: h.tensor_tensor(out=hp(m_y), in0=A1i, in1=li_, op=ALU.mult), T, T)
        dv(lambda h, hp=hp: h.tensor_tensor(out=hp(m_x), in0=hp(m_x), in1=hp(m_y), op=ALU.add), T, T)
        dv(lambda h, hp=hp: h.tensor_tensor(out=hp(m_cre), in0=hp(m_x), in1=hp(m_den), op=ALU.mult), T, T)
        dv(lambda h, hp=hp, lr_=lr_, A1i=A1i: h.tensor_tensor(out=hp(m_x), in0=A1i, in1=lr_, op=ALU.mult), T, T)
        dv(lambda h, hp=hp, li_=li_: h.tensor_tensor(out=hp(m_y), in0=hp(m_pr), in1=li_, op=ALU.mult), T, T)
        dv(lambda h, hp=hp: h.tensor_tensor(out=hp(m_x), in0=hp(m_x), in1=hp(m_y), op=ALU.subtract), T, T)
        dv(lambda h, hp=hp: h.tensor_tensor(out=hp(m_cim), in0=hp(m_x), in1=hp(m_den), op=ALU.mult), T, T)
    bcc = lambda a: cv(a, 0, 128, 0, [[1, 32], [0, 16]])
    g16 = lambda a, off: cv(a, 0, 128, off, [[16, 32], [1, 16]])
    for (o_off, a_, x_off, b_, y_off, op) in ((0, m_cre, 0, m_cim, 512, ALU.subtract), (512, m_cre, 512, m_cim, 0, ALU.add)):
        dv(lambda h, a_=a_, x_off=x_off: h.tensor_tensor(out=g16(t1, 0), in0=bcc(a_), in1=g16(Bp, x_off), op=ALU.mult), T, T)
        dv(lambda h, b_=b_, y_off=y_off: h.tensor_tensor(out=g16(t2, 0), in0=bcc(b_), in1=g16(Bp, y_off), op=ALU.mult), T, T)
        dv(lambda h, o_off=o_off, op=op: h.tensor_tensor(out=Bb[:, o_off:o_off + 512], in0=t1[:, 0:512], in1=t2[:, 0:512], op=op), T, T)

    def fam(dst, d_goff, d_rioff, V, negim, j0):
        np_, p0, js = 128, 0, 1
        for qq in range(4):
            Ar = cv(Apw, p0, np_, j0 * 32 + 8 * qq, [[1, 8], [js * 32, 8], [0, 16]])
            Ai = cv(Apw, p0, np_, 1024 + j0 * 32 + 8 * qq, [[1, 8], [js * 32, 8], [0, 16]])
            Vr = cv(V, p0, np_, 8 * qq * 16, [[16, 8], [0, 8], [1, 16]])
            Vi = cv(V, p0, np_, 512 + 8 * qq * 16, [[16, 8], [0, 8], [1, 16]])
            T1 = cv(t1, p0, np_, 0, [[128, 8], [16, 8], [1, 16]])
            T2 = cv(t2, p0, np_, 0, [[128, 8], [16, 8], [1, 16]])
            dre = cv(dst, p0, np_, 8 * qq * d_goff, [[d_goff, 8], [16, 8], [1, 16]])
            dim_ = cv(dst, p0, np_, 8 * qq * d_goff + d_rioff, [[d_goff, 8], [16, 8], [1, 16]])
            dv(lambda h, Ar=Ar, Vr=Vr, T1=T1: h.tensor_tensor(out=T1, in0=Ar, in1=Vr, op=ALU.mult), T, T)
            dv(lambda h, Ai=Ai, Vi=Vi, T2=T2: h.tensor_tensor(out=T2, in0=Ai, in1=Vi, op=ALU.mult), T, T)
            dv(lambda h, dre=dre, T1=T1, T2=T2: h.tensor_tensor(out=dre, in0=T1, in1=T2, op=ALU.subtract), T, T)
            dv(lambda h, Ar=Ar, Vi=Vi, T1=T1: h.tensor_tensor(out=T1, in0=Ar, in1=Vi, op=ALU.mult), T, T)
            dv(lambda h, Ai=Ai, Vr=Vr, T2=T2: h.tensor_tensor(out=T2, in0=Ai, in1=Vr, op=ALU.mult), T, T)
            if negim:
                dv(lambda h, T1=T1: h.tensor_scalar(out=T1, in0=T1, scalar1=-1.0, scalar2=None, op0=ALU.mult), T, T)
                dv(lambda h, dim_=dim_, T1=T1, T2=T2: h.tensor_tensor(out=dim_, in0=T1, in1=T2, op=ALU.subtract), T, T)
            else:
                dv(lambda h, dim_=dim_, T1=T1, T2=T2: h.tensor_tensor(out=dim_, in0=T1, in1=T2, op=ALU.add), T, T)

    fam(Qt, 128, 4096, Bb, False, 0)
    fam(Pt, 128, 4096, Cp, True, 8)
    fam(Gt, 256, 128, Cp, True, 16)
    fam(Hp, 128, 4096, Bb, False, 24)
    for (p0_, j8) in ((0, 23), (64, 16)):
        Dr = cv(Apw, p0_, 64, j8 * 32, [[1, 32]])
        Di = cv(Apw, p0_, 64, 1024 + j8 * 32, [[1, 32]])
        hq = lambda a, off, p0_=p0_: cv(a, p0_, 64, off, [[1, 32]])
        dv(lambda h, hq=hq, Dr=Dr: h.tensor_copy(out=hq(TA, 0), in_=Dr), T, T)
        dv(lambda h, hq=hq, Dr=Dr: h.tensor_copy(out=hq(TA, 32), in_=Dr), T, T)
        dv(lambda h, hq=hq, Di=Di: h.tensor_scalar(out=hq(TB, 0), in0=Di, scalar1=-1.0, scalar2=None, op0=ALU.mult), T, T)
        dv(lambda h, hq=hq, Di=Di: h.tensor_copy(out=hq(TB, 32), in_=Di), T, T)
    for g in range(32):
        u = g % 2
        pf, pb = ps[2 * u], ps[2 * u + 1]
        sf_, sb_ = pss[2 * u], pss[2 * u + 1]
        for (pp, sp_, p0) in ((pf, sf_, 0), (pb, sb_, 64)):
            for ri in range(2):
                P.op("pe", lambda h, pp=pp, p0=p0, ri=ri, g=g: h.matmul(
                    pp[:, 0:128], lhsT=Qt[p0:p0 + 64, ri, g * 128:(g + 1) * 128], rhs=Pt[p0:p0 + 64, ri, g * 128:(g + 1) * 128],
                    start=(ri == 0), stop=(ri == 1)), reads=T, writes=[sp_])
        dv(lambda h, pf=pf: h.tensor_tensor(out=tmpA, in0=pf[:, 0:128], in1=msk[:, 0, :], op=ALU.mult), [sf_] + T, [s_tmp])
        dv(lambda h, pb=pb: h.tensor_tensor(out=tmpB, in0=pb[:, 0:128], in1=msk[:, 1, :], op=ALU.mult), [sb_] + T, [s_tmp])
        dv(lambda h: h.tensor_tensor(out=tmpA, in0=tmpA, in1=tmpB, op=ALU.add), [s_tmp], [s_tmp])
        dv(lambda h, g=g: h.scalar_tensor_tensor(out=Mt[:, g, :], in0=ident[:], scalar=dcol[:, g:g + 1], in1=tmpA,
                                                  op0=ALU.mult, op1=ALU.add), [s_tmp, c.ident_slot] + T, [s_M])
    for b in range(8):
        for gg in range(4):
            for ri in range(2):
                g = b * 4 + gg
                sl = gg * 2 + ri
                P.op("pe", lambda h, g=g, ri=ri, sl=sl: h.transpose(
                    out=c.psb[:, sl * 128:(sl + 1) * 128], in_=Hp[:, ri, g * 128:(g + 1) * 128], identity=ident[:]),
                    reads=T + [c.ident_slot], writes=[c.psb_slot])
        ac(lambda h, b=b: h.copy(out=Ht[:, b * 4:(b + 1) * 4, :, :].rearrange("p a b c -> p (a b c)"), in_=c.psb[:, :]),
           [c.psb_slot], [s_H])
    P.barrier()

    for sq_ in range(ntok // SEQ):
        tok0 = sq_ * SEQ
        s_hT = [S(f"hT{t}") for t in range(16)]
        s_xt1 = [S("xt1_0"), S("xt1_1")]
        for t in range(16):
            j = t % 2
            r0 = tok0 + t * 128
            P.dma("sp", f"abxt{j}", lambda h, j=j, r0=r0: h.dma_start(out=xt[j], in_=xin[r0:r0 + 128, :]),
                  reads=[xin_slot], writes=[s_xt1[j]])
            norm_a(c, xt[j], s_xt1[j], gbc, s_gbc, xn[j], s_xn[j], ss[j], s_ss[j])
            if t >= 1:
                norm_b(c, xn[1 - j], s_xn[1 - j], hT, s_hT[t - 1], (t - 1) * 128)
        norm_b(c, xn[1], s_xn[1], hT, s_hT[15], 15 * 128)
        P.barrier()
        s_sg = [S("sg0"), S("sg1")]
        s_aTq = [S(f"aT{q}") for q in range(4)]
        s_uTq = [S(f"uT{q}") for q in range(4)]
        for q in range(4):
            dv(lambda h, q=q: h.memset(aT[:, q, 0:15], 0.0), [], [s_aTq[q]], force=False)
            dv(lambda h, q=q: h.memset(aT[:, q, SEQ + 15:SEQ + 30], 0.0), [], [s_aTq[q]], force=False)
        wi = 0
        ui = 0

        def load_slab(oc):
            nonlocal wi
            j = wi % 2
            wi += 1
            P.dma("pool", f"abw{j}", lambda h, j=j, oc=oc: h.dma_start(out=wsl[j], in_=W["win"][oc]), writes=[s_wsl[j]])
            return j

        for q in range(4):
            ja = load_slab(q)
            jg = load_slab(q + 4)
            for tb in range(4):
                u = ui % 2
                ui += 1
                pa, pg = ps[2 * u], ps[2 * u + 1]
                sa, sg_ = pss[2 * u], pss[2 * u + 1]
                for (pp, sp_, jj) in ((pa, sa, ja), (pg, sg_, jg)):
                    for k in range(8):
                        P.op("pe", lambda h, pp=pp, jj=jj, k=k, tb=tb: h.matmul(
                            pp[:, :], lhsT=wsl[jj][:, k, :], rhs=hT[:, k, tb * 512:(tb + 1) * 512],
                            start=(k == 0), stop=(k == 7)), reads=[s_wsl[jj]] + s_hT[tb * 4:(tb + 1) * 4], writes=[sp_])
                ac(lambda h, u=u, pg=pg: h.activation(out=sgm[u], in_=pg[:, :], func=AF.Sigmoid), [sg_], [s_sg[u]], force=False)
                dv(lambda h, u=u, pa=pa, q=q, tb=tb: h.tensor_tensor(
                    out=aT[:, q, 15 + tb * 512:15 + (tb + 1) * 512], in0=sgm[u], in1=pa[:, :], op=ALU.mult),
                    [s_sg[u], sa], [s_aTq[q]], force=False)
        for q in range(4):
            ju = load_slab(q + 8)
            for tb in range(4):
                u = ui % 2
                ui += 1
                pu_, su_ = ps[4 + u], pss[4 + u]
                for k in range(8):
                    P.op("pe", lambda h, pu_=pu_, ju=ju, k=k, tb=tb: h.matmul(
                        pu_[:, :], lhsT=wsl[ju][:, k, :], rhs=hT[:, k, tb * 512:(tb + 1) * 512],
                        start=(k == 0), stop=(k == 7)), reads=[s_wsl[ju]] + s_hT[tb * 4:(tb + 1) * 4], writes=[su_])
                ac(lambda h, pu_=pu_, q=q, tb=tb: h.copy(
                    out=cv(uT, 0, 128, q * SEQ + tb * 64, [[256, 8], [1, 64]]),
                    in_=pu_[:, :].rearrange("p (k t) -> p t k", t=8)), [su_], [s_uTq[q]], force=False)
        P.barrier()
        s_cd = [S(f"cd{q}") for q in range(4)]
        for q in range(4):
            for k in range(31):
                dv(lambda h, q=q, k=k: h.tensor_scalar(out=cdiag[:, q, k, :], in0=ident[:], scalar1=cw[:, q, k:k + 1],
                                                        scalar2=None, op0=ALU.mult), [c.ident_slot], [s_cd[q]], force=False)
        s_ac = [S(f"ac{q}") for q in range(4)]
        s_sq = [S("sq0"), S("sq1")]
        s_mean, s_rstd, s_tv = S("mean"), S("rstd"), S("tmpv")
        s_yn = [S("yn0"), S("yn1")]
        for tb in range(4):
            for q in range(4):
                pc, spc = ps[q % 2], pss[q % 2]
                for k in range(31):
                    P.op("pe", lambda h, pc=pc, q=q, k=k, tb=tb: h.matmul(
                        pc[:, :], lhsT=cdiag[:, q, k, :], rhs=aT[:, q, tb * 512 + k: tb * 512 + k + 512],
                        start=(k == 0), stop=(k == 30)), reads=[s_cd[q]], writes=[spc])
                ac(lambda h, pc=pc, q=q: h.activation(out=acv[:, q, :], in_=pc[:, :], func=AF.Identity,
                                                      bias=vec[:, q:q + 1], scale=1.0), [spc], [s_ac[q]], force=False)
            pm, spm = ps[2], pss[2]
            pe2, spe2 = ps[3], pss[3]
            for q in range(4):
                P.op("pe", lambda h, q=q: h.matmul(pm[:, :], lhsT=cv(ones32, 0, 128, 0, [[1, 128]]), rhs=acv[:, q, :],
                                                   start=(q == 0), stop=(q == 3)), reads=[s_ac[q]], writes=[spm])
            for q in range(4):
                u = q % 2
                ac(lambda h, q=q, u=u: h.activation(out=sqv[u], in_=acv[:, q, :], func=AF.Square), [s_ac[q]], [s_sq[u]], force=False)
                P.op("pe", lambda h, q=q, u=u: h.matmul(pe2[:, :], lhsT=cv(ones32, 0, 128, 0, [[1, 128]]), rhs=sqv[u],
                                                        start=(q == 0), stop=(q == 3)), reads=[s_sq[u]], writes=[spe2])
            dv(lambda h: h.tensor_copy(out=meanv, in_=pm[:, :]), [spm], [s_mean], force=False)
            dv(lambda h: h.tensor_tensor(out=tmpv, in0=meanv, in1=meanv, op=ALU.mult), [s_mean], [s_tv])
            dv(lambda h: h.tensor_tensor(out=tmpv, in0=pe2[:, :], in1=tmpv, op=ALU.subtract), [spe2, s_tv], [s_tv])
            ac(lambda h: h.activation(out=rstdv, in_=tmpv, func=AF.Ln, bias=c.cst[:, 2:3], scale=1.0), [s_tv, c.cst_slot], [s_rstd])
            ac(lambda h: h.activation(out=rstdv, in_=rstdv, func=AF.Exp, scale=-0.5), [s_rstd], [s_rstd])
            for q in range(4):
                u = q % 2
                dv(lambda h, q=q, u=u: h.tensor_tensor(out=ynv[u], in0=acv[:, q, :], in1=meanv, op=ALU.subtract),
                   [s_ac[q], s_mean], [s_yn[u]], force=False)
                dv(lambda h, u=u: h.tensor_tensor(out=ynv[u], in0=ynv[u], in1=rstdv, op=ALU.mult), [s_yn[u], s_rstd], [s_yn[u]])
                ac(lambda h, q=q, u=u, tb=tb: h.activation(out=yaT[:, q, tb * 512:(tb + 1) * 512], in_=ynv[u], func=AF.Silu,
                                                           scale=vec[:, 4 + q:5 + q], bias=vec[:, 8 + q:9 + q]),
                   [s_yn[u]], [s_ya], force=False)
        P.barrier()
        s_UY = [S(f"UY{g}") for g in range(32)]
        s_X = [S("Xf"), S("Xb")]
        s_hist = [S("histf"), S("histb")]
        XSv = lambda p0, ri, g, c0, n: cv(XS, p0, 64, (ri * 32 + g) * NK + c0, [[1, n]])
        dv(lambda h: h.memset(cv(XS, 0, 64, 0, [[NK, 64]]), 0.0), [], [s_hist[0]], force=False)
        dv(lambda h: h.memset(cv(XS, 64, 64, 255, [[NK, 64]]), 0.0), [], [s_hist[1]], force=False)
        for g in range(32):
            q, j, par = g // 8, (g % 8) // 2, g % 2
            u = g % 2
            pu_, su_ = ps[u], pss[u]
            for tp in range(8):
                rhs = cv(uT, 32 * j, 32, q * SEQ + tp * 256, [[1, 256]])
                P.op("pe", lambda h, pu_=pu_, j=j, par=par, tp=tp, rhs=rhs: h.matmul(
                    pu_[:, 0:256], lhsT=selin[32 * j:32 * j + 32, par, tp, :], rhs=rhs, start=(tp == 0), stop=(tp == 7),
                    tile_position=(32 * j, 0)), reads=[s_uTq[q], s_sel], writes=[su_])
            ac(lambda h, pu_=pu_, g=g: h.copy(out=UY[:, g, :], in_=pu_[:, 0:256]), [su_], [s_UY[g]], force=False)
            for ri in range(2):
                px, spx = ps[2 + ri], pss[2 + ri]
                P.op("pe", lambda h, px=px, g=g, ri=ri: h.matmul(px[:, 0:256], lhsT=Ht[:, g, ri, :], rhs=UY[:, g, :],
                                                               start=True, stop=True), reads=[s_H, s_UY[g]], writes=[spx])
                dv(lambda h, px=px, g=g, ri=ri: h.tensor_copy(out=XSv(0, ri, g, 1, 256), in_=px[0:64, 0:256]),
                   [spx], [s_X[0]], force=False)
                dv(lambda h, px=px, g=g, ri=ri: h.tensor_copy(out=XSv(64, ri, g, 0, 255), in_=px[64:128, 1:256]),
                   [spx], [s_X[1]], force=False)
        dirs = []
        for (eng, p0, d, cols) in ((DBG.get("scan_f", "dve"), 0, 0, list(range(1, 256))),
                                   (DBG.get("scan_b", "dve"), 64, 1, list(range(254, -1, -1)))):
            s_S = [S(f"S{d}_0"), S(f"S{d}_1")]
            s_w_, s_v_ = S(f"w_{d}"), S(f"v_{d}")
            P.op(eng, lambda h, p0=p0: h.memset(cv(Sf, p0, 64, 0, [[1, 128]]), 0.0), writes=[s_S[0]])
            dirs.append((eng, p0, d, cols, s_S, s_w_, s_v_))
        for i in range(255):
            for (eng, p0, d, cols, s_S, s_w_, s_v_) in dirs:
                col = cols[i]
                pv_, cu = i % 2, (i + 1) % 2
                xcol = cv(XS, p0, 64, col, [[32 * NK, 2], [NK, 32]])
                xcol2 = cv(XS, p0, 64, col, [[0, 2], [32 * NK, 2], [NK, 32]])
                P.op(eng, lambda h, p0=p0, pv_=pv_: h.tensor_tensor(
                    out=cv(st1, p0, 64, 0, [[64, 2], [32, 2], [1, 32]]), in0=cv(TA, p0, 64, 0, [[64, 2], [32, 2], [1, 32]]),
                    in1=cv(Sf, p0, 64, pv_ * 128, [[32, 2], [32, 2], [1, 32]]), op=ALU.mult),
                    reads=[s_S[pv_]], writes=[s_w_], force=DBG.get('scanforce', False))
                P.op(eng, lambda h, p0=p0: h.tensor_tensor(
                    out=cv(tmpA, p0, 64, 0, [[1, 64]]), in0=cv(st1, p0, 64, 0, [[1, 64]]), in1=cv(st1, p0, 64, 64, [[1, 64]]),
                    op=ALU.add), reads=[s_w_], writes=[s_v_], force=DBG.get('scanforce', False))
                P.op(eng, lambda h, p0=p0, cu=cu, xcol2=xcol2: h.tensor_tensor(
                    out=cv(Sf, p0, 64, cu * 128, [[64, 2], [32, 2], [1, 32]]), in0=cv(tmpA, p0, 64, 0, [[0, 2], [32, 2], [1, 32]]),
                    in1=xcol2, op=ALU.add), reads=[s_v_, s_X[d]], writes=[s_S[cu]], force=DBG.get('scanforce', False))
                P.op("act", lambda h, p0=p0, cu=cu, xcol=xcol: h.copy(out=xcol, in_=cv(Sf, p0, 64, cu * 128, [[32, 2], [1, 32]])),
                     reads=[s_S[cu]], writes=[s_hist[d]])
        for g in range(32):
            u = g % 2
            py, spy = ps[4 + u], pss[4 + u]
            P.op("pe", lambda h, py=py, g=g: h.matmul(py[:, 0:256], lhsT=Mt[:, g, :], rhs=UY[:, g, :], start=True, stop=False),
                 reads=[s_M, s_UY[g]], writes=[spy])
            for ri in range(2):
                P.op("pe", lambda h, py=py, g=g, ri=ri: h.matmul(
                    py[:, 0:256], lhsT=Gt[:, g, ri, :], rhs=cv(XS, 0, 128, (ri * 32 + g) * NK, [[1, 256]]),
                    start=False, stop=(ri == 1)), reads=[s_G, s_hist[0], s_hist[1], s_X[0], s_X[1]], writes=[spy])
            ac(lambda h, py=py, g=g: h.copy(out=UY[:, g, :], in_=py[:, 0:256]), [spy], [s_UY[g]], force=False)
        s_Wo = S("Wo")
        for q in range(4):
            P.dma("pool", "abWo", lambda h, q=q: h.dma_start(out=Wo[:, q * 2:(q + 1) * 2, :], in_=W["wout"][:, q * 2:(q + 1) * 2, :]),
                  reads=[], writes=[s_Wo, s_X[0], s_X[1], s_hist[0], s_hist[1]])
        s_yg = [S(f"yg{q}") for q in range(4)]
        ci = 0
        for q in range(4):
            for tau in range(8):
                u = ci % 2
                ci += 1
                pz, spz = ps[u], pss[u]
                for g8 in range(8):
                    P.op("pe", lambda h, pz=pz, tau=tau, g8=g8, q=q: h.matmul(
                        pz[:, 0:256], lhsT=selout[:, tau, g8, :], rhs=UY[:, 8 * q + g8, :], start=(g8 == 0), stop=(g8 == 7)),
                        reads=[s_sel, s_UY[8 * q + g8]], writes=[spz])
                gelu_evac(c, pz, spz, cv(uT, 0, 128, q * SEQ + tau * 256, [[1, 256]]), s_yg[q], s_uTq[q])
        s_sg2 = [S("sg2_0"), S("sg2_1")]
        for tb in range(4):
            for o in range(4):
                pz, spz = ps[2 + o], pss[2 + o]
                for kq in range(4):
                    P.op("pe", lambda h, pz=pz, kq=kq, o=o, tb=tb: h.matmul(
                        pz[:, :], lhsT=gluw[:, kq, o * 128:(o + 1) * 128], rhs=uT[:, kq, tb * 512:(tb + 1) * 512],
                        start=(kq == 0), stop=(kq == 3)), reads=[s_gluw] + s_yg, writes=[spz])
            for o in range(4):
                u = o % 2
                pz, spz = ps[2 + o], pss[2 + o]
                ac(lambda h, pz=pz, o=o, u=u: h.activation(out=sgm[u], in_=pz[:, :], func=AF.Sigmoid,
                                                           bias=vec[:, 12 + o:13 + o], scale=1.0), [spz], [s_sg2[u]], force=False)
                dv(lambda h, o=o, u=u, tb=tb: h.tensor_tensor(out=uT[:, o, tb * 512:(tb + 1) * 512],
                                                              in0=uT[:, o, tb * 512:(tb + 1) * 512], in1=sgm[u], op=ALU.mult),
                   [s_sg2[u], s_yg[o]], [s_yg[o]], force=False)
        P.barrier()
        s_xt2 = [S("xt2_0"), S("xt2_1")]
        s_ost = [S("ost0"), S("ost1")]
        xin_v = xin[tok0:tok0 + SEQ, :].rearrange("(k t) d -> t k d", t=8)
        xout_v = xout[tok0:tok0 + SEQ, :].rearrange("(k t) d -> t k d", t=8)
        for t in range(16):
            j = t % 2
            tau, kb = t // 2, t % 2
            P.dma("sp", f"abxt{j}", lambda h, j=j, tau=tau, kb=kb, xin_v=xin_v: h.dma_start(out=xt[j], in_=xin_v[tau, 128 * kb:128 * kb + 128, :]),
                  reads=[xin_slot], writes=[s_xt2[j]])
            for half in range(2):
                pd, sd = ps[half], pss[half]
                for k in range(8):
                    if k < 4:
                        lw_ = cv(yaT, 0, 128, k * SEQ + 1024 * kb + tau, [[8, 128]])
                    else:
                        lw_ = cv(uT, 0, 128, (k - 4) * SEQ + tau * 256 + 128 * kb, [[1, 128]])
                    P.op("pe", lambda h, k=k, half=half, pd=pd, lw_=lw_: h.matmul(
                        pd[:, :], lhsT=lw_, rhs=Wo[:, k, half * 512:(half + 1) * 512],
                        start=(k == 0), stop=(k == 7)), reads=[s_Wo], writes=[sd])
                dv(lambda h, j=j, half=half, pd=pd: h.tensor_tensor(
                    out=ost[j][:, half * 512:(half + 1) * 512], in0=pd[:, :], in1=xt[j][:, half * 512:(half + 1) * 512],
                    op=ALU.add), [sd, s_xt2[j]], [s_ost[j]], force=False)
            P.dma("sp", f"abost{j}", lambda h, j=j, tau=tau, kb=kb, xout_v=xout_v: h.dma_start(out=xout_v[tau, 128 * kb:128 * kb + 128, :], in_=ost[j]),
                  reads=[s_ost[j]], writes=[xout_slot])
        P.barrier()


def gelu_evac(c, pz, spz, dst, s_dst, s_dst2):
    P = c.P
    P.op("act", lambda h: h.activation(out=dst, in_=pz[:, 0:256], func=AF.Gelu_apprx_tanh), reads=[spz], writes=[s_dst, s_dst2])

def build_program(plan=None, ntok=TOK):
    if plan is None:
        plan = []
        for l in range(DEPTH):
            plan.append(("ab" if l % 2 == 0 else "na", l))
            plan.append(("ffn", l))
    nc = bass.Bass("TRN2", target_bir_lowering=False)
    c = setup_ctx(nc)
    P = c.P
    x = nc.dram_tensor("x", [ntok, D], F32, kind="ExternalInput").ap()
    out = nc.dram_tensor("out", [ntok, D], F32, kind="ExternalOutput").ap()
    ident = nc.dram_tensor("ident", [2, 128, 128], F32, kind="ExternalInput").ap()
    wg = nc.dram_tensor("ffn_wg", [DEPTH, NM, 128, 8, 128], F32, kind="ExternalInput").ap()
    wu = nc.dram_tensor("ffn_wu", [DEPTH, NM, 128, 8, 128], F32, kind="ExternalInput").ap()
    wd = nc.dram_tensor("ffn_wd", [DEPTH, FF, D], F32, kind="ExternalInput").ap()
    fnorm = nc.dram_tensor("ffn_norm", [DEPTH, D], F32, kind="ExternalInput").ap()
    mnorm = nc.dram_tensor("mix_norm", [DEPTH, D], F32, kind="ExternalInput").ap()
    na_wqkv = nc.dram_tensor("na_wqkv", [2, 4, 128, 8, 768], F32, kind="ExternalInput").ap()
    na_wout = nc.dram_tensor("na_wout", [2, 128, 8, 1024], F32, kind="ExternalInput").ap()
    na_bias = nc.dram_tensor("na_bias", [2, 4, 128, 4, 14, 64], F32, kind="ExternalInput").ap()
    na_qkg = nc.dram_tensor("na_qkg", [2, 128, 2], F32, kind="ExternalInput").ap()
    abd = {}
    for nm, shp in AB_SHAPES.items():
        abd[nm] = nc.dram_tensor("ab_" + nm, list(shp), F32, kind="ExternalInput").ap()
    scr = [nc.dram_tensor(f"scr{i}", [ntok, D], F32, kind="Internal").ap() for i in range(2)]
    if DBG.get('dump_on'):
        DBG['dump'] = nc.dram_tensor("dbg", [128, 16384], F32, kind="ExternalOutput").ap()
    s_scr = [P.slot("scr0"), P.slot("scr1")]
    s_x = P.slot("x_dram")
    s_out = P.slot("out_dram")
    load_ident(c, ident)
    cur, s_cur = x, s_x
    for si, st in enumerate(plan):
        if si == len(plan) - 1:
            dst, s_dst = out, s_out
        else:
            dst, s_dst = scr[si % 2], s_scr[si % 2]
        kind, l = st
        if kind == "ffn":
            ffn_stage(c, cur, s_cur, dst, s_dst, wg[l], wu[l], wd[l], fnorm[l:l + 1, :], ntok=ntok)
        elif kind == "na":
            i = l // 2
            for sq_ in range(ntok // SEQ):
                na_stage(c, cur, s_cur, dst, s_dst, na_wqkv[i], na_wout[i], na_bias[i], na_qkg[i],
                         mnorm[l:l + 1, :], sq_ * SEQ)
        elif kind == "ab":
            i = l // 2
            Wd_ = {k: (v[i] if k in AB_PER_LAYER else v) for k, v in abd.items()}
            ab_stage(c, cur, s_cur, dst, s_dst, Wd_, mnorm[l:l + 1, :], ntok)
        cur, s_cur = dst, s_dst
    fin = Ins("sp", None, False, None, 0)
    for e in ENGS:
        for i in P.streams[e]:
            if i.is_dma:
                fin.deps.append(i)
    fin.idx = len(P.streams["sp"])
    P.streams["sp"].append(fin)
    P.emit()
    return nc


AB_PER_LAYER = ("win", "wout", "gluw", "vec", "cw", "lam", "B", "C", "dcol")
AB_SHAPES = {
    "win": (2, 12, 128, 8, 128), "wout": (2, 128, 8, 1024), "gluw": (2, 128, 4, 512), "vec": (2, 128, 20),
    "cw": (2, 128, 4, 31), "lam": (2, 128, 3, 32), "B": (2, 128, 2, 32, 16), "C": (2, 128, 2, 32, 16),
    "dcol": (2, 128, 32), "selin": (128, 2, 8, 128), "selout": (128, 8, 8, 128), "mask": (128, 2, 128),
    "etab": (128, 32), "ones": (128, 128),
}


def ab_host_layout(inputs):
    g = np.ascontiguousarray
    f = lambda k: np.asarray(inputs[k], dtype=np.float32)
    d = {}
    d["win"] = g(f("ab_w_in").reshape(2, 8, 128, 12, 128).transpose(0, 3, 2, 1, 4))
    d["wout"] = g(f("ab_w_out").reshape(2, 8, 128, 1024).transpose(0, 2, 1, 3))
    d["gluw"] = g(f("ssm_glu_w").reshape(2, 4, 128, 512).transpose(0, 2, 1, 3))
    vec = np.zeros((2, 128, 20), np.float32)
    for j, k in enumerate(("conv_b", "conv_ln_g", "conv_ln_b", "ssm_glu_b")):
        vec[:, :, 4 * j:4 * j + 4] = f(k).reshape(2, 4, 128).transpose(0, 2, 1)
    d["vec"] = vec
    d["cw"] = g(f("conv_w").reshape(2, 31, 4, 128).transpose(0, 3, 2, 1))
    lam = np.empty((2, 2, 64, 3, 32), np.float32)
    lam[:, :, :, 0] = f("ssm_lambda_re").transpose(0, 1, 3, 2)
    lam[:, :, :, 1] = f("ssm_lambda_im").transpose(0, 1, 3, 2)
    lam[:, :, :, 2] = f("ssm_log_step")[:, :, None, :]
    d["lam"] = g(lam.reshape(2, 128, 3, 32))
    B = np.stack([f("ssm_b_re"), f("ssm_b_im")], axis=1)
    d["B"] = g(B.transpose(0, 2, 4, 1, 3, 5).reshape(2, 128, 2, 32, 16))
    C = np.stack([f("ssm_c_re"), f("ssm_c_im")], axis=1)
    d["C"] = g(C.transpose(0, 2, 5, 1, 3, 4).reshape(2, 128, 2, 32, 16))
    dsk = f("ssm_d").reshape(2, 32, 16)
    d["dcol"] = g(np.broadcast_to(dsk.transpose(0, 2, 1)[:, None], (2, 8, 16, 32)).reshape(2, 128, 32))
    selin = np.zeros((4, 2, 16, 2, 8, 8, 16), np.float32)
    selout = np.zeros((8, 16, 8, 8, 8, 16), np.float32)
    for cc in range(16):
        for t in range(8):
            selin[:, 0, cc, 0, t, t, cc] = 1.0
            selin[:, 1, cc, 1, t, t, cc] = 1.0
            for g8 in range(8):
                selout[t, cc, t, g8, g8, cc] = 1.0
    d["selin"] = selin.reshape(128, 2, 8, 128)
    d["selout"] = selout.reshape(128, 8, 8, 128)
    tp = np.repeat(np.arange(8), 16)
    d["mask"] = g(np.stack([(tp[:, None] <= tp[None, :]), (tp[:, None] >= tp[None, :])], axis=1).astype(np.float32))
    tt = np.arange(8, dtype=np.float32)
    ef = np.concatenate([-tt, tt, tt + 1, 7 - tt])
    eb = np.concatenate([tt, -tt, 8 - tt, tt])
    d["etab"] = g(np.concatenate([np.broadcast_to(ef, (64, 32)), np.broadcast_to(eb, (64, 32))], axis=0))
    d["ones"] = np.full((128, 128), 1.0 / 512, np.float32)
    return {"ab_" + k: v for k, v in d.items()}


def host_layout(inputs):
    g = np.ascontiguousarray
    d = {}
    bd = np.zeros((128, 128), np.float32)
    bd[:64, :64] = 1.0
    bd[64:, 64:] = 1.0
    d["ident"] = np.stack([np.eye(128, dtype=np.float32), bd])
    for nm, key in (("ffn_wg", "ffn_w_gate"), ("ffn_wu", "ffn_w_up")):
        w = np.asarray(inputs[key], dtype=np.float32).reshape(DEPTH, 8, 128, NM, 128)
        d[nm] = g(w.transpose(0, 3, 2, 1, 4))
    d["ffn_wd"] = g(np.asarray(inputs["ffn_w_down"], dtype=np.float32))
    d["ffn_norm"] = g(np.asarray(inputs["ffn_norm"], dtype=np.float32))
    d["mix_norm"] = g(np.asarray(inputs["mix_norm"], dtype=np.float32))
    wqkv = np.asarray(inputs["na_w_qkv"], dtype=np.float32).reshape(2, 8, 128, 3, 4, 256)
    d["na_wqkv"] = g(wqkv.transpose(0, 4, 2, 1, 3, 5).reshape(2, 4, 128, 8, 768))
    d["na_wout"] = g(np.asarray(inputs["na_w_out"], dtype=np.float32).reshape(2, 8, 128, 1024).transpose(0, 2, 1, 3))
    rpb = np.asarray(inputs["na_rpb"], dtype=np.float32)
    qc = np.arange(64)[None, :]
    kc = np.arange(64)[:, None]
    cidx = np.clip(kc - qc, -15, 15) + 15
    cstart = np.clip(qc - 8, 0, 48)
    cmask = (kc >= cstart) & (kc < cstart + 16)
    tab = np.empty((2, 16, 14, 2, 64, 64), np.float32)
    for rho in range(14):
        for jj in range(2):
            tab[:, :, rho, jj] = np.where(cmask[None, None], rpb[:, :, rho + jj][:, :, cidx], np.float32(-30000.0))
    tab = tab.reshape(2, 4, 4, 14, 2, 64, 64).transpose(0, 1, 4, 5, 2, 3, 6).reshape(2, 4, 128, 4, 14, 64)
    d["na_bias"] = g(tab)
    qg = np.asarray(inputs["na_q_norm"], dtype=np.float32)
    kg = np.asarray(inputs["na_k_norm"], dtype=np.float32)
    d["na_qkg"] = g(np.stack([np.tile(qg, (1, 2)), np.tile(kg, (1, 2))], axis=-1))
    d.update(ab_host_layout(inputs))
    return d


def kernel(**inputs):
    x = np.asarray(inputs["x"], dtype=np.float32)
    shared = host_layout(inputs)
    nc = build_program()
    in_maps = []
    for i in range(NCORES):
        m = dict(shared)
        m["x"] = np.ascontiguousarray(x[2 * i:2 * i + 2].reshape(TOK, D))
        in_maps.append(m)
    res = run_bass_kernel_spmd(nc, in_maps, core_ids=list(range(NCORES)))
    outs = [res.results[i]["out"].reshape(2, SEQ, D) for i in range(NCORES)]
    return np.concatenate(outs, axis=0).astype(np.float32)
```
